# Optimizing a Trainium2 kernel written in Bass

```python
import math
import jax
import jax.numpy as jnp
from jax import lax
import numpy as np

D_MODEL = 1024
BATCH = 8
SEQ = 4096
DEPTH = 4
DEC_BATCH = 32
DEC_SEQ = 16
PAST_LEN = 2048

CHUNK = 64
Q_BLOCK = 128
HEAD_DIM = 64
D_MIX = D_MODEL
H_RWKV = D_MIX // (4 * HEAD_DIM)
H_FOX = D_MIX // (2 * HEAD_DIM)
H_GDN = D_MIX // (4 * HEAD_DIM)
D_RWKV = H_RWKV * HEAD_DIM
D_FOX = H_FOX * HEAD_DIM
D_GDN = H_GDN * HEAD_DIM
RWKV_DECAY_RANK = 64
RWKV_ICL_RANK = 64
N_SHIFT = 3 * D_RWKV + RWKV_DECAY_RANK + RWKV_ICL_RANK
GDN_CONV = 4
N_CONV = 3 * D_GDN
SPLIT = [N_SHIFT, D_RWKV,
         D_FOX, D_FOX, D_FOX, H_FOX, D_FOX,
         N_CONV, H_GDN, H_GDN, D_GDN]
N_IN = sum(SPLIT)
ALPHA = (2 * DEPTH) ** 0.25
OUT_SCALE = (8 * DEPTH) ** -0.25
LN_EPS = 1e-5
RWKV_GN_EPS = 64e-5
GDN_NORM_EPS = 1e-6
L2_EPS = 1e-6

kernel_name = "hybrid_rwkv7_fox_gdn_stream_step"


def split_cols(t, sizes):
    offs, acc = [], 0
    for s in sizes[:-1]:
        acc += s
        offs.append(acc)
    return jnp.split(t, offs, axis=-1)


def layer_norm(x, g, b):
    xf = x.astype(jnp.float32)
    mu = jnp.mean(xf, -1, keepdims=True)
    var = jnp.mean(jnp.square(xf - mu), -1, keepdims=True)
    return ((xf - mu) * lax.rsqrt(var + LN_EPS) * g.astype(jnp.float32) + b.astype(jnp.float32)).astype(x.dtype)


def l2norm(t):
    return t * lax.rsqrt(jnp.sum(t * t, -1, keepdims=True) + L2_EPS)


def rwkv_mix(p_sh, z, prev_row, s0, lp):
    bsz, t_len, _ = p_sh.shape
    f32 = jnp.float32
    prev = jnp.concatenate([prev_row[:, None, :].astype(p_sh.dtype), p_sh[:, :-1]], axis=1)
    xs = (p_sh + (prev - p_sh) * lp["rwkv_mu"]).astype(f32)
    r, k, v, w_lo, a_lo = split_cols(xs, [D_RWKV, D_RWKV, D_RWKV, RWKV_DECAY_RANK, RWKV_ICL_RANK])
    w_ll = -jax.nn.softplus(-(lp["rwkv_w0"].astype(f32) + jnp.tanh(w_lo) @ lp["rwkv_w2"].astype(f32))) - 0.5
    decay = jnp.exp(-jnp.exp(w_ll))
    a = jax.nn.sigmoid(lp["rwkv_a0"].astype(f32) + a_lo @ lp["rwkv_a2"].astype(f32))
    hd = lambda t: t.reshape(bsz, t_len, H_RWKV, HEAD_DIM)
    r, k, v, decay, a = hd(r), hd(k), hd(v), hd(decay), hd(a)
    kk = k * lp["rwkv_k_k"].astype(f32).reshape(H_RWKV, HEAD_DIM)
    kk = kk / jnp.maximum(jnp.sqrt(jnp.sum(kk * kk, -1, keepdims=True)), 1e-12)
    k = k * (1.0 + (a - 1.0) * lp["rwkv_k_a"].astype(f32).reshape(H_RWKV, HEAD_DIM))

    def step(s, inp):
        r_t, k_t, v_t, w_t, kk_t, a_t = inp
        s_kk = jnp.einsum('bhvk,bhk->bhv', s, kk_t)
        s = (s * w_t[:, :, None, :] - s_kk[..., None] * (kk_t * a_t)[:, :, None, :]
             + v_t[..., None] * k_t[:, :, None, :])
        return s, jnp.einsum('bhvk,bhk->bhv', s, r_t)

    seq = tuple(jnp.moveaxis(t, 1, 0) for t in (r, k, v, decay, kk, a))
    s_fin, o = lax.scan(step, s0.astype(f32), seq)
    o = jnp.moveaxis(o, 0, 1)
    mean = jnp.mean(o, -1, keepdims=True)
    var = jnp.mean(jnp.square(o - mean), -1, keepdims=True)
    o = (((o - mean) * lax.rsqrt(var + RWKV_GN_EPS)).reshape(bsz, t_len, D_RWKV)
         * lp["rwkv_gn_g"].astype(f32) + lp["rwkv_gn_b"].astype(f32))
    bonus = jnp.sum(r * k * lp["rwkv_r_k"].astype(f32), -1, keepdims=True) * v
    o = (o + bonus.reshape(bsz, t_len, D_RWKV)) * jax.nn.silu(z.astype(f32))
    return o.astype(p_sh.dtype), p_sh[:, -1], s_fin


def fox_attend(q, cq, qpos, k, v, ck, kpos):
    f32 = jnp.float32
    s = jnp.einsum('bqhd,bkhd->bhqk', q.astype(f32), k.astype(f32)) * (HEAD_DIM ** -0.5)
    s = s + jnp.transpose(cq, (0, 2, 1))[..., :, None] - jnp.transpose(ck, (0, 2, 1))[..., None, :]
    s = jnp.where(kpos[None, :] <= qpos[:, None], s, -jnp.inf)
    p = jax.nn.softmax(s, axis=-1)
    return jnp.einsum('bhqk,bkhd->bqhd', p, v.astype(f32))


def fox_prompt(q, k, v, logf):
    bsz, t_len, n_h, d = q.shape
    c = jnp.cumsum(logf, axis=1)
    kpos = jnp.arange(t_len)

    def block(i):
        s0 = i * Q_BLOCK
        qb = lax.dynamic_slice_in_dim(q, s0, Q_BLOCK, axis=1)
        cb = lax.dynamic_slice_in_dim(c, s0, Q_BLOCK, axis=1)
        return fox_attend(qb, cb, s0 + jnp.arange(Q_BLOCK), k, v, c, kpos)

    o = lax.map(block, jnp.arange(t_len // Q_BLOCK))
    return jnp.moveaxis(o, 0, 1).reshape(bsz, t_len, n_h, d)


def fox_cached(q, k, v, logf, cache_k, cache_v, cache_logf):
    past, t_len = cache_k.shape[1], q.shape[1]
    k_all = jnp.concatenate([cache_k.astype(k.dtype), k], axis=1)
    v_all = jnp.concatenate([cache_v.astype(v.dtype), v], axis=1)
    c = jnp.cumsum(jnp.concatenate([cache_logf.astype(jnp.float32), logf], axis=1), axis=1)
    return fox_attend(q, c[:, past:], past + jnp.arange(t_len), k_all, v_all, c, jnp.arange(past + t_len))


def gated_delta_chunked(q, k, v, beta, g, s0):
    bsz, t_len, n_h, d = q.shape
    c = min(CHUNK, t_len)
    n = t_len // c

    def chunks(t):
        t = t.reshape((bsz, n, c, n_h) + t.shape[3:])
        return jnp.moveaxis(t, (1, 3), (0, 2))

    q, k, v, beta, g = chunks(q), chunks(k), chunks(v), chunks(beta), chunks(g)
    gc = jnp.cumsum(g, axis=-1)
    incl = jnp.tril(jnp.ones((c, c), bool))
    strict = jnp.tril(jnp.ones((c, c), bool), -1)
    diff = gc[..., :, None] - gc[..., None, :]
    dmat = jnp.where(incl, jnp.exp(jnp.where(incl, diff, 0.0)), 0.0)
    kb = k * beta[..., None]
    m = jnp.where(strict, jnp.einsum('...id,...jd->...ij', kb, k) * dmat, 0.0)
    a_mat = m + jnp.eye(c, dtype=m.dtype)
    rhs = jnp.concatenate([v * beta[..., None], kb * jnp.exp(gc)[..., None]], axis=-1)
    sol = lax.linalg.triangular_solve(a_mat, rhs, left_side=True, lower=True, unit_diagonal=True)
    u, w = sol[..., :d], sol[..., d:]
    qk = jnp.einsum('...id,...jd->...ij', q, k) * dmat
    qg = q * jnp.exp(gc)[..., None]
    kd = k * jnp.exp(gc[..., -1:] - gc)[..., None]
    gl = jnp.exp(gc[..., -1])

    def step(s, xs):
        u_c, w_c, qk_c, qg_c, kd_c, gl_c = xs
        v_new = u_c - jnp.einsum('bhcd,bhde->bhce', w_c, s)
        o = jnp.einsum('bhcd,bhde->bhce', qg_c, s) + jnp.einsum('bhij,bhje->bhie', qk_c, v_new)
        s = s * gl_c[..., None, None] + jnp.einsum('bhcd,bhce->bhde', kd_c, v_new)
        return s, o

    s_fin, o = lax.scan(step, s0, (u, w, qk, qg, kd, gl))
    o = jnp.moveaxis(o, (0, 2), (1, 3)).reshape(bsz, t_len, n_h, d)
    return o, s_fin


def gdn_mix(p_qkv, b_col, a_col, z, conv_prev, s0, lp):
    bsz, t_len, _ = p_qkv.shape
    f32 = jnp.float32
    conv_w = lp["gdn_conv_w"]
    xp = jnp.concatenate([conv_prev.astype(p_qkv.dtype), p_qkv], axis=1)
    y = xp[:, 0:t_len] * conv_w[0]
    for i in range(1, GDN_CONV):
        y = y + xp[:, i:i + t_len] * conv_w[i]
    y = jax.nn.silu(y.astype(f32))
    q, k, v = [t.reshape(bsz, t_len, H_GDN, HEAD_DIM) for t in split_cols(y, [D_GDN, D_GDN, D_GDN])]
    q = l2norm(q) * (HEAD_DIM ** -0.5)
    k = l2norm(k)
    beta = jax.nn.sigmoid(b_col.astype(f32))
    g = -jnp.exp(lp["gdn_a_log"].astype(f32)) * jax.nn.softplus(a_col.astype(f32) + lp["gdn_dt_bias"].astype(f32))
    o, s_fin = gated_delta_chunked(q, k, v, beta, g, s0.astype(f32))
    o = o * lax.rsqrt(jnp.mean(o * o, -1, keepdims=True) + GDN_NORM_EPS) * lp["gdn_norm_g"].astype(f32)
    o = o.reshape(bsz, t_len, D_GDN) * jax.nn.silu(z.astype(f32))
    return o.astype(p_qkv.dtype), xp[:, t_len:], s_fin


def trunk_layer(x, lp, rwkv_prev, rwkv_s0, gdn_conv_prev, gdn_s0, fox_cache):
    bsz, t_len, _ = x.shape
    f32 = jnp.float32
    proj = x @ lp["w_in"]
    a_sh, a_z, b_q, b_k, b_v, b_f, b_z, c_qkv, c_b, c_a, c_z = split_cols(proj, SPLIT)
    o_a, rwkv_last, rwkv_s = rwkv_mix(a_sh, a_z, rwkv_prev, rwkv_s0, lp)
    heads = lambda t: t.reshape(bsz, t_len, H_FOX, HEAD_DIM)
    q, k, v = heads(b_q), heads(b_k), heads(b_v)
    logf = jax.nn.log_sigmoid(b_f.astype(f32) + lp["fox_b_f"].astype(f32))
    if fox_cache is None:
        o_b = fox_prompt(q, k, v, logf)
    else:
        o_b = fox_cached(q, k, v, logf, *fox_cache)
    o_b = (o_b.reshape(bsz, t_len, D_FOX) * jax.nn.silu(b_z.astype(f32))).astype(x.dtype)
    o_c, conv_last, gdn_s = gdn_mix(c_qkv, c_b, c_a, c_z, gdn_conv_prev, gdn_s0, lp)
    h = jnp.concatenate([o_a, o_b, o_c], axis=-1) @ lp["w_out"]
    x_new = layer_norm(ALPHA * x + h, lp["ln_g"], lp["ln_b"])
    return x_new, k, v, logf, rwkv_last, rwkv_s, conv_last, gdn_s


def setup_inputs(seed: int = 0) -> dict:
    key = jax.random.key(seed)
    ks = iter(jax.random.split(key, 40))
    nrm = lambda shape, s=1.0: s * jax.random.normal(next(ks), shape, jnp.float32)
    uni = lambda shape, lo, hi: jax.random.uniform(next(ks), shape, jnp.float32, lo, hi)
    x_prompt = nrm((BATCH, SEQ, D_MODEL))
    x_sample = nrm((DEC_BATCH, DEC_SEQ, D_MODEL))
    cache_fox_k = nrm((DEPTH, DEC_BATCH, PAST_LEN, H_FOX, HEAD_DIM))
    cache_fox_v = nrm((DEPTH, DEC_BATCH, PAST_LEN, H_FOX, HEAD_DIM))
    cache_fox_logf = jax.nn.log_sigmoid(nrm((DEPTH, DEC_BATCH, PAST_LEN, H_FOX)) + 3.0)
    state_rwkv_shift = nrm((DEPTH, DEC_BATCH, N_SHIFT))
    state_rwkv_wkv = nrm((DEPTH, DEC_BATCH, H_RWKV, HEAD_DIM, HEAD_DIM), 0.3)
    state_gdn_conv = nrm((DEPTH, DEC_BATCH, GDN_CONV - 1, N_CONV))
    state_gdn_wkv = nrm((DEPTH, DEC_BATCH, H_GDN, HEAD_DIM, HEAD_DIM), 0.3)
    dt = jnp.exp(uni((DEPTH, H_GDN), math.log(1e-3), math.log(1e-1)))
    return {
        "x_prompt": x_prompt,
        "x_sample": x_sample,
        "cache_fox_k": cache_fox_k,
        "cache_fox_v": cache_fox_v,
        "cache_fox_logf": cache_fox_logf,
        "state_rwkv_shift": state_rwkv_shift,
        "state_rwkv_wkv": state_rwkv_wkv,
        "state_gdn_conv": state_gdn_conv,
        "state_gdn_wkv": state_gdn_wkv,
        "ln_in_g": 1.0 + nrm((D_MODEL,), 0.02),
        "ln_in_b": nrm((D_MODEL,), 0.02),
        "w_in": nrm((DEPTH, D_MODEL, N_IN), D_MODEL ** -0.5),
        "rwkv_mu": uni((DEPTH, N_SHIFT), 0.0, 1.0),
        "rwkv_w0": uni((DEPTH, D_RWKV), -6.0, -1.0),
        "rwkv_w2": nrm((DEPTH, RWKV_DECAY_RANK, D_RWKV), 0.5 * RWKV_DECAY_RANK ** -0.5),
        "rwkv_a0": nrm((DEPTH, D_RWKV), 0.1),
        "rwkv_a2": nrm((DEPTH, RWKV_ICL_RANK, D_RWKV), 0.5 * RWKV_ICL_RANK ** -0.5),
        "rwkv_k_k": 0.85 + nrm((DEPTH, D_RWKV), 0.05),
        "rwkv_k_a": 1.0 + nrm((DEPTH, D_RWKV), 0.05),
        "rwkv_r_k": nrm((DEPTH, H_RWKV, HEAD_DIM), 0.1),
        "rwkv_gn_g": 1.0 + nrm((DEPTH, D_RWKV), 0.02),
        "rwkv_gn_b": nrm((DEPTH, D_RWKV), 0.02),
        "fox_b_f": uni((DEPTH, H_FOX), 1.0, 5.0),
        "gdn_conv_w": nrm((DEPTH, GDN_CONV, N_CONV), GDN_CONV ** -0.5),
        "gdn_a_log": jnp.log(uni((DEPTH, H_GDN), 1.0, 16.0)),
        "gdn_dt_bias": dt + jnp.log(-jnp.expm1(-dt)),
        "gdn_norm_g": 1.0 + nrm((DEPTH, HEAD_DIM), 0.02),
        "w_out": nrm((DEPTH, D_MIX, D_MODEL), OUT_SCALE * D_MIX ** -0.5),
        "ln_post_g": 1.0 + nrm((DEPTH, D_MODEL), 0.02),
        "ln_post_b": nrm((DEPTH, D_MODEL), 0.02),
    }


def reference(x_prompt, x_sample, cache_fox_k, cache_fox_v, cache_fox_logf, state_rwkv_shift, state_rwkv_wkv,
              state_gdn_conv, state_gdn_wkv, ln_in_g, ln_in_b, w_in, rwkv_mu, rwkv_w0, rwkv_w2, rwkv_a0, rwkv_a2,
              rwkv_k_k, rwkv_k_a, rwkv_r_k, rwkv_gn_g, rwkv_gn_b, fox_b_f, gdn_conv_w, gdn_a_log, gdn_dt_bias,
              gdn_norm_g, w_out, ln_post_g, ln_post_b):
    xp = layer_norm(x_prompt, ln_in_g, ln_in_b)
    xs = layer_norm(x_sample, ln_in_g, ln_in_b)
    bp = xp.shape[0]
    fk_p, fv_p, fl_p, sh_p, rw_p, cv_p, gd_p = [], [], [], [], [], [], []
    fk_s, fv_s, fl_s, sh_s, rw_s, cv_s, gd_s = [], [], [], [], [], [], []
    for l in range(DEPTH):
        lp = dict(w_in=w_in[l], rwkv_mu=rwkv_mu[l], rwkv_w0=rwkv_w0[l], rwkv_w2=rwkv_w2[l], rwkv_a0=rwkv_a0[l],
                  rwkv_a2=rwkv_a2[l], rwkv_k_k=rwkv_k_k[l], rwkv_k_a=rwkv_k_a[l], rwkv_r_k=rwkv_r_k[l],
                  rwkv_gn_g=rwkv_gn_g[l], rwkv_gn_b=rwkv_gn_b[l], fox_b_f=fox_b_f[l], gdn_conv_w=gdn_conv_w[l],
                  gdn_a_log=gdn_a_log[l], gdn_dt_bias=gdn_dt_bias[l], gdn_norm_g=gdn_norm_g[l], w_out=w_out[l],
                  ln_g=ln_post_g[l], ln_b=ln_post_b[l])
        xp, k, v, lf, sh, rw, cv, gd = trunk_layer(
            xp, lp,
            jnp.zeros((bp, N_SHIFT), xp.dtype),
            jnp.zeros((bp, H_RWKV, HEAD_DIM, HEAD_DIM), jnp.float32),
            jnp.zeros((bp, GDN_CONV - 1, N_CONV), xp.dtype),
            jnp.zeros((bp, H_GDN, HEAD_DIM, HEAD_DIM), jnp.float32),
            None)
        fk_p.append(k); fv_p.append(v); fl_p.append(lf); sh_p.append(sh); rw_p.append(rw); cv_p.append(cv); gd_p.append(gd)
        xs, k, v, lf, sh, rw, cv, gd = trunk_layer(
            xs, lp, state_rwkv_shift[l], state_rwkv_wkv[l], state_gdn_conv[l], state_gdn_wkv[l],
            (cache_fox_k[l], cache_fox_v[l], cache_fox_logf[l]))
        fk_s.append(k); fv_s.append(v); fl_s.append(lf); sh_s.append(sh); rw_s.append(rw); cv_s.append(cv); gd_s.append(gd)
    return (xp, xs,
            jnp.stack(fk_p), jnp.stack(fv_p), jnp.stack(fl_p), jnp.stack(sh_p), jnp.stack(rw_p),
            jnp.stack(cv_p), jnp.stack(gd_p),
            jnp.stack(fk_s), jnp.stack(fv_s), jnp.stack(fl_s), jnp.stack(sh_s), jnp.stack(rw_s),
            jnp.stack(cv_s), jnp.stack(gd_s))
```

```python
import math
from contextlib import ExitStack
import numpy as np
import concourse.bass as bass
import concourse.mybir as mybir
from concourse.bass_utils import run_bass_kernel_spmd

F32 = mybir.dt.float32
BF = mybir.dt.bfloat16
AF = mybir.ActivationFunctionType
ALU = mybir.AluOpType

D_MODEL = 1024
L = 4
N_IN = 4240
OFF = dict(a_sh=0, a_z=896, b_q=1152, b_k=1664, b_v=2176, b_f=2688, b_z=2696,
           c_qkv=3208, c_b=3976, c_a=3980, c_z=3984)
ALPHA = (2 * L) ** 0.25
LN_EPS = 1e-5
RWKV_GN_EPS = 64e-5
GDN_NORM_EPS = 1e-6
L2_EPS = 1e-6
import os
NCORE = int(os.environ.get("KCORES", "8"))
BSTAGE = int(os.environ.get("BSTAGE", "9"))
NS = 4
TS = 16


def pv_layout():
    off = {}
    n = 0

    def add(name, w):
        nonlocal n
        off[name] = n
        n += w
    add("ln_in_g", 8)
    add("ln_in_b", 8)
    for l in range(L):
        for nm, w in (("lng", 8), ("lnb", 8), ("mu", 7), ("w0", 2), ("a0", 2), ("kk", 2), ("ka", 2),
                      ("rk", 2), ("gng", 2), ("gnb", 2), ("cw", 24), ("gnorm", 1), ("bf", 1),
                      ("alog", 1), ("dtb", 1)):
            add("%s%d" % (nm, l), w)
    return off, n


PVO, NPV = pv_layout()


def cst_layout():
    off = {}
    n = 0
    for nm, w in (("ident", 128), ("onesbd", 128), ("ones", 128), ("tri", 128), ("msu", 128), ("miu", 128),
                  ("reset", 256), ("selb", 256), ("selg", 256), ("selh", 512)):
        off[nm] = n
        n += w
    return off, n


CSO, NCST = cst_layout()


def make_cst():
    c = np.zeros((128, NCST), np.float32)
    p = np.arange(128)[:, None]
    f = np.arange(128)[None, :]
    c[:, CSO["ident"]:CSO["ident"] + 128] = (p == f)
    c[:, CSO["onesbd"]:CSO["onesbd"] + 128] = (p // 64 == f // 64)
    c[:, CSO["ones"]:CSO["ones"] + 128] = 1.0
    c[:, CSO["tri"]:CSO["tri"] + 128] = (p <= f)
    c[:, CSO["msu"]:CSO["msu"] + 128] = (p < f) & (p // 64 == f // 64)
    c[:, CSO["miu"]:CSO["miu"] + 128] = (p <= f) & (p // 64 == f // 64)
    r = np.ones(256, np.float32)
    r[::64] = 0
    c[:, CSO["reset"]:CSO["reset"] + 256] = r[None, :]
    selb = np.zeros((128, 2, 128), np.float32)
    selg = np.zeros((128, 2, 128), np.float32)
    for pr in range(2):
        for hh in range(2):
            selb[2 * pr + hh, pr, hh * 64:(hh + 1) * 64] = 1
            selg[32 + 2 * pr + hh, pr, hh * 64:(hh + 1) * 64] = 1
    c[:, CSO["selb"]:CSO["selb"] + 256] = selb.reshape(128, 256)
    c[:, CSO["selg"]:CSO["selg"] + 256] = selg.reshape(128, 256)
    selh = np.zeros((128, 4, 128), np.float32)
    for h in range(4):
        selh[32 + h, h, :] = 1
    c[:, CSO["selh"]:CSO["selh"] + 512] = selh.reshape(128, 512)
    return c


def make_pv(inp):
    pv = np.zeros((128, NPV), np.float32)

    def put(name, vec, w):
        pv[:, PVO[name]:PVO[name] + w] = np.asarray(vec, np.float32).reshape(w, 128).T
    put("ln_in_g", inp["ln_in_g"], 8)
    put("ln_in_b", inp["ln_in_b"], 8)
    for l in range(L):
        put("lng%d" % l, inp["ln_post_g"][l], 8)
        put("lnb%d" % l, inp["ln_post_b"][l], 8)
        put("mu%d" % l, inp["rwkv_mu"][l], 7)
        put("w0%d" % l, inp["rwkv_w0"][l], 2)
        put("a0%d" % l, inp["rwkv_a0"][l], 2)
        put("kk%d" % l, inp["rwkv_k_k"][l], 2)
        put("ka%d" % l, inp["rwkv_k_a"][l], 2)
        put("rk%d" % l, np.asarray(inp["rwkv_r_k"][l]).reshape(256), 2)
        put("gng%d" % l, inp["rwkv_gn_g"][l], 2)
        put("gnb%d" % l, inp["rwkv_gn_b"][l], 2)
        cw = np.asarray(inp["gdn_conv_w"][l], np.float32)
        for i in range(4):
            pv[:, PVO["cw%d" % l] + i * 6:PVO["cw%d" % l] + i * 6 + 6] = cw[i].reshape(6, 128).T
        gn = np.asarray(inp["gdn_norm_g"][l], np.float32)
        pv[:, PVO["gnorm%d" % l]] = np.concatenate([gn, gn])
        bfv = np.asarray(inp["fox_b_f"][l], np.float32)
        for g in (0, 32, 64):
            pv[g:g + 8, PVO["bf%d" % l]] = bfv
        pv[32:36, PVO["alog%d" % l]] = np.asarray(inp["gdn_a_log"][l], np.float32)
        pv[32:36, PVO["dtb%d" % l]] = np.asarray(inp["gdn_dt_bias"][l], np.float32)
    return pv


class Res:
    __slots__ = ("w", "r", "psum")

    def __init__(self):
        self.w = None
        self.r = {}
        self.psum = False


class Eng:
    def __init__(self, name, unit):
        self.name = name
        self.unit = unit
        self.n = 0
        self.known = {}
        self.hist = {}
        self.ops = []
        self.sem = None


class MK:
    NDQ = 12
    CE = ("pe", "act", "dve", "pool", "sp")

    def __init__(self):
        self.E = {}
        for nm in self.CE:
            self.E[nm] = Eng(nm, 1)
        for i in range(self.NDQ):
            self.E["dq%d" % i] = Eng("dq%d" % i, 16)
        self.dq_rr = 0
        self.n_wait = 0
        self.n_ins = 0

    def _need(self, W, reads, writes, is_dma=False):
        toks = {}

        def add(tok, same_ok):
            if tok is None:
                return
            e, n = tok
            if same_ok and e is W and W.name == "pe" and not is_dma:
                return
            if W.known.get(e.name, 0) >= n:
                return
            if toks.get(e.name, (None, 0))[1] < n:
                toks[e.name] = (e, n)

        for r in reads:
            add(r.w, False)
            if r.psum:
                for t in r.r.values():
                    if t[0] is not W:
                        add(t, True)
        for r in writes:
            add(r.w, True)
            for t in r.r.values():
                add(t, True)
        return list(toks.values())

    def _merge(self, W, e, n):
        snap = e.hist.get(n)
        if snap:
            for k, v in snap.items():
                if W.known.get(k, 0) < v:
                    W.known[k] = v
        if W.known.get(e.name, 0) < n:
            W.known[e.name] = n

    def _record(self, te, tn, reads, writes, snap):
        te.hist[tn] = snap
        tok = (te, tn)
        for r in reads:
            r.r[te.name] = tok
        for r in writes:
            r.w = tok
            r.r = {}

    def op(self, eng, fn, reads=(), writes=()):
        W = self.E[eng]
        waits = self._need(W, reads, writes)
        for e, n in waits:
            self._merge(W, e, n)
        W.n += 1
        n = W.n
        snap = dict(W.known)
        snap[W.name] = n
        self._record(W, n, reads, writes, snap)
        W.ops.append((fn, [(e, n_ * e.unit) for e, n_ in waits], W))
        self.n_wait += len(waits)
        self.n_ins += 1

    def dma(self, out, in_, reads=(), writes=(), queue="sp", **kw):
        Q = self.E[queue]
        k = self.dq_rr
        self.dq_rr = (self.dq_rr + 1) % self.NDQ
        Dq = self.E["dq%d" % k]
        waits = self._need(Q, reads, writes, is_dma=True)
        if Dq.n > 0 and Q.known.get(Dq.name, 0) < Dq.n:
            waits = [w for w in waits if w[0] is not Dq] + [(Dq, Dq.n)]
        for e, n in waits:
            self._merge(Q, e, n)
        Dq.n += 1
        n = Dq.n
        snap = dict(Q.known)
        snap[Dq.name] = n
        self._record(Dq, n, reads, writes, snap)

        def fn(eng, out=out, in_=in_, kw=kw):
            return eng.dma_start(out=out, in_=in_, **kw)
        Q.ops.append((fn, [(e, n_ * e.unit) for e, n_ in waits], Dq))
        self.n_wait += len(waits)
        self.n_ins += 1

    def barrier(self):
        for nm in self.CE:
            W = self.E[nm]
            waits = []
            for e in self.E.values():
                if e is W or e.n == 0:
                    continue
                if W.known.get(e.name, 0) < e.n:
                    waits.append((e, e.n))
            for e, n in waits:
                self._merge(W, e, n)
            W.ops.append((None, [(e, n * e.unit) for e, n in waits], None))

    def runner(self, sems):
        for e in self.E.values():
            e.sem = sems[e.name]

        def run(engobj, E):
            for fn, waits, inc_e in E.ops:
                for e, v in waits:
                    engobj.wait_ge(e.sem, v)
                if fn is None:
                    continue
                fn(engobj).then_inc(inc_e.sem, inc_e.unit)
        return run


class Tile:
    def __init__(self, t):
        self.t = t
        self.r = Res()

    def __getitem__(self, k):
        return self.t[k]


class _V:
    def __init__(self, t, i):
        self.t = t
        self.i = i
        self.r = t.r

    def __getitem__(self, k):
        return self.t.t[(k[0], self.i) + tuple(k[1:])]


class Scope:
    def __init__(self, kb):
        self.kb = kb
        self.st = ExitStack()
        self.pools = {}

    def __enter__(self):
        self.st.__enter__()
        return self

    def __exit__(self, *a):
        self.kb.mk.barrier()
        return self.st.__exit__(*a)

    def sb(self, name, shape, dt=F32):
        self.kb.uid += 1
        return Tile(self.st.enter_context(self.kb.nc.sbuf_tensor("%s_%d" % (name, self.kb.uid), list(shape), dt)))

    def pool(self, name, n, shape, dt=F32):
        self.pools[name] = [[self.sb(name + str(i), shape, dt) for i in range(n)], 0]

    def get(self, name):
        p = self.pools[name]
        t = p[0][p[1] % len(p[0])]
        p[1] += 1
        return t


class KB:
    def __init__(self, T, PAST, with_sample=True, nlayers=L):
        self.T = T
        self.PAST = PAST
        self.with_sample = with_sample
        self.nl = nlayers
        self.nc = bass.Bass("TRN2", target_bir_lowering=False)
        self.mk = MK()
        self.st = ExitStack()
        self.uid = 0
        self.psp = {}
        self.phases = "ABCD"

    def sb(self, name, shape, dt=F32):
        return Tile(self.st.enter_context(self.nc.sbuf_tensor(name, list(shape), dt)))

    def din(self, name, shape, dt=F32):
        return self.nc.dram_tensor(name, list(shape), dt, kind="ExternalInput").ap()

    def dout(self, name, shape, dt=F32):
        return self.nc.dram_tensor(name, list(shape), dt, kind="ExternalOutput").ap()

    def ps(self, pool="g"):
        p = self.psp[pool]
        t = p[0][p[1] % len(p[0])]
        p[1] += 1
        return t

    @staticmethod
    def _rs(xs):
        return [x.r if isinstance(x, (Tile, _V)) else x for x in xs]

    def tt(self, eng, out, in0, in1, op, R, W):
        self.mk.op(eng, lambda e: e.tensor_tensor(out=out, in0=in0, in1=in1, op=op), self._rs(R), self._rs(W))

    def ts(self, eng, out, in0, s1, op0, R, W, s2=None, op1=None):
        if op1 is None:
            self.mk.op(eng, lambda e: e.tensor_scalar(out=out, in0=in0, scalar1=s1, scalar2=None, op0=op0),
                       self._rs(R), self._rs(W))
        else:
            self.mk.op(eng, lambda e: e.tensor_scalar(out=out, in0=in0, scalar1=s1, scalar2=s2, op0=op0, op1=op1),
                       self._rs(R), self._rs(W))

    def stt(self, out, in0, scalar, in1, op0, op1, R, W):
        self.mk.op("dve", lambda e: e.scalar_tensor_tensor(out=out, in0=in0, scalar=scalar, in1=in1, op0=op0, op1=op1),
                   self._rs(R), self._rs(W))

    def cp(self, eng, out, in_, R, W):
        if eng == "act":
            self.mk.op(eng, lambda e: e.copy(out=out, in_=in_), self._rs(R), self._rs(W))
        else:
            self.mk.op(eng, lambda e: e.tensor_copy(out=out, in_=in_), self._rs(R), self._rs(W))

    def act(self, out, in_, func, R, W, bias=0.0, scale=1.0):
        self.mk.op("act", lambda e: e.activation(out=out, in_=in_, func=func, bias=bias, scale=scale),
                   self._rs(R), self._rs(W))

    def mm(self, out, lhsT, rhs, start, stop, R, W):
        self.mk.op("pe", lambda e: e.matmul(out, lhsT=lhsT, rhs=rhs, start=start, stop=stop),
                   self._rs(R), self._rs(W))

    def tr(self, out, in_, ident, R, W):
        self.mk.op("pe", lambda e: e.transpose(out, in_, ident), self._rs(R), self._rs(W))

    def recip(self, out, in_, R, W):
        self.mk.op("dve", lambda e: e.reciprocal(out=out, in_=in_), self._rs(R), self._rs(W))

    def memset(self, eng, ap, val, W):
        self.mk.op(eng, lambda e: e.memset(ap, val), [], self._rs(W))

    def scan(self, out, d0, d1, R, W):
        self.mk.op("dve", lambda e: e.tensor_tensor_scan(out=out, data0=d0, data1=d1, initial=0.0,
                                                           op0=ALU.mult, op1=ALU.add), self._rs(R), self._rs(W))

    def dma(self, out, in_, R, W, **kw):
        self.mk.dma(out, in_, self._rs(R), self._rs(W), **kw)

    def pvc(self, name, c=0, rows=slice(0, 128)):
        o = PVO[name] + c
        return self.PV[rows, o:o + 1]

    def cst(self, name, w=128, rows=slice(0, 128), c0=0):
        o = CSO[name] + c0
        return self.CST[rows, o:o + w]

    def load_w(self, src3, col_ranges):
        for (sc, w, dc) in col_ranges:
            o = 0
            while o < w:
                ww = min(64, w - o)
                stg = self.get_ws()
                self.dma(stg[:, :, 0:ww], src3[:, :, sc + o:sc + o + ww], [], [stg])
                self.cp("pool", self.WB[:, :, dc + o:dc + o + ww], stg[:, :, 0:ww], [stg], [self.WB])
                o += ww

    def get_ws(self):
        t = self.WS[self.ws_i % len(self.WS)]
        self.ws_i += 1
        return t

    def proj_fm(self, ps_ap, wcol, ncols, xt, t0, n, pst):
        for k in range(8):
            self.mm(ps_ap, self.WB[:, k, wcol:wcol + ncols], xt[:, k, t0:t0 + n], k == 0, k == 7,
                    [self.WB, xt], [pst])

    def ln_fm(self, sc, Vt, n, gname, bname, out_bf=None, out_f32=None):
        VB = sc.get("lnvb")
        VQ = sc.get("lnvq")
        self.cp("act", VB[:, :, 0:n], Vt[:, :, 0:n], [Vt], [VB])
        self.act(VQ[:, :, 0:n], Vt[:, :, 0:n], AF.Square, [Vt], [VQ])
        p1 = self.ps()
        p2 = self.ps()
        for c in range(8):
            self.mm(p1[:, 0:n], self.ONESB[:, :], VB[:, c, 0:n], c == 0, c == 7, [self.CB, VB], [p1])
        for c in range(8):
            self.mm(p2[:, 0:n], self.ONESB[:, :], VQ[:, c, 0:n], c == 0, c == 7, [self.CB, VQ], [p2])
        ME = sc.get("lnt")
        MS = sc.get("lnt")
        VA = sc.get("lnt")
        RS = sc.get("lnt")
        self.ts("dve", ME[:, 0:n], p1[:, 0:n], 1.0 / D_MODEL, ALU.mult, [p1], [ME])
        self.tt("pool", MS[:, 0:n], ME[:, 0:n], ME[:, 0:n], ALU.mult, [ME], [MS])
        self.stt(VA[:, 0:n], p2[:, 0:n], 1.0 / D_MODEL, MS[:, 0:n], ALU.mult, ALU.subtract, [p2, MS], [VA])
        self.act(RS[:, 0:n], VA[:, 0:n], AF.Ln, [VA], [RS], bias=self.EPS[:, 0:1], scale=1.0)
        self.act(RS[:, 0:n], RS[:, 0:n], AF.Exp, [RS], [RS], scale=-0.5)
        for c in range(8):
            Dd = sc.get("lnd")
            self.tt("pool", Dd[:, 0:n], Vt[:, c, 0:n], ME[:, 0:n], ALU.subtract, [Vt, ME], [Dd])
            self.tt("dve", Dd[:, 0:n], Dd[:, 0:n], RS[:, 0:n], ALU.mult, [Dd, RS], [Dd])
            if out_bf is not None:
                ap, tl = out_bf(c)
                self.act(ap, Dd[:, 0:n], AF.Identity, [Dd, self.PV], [tl],
                         bias=self.pvc(bname, c), scale=self.pvc(gname, c))
            if out_f32 is not None:
                ap, tl = out_f32(c)
                self.act(ap, Dd[:, 0:n], AF.Identity, [Dd, self.PV], [tl],
                         bias=self.pvc(bname, c), scale=self.pvc(gname, c))

    def build(self):
        nc, mk = self.nc, self.mk
        T, PAST = self.T, self.PAST
        NB = T // 128
        d = {}
        d["xp"] = self.din("xp", [T, D_MODEL])
        d["w_in"] = self.din("w_in", [L, D_MODEL, N_IN])
        d["w_out"] = self.din("w_out", [L, D_MODEL, D_MODEL])
        d["w2"] = self.din("rwkv_w2", [L, 64, 256])
        d["a2"] = self.din("rwkv_a2", [L, 64, 256])
        d["pv"] = self.din("pv", [128, NPV])
        d["cst"] = self.din("cst", [128, NCST])
        o = {}
        o["y_p"] = self.dout("y_p", [T, D_MODEL])
        o["fk_p"] = self.dout("fk_p", [L, T, 512])
        o["fv_p"] = self.dout("fv_p", [L, T, 512])
        o["fl_p"] = self.dout("fl_p", [L, T, 8])
        o["sh_p"] = self.dout("sh_p", [L, 896])
        o["rw_p"] = self.dout("rw_p", [L, 4, 64, 64])
        o["cv_p"] = self.dout("cv_p", [L, 3, 768])
        o["gd_p"] = self.dout("gd_p", [L, 4, 64, 64])
        if self.with_sample:
            d["xs"] = self.din("xs", [NS * TS, D_MODEL])
            d["ckT"] = self.din("ckT", [L, NS, 8, 64, PAST])
            d["cv"] = self.din("cv", [L, NS, PAST, 512])
            d["clf"] = self.din("clf", [L, NS, 8, PAST])
            d["st_sh"] = self.din("st_sh", [L, NS, 896])
            d["st_rw"] = self.din("st_rw", [L, NS, 4, 64, 64])
            d["st_cv"] = self.din("st_cv", [L, NS, 3, 768])
            d["st_gd"] = self.din("st_gd", [L, NS, 4, 64, 64])
            o["y_s"] = self.dout("y_s", [NS * TS, D_MODEL])
            o["fk_s"] = self.dout("fk_s", [L, NS, TS, 512])
            o["fv_s"] = self.dout("fv_s", [L, NS, TS, 512])
            o["fl_s"] = self.dout("fl_s", [L, NS, TS, 8])
            o["sh_s"] = self.dout("sh_s", [L, NS, 896])
            o["rw_s"] = self.dout("rw_s", [L, NS, 4, 64, 64])
            o["cv_s"] = self.dout("cv_s", [L, NS, 3, 768])
            o["gd_s"] = self.dout("gd_s", [L, NS, 4, 64, 64])
        self.d, self.o = d, o

        with self.st:
            self.XT = self.sb("XT", [128, 8, T], BF)
            self.OT = self.sb("OT", [128, 8, T], BF)
            self.XTs = self.sb("XTs", [128, 8, NS * TS], BF)
            self.OTs = self.sb("OTs", [128, 8, NS * TS], BF)
            self.PV = self.sb("PV", [128, NPV])
            self.CST = self.sb("CST", [128, NCST])
            self.CB = self.sb("CB", [128, 6, 128], BF)
            self.WB = self.sb("WB", [128, 8, 1024], BF)
            self.WS = [self.sb("WS%d" % i, [128, 8, 64]) for i in range(2)]
            self.ws_i = 0
            self.EPS = self.sb("EPS", [128, 4])
            self.PD = self.sb("PD", [128, L, 12])
            g = [Tile(self.st.enter_context(nc.psum_tensor("PG%d" % i, [128, 512], F32))) for i in range(5)]
            a = [Tile(self.st.enter_context(nc.psum_tensor("PA%d" % i, [128, 512], F32))) for i in range(2)]
            tb = [Tile(self.st.enter_context(nc.psum_tensor("PT%d" % i, [128, 1024], BF))) for i in range(1)]
            for t_ in g + a + tb:
                t_.r.psum = True
            self.psp = {"g": [g, 0], "a": [a, 0], "tb": [tb, 0]}

            self.IDB = self.CB[:, 0, :]
            self.ONESBD = self.CB[:, 1, :]
            self.ONESB = self.CB[:, 2, :]
            self.TRIB = self.CB[:, 3, :]
            self.MSUB = self.CB[:, 4, :]
            self.MIUB = self.CB[:, 5, :]

            self.dma(self.PV[:, :], d["pv"], [], [self.PV])
            self.dma(self.CST[:, :], d["cst"], [], [self.CST])
            for i, nm in enumerate(("ident", "onesbd", "ones", "tri", "msu", "miu")):
                self.cp("pool", self.CB[:, i, :], self.cst(nm), [self.CST], [self.CB])
            self.memset("pool", self.EPS[:, 0:1], LN_EPS, [self.EPS])
            self.memset("pool", self.EPS[:, 1:2], RWKV_GN_EPS, [self.EPS])
            self.memset("pool", self.EPS[:, 2:3], GDN_NORM_EPS, [self.EPS])
            self.memset("pool", self.EPS[:, 3:4], L2_EPS, [self.EPS])
            for l in range(self.nl):
                for c in range(2):
                    self.ts("pool", self.PD[:, l, c:c + 1], self.pvc("w0%d" % l, c), -1.0, ALU.mult, [self.PV], [self.PD])
                    self.ts("pool", self.PD[:, l, 2 + c:3 + c], self.pvc("a0%d" % l, c), -1.0, ALU.mult, [self.PV], [self.PD])
                    self.ts("pool", self.PD[:, l, 4 + c:5 + c], self.pvc("ka%d" % l, c), -1.0, ALU.mult, [self.PV], [self.PD],
                            s2=1.0, op1=ALU.add)
                self.ts("pool", self.PD[:, l, 6:7], self.pvc("bf%d" % l), -1.0, ALU.mult, [self.PV], [self.PD])
                self.act(self.PD[:, l, 7:8], self.pvc("alog%d" % l), AF.Exp, [self.PV], [self.PD])
                self.ts("pool", self.PD[:, l, 7:8], self.PD[:, l, 7:8], -1.0, ALU.mult, [self.PD], [self.PD])
            mk.barrier()

            self.phase0()
            for l in range(self.nl):
                self.layer(l)
            mk.barrier()

            sems = {}
            for nm in mk.E:
                sems[nm] = self.st.enter_context(nc.semaphore("s_" + nm))
            run = mk.runner(sems)
            with nc.Block() as block:
                @block.tensor
                def _(e):
                    run(e, mk.E["pe"])

                @block.vector
                def _(e):
                    run(e, mk.E["dve"])

                @block.scalar
                def _(e):
                    run(e, mk.E["act"])

                @block.gpsimd
                def _(e):
                    run(e, mk.E["pool"])

                @block.sync
                def _(e):
                    run(e, mk.E["sp"])
        return nc

    def ln_pools(self, sc, n):
        sc.pool("v", 1, [128, 8, n])
        sc.pool("lnvb", 1, [128, 8, n], BF)
        sc.pool("lnvq", 1, [128, 8, n], BF)
        sc.pool("lnt", 4, [128, n])
        sc.pool("lnd", 3, [128, n])

    def phase0(self):
        n = 256
        with Scope(self) as sc:
            sc.pool("xin", 1, [128, 2, D_MODEL])
            self.ln_pools(sc, n)
            segs = [(self.d["xp"], self.XT, self.T)]
            if self.with_sample:
                segs.append((self.d["xs"], self.XTs, NS * TS))
            for (xd, XT, TT) in segs:
                for t0 in range(0, TT, n):
                    nn = min(n, TT - t0)
                    XI = sc.get("xin")
                    nb = (nn + 127) // 128
                    bw = min(128, nn)
                    self.dma(XI[0:bw, 0:nb, :], xd[t0:t0 + nn, :].rearrange("(b p) f -> p b f", p=bw), [], [XI])
                    Vt = sc.get("v")
                    for c in range(8):
                        pp = self.ps()
                        for bb in range(nb):
                            self.tr(pp[:, bb * 128:bb * 128 + bw], XI[0:bw, bb, c * 128:(c + 1) * 128],
                                    self.cst("ident", bw, slice(0, bw)), [XI, self.CST], [pp])
                        self.cp("act" if c % 2 else "dve", Vt[:, c, 0:nn], pp[:, 0:nn], [pp], [Vt])
                    self.ln_fm(sc, Vt, nn, "ln_in_g", "ln_in_b",
                               out_bf=lambda c, t0=t0, nn=nn, XT=XT: (XT[:, c, t0:t0 + nn], XT))

    def phaseD(self, l):
        last = (l == L - 1)
        w3 = self.d["w_out"][l].rearrange("(k p) n -> p k n", p=128)
        self.load_w(w3, [(0, 1024, 0)])
        n = 256
        with Scope(self) as sc:
            self.ln_pools(sc, n)
            if last:
                sc.pool("yf", 1, [128, 8, n])
                sc.pool("yt", 2, [128, D_MODEL])
            segs = [(self.XT, self.OT, self.T, self.o["y_p"])]
            if self.with_sample:
                segs.append((self.XTs, self.OTs, NS * TS, self.o["y_s"]))
            for (XT, OT, TT, yd) in segs:
                for t0 in range(0, TT, n):
                    nn = min(n, TT - t0)
                    Vt = sc.get("v")
                    for c in range(8):
                        pp = self.ps()
                        for k in range(8):
                            self.mm(pp[:, 0:nn], self.WB[:, k, c * 128:(c + 1) * 128], OT[:, k, t0:t0 + nn],
                                    k == 0, k == 7, [self.WB, OT], [pp])
                        self.stt(Vt[:, c, 0:nn], XT[:, c, t0:t0 + nn], ALPHA, pp[:, 0:nn], ALU.mult, ALU.add,
                                 [XT, pp], [Vt])
                    if not last:
                        self.ln_fm(sc, Vt, nn, "lng%d" % l, "lnb%d" % l,
                                   out_bf=lambda c, t0=t0, nn=nn, XT=XT: (XT[:, c, t0:t0 + nn], XT))
                    else:
                        YF = sc.get("yf")
                        self.ln_fm(sc, Vt, nn, "lng%d" % l, "lnb%d" % l,
                                   out_f32=lambda c, nn=nn, YF=YF: (YF[:, c, 0:nn], YF))
                        bw = min(128, nn)
                        for bb in range((nn + 127) // 128):
                            YT = sc.get("yt")
                            for half in range(2):
                                pp = self.ps()
                                for cc in range(4):
                                    c = half * 4 + cc
                                    self.tr(pp[0:bw, cc * 128:(cc + 1) * 128], YF[:, c, bb * 128:bb * 128 + bw],
                                            self.cst("ident"), [YF, self.CST], [pp])
                                self.cp("act" if half else "dve", YT[0:bw, half * 512:(half + 1) * 512], pp[0:bw, :], [pp], [YT])
                            self.dma(yd[t0 + bb * 128:t0 + bb * 128 + bw, :], YT[0:bw, :], [YT], [])

    def phaseB(self, l):
        T = self.T
        NT = T // 512
        NB = T // 128
        w3 = self.d["w_in"][l].rearrange("(k p) n -> p k n", p=128)
        with Scope(self) as so:
            HL3 = so.sb("HL3", [128, T], BF)
            NCK = so.sb("NCK", [128, NB, 8])
            if self.with_sample:
                nkb_s = self.PAST // 128
                self.HL3s = so.sb("HL3s", [128, NS, TS], BF)
                self.NCKs = so.sb("NCKs", [128, NS, nkb_s + 1, 8])
            with Scope(self) as sc:
                WF = sc.sb("WF", [128, 8, 72], BF)
                LFT = sc.sb("LFT", [128, NB, 8])
                CAR = sc.sb("CAR", [128, 2])
                sc.pool("t", 6, [128, 512])
                sc.pool("tb", 3, [128, 512], BF)
                self.memset("pool", WF[:, :, :], 0.0, [WF])
                self.memset("pool", HL3[:, :], 0.0, [HL3])
                self.memset("pool", CAR[:, :], 0.0, [CAR])
                stg = self.get_ws()
                self.dma(stg[:, :, 0:8], w3[:, :, OFF["b_f"]:OFF["b_f"] + 8], [], [stg])
                for g in (0, 32, 64):
                    self.cp("pool", WF[:, :, g:g + 8], stg[:, :, 0:8], [stg], [WF])
                ones_b = self.cst("ones", 1, slice(0, 72)).to_broadcast([72, 512])
                for tt in range(NT):
                    t0 = tt * 512
                    pp = self.ps()
                    for k in range(8):
                        self.mm(pp[0:72, :], WF[:, k, :], self.XT[:, k, t0:t0 + 512], k == 0, k == 7, [WF, self.XT], [pp])
                    LS = sc.get("t")
                    self.act(LS[0:72, :], pp[0:72, :], AF.Exp, [pp, self.PD], [LS], bias=self.PD[0:72, l, 6:7], scale=-1.0)
                    self.act(LS[0:72, :], LS[0:72, :], AF.Ln, [LS], [LS], bias=1.0, scale=1.0)
                    CUMN = sc.get("t")
                    self.mk.op("dve", lambda e, CUMN=CUMN, LS=LS, tt=tt: e.tensor_tensor_scan(
                        out=CUMN[0:72, :], data0=ones_b, data1=LS[0:72, :], initial=CAR[0:72, tt % 2:tt % 2 + 1],
                        op0=ALU.mult, op1=ALU.add), self._rs([self.CST, LS, CAR]), self._rs([CUMN]))
                    self.cp("pool", CAR[0:72, (tt + 1) % 2:(tt + 1) % 2 + 1], CUMN[0:72, 511:512], [CUMN], [CAR])
                    HI = sc.get("tb")
                    self.ts("dve", HI[0:72, :], CUMN[0:72, :], -1.0, ALU.mult, [CUMN], [HI])
                    self.cp("pool", HL3[0:8, t0:t0 + 512], HI[0:8, :], [HI], [HL3])
                    R1 = sc.get("t")
                    self.stt(R1[0:72, :], CUMN[0:72, :], -1.0, HI[0:72, :], ALU.mult, ALU.subtract, [CUMN, HI], [R1])
                    MI = sc.get("tb")
                    self.cp("pool", MI[0:72, :], R1[0:72, :], [R1], [MI])
                    self.cp("pool", HL3[32:40, t0:t0 + 512], MI[32:40, :], [MI], [HL3])
                    R2 = sc.get("t")
                    self.tt("dve", R2[64:72, :], R1[64:72, :], MI[64:72, :], ALU.subtract, [R1, MI], [R2])
                    self.cp("pool", HL3[64:72, t0:t0 + 512], R2[64:72, :], [R2], [HL3])
                    p1 = self.ps()
                    p2 = self.ps()
                    for bb in range(4):
                        self.tr(p1[:, bb * 8:(bb + 1) * 8], CUMN[0:8, bb * 128:(bb + 1) * 128],
                                self.cst("ident", 8, slice(0, 8)), [CUMN, self.CST], [p1])
                        self.tr(p2[:, bb * 8:(bb + 1) * 8], LS[0:8, bb * 128:(bb + 1) * 128],
                                self.cst("ident", 8, slice(0, 8)), [LS, self.CST], [p2])
                    self.cp("dve", NCK[:, tt * 4:tt * 4 + 4, :], p1[:, 0:32].rearrange("p (b h) -> p b h", h=8), [p1], [NCK])
                    self.ts("dve", LFT[:, tt * 4:tt * 4 + 4, :], p2[:, 0:32].rearrange("p (b h) -> p b h", h=8), -1.0, ALU.mult, [p2], [LFT])
                for b0 in range(0, NB, 8):
                    self.dma(self.o["fl_p"][l, b0 * 128:(b0 + 8) * 128 if b0 + 8 <= NB else NB * 128, :].rearrange("(b p) h -> p b h", p=128),
                             LFT[:, b0:min(b0 + 8, NB), :], [LFT], [])
                if self.with_sample:
                    self.fox_sample_setup(l, sc, WF, so)
            for h in range(8 if BSTAGE >= 1 else 0):
                self.fox_head(l, h, so, HL3, NCK, w3)

    def fox_head(self, l, h, so, HL3, NCK, w3):
        T = self.T
        NT = T // 512
        NB = T // 128
        hp, hh = h // 2, h % 2
        self.load_w(w3, [(OFF["b_q"] + h * 64, 64, 0), (OFF["b_k"] + h * 64, 64, 64),
                         (OFF["b_v"] + h * 64, 64, 128), (OFF["b_z"] + h * 64, 64, 192)])
        with Scope(self) as sc:
            QA = sc.sb("QA", [128, T], BF)
            KA = sc.sb("KA", [128, T], BF)
            VA = sc.sb("VA", [128, NB, 128], BF)
            sc.pool("kvo", 1, [128, 4, 128])
            sc.pool("pt", 3, [128, 512], BF)
            sc.pool("t", 5, [128, 256])
            if "m" not in os.environ.get("SKIP", ""):
                self.memset("pool", QA[:, :], 0.0, [QA])
                self.memset("pool", KA[:, :], 0.0, [KA])
                self.memset("pool", KA[64:67, :], 1.0, [KA])
                self.memset("pool", VA[:, :, 64:128], 1.0, [VA])
            for i, g in enumerate((0, 32, 64)):
                if os.environ.get("NOSB2SB"):
                    continue
                self.dma(QA[64 + i:65 + i, :], HL3[g + h:g + h + 1, :], [HL3], [QA])
            SK = os.environ.get("SKIP", "")
            for tt in range(NT):
                t0 = tt * 512
                if "q" not in SK:
                    pq = self.ps()
                    self.proj_fm(pq[0:64, :], 0, 64, self.XT, t0, 512, pq)
                    self.act(QA[0:64, t0:t0 + 512], pq[0:64, :], AF.Identity, [pq], [QA], scale=0.125)
                if "k" not in SK:
                    pk = self.ps()
                    self.proj_fm(pk[0:64, :], 64, 64, self.XT, t0, 512, pk)
                    self.cp("dve", KA[0:64, t0:t0 + 512], pk[0:64, :], [pk], [KA])
                if "t" in SK:
                    continue
                KVO = sc.get("kvo")
                pkv = self.ps()
                for b in range(4):
                    for k in range(8):
                        self.mm(pkv[:, b * 128:(b + 1) * 128], self.XT[:, k, t0 + b * 128:t0 + (b + 1) * 128],
                                self.WB[:, k, 64:192], k == 0, k == 7, [self.XT, self.WB], [pkv])
                self.cp("act", KVO[:, :, :], pkv[:, :].rearrange("p (b c) -> p b c", c=128), [pkv], [KVO])
                if "v" not in SK:
                    self.cp("dve", VA[:, tt * 4:(tt + 1) * 4, 0:64], pkv[:, :].rearrange("p (b c) -> p b c", c=128)[:, :, 64:128], [pkv], [VA])
                if not os.environ.get("NOKVOUT"):
                    self.dma(self.o["fk_p"][l, t0:t0 + 512, h * 64:(h + 1) * 64].rearrange("(b p) c -> p b c", p=128),
                             KVO[:, :, 0:64], [KVO], [])
                    self.dma(self.o["fv_p"][l, t0:t0 + 512, h * 64:(h + 1) * 64].rearrange("(b p) c -> p b c", p=128),
                             KVO[:, :, 64:128], [KVO], [])
            for qt in range(NT if BSTAGE >= 2 else 0):
                q0 = qt * 512
                acc = self.ps("a")
                nkb = 4 * qt + 4
                for kb in range(nkb):
                    c0 = max(0, kb * 128 - q0)
                    sp = self.ps()
                    self.mm(sp[:, c0:512], KA[:, kb * 128:(kb + 1) * 128], QA[:, q0 + c0:q0 + 512], True, True, [KA, QA], [sp])
                    pt = sc.get("pt")
                    self.act(pt[:, c0:512], sp[:, c0:512], AF.Exp, [sp, NCK], [pt], bias=NCK[:, kb, h:h + 1], scale=1.0)
                    if kb * 128 >= q0:
                        self.tt("pool", pt[:, c0:c0 + 128], pt[:, c0:c0 + 128], self.TRIB, ALU.mult, [pt, self.CB], [pt])
                    self.mm(acc[:, c0:512], VA[:, kb, :], pt[:, c0:512], kb == 0, kb == nkb - 1, [VA, pt], [acc])
                for hc in (0, 256):
                    RC = sc.get("t")
                    self.recip(RC[0:64, :], acc[64:128, hc:hc + 256], [acc], [RC])
                    ON = sc.get("t")
                    self.tt("dve", ON[0:64, :], acc[0:64, hc:hc + 256], RC[0:64, :], ALU.mult, [acc, RC], [ON])
                    pz = self.ps()
                    self.proj_fm(pz[0:64, 0:256], 192, 64, self.XT, q0 + hc, 256, pz)
                    E = sc.get("t")
                    self.act(E[0:64, :], pz[0:64, 0:256], AF.Exp, [pz], [E], scale=-1.0)
                    self.ts("pool", E[0:64, :], E[0:64, :], 1.0, ALU.add, [E], [E])
                    R_ = sc.get("t")
                    self.recip(R_[0:64, :], E[0:64, :], [E], [R_])
                    self.tt("dve", R_[0:64, :], pz[0:64, 0:256], R_[0:64, :], ALU.mult, [pz, R_], [R_])
                    self.tt("dve", self.OT[hh * 64:(hh + 1) * 64, 2 + hp, q0 + hc:q0 + hc + 256], ON[0:64, :], R_[0:64, :], ALU.mult,
                            [ON, R_], [self.OT])
        if self.with_sample:
            self.fox_sample_head(l, h)

    def fox_sample_setup(self, l, sc, WF, so):
        PAST = self.PAST
        nkb = PAST // 128
        W = PAST + TS
        LSs = sc.sb("LSs", [128, W])
        CUMs = sc.sb("CUMs", [128, W])
        LFs = sc.sb("LFs", [TS, 8])
        self.memset("pool", LSs[:, :], 0.0, [LSs])
        self.memset("pool", self.HL3s[:, :, :], 0.0, [self.HL3s])
        ones_b = self.cst("ones", 1, slice(0, 72)).to_broadcast([72, W])
        for s in range(NS):
            for g in (0, 32, 64):
                self.dma(LSs[g:g + 8, 0:PAST], self.d["clf"][l, s], [], [LSs])
            self.ts("pool", LSs[0:72, 0:PAST], LSs[0:72, 0:PAST], -1.0, ALU.mult, [LSs], [LSs])
            pp = self.ps()
            for k in range(8):
                self.mm(pp[0:72, 0:TS], WF[:, k, :], self.XTs[:, k, s * TS:(s + 1) * TS], k == 0, k == 7, [WF, self.XTs], [pp])
            E = sc.get("t")
            self.act(E[0:72, 0:TS], pp[0:72, 0:TS], AF.Exp, [pp, self.PD], [E], bias=self.PD[0:72, l, 6:7], scale=-1.0)
            self.act(LSs[0:72, PAST:W], E[0:72, 0:TS], AF.Ln, [E], [LSs], bias=1.0, scale=1.0)
            self.scan(CUMs[0:72, :], ones_b, LSs[0:72, :], [self.CST, LSs], [CUMs])
            HI = sc.get("tb")
            self.ts("dve", HI[0:72, 0:TS], CUMs[0:72, PAST:W], -1.0, ALU.mult, [CUMs], [HI])
            self.cp("pool", self.HL3s[0:8, s, :], HI[0:8, 0:TS], [HI], [self.HL3s])
            R1 = sc.get("t")
            self.stt(R1[0:72, 0:TS], CUMs[0:72, PAST:W], -1.0, HI[0:72, 0:TS], ALU.mult, ALU.subtract, [CUMs, HI], [R1])
            MI = sc.get("tb")
            self.cp("pool", MI[0:72, 0:TS], R1[0:72, 0:TS], [R1], [MI])
            self.cp("pool", self.HL3s[32:40, s, :], MI[32:40, 0:TS], [MI], [self.HL3s])
            R2 = sc.get("t")
            self.tt("dve", R2[64:72, 0:TS], R1[64:72, 0:TS], MI[64:72, 0:TS], ALU.subtract, [R1, MI], [R2])
            self.cp("pool", self.HL3s[64:72, s, :], R2[64:72, 0:TS], [R2], [self.HL3s])
            p1 = self.ps()
            for kb in range(nkb):
                self.tr(p1[:, kb * 8:(kb + 1) * 8], CUMs[0:8, kb * 128:(kb + 1) * 128], self.cst("ident", 8, slice(0, 8)),
                        [CUMs, self.CST], [p1])
            self.tr(p1[0:TS, nkb * 8:(nkb + 1) * 8], CUMs[0:8, PAST:W], self.cst("ident", 8, slice(0, 8)), [CUMs, self.CST], [p1])
            self.cp("dve", self.NCKs[:, s, 0:nkb, :], p1[:, 0:nkb * 8].rearrange("p (b h) -> p b h", h=8), [p1], [self.NCKs])
            self.cp("dve", self.NCKs[0:TS, s, nkb, :], p1[0:TS, nkb * 8:(nkb + 1) * 8], [p1], [self.NCKs])
            p2 = self.ps()
            self.tr(p2[0:TS, 0:8], LSs[0:8, PAST:W], self.cst("ident", 8, slice(0, 8)), [LSs, self.CST], [p2])
            self.ts("dve", LFs[:, :], p2[0:TS, 0:8], -1.0, ALU.mult, [p2], [LFs])
            self.dma(self.o["fl_s"][l, s], LFs[:, :], [LFs], [])

    def fox_sample_head(self, l, h):
        PAST = self.PAST
        nkb = PAST // 128
        W = PAST + TS
        hp, hh = h // 2, h % 2
        with Scope(self) as sc:
            KAs = sc.sb("KAs", [128, W], BF)
            QAs = sc.sb("QAs", [128, TS], BF)
            VAs = sc.sb("VAs", [128, nkb + 1, 128], BF)
            sc.pool("kst", 2, [64, PAST])
            sc.pool("vst", 2, [128, nkb, 64])
            sc.pool("kvo", 2, [TS, 128])
            sc.pool("pt", 3, [128, TS], BF)
            sc.pool("t", 5, [64, TS])
            self.memset("pool", KAs[:, :], 0.0, [KAs])
            self.memset("pool", KAs[64:67, :], 1.0, [KAs])
            self.memset("pool", QAs[:, :], 0.0, [QAs])
            self.memset("pool", VAs[:, :, :], 0.0, [VAs])
            self.memset("pool", VAs[:, :, 64:128], 1.0, [VAs])
            for s in range(NS):
                s0 = s * TS
                KS = sc.get("kst")
                self.dma(KS[:, :], self.d["ckT"][l, s, h], [], [KS])
                self.cp("pool", KAs[0:64, 0:PAST], KS[:, :], [KS], [KAs])
                VS = sc.get("vst")
                self.dma(VS[:, :, :], self.d["cv"][l, s][:, h * 64:(h + 1) * 64].rearrange("(b p) c -> p b c", p=128), [], [VS])
                self.cp("pool", VAs[:, 0:nkb, 0:64], VS[:, :, :], [VS], [VAs])
                for i, g in enumerate((0, 32, 64)):
                    self.dma(QAs[64 + i:65 + i, :], self.HL3s[g + h:g + h + 1, s, :], [self.HL3s], [QAs])
                pq = self.ps()
                self.proj_fm(pq[0:64, 0:TS], 0, 64, self.XTs, s0, TS, pq)
                self.act(QAs[0:64, :], pq[0:64, 0:TS], AF.Identity, [pq], [QAs], scale=0.125)
                pk = self.ps()
                self.proj_fm(pk[0:64, 0:TS], 64, 64, self.XTs, s0, TS, pk)
                self.cp("dve", KAs[0:64, PAST:W], pk[0:64, 0:TS], [pk], [KAs])
                pkv = self.ps()
                for k in range(8):
                    self.mm(pkv[0:TS, 0:128], self.XTs[:, k, s0:s0 + TS], self.WB[:, k, 64:192], k == 0, k == 7,
                            [self.XTs, self.WB], [pkv])
                KVO = sc.get("kvo")
                self.cp("act", KVO[:, :], pkv[0:TS, 0:128], [pkv], [KVO])
                self.cp("pool", VAs[0:TS, nkb, 0:64], KVO[:, 64:128], [KVO], [VAs])
                self.dma(self.o["fk_s"][l, s, :, h * 64:(h + 1) * 64], KVO[:, 0:64], [KVO], [])
                self.dma(self.o["fv_s"][l, s, :, h * 64:(h + 1) * 64], KVO[:, 64:128], [KVO], [])
                acc = self.ps("a")
                for kb in range(nkb + 1):
                    kw = 128 if kb < nkb else TS
                    sp = self.ps()
                    self.mm(sp[0:kw, 0:TS], KAs[:, kb * 128:kb * 128 + kw], QAs[:, :], True, True, [KAs, QAs], [sp])
                    pt = sc.get("pt")
                    self.act(pt[0:kw, :], sp[0:kw, 0:TS], AF.Exp, [sp, self.NCKs], [pt], bias=self.NCKs[0:kw, s, kb, h:h + 1], scale=1.0)
                    if kb == nkb:
                        self.tt("pool", pt[0:TS, :], pt[0:TS, :], self.CB[0:TS, 3, 0:TS], ALU.mult, [pt, self.CB], [pt])
                    self.mm(acc[:, 0:TS], VAs[0:kw, kb, :], pt[0:kw, :], kb == 0, kb == nkb, [VAs, pt], [acc])
                RC = sc.get("t")
                self.recip(RC[:, :], acc[64:128, 0:TS], [acc], [RC])
                ON = sc.get("t")
                self.tt("dve", ON[:, :], acc[0:64, 0:TS], RC[:, :], ALU.mult, [acc, RC], [ON])
                pz = self.ps()
                self.proj_fm(pz[0:64, 0:TS], 192, 64, self.XTs, s0, TS, pz)
                E = sc.get("t")
                self.act(E[:, :], pz[0:64, 0:TS], AF.Exp, [pz], [E], scale=-1.0)
                self.ts("pool", E[:, :], E[:, :], 1.0, ALU.add, [E], [E])
                R_ = sc.get("t")
                self.recip(R_[:, :], E[:, :], [E], [R_])
                self.tt("dve", R_[:, :], pz[0:64, 0:TS], R_[:, :], ALU.mult, [pz, R_], [R_])
                self.tt("dve", self.OTs[hh * 64:(hh + 1) * 64, 2 + hp, s0:s0 + TS], ON[:, :], R_[:, :], ALU.mult,
                        [ON, R_], [self.OTs])

    def delta_alloc(self, sc, n, rw):
        d = {}
        d["SC"] = [[sc.sb("SC%d_%d" % (i, hh), [128, 4 if rw else 2, 128], BF) for hh in range(2)] for i in range(2)]
        d["A"] = [sc.sb("DA%d" % i, [128, 2, 128], BF) for i in range(3)]
        d["X"] = [sc.sb("DX%d" % i, [128, 2, 128], BF) for i in range(3)]
        d["P"] = [sc.sb("DP%d" % i, [128, 2, 128], BF) for i in range(3)]
        d["TM"] = [sc.sb("TM%d" % i, [128, 4, 128], BF) for i in range(2)]
        d["VZ"] = [sc.sb("VZ%d" % i, [128, 2, 128], BF) for i in range(2)]
        d["WT"] = [sc.sb("WT%d" % i, [128, 128], BF) for i in range(2)]
        d["YB"] = [sc.sb("YB%d" % i, [128, 128], BF) for i in range(2)]
        d["UT"] = [sc.sb("UT%d" % i, [128, 128]) for i in range(2)]
        d["UP"] = sc.sb("UP", [128, 128], BF)
        d["UZ"] = sc.sb("UZ", [128, 2, 128], BF)
        d["H"] = sc.sb("H", [128, 128])
        d["HB"] = sc.sb("HB", [128, 128], BF)
        d["i"] = 0
        for t in d["VZ"] + [d["UZ"], d["UP"]]:
            self.memset("pool", t[:, :] if len(t.t.shape) == 2 else t[:, :, :], 0.0, [t])
        return d

    def delta_group(self, d, G, C, g0, masks, score_mms, tm_srcs, rw, Rf, OA, dec_ap, dec_res):
        gi = d["i"]
        d["i"] += 1
        nm = 4 if rw else 2
        SC = d["SC"][gi % 2]
        for hh in range(2):
            pp = self.ps()
            rows = slice(hh * 64, hh * 64 + 64)
            for (Lf, R2, ncol, col0) in score_mms:
                if G == 128:
                    self.mm(pp[0:G, col0 * 128:(col0 + ncol) * 128].rearrange("p (m c) -> p m c", c=128)[:, :, 0:G],
                            Lf[rows, g0:g0 + G], R2[rows, 0:ncol, g0:g0 + G], True, True, [Lf, R2], [pp])
                else:
                    for m_ in range(ncol):
                        self.mm(pp[0:G, (col0 + m_) * 128:(col0 + m_) * 128 + G],
                                Lf[rows, g0:g0 + G], R2[rows, m_, g0:g0 + G], True, True, [Lf, R2], [pp])
            map_, mres = masks(hh)
            self.tt("dve", SC[hh][0:G, 0:nm, 0:G], pp[0:G, 0:nm * 128].rearrange("p (m c) -> p m c", c=128)[:, :, 0:G],
                    map_, ALU.mult, [pp] + mres, [SC[hh]])
        TM = d["TM"][gi % 2]
        VZ = d["VZ"][gi % 2]
        pt = self.ps("tb")
        for q, Ft in enumerate(tm_srcs):
            self.tr(pt[0:G, q * 128:(q + 1) * 128], Ft[:, g0:g0 + G], self.IDB, [Ft, self.CB], [pt])
        nq = len(tm_srcs)
        self.cp("act", TM[0:G, 0:nq, :], pt[0:G, 0:nq * 128].rearrange("p (q c) -> p q c", c=128), [pt], [TM])
        if rw:
            for hh in range(2):
                self.cp("pool", VZ[0:G, hh, hh * 64:hh * 64 + 64], TM[0:G, 1, hh * 64:hh * 64 + 64], [TM], [VZ])
        pt2 = self.ps("tb")
        for hh in range(2):
            self.tr(pt2[0:G, hh * 128:hh * 128 + G], SC[hh][0:G, 0, 0:G], self.IDB[0:G, 0:G], [SC[hh], self.CB], [pt2])
        A = d["A"][0]
        self.cp("dve", A[0:G, :, 0:G], pt2[0:G, 0:256].rearrange("p (h c) -> p h c", c=128)[:, :, 0:G], [pt2], [A])
        P = d["P"][0]
        for hh in range(2):
            self.tt("pool", P[0:G, hh, 0:G], SC[hh][0:G, 0, 0:G], self.IDB[0:G, 0:G], ALU.add, [SC[hh], self.CB], [P])
        nsteps = int(round(math.log2(C))) - 1
        Xc = None
        ai, xi, pi = 0, 0, 0
        for i in range(nsteps):
            last = (i == nsteps - 1)
            pa = self.ps()
            for hh in range(2):
                xl = SC[hh][0:G, 0, 0:G] if Xc is None else Xc[0:G, hh, 0:G]
                xr = SC[hh] if Xc is None else Xc
                self.mm(pa[0:G, hh * 128:hh * 128 + G], xl, A[0:G, hh, 0:G], True, True, [xr, A], [pa])
            if not last:
                px = self.ps()
                for hh in range(2):
                    xl = SC[hh][0:G, 0, 0:G] if Xc is None else Xc[0:G, hh, 0:G]
                    xr = SC[hh] if Xc is None else Xc
                    self.mm(px[0:G, hh * 128:hh * 128 + G], A[0:G, hh, 0:G], xl, True, True, [xr, A], [px])
            ai += 1
            An = d["A"][ai % 3]
            self.cp("act", An[0:G, :, 0:G], pa[0:G, 0:256].rearrange("p (h c) -> p h c", c=128)[:, :, 0:G], [pa], [An])
            if not last:
                xi += 1
                Xn = d["X"][xi % 3]
                self.cp("dve", Xn[0:G, :, 0:G], px[0:G, 0:256].rearrange("p (h c) -> p h c", c=128)[:, :, 0:G], [px], [Xn])
                Xc = Xn
            A = An
            pq = self.ps()
            for hh in range(2):
                self.mm(pq[0:G, hh * 128:hh * 128 + G], A[0:G, hh, 0:G], P[0:G, hh, 0:G], True, True, [A, P], [pq])
            pi += 1
            Pn = d["P"][pi % 3]
            self.tt("dve", Pn[0:G, :, 0:G], pq[0:G, 0:256].rearrange("p (h c) -> p h c", c=128)[:, :, 0:G], P[0:G, :, 0:G],
                    ALU.add, [pq, P], [Pn])
            P = Pn
        WT = d["WT"][gi % 2]
        pw = self.ps()
        for hh in range(2):
            self.mm(pw[:, hh * 128:hh * 128 + G], TM[0:G, 0, :], P[0:G, hh, 0:G], True, True, [TM, P], [pw])
        self.cp("act", WT[0:64, 0:G], pw[0:64, 0:G], [pw], [WT])
        self.cp("act", WT[64:128, 0:G], pw[64:128, 128:128 + G], [pw], [WT])
        if rw:
            YB = d["YB"][gi % 2]
            py = self.ps()
            for hh in range(2):
                self.mm(py[0:G, hh * 64:hh * 64 + 64], SC[hh][0:G, 2, 0:G], TM[0:G, 1, hh * 64:hh * 64 + 64], True, True,
                        [SC[hh], TM], [py])
            self.cp("dve", YB[0:G, :], py[0:G, 0:128], [py], [YB])
            usrc, ures = YB, YB
        UT = d["UT"][gi % 2]
        pu = self.ps()
        for hh in range(2):
            if rw:
                rhs = YB[0:G, hh * 64:hh * 64 + 64]
                rr = YB
            else:
                rhs = TM[0:G, 1, hh * 64:hh * 64 + 64]
                rr = TM
            self.mm(pu[0:G, hh * 64:hh * 64 + 64], P[0:G, hh, 0:G], rhs, True, True, [P, rr], [pu])
        self.cp("dve", UT[0:G, :], pu[0:G, 0:128], [pu], [UT])
        H, HB, UP, UZ = d["H"], d["HB"], d["UP"], d["UZ"]
        for ci in range(G // C):
            cs = slice(ci * C, ci * C + C)
            tc0 = g0 + ci * C
            pU = self.ps()
            self.mm(pU[0:G, 0:128], WT[:, 0:G], HB[:, :], True, True, [WT, HB], [pU])
            if rw:
                self.tt("dve", UP[cs, :], pU[cs, 0:128], UT[cs, :], ALU.add, [pU, UT], [UP])
            else:
                self.tt("dve", UP[cs, :], UT[cs, :], pU[cs, 0:128], ALU.subtract, [pU, UT], [UP])
            for hh in range(2):
                self.cp("pool", UZ[cs, hh, hh * 64:hh * 64 + 64], UP[cs, hh * 64:hh * 64 + 64], [UP], [UZ])
            po = self.ps("a")
            self.mm(po[:, 0:C], HB[:, :], Rf[:, tc0:tc0 + C], True, False, [HB, Rf], [po])
            for hh in range(2):
                lastmm = (not rw) and hh == 1
                self.mm(po[:, 0:C], UZ[0:G, hh, :], SC[hh][0:G, 1, ci * C:ci * C + C], False, lastmm, [UZ, SC[hh]], [po])
            if rw:
                for hh in range(2):
                    self.mm(po[:, 0:C], VZ[0:G, hh, :], SC[hh][0:G, 3, ci * C:ci * C + C], False, hh == 1, [VZ, SC[hh]], [po])
            self.cp("act", OA[:, tc0:tc0 + C], po[:, 0:C], [po], [OA])
            pS = self.ps()
            self.mm(pS[:, 0:128], TM[cs, 2, :], UP[cs, :], True, not rw, [TM, UP], [pS])
            if rw:
                self.mm(pS[:, 0:128], TM[cs, 3, :], TM[cs, 1, :], False, True, [TM], [pS])
            for hh in range(2):
                rows = slice(hh * 64, hh * 64 + 64)
                cols = slice(hh * 64, hh * 64 + 64)
                self.stt(H[rows, cols], H[rows, cols], dec_ap(tc0 + C - 1)[rows, :], pS[rows, cols], ALU.mult, ALU.add,
                         [H, pS] + dec_res, [H])
                self.cp("pool", HB[rows, cols], H[rows, cols], [H], [HB])

    def silu_mul(self, sc, n, zap, zres, Dn, out_ap, out_res, extra_scale=None):
        E = sc.get("f")
        self.act(E[:, 0:n], zap, AF.Exp, zres, [E], scale=-1.0)
        self.ts("pool", E[:, 0:n], E[:, 0:n], 1.0, ALU.add, [E], [E])
        R_ = sc.get("f")
        self.recip(R_[:, 0:n], E[:, 0:n], [E], [R_])
        self.tt("pool", R_[:, 0:n], R_[:, 0:n], zap, ALU.mult, [R_] + zres, [R_])
        if extra_scale is not None:
            self.stt(out_ap, Dn[:, 0:n], extra_scale, R_[:, 0:n], ALU.mult, ALU.mult, [Dn, R_, self.PV], out_res)
        else:
            self.tt("dve", out_ap, Dn[:, 0:n], R_[:, 0:n], ALU.mult, [Dn, R_], out_res)

    def phaseA(self, l, pr):
        w3 = self.d["w_in"][l].rearrange("(k p) n -> p k n", p=128)
        a = OFF["a_sh"]
        self.load_w(w3, [(a + pr * 128, 128, 0), (a + 256 + pr * 128, 128, 128), (a + 512 + pr * 128, 128, 256),
                         (a + 768, 128, 384), (OFF["a_z"] + pr * 128, 128, 512)])
        n, G, C = 256, 128, 64
        with Scope(self) as sc:
            W2B = sc.sb("W2B", [128, 128], BF)
            stg = self.get_ws()
            self.dma(stg[0:64, 0:2, :], self.d["w2"][l][:, pr * 128:(pr + 1) * 128].rearrange("p (a c) -> p a c", c=64), [], [stg])
            self.dma(stg[64:128, 0:2, :], self.d["a2"][l][:, pr * 128:(pr + 1) * 128].rearrange("p (a c) -> p a c", c=64), [], [stg])
            self.cp("pool", W2B[:, :].rearrange("p (a c) -> p a c", c=64), stg[:, 0:2, :], [stg], [W2B])
            PB = sc.sb("PB", [128, 5, n + 1])
            LAST = sc.sb("LAST", [128, 4])
            BR = sc.sb("BR", [128, 2, n], BF)
            OA = sc.sb("OA", [128, n])
            BON = sc.sb("BON", [128, n])
            PCt = sc.sb("PCt", [128, n])
            M4 = sc.sb("M4", [128, 4, 128], BF)
            sc.pool("f", 12, [128, n])
            sc.pool("b", 8, [128, n], BF)
            for m_, nm_ in enumerate(("msu", "miu", "msu", "miu")):
                self.cp("pool", M4[:, m_, :], self.cst(nm_), [self.CST], [M4])
            dd = self.delta_alloc(sc, n, True)
            self.rwkv_seq(l, pr, sc, dd, dict(n=n, G=G, C=C, nt=self.T // n, XT=self.XT, OT=self.OT, x0=0,
                                               prompt=True, o_sh=self.o["sh_p"][l], o_rw=self.o["rw_p"][l]),
                          PB, LAST, BR, OA, BON, PCt, M4, W2B)
            for s in range(NS if self.with_sample else 0):
                self.rwkv_seq(l, pr, sc, dd, dict(n=TS, G=TS, C=TS, nt=1, XT=self.XTs, OT=self.OTs, x0=s * TS,
                                                   prompt=False, s=s, o_sh=self.o["sh_s"][l, s], o_rw=self.o["rw_s"][l, s]),
                              PB, LAST, BR, OA, BON, PCt, M4, W2B)

    def rwkv_seq(self, l, pr, sc, dd, sq, PB, LAST, BR, OA, BON, PCt, M4, W2B):
        n, G, C = sq["n"], sq["G"], sq["C"]
        XT, OT = sq["XT"], sq["OT"]
        H, HB = dd["H"], dd["HB"]
        mu_idx = (pr, 2 + pr, 4 + pr, 6)
        self.memset("pool", H[:, :], 0.0, [H])
        if sq["prompt"]:
            self.memset("pool", HB[:, :], 0.0, [HB])
            self.memset("pool", LAST[:, :], 0.0, [LAST])
        else:
            s = sq["s"]
            cols_ = (pr * 128, 256 + pr * 128, 512 + pr * 128, 768)
            for j in range(4):
                self.dma(LAST[:, j:j + 1], self.d["st_sh"][l, s, cols_[j]:cols_[j] + 128].rearrange("(p o) -> p o", o=1), [], [LAST])
            for hh in range(2):
                self.dma(H[hh * 64:hh * 64 + 64, hh * 64:hh * 64 + 64], self.d["st_rw"][l, s, 2 * pr + hh], [], [H])
            self.cp("pool", HB[:, :], H[:, :], [H], [HB])
        for tt in range(sq["nt"]):
            t0 = sq["x0"] + tt * n
            f = lambda: sc.get("f")
            b = lambda: sc.get("b")
            self.cp("pool", PB[:, 0:4, 0], LAST[:, :], [LAST], [PB])
            for j in range(5):
                pp = self.ps()
                self.proj_fm(pp[:, 0:n], j * 128, 128, XT, t0, n, pp)
                self.cp("act" if j % 2 == 0 else "dve", PB[:, j, 1:n + 1], pp[:, 0:n], [pp], [PB])
            self.cp("pool", LAST[:, :], PB[:, 0:4, n], [PB], [LAST])
            for j in range(4):
                Dt = f()
                self.tt("pool", Dt[:, 0:n], PB[:, j, 0:n], PB[:, j, 1:n + 1], ALU.subtract, [PB], [Dt])
                self.stt(PB[:, j, 1:n + 1], Dt[:, 0:n], self.pvc("mu%d" % l, mu_idx[j]), PB[:, j, 1:n + 1], ALU.mult, ALU.add,
                         [Dt, PB, self.PV], [PB])
            r_, k_, v_ = PB[:, 0, 1:n + 1], PB[:, 1, 1:n + 1], PB[:, 2, 1:n + 1]
            z_ = PB[:, 4, 1:n + 1]
            E = f()
            self.act(E[0:64, 0:n], PB[0:64, 3, 1:n + 1], AF.Exp, [PB], [E], scale=-2.0)
            self.ts("pool", E[0:64, 0:n], E[0:64, 0:n], 1.0, ALU.add, [E], [E])
            self.recip(E[0:64, 0:n], E[0:64, 0:n], [E], [E])
            TH = b()
            self.ts("dve", TH[0:64, 0:n], E[0:64, 0:n], 2.0, ALU.mult, [E], [TH], s2=-1.0, op1=ALU.add)
            self.cp("pool", TH[64:128, 0:n], PB[64:128, 3, 1:n + 1], [PB], [TH])
            pw = self.ps()
            self.mm(pw[:, 0:n], W2B[0:64, :], TH[0:64, 0:n], True, True, [W2B, TH], [pw])
            pa = self.ps()
            self.mm(pa[:, 0:n], W2B[64:128, :], TH[64:128, 0:n], True, True, [W2B, TH], [pa])
            SG = f()
            self.act(SG[:, 0:n], pw[:, 0:n], AF.Exp, [pw, self.PD], [SG], bias=self.PD[:, l, pr:pr + 1], scale=-1.0)
            self.ts("pool", SG[:, 0:n], SG[:, 0:n], 1.0, ALU.add, [SG], [SG])
            self.recip(SG[:, 0:n], SG[:, 0:n], [SG], [SG])
            AA = f()
            self.act(AA[:, 0:n], pa[:, 0:n], AF.Exp, [pa, self.PD], [AA], bias=self.PD[:, l, 2 + pr:3 + pr], scale=-1.0)
            self.ts("pool", AA[:, 0:n], AA[:, 0:n], 1.0, ALU.add, [AA], [AA])
            self.recip(AA[:, 0:n], AA[:, 0:n], [AA], [AA])
            KKu = f()
            self.ts("dve", KKu[:, 0:n], k_, self.pvc("kk%d" % l, pr), ALU.mult, [PB, self.PV], [KKu])
            KQ = b()
            self.act(KQ[:, 0:n], KKu[:, 0:n], AF.Square, [KKu], [KQ])
            pss = self.ps()
            self.mm(pss[:, 0:n], self.ONESBD, KQ[:, 0:n], True, True, [self.CB, KQ], [pss])
            RN = f()
            self.act(RN[:, 0:n], pss[:, 0:n], AF.Ln, [pss], [RN])
            self.act(RN[:, 0:n], RN[:, 0:n], AF.Exp, [RN], [RN], scale=-0.5)
            KKn = f()
            self.stt(KKn[:, 0:n], RN[:, 0:n], 1e12, KKu[:, 0:n], ALU.min, ALU.mult, [RN, KKu], [KKn])
            KM = f()
            self.ts("dve", KM[:, 0:n], AA[:, 0:n], self.pvc("ka%d" % l, pr), ALU.mult, [AA, self.PV, self.PD], [KM],
                    s2=self.PD[:, l, 4 + pr:5 + pr], op1=ALU.add)
            self.tt("pool", KM[:, 0:n], KM[:, 0:n], k_, ALU.mult, [KM, PB], [KM])
            RK = f()
            self.tt("pool", RK[:, 0:n], r_, KM[:, 0:n], ALU.mult, [PB, KM], [RK])
            RKB = b()
            self.ts("dve", RKB[:, 0:n], RK[:, 0:n], self.pvc("rk%d" % l, pr), ALU.mult, [RK, self.PV], [RKB])
            pbs = self.ps()
            self.mm(pbs[:, 0:n], self.ONESBD, RKB[:, 0:n], True, True, [self.CB, RKB], [pbs])
            self.tt("dve", BON[:, 0:n], pbs[:, 0:n], v_, ALU.mult, [pbs, PB], [BON])
            LW = f()
            self.ts("pool", LW[:, 0:n], SG[:, 0:n], -math.exp(-0.5), ALU.mult, [SG], [LW])
            LC = f()
            self.scan(LC[:, 0:n], self.cst("reset", n), LW[:, 0:n], [self.CST, LW], [LC])
            LCX = f()
            self.tt("pool", LCX[:, 0:n], LC[:, 0:n], LW[:, 0:n], ALU.subtract, [LC, LW], [LCX])
            Pm = f()
            self.act(Pm[:, 0:n], LC[:, 0:n], AF.Exp, [LC], [Pm])
            self.act(LCX[:, 0:n], LCX[:, 0:n], AF.Exp, [LCX], [LCX])
            PINV = f()
            self.act(PINV[:, 0:n], LC[:, 0:n], AF.Exp, [LC], [PINV], scale=-1.0)
            KA_ = f()
            self.tt("pool", KA_[:, 0:n], KKn[:, 0:n], AA[:, 0:n], ALU.mult, [KKn, AA], [KA_])
            self.tt("dve", BR[:, 0, 0:n], KKn[:, 0:n], LCX[:, 0:n], ALU.mult, [KKn, LCX], [BR])
            self.tt("dve", BR[:, 1, 0:n], r_, Pm[:, 0:n], ALU.mult, [PB, Pm], [BR])
            AT = b()
            self.stt(AT[:, 0:n], KA_[:, 0:n], -1.0, PINV[:, 0:n], ALU.mult, ALU.mult, [KA_, PINV], [AT])
            KTt = b()
            self.tt("pool", KTt[:, 0:n], KM[:, 0:n], PINV[:, 0:n], ALU.mult, [KM, PINV], [KTt])
            LCD = f()
            nch = n // C
            lc3 = LC[:, 0:n].rearrange("p (c t) -> p c t", t=C)
            self.tt("pool", LCD[:, 0:n].rearrange("p (c t) -> p c t", t=C), lc3[:, :, C - 1:C].to_broadcast([128, nch, C]), lc3,
                    ALU.subtract, [LC], [LCD])
            self.act(LCD[:, 0:n], LCD[:, 0:n], AF.Exp, [LCD], [LCD])
            ADF = b()
            self.stt(ADF[:, 0:n], KA_[:, 0:n], -1.0, LCD[:, 0:n], ALU.mult, ALU.mult, [KA_, LCD], [ADF])
            KDF = b()
            self.tt("pool", KDF[:, 0:n], KM[:, 0:n], LCD[:, 0:n], ALU.mult, [KM, LCD], [KDF])
            VB = b()
            self.cp("pool", VB[:, 0:n], v_, [PB], [VB])
            self.cp("pool", PCt[:, 0:n], Pm[:, 0:n], [Pm], [PCt])
            for g in range(n // G):
                g0 = g * G
                self.delta_group(dd, G, C, g0,
                                 masks=lambda hh: (M4[0:G, :, 0:G], [M4]),
                                 score_mms=[(AT, BR, 2, 0), (KTt, BR, 2, 2)],
                                 tm_srcs=[BR[:, 0, :], VB, ADF, KDF] if False else [_V(BR, 0), VB, ADF, KDF],
                                 rw=True, Rf=_V(BR, 1), OA=OA,
                                 dec_ap=lambda col: PCt[:, col:col + 1], dec_res=[PCt])
            OB = b()
            self.cp("act", OB[:, 0:n], OA[:, 0:n], [OA], [OB])
            OQ = b()
            self.act(OQ[:, 0:n], OA[:, 0:n], AF.Square, [OA], [OQ])
            p1 = self.ps()
            self.mm(p1[:, 0:n], self.ONESBD, OB[:, 0:n], True, True, [self.CB, OB], [p1])
            p2 = self.ps()
            self.mm(p2[:, 0:n], self.ONESBD, OQ[:, 0:n], True, True, [self.CB, OQ], [p2])
            ME = f()
            self.ts("dve", ME[:, 0:n], p1[:, 0:n], 1.0 / 64, ALU.mult, [p1], [ME])
            MS = f()
            self.tt("pool", MS[:, 0:n], ME[:, 0:n], ME[:, 0:n], ALU.mult, [ME], [MS])
            VA_ = f()
            self.stt(VA_[:, 0:n], p2[:, 0:n], 1.0 / 64, MS[:, 0:n], ALU.mult, ALU.subtract, [p2, MS], [VA_])
            self.act(VA_[:, 0:n], VA_[:, 0:n], AF.Ln, [VA_, self.EPS], [VA_], bias=self.EPS[:, 1:2])
            self.act(VA_[:, 0:n], VA_[:, 0:n], AF.Exp, [VA_], [VA_], scale=-0.5)
            Dn = f()
            self.tt("pool", Dn[:, 0:n], OA[:, 0:n], ME[:, 0:n], ALU.subtract, [OA, ME], [Dn])
            self.tt("dve", Dn[:, 0:n], Dn[:, 0:n], VA_[:, 0:n], ALU.mult, [Dn, VA_], [Dn])
            self.act(Dn[:, 0:n], Dn[:, 0:n], AF.Identity, [Dn, self.PV], [Dn], bias=self.pvc("gnb%d" % l, pr),
                     scale=self.pvc("gng%d" % l, pr))
            self.tt("pool", Dn[:, 0:n], Dn[:, 0:n], BON[:, 0:n], ALU.add, [Dn, BON], [Dn])
            self.silu_mul(sc, n, z_, [PB], Dn, OT[:, pr, t0:t0 + n], [OT])
        o_sh, o_rw = sq["o_sh"], sq["o_rw"]
        cols = (pr * 128, 256 + pr * 128, 512 + pr * 128, 768)
        for j in range(4 if pr == 0 else 3):
            self.dma(o_sh[cols[j]:cols[j] + 128].rearrange("(p o) -> p o", o=1), LAST[:, j:j + 1], [LAST], [])
        HT = sc.get("f")
        for hh in range(2):
            pt = self.ps()
            rows = slice(hh * 64, hh * 64 + 64)
            self.tr(pt[0:64, 0:64], H[rows, hh * 64:hh * 64 + 64], self.cst("ident", 64, rows, hh * 64),
                    [H, self.CST], [pt])
            self.cp("dve", HT[0:64, hh * 64:hh * 64 + 64], pt[0:64, 0:64], [pt], [HT])
        for hh in range(2):
            self.dma(o_rw[2 * pr + hh], HT[0:64, hh * 64:hh * 64 + 64], [HT], [])

    def phaseC(self, l, pr):
        w3 = self.d["w_in"][l].rearrange("(k p) n -> p k n", p=128)
        c = OFF["c_qkv"]
        self.memset("pool", self.WB[:, :, 512:576], 0.0, [self.WB])
        self.load_w(w3, [(c + pr * 128, 128, 0), (c + 256 + pr * 128, 128, 128), (c + 512 + pr * 128, 128, 256),
                         (OFF["c_z"] + pr * 128, 128, 384), (OFF["c_b"], 4, 512), (OFF["c_a"], 4, 544)])
        n, G, C = 256, 128, 64
        with Scope(self) as sc:
            PC = sc.sb("PC", [128, 3, n + 3])
            KQ = sc.sb("KQ", [128, 2, n], BF)
            OA = sc.sb("OA", [128, n])
            EGCb = sc.sb("EGCb", [128, n])
            GC = sc.sb("GC", [128, n])
            GCT = sc.sb("GCT", [128, 64])
            DMM = [[sc.sb("DMM%d_%d" % (i, hh), [128, 2, 128]) for hh in range(2)] for i in range(2)]
            NMSU = sc.sb("NMSU", [128, 128])
            self.gd_long = [sc.sb("GL%d" % i, [128, n]) for i in range(4)]
            sc.pool("f", 9, [128, n])
            sc.pool("b", 8, [128, n], BF)
            sc.pool("df", 2, [128, 2, 128])
            self.ts("pool", NMSU[:, :], self.cst("msu"), -1.0, ALU.mult, [self.CST], [NMSU])
            dd = self.delta_alloc(sc, n, False)
            self.gdn_seq(l, pr, sc, dd, dict(n=n, G=G, C=C, nt=self.T // n, XT=self.XT, OT=self.OT, x0=0, prompt=True,
                                              o_cv=self.o["cv_p"][l], o_gd=self.o["gd_p"][l]),
                         PC, KQ, OA, EGCb, GC, GCT, DMM, NMSU)
            for s in range(NS if self.with_sample else 0):
                self.gdn_seq(l, pr, sc, dd, dict(n=TS, G=TS, C=TS, nt=1, XT=self.XTs, OT=self.OTs, x0=s * TS, prompt=False,
                                                  s=s, o_cv=self.o["cv_s"][l, s], o_gd=self.o["gd_s"][l, s]),
                             PC, KQ, OA, EGCb, GC, GCT, DMM, NMSU)

    def gdn_seq(self, l, pr, sc, dd, sq, PC, KQ, OA, EGCb, GC, GCT, DMM, NMSU):
        n, G, C = sq["n"], sq["G"], sq["C"]
        XT, OT = sq["XT"], sq["OT"]
        H, HB = dd["H"], dd["HB"]
        self.memset("pool", H[:, :], 0.0, [H])
        self.memset("pool", PC[:, :, :], 0.0, [PC])
        if sq["prompt"]:
            self.memset("pool", HB[:, :], 0.0, [HB])
        else:
            s = sq["s"]
            for j in range(3):
                jj = 2 * j + pr
                self.dma(PC[:, j, 0:3], self.d["st_cv"][l, s][:, jj * 128:(jj + 1) * 128].rearrange("i p -> p i"), [], [PC],
                         allow_slow_non_contiguous=True)
            for hh in range(2):
                self.dma(H[hh * 64:hh * 64 + 64, hh * 64:hh * 64 + 64], self.d["st_gd"][l, s, 2 * pr + hh], [], [H])
            self.cp("pool", HB[:, :], H[:, :], [H], [HB])
        for tt in range(sq["nt"]):
            t0 = sq["x0"] + tt * n
            f = lambda: sc.get("f")
            b = lambda: sc.get("b")
            if tt > 0:
                CR = f()
                self.cp("pool", CR[:, 0:9].rearrange("p (j i) -> p j i", i=3), PC[:, :, n:n + 3], [PC], [CR])
                self.cp("pool", PC[:, :, 0:3], CR[:, 0:9].rearrange("p (j i) -> p j i", i=3), [CR], [PC])
            for j in range(3):
                pp = self.ps()
                self.proj_fm(pp[:, 0:n], j * 128, 128, XT, t0, n, pp)
                self.cp("act" if j % 2 == 0 else "dve", PC[:, j, 3:n + 3], pp[:, 0:n], [pp], [PC])
            PZ = self.gd_long[3]
            pp = self.ps()
            self.proj_fm(pp[:, 0:n], 384, 128, XT, t0, n, pp)
            self.cp("act", PZ[:, 0:n], pp[:, 0:n], [pp], [PZ])
            BA = f()
            pp = self.ps()
            self.proj_fm(pp[0:64, 0:n], 512, 64, XT, t0, n, pp)
            self.cp("dve", BA[0:64, 0:n], pp[0:64, 0:n], [pp], [BA])
            Y = []
            for j in range(3):
                jj = 2 * j + pr
                Yj = self.gd_long[j]
                self.ts("dve", Yj[:, 0:n], PC[:, j, 3:n + 3], self.pvc("cw%d" % l, 3 * 6 + jj), ALU.mult, [PC, self.PV], [Yj])
                for i in (2, 1, 0):
                    self.stt(Yj[:, 0:n], PC[:, j, i:i + n], self.pvc("cw%d" % l, i * 6 + jj), Yj[:, 0:n], ALU.mult, ALU.add,
                             [PC, self.PV, Yj], [Yj])
                E = f()
                self.act(E[:, 0:n], Yj[:, 0:n], AF.Exp, [Yj], [E], scale=-1.0)
                self.ts("pool", E[:, 0:n], E[:, 0:n], 1.0, ALU.add, [E], [E])
                self.recip(E[:, 0:n], E[:, 0:n], [E], [E])
                self.tt("pool", Yj[:, 0:n], Yj[:, 0:n], E[:, 0:n], ALU.mult, [Yj, E], [Yj])
                Y.append(Yj)
            q_, k_, v_ = Y
            for (src_, scl) in ((q_, 0.125), (k_, 1.0)):
                Q2 = b()
                self.act(Q2[:, 0:n], src_[:, 0:n], AF.Square, [src_], [Q2])
                pss = self.ps()
                self.mm(pss[:, 0:n], self.ONESBD, Q2[:, 0:n], True, True, [self.CB, Q2], [pss])
                RN = f()
                self.act(RN[:, 0:n], pss[:, 0:n], AF.Ln, [pss, self.EPS], [RN], bias=self.EPS[:, 3:4])
                self.act(RN[:, 0:n], RN[:, 0:n], AF.Exp, [RN], [RN], scale=-0.5)
                self.stt(src_[:, 0:n], src_[:, 0:n], scl, RN[:, 0:n], ALU.mult, ALU.mult, [src_, RN], [src_])
            BG = f()
            self.act(BG[0:64, 0:n], BA[0:64, 0:n], AF.Exp, [BA], [BG], scale=-1.0)
            self.ts("pool", BG[0:64, 0:n], BG[0:64, 0:n], 1.0, ALU.add, [BG], [BG])
            self.recip(BG[0:64, 0:n], BG[0:64, 0:n], [BG], [BG])
            SP = f()
            self.act(SP[0:64, 0:n], BA[0:64, 0:n], AF.Exp, [BA, self.PV], [SP], bias=self.pvc("dtb%d" % l, 0, slice(0, 64)))
            self.act(SP[0:64, 0:n], SP[0:64, 0:n], AF.Ln, [SP], [SP], bias=1.0)
            self.ts("pool", SP[0:64, 0:n], SP[0:64, 0:n], self.PD[0:64, l, 7:8], ALU.mult, [SP, self.PD], [SP])
            self.scan(GC[0:64, 0:n], self.cst("reset", n, slice(0, 64)), SP[0:64, 0:n], [self.CST, SP], [GC])
            EGC = f()
            self.act(EGC[0:64, 0:n], GC[0:64, 0:n], AF.Exp, [GC], [EGC])
            EGD = f()
            nch = n // C
            gc3 = GC[0:64, 0:n].rearrange("p (c t) -> p c t", t=C)
            self.tt("pool", EGD[0:64, 0:n].rearrange("p (c t) -> p c t", t=C), gc3[:, :, C - 1:C].to_broadcast([64, nch, C]), gc3,
                    ALU.subtract, [GC], [EGD])
            self.act(EGD[0:64, 0:n], EGD[0:64, 0:n], AF.Exp, [EGD], [EGD])
            pb_ = self.ps()
            self.mm(pb_[:, 0:n], self.cst("selb", 128, slice(0, 64), pr * 128), BG[0:64, 0:n], True, True, [self.CST, BG], [pb_])
            BETb = f()
            self.cp("act", BETb[:, 0:n], pb_[:, 0:n], [pb_], [BETb])
            pg_ = self.ps()
            self.mm(pg_[:, 0:n], self.cst("selg", 128, slice(0, 64), pr * 128), EGC[0:64, 0:n], True, True, [self.CST, EGC], [pg_])
            self.cp("act", EGCb[:, 0:n], pg_[:, 0:n], [pg_], [EGCb])
            pd_ = self.ps()
            self.mm(pd_[:, 0:n], self.cst("selg", 128, slice(0, 64), pr * 128), EGD[0:64, 0:n], True, True, [self.CST, EGD], [pd_])
            KD = b()
            self.tt("dve", KD[:, 0:n], k_[:, 0:n], pd_[:, 0:n], ALU.mult, [k_, pd_], [KD])
            KBf = f()
            self.tt("pool", KBf[:, 0:n], k_[:, 0:n], BETb[:, 0:n], ALU.mult, [k_, BETb], [KBf])
            self.cp("pool", KQ[:, 0, 0:n], KBf[:, 0:n], [KBf], [KQ])
            self.cp("pool", KQ[:, 1, 0:n], q_[:, 0:n], [q_], [KQ])
            QG = b()
            self.tt("dve", QG[:, 0:n], q_[:, 0:n], EGCb[:, 0:n], ALU.mult, [q_, EGCb], [QG])
            KBG = b()
            self.tt("dve", KBG[:, 0:n], KBf[:, 0:n], EGCb[:, 0:n], ALU.mult, [KBf, EGCb], [KBG])
            VBt = b()
            self.tt("pool", VBt[:, 0:n], v_[:, 0:n], BETb[:, 0:n], ALU.mult, [v_, BETb], [VBt])
            Kf = b()
            self.cp("pool", Kf[:, 0:n], k_[:, 0:n], [k_], [Kf])
            for g in range(n // G):
                g0 = g * G
                gi = dd["i"]
                DMg = DMM[gi % 2]
                pT = self.ps()
                self.tr(pT[0:G, 0:64], GC[0:64, g0:g0 + G], self.cst("ident", 64, slice(0, 64)), [GC, self.CST], [pT])
                self.cp("dve", GCT[0:G, :], pT[0:G, 0:64], [pT], [GCT])
                pR = self.ps()
                for hh in range(2):
                    h = 2 * pr + hh
                    self.mm(pR[0:G, hh * 128:hh * 128 + G], self.cst("selh", G, slice(0, 64), h * 128), GC[0:64, g0:g0 + G],
                            True, True, [self.CST, GC], [pR])
                DF = sc.get("df")
                for hh in range(2):
                    h = 2 * pr + hh
                    self.stt(DF[0:G, hh, 0:G], pR[0:G, hh * 128:hh * 128 + G], GCT[0:G, 32 + h:33 + h], self.cst("miu", G, slice(0, G)),
                             ALU.subtract, ALU.mult, [pR, GCT, self.CST], [DF])
                self.act(DF[0:G, :, 0:G], DF[0:G, :, 0:G], AF.Exp, [DF], [DF])
                for hh in range(2):
                    self.tt("pool", DMg[hh][0:G, 0, 0:G], DF[0:G, hh, 0:G], NMSU[0:G, 0:G], ALU.mult, [DF, NMSU], [DMg[hh]])
                    self.tt("pool", DMg[hh][0:G, 1, 0:G], DF[0:G, hh, 0:G], self.cst("miu", G, slice(0, G)), ALU.mult,
                            [DF, self.CST], [DMg[hh]])
                self.delta_group(dd, G, C, g0,
                                 masks=lambda hh, DMg=DMg: (DMg[hh][0:G, :, 0:G], [DMg[hh]]),
                                 score_mms=[(Kf, KQ, 2, 0)],
                                 tm_srcs=[KBG, VBt, KD],
                                 rw=False, Rf=QG, OA=OA,
                                 dec_ap=lambda col: EGCb[:, col:col + 1], dec_res=[EGCb])
            OQ = b()
            self.act(OQ[:, 0:n], OA[:, 0:n], AF.Square, [OA], [OQ])
            p2 = self.ps()
            self.mm(p2[:, 0:n], self.ONESBD, OQ[:, 0:n], True, True, [self.CB, OQ], [p2])
            RS = f()
            self.act(RS[:, 0:n], p2[:, 0:n], AF.Ln, [p2, self.EPS], [RS], bias=self.EPS[:, 2:3], scale=1.0 / 64)
            self.act(RS[:, 0:n], RS[:, 0:n], AF.Exp, [RS], [RS], scale=-0.5)
            Dn = f()
            self.tt("dve", Dn[:, 0:n], OA[:, 0:n], RS[:, 0:n], ALU.mult, [OA, RS], [Dn])
            self.silu_mul(sc, n, PZ[:, 0:n], [PZ], Dn, OT[:, 6 + pr, t0:t0 + n], [OT], extra_scale=self.pvc("gnorm%d" % l))
        o_cv, o_gd = sq["o_cv"], sq["o_gd"]
        for j in range(3):
            jj = 2 * j + pr
            self.dma(o_cv[:, jj * 128:(jj + 1) * 128].rearrange("i p -> p i"), PC[:, j, n:n + 3], [PC], [],
                     allow_slow_non_contiguous=True)
        for hh in range(2):
            self.dma(o_gd[2 * pr + hh], H[hh * 64:hh * 64 + 64, hh * 64:hh * 64 + 64], [H], [])

    def layer(self, l):
        if not getattr(self, "ot_zeroed", False):
            self.memset("pool", self.OT[:, :, :], 0.0, [self.OT])
            self.memset("pool", self.OTs[:, :, :], 0.0, [self.OTs])
            self.ot_zeroed = True
        if "A" in self.phases:
            for pr in range(2):
                self.phaseA(l, pr)
        if "C" in self.phases:
            for pr in range(2):
                self.phaseC(l, pr)
        if "B" in self.phases:
            self.phaseB(l)
        self.phaseD(l)


_CACHE = {}


def get_nc(T, PAST, **kw):
    key = (T, PAST, tuple(sorted(kw.items())))
    if key not in _CACHE:
        kb = KB(T, PAST, **kw)
        _CACHE[key] = (kb.build(), kb)
    return _CACHE[key]


def kernel(**inp):
    inp = {k: np.asarray(v) for k, v in inp.items()}
    B, T, _ = inp["x_prompt"].shape
    PAST = inp["cache_fox_k"].shape[2]
    nc, kb = get_nc(T, PAST)
    pv = make_pv(inp)
    cst = make_cst()
    in_maps = []
    for c in range(NCORE):
        m = {"xp": np.ascontiguousarray(inp["x_prompt"][c]), "w_in": inp["w_in"], "w_out": inp["w_out"],
             "rwkv_w2": inp["rwkv_w2"], "rwkv_a2": inp["rwkv_a2"], "pv": pv, "cst": cst}
        if kb.with_sample:
            ss = slice(c * NS, (c + 1) * NS)
            m["xs"] = np.ascontiguousarray(inp["x_sample"][ss]).reshape(NS * TS, D_MODEL)
            m["ckT"] = np.ascontiguousarray(inp["cache_fox_k"][:, ss].transpose(0, 1, 3, 4, 2))
            m["cv"] = np.ascontiguousarray(inp["cache_fox_v"][:, ss]).reshape(L, NS, PAST, 512)
            m["clf"] = np.ascontiguousarray(inp["cache_fox_logf"][:, ss].transpose(0, 1, 3, 2))
            m["st_sh"] = np.ascontiguousarray(inp["state_rwkv_shift"][:, ss])
            m["st_rw"] = np.ascontiguousarray(inp["state_rwkv_wkv"][:, ss].transpose(0, 1, 2, 4, 3))
            m["st_cv"] = np.ascontiguousarray(inp["state_gdn_conv"][:, ss])
            m["st_gd"] = np.ascontiguousarray(inp["state_gdn_wkv"][:, ss])
        in_maps.append(m)
    res = run_bass_kernel_spmd(nc, in_maps, core_ids=list(range(NCORE)))
    R = res.results
    SB = NCORE * NS

    def gp(name, shape_tail, axis_layer=True):
        if name not in R[0]:
            return None
        return np.stack([np.asarray(R[c][name]) for c in range(NCORE)], axis=1)

    y_p = np.stack([R[c]["y_p"] for c in range(NCORE)])
    fk_p = gp("fk_p", None).reshape(L, NCORE, T, 8, 64)
    fv_p = gp("fv_p", None).reshape(L, NCORE, T, 8, 64)
    fl_p = gp("fl_p", None)
    sh_p = gp("sh_p", None)
    rw_p = gp("rw_p", None)
    cv_p = gp("cv_p", None)
    gd_p = gp("gd_p", None)

    def gs(name, tail):
        if name not in R[0]:
            return np.zeros((L, SB) + tail, np.float32)
        a = np.stack([np.asarray(R[c][name]) for c in range(NCORE)], axis=1)
        return a.reshape((L, SB) + tail)
    if "y_s" in R[0]:
        y_s = np.stack([R[c]["y_s"] for c in range(NCORE)]).reshape(SB, TS, D_MODEL)
    else:
        y_s = np.zeros((SB, TS, D_MODEL), np.float32)
    fk_s = gs("fk_s", (TS, 8, 64))
    fv_s = gs("fv_s", (TS, 8, 64))
    fl_s = gs("fl_s", (TS, 8))
    sh_s = gs("sh_s", (896,))
    rw_s = gs("rw_s", (4, 64, 64))
    cv_s = gs("cv_s", (3, 768))
    gd_s = gs("gd_s", (4, 64, 64))
    return (y_p, y_s, fk_p, fv_p, fl_p, sh_p, rw_p, cv_p, gd_p, fk_s, fv_s, fl_s, sh_s, rw_s, cv_s, gd_s)
```

```python
import math
from contextlib import ExitStack
import numpy as np
import concourse.bass as bass
import concourse.mybir as mybir
from concourse.bass_utils import run_bass_kernel_spmd

F32 = mybir.dt.float32
BF = mybir.dt.bfloat16
AF = mybir.ActivationFunctionType
ALU = mybir.AluOpType

D_MODEL = 1024
L = 4
N_IN = 4240
OFF = dict(a_sh=0, a_z=896, b_q=1152, b_k=1664, b_v=2176, b_f=2688, b_z=2696,
           c_qkv=3208, c_b=3976, c_a=3980, c_z=3984)
ALPHA = (2 * L) ** 0.25
LN_EPS = 1e-5
RWKV_GN_EPS = 64e-5
GDN_NORM_EPS = 1e-6
L2_EPS = 1e-6
import os
NCORE = int(os.environ.get("KCORES", "8"))
BSTAGE = int(os.environ.get("BSTAGE", "9"))
NS = 4
TS = 16


def pv_layout():
    off = {}
    n = 0

    def add(name, w):
        nonlocal n
        off[name] = n
        n += w
    add("ln_in_g", 8)
    add("ln_in_b", 8)
    for l in range(L):
        for nm, w in (("lng", 8), ("lnb", 8), ("mu", 7), ("w0", 2), ("a0", 2), ("kk", 2), ("ka", 2),
                      ("rk", 2), ("gng", 2), ("gnb", 2), ("cw", 24), ("gnorm", 1), ("bf", 1),
                      ("alog", 1), ("dtb", 1)):
            add("%s%d" % (nm, l), w)
    return off, n


PVO, NPV = pv_layout()


def cst_layout():
    off = {}
    n = 0
    for nm, w in (("ident", 128), ("onesbd", 128), ("ones", 128), ("tri", 128), ("msu", 128), ("miu", 128),
                  ("reset", 256), ("selb", 256), ("selg", 256), ("selh", 512)):
        off[nm] = n
        n += w
    return off, n


CSO, NCST = cst_layout()


def make_cst():
    c = np.zeros((128, NCST), np.float32)
    p = np.arange(128)[:, None]
    f = np.arange(128)[None, :]
    c[:, CSO["ident"]:CSO["ident"] + 128] = (p == f)
    c[:, CSO["onesbd"]:CSO["onesbd"] + 128] = (p // 64 == f // 64)
    c[:, CSO["ones"]:CSO["ones"] + 128] = 1.0
    c[:, CSO["tri"]:CSO["tri"] + 128] = (p <= f)
    c[:, CSO["msu"]:CSO["msu"] + 128] = (p < f) & (p // 64 == f // 64)
    c[:, CSO["miu"]:CSO["miu"] + 128] = (p <= f) & (p // 64 == f // 64)
    r = np.ones(256, np.float32)
    r[::64] = 0
    c[:, CSO["reset"]:CSO["reset"] + 256] = r[None, :]
    selb = np.zeros((128, 2, 128), np.float32)
    selg = np.zeros((128, 2, 128), np.float32)
    for pr in range(2):
        for hh in range(2):
            selb[2 * pr + hh, pr, hh * 64:(hh + 1) * 64] = 1
            selg[32 + 2 * pr + hh, pr, hh * 64:(hh + 1) * 64] = 1
    c[:, CSO["selb"]:CSO["selb"] + 256] = selb.reshape(128, 256)
    c[:, CSO["selg"]:CSO["selg"] + 256] = selg.reshape(128, 256)
    selh = np.zeros((128, 4, 128), np.float32)
    for h in range(4):
        selh[32 + h, h, :] = 1
    c[:, CSO["selh"]:CSO["selh"] + 512] = selh.reshape(128, 512)
    return c


def make_pv(inp):
    pv = np.zeros((128, NPV), np.float32)

    def put(name, vec, w):
        pv[:, PVO[name]:PVO[name] + w] = np.asarray(vec, np.float32).reshape(w, 128).T
    put("ln_in_g", inp["ln_in_g"], 8)
    put("ln_in_b", inp["ln_in_b"], 8)
    for l in range(L):
        put("lng%d" % l, inp["ln_post_g"][l], 8)
        put("lnb%d" % l, inp["ln_post_b"][l], 8)
        put("mu%d" % l, inp["rwkv_mu"][l], 7)
        put("w0%d" % l, inp["rwkv_w0"][l], 2)
        put("a0%d" % l, inp["rwkv_a0"][l], 2)
        put("kk%d" % l, inp["rwkv_k_k"][l], 2)
        put("ka%d" % l, inp["rwkv_k_a"][l], 2)
        put("rk%d" % l, np.asarray(inp["rwkv_r_k"][l]).reshape(256), 2)
        put("gng%d" % l, inp["rwkv_gn_g"][l], 2)
        put("gnb%d" % l, inp["rwkv_gn_b"][l], 2)
        cw = np.asarray(inp["gdn_conv_w"][l], np.float32)
        for i in range(4):
            pv[:, PVO["cw%d" % l] + i * 6:PVO["cw%d" % l] + i * 6 + 6] = cw[i].reshape(6, 128).T
        gn = np.asarray(inp["gdn_norm_g"][l], np.float32)
        pv[:, PVO["gnorm%d" % l]] = np.concatenate([gn, gn])
        bfv = np.asarray(inp["fox_b_f"][l], np.float32)
        for g in (0, 32, 64):
            pv[g:g + 8, PVO["bf%d" % l]] = bfv
        pv[32:36, PVO["alog%d" % l]] = np.asarray(inp["gdn_a_log"][l], np.float32)
        pv[32:36, PVO["dtb%d" % l]] = np.asarray(inp["gdn_dt_bias"][l], np.float32)
    return pv


class Res:
    __slots__ = ("w", "r", "psum")

    def __init__(self):
        self.w = None
        self.r = {}
        self.psum = False


class Eng:
    def __init__(self, name, unit):
        self.name = name
        self.unit = unit
        self.n = 0
        self.known = {}
        self.hist = {}
        self.ops = []
        self.sem = None


class MK:
    NDQ = 12
    CE = ("pe", "act", "dve", "pool", "sp")

    def __init__(self):
        self.E = {}
        for nm in self.CE:
            self.E[nm] = Eng(nm, 1)
        for i in range(self.NDQ):
            self.E["dq%d" % i] = Eng("dq%d" % i, 16)
        self.dq_rr = 0
        self.n_wait = 0
        self.n_ins = 0
        import threading
        self.tl = threading.local()

    def _need(self, W, reads, writes, is_dma=False):
        toks = {}

        def add(tok, same_ok):
            if tok is None:
                return
            e, n = tok
            if same_ok and e is W and W.name == "pe" and not is_dma:
                return
            if W.known.get(e.name, 0) >= n:
                return
            if toks.get(e.name, (None, 0))[1] < n:
                toks[e.name] = (e, n)

        for r in reads:
            add(r.w, False)
            if r.psum:
                for t in r.r.values():
                    if t[0] is not W:
                        add(t, True)
        for r in writes:
            add(r.w, True)
            for t in r.r.values():
                add(t, True)
        return list(toks.values())

    def _merge(self, W, e, n):
        snap = e.hist.get(n)
        if snap:
            for k, v in snap.items():
                if W.known.get(k, 0) < v:
                    W.known[k] = v
        if W.known.get(e.name, 0) < n:
            W.known[e.name] = n

    def _record(self, te, tn, reads, writes, snap):
        te.hist[tn] = snap
        tok = (te, tn)
        for r in reads:
            r.r[te.name] = tok
        for r in writes:
            r.w = tok
            r.r = {}

    def op(self, eng, fn, reads=(), writes=()):
        W = self.E[eng]
        waits = self._need(W, reads, writes)
        for e, n in waits:
            self._merge(W, e, n)
        W.n += 1
        n = W.n
        snap = dict(W.known)
        snap[W.name] = n
        self._record(W, n, reads, writes, snap)
        W.ops.append((fn, [(e, n_ * e.unit) for e, n_ in waits], W))
        self.n_wait += len(waits)
        self.n_ins += 1
        hk = getattr(self.tl, "hook", None)
        if hk:
            hk()

    def dma(self, out, in_, reads=(), writes=(), queue="sp", **kw):
        Q = self.E[queue]
        k = self.dq_rr
        self.dq_rr = (self.dq_rr + 1) % self.NDQ
        Dq = self.E["dq%d" % k]
        waits = self._need(Q, reads, writes, is_dma=True)
        if Dq.n > 0 and Q.known.get(Dq.name, 0) < Dq.n:
            waits = [w for w in waits if w[0] is not Dq] + [(Dq, Dq.n)]
        for e, n in waits:
            self._merge(Q, e, n)
        Dq.n += 1
        n = Dq.n
        snap = dict(Q.known)
        snap[Dq.name] = n
        self._record(Dq, n, reads, writes, snap)

        def fn(eng, out=out, in_=in_, kw=kw):
            return eng.dma_start(out=out, in_=in_, **kw)
        Q.ops.append((fn, [(e, n_ * e.unit) for e, n_ in waits], Dq))
        self.n_wait += len(waits)
        self.n_ins += 1
        hk = getattr(self.tl, "hook", None)
        if hk:
            hk()

    def barrier(self):
        for nm in self.CE:
            W = self.E[nm]
            waits = []
            for e in self.E.values():
                if e is W or e.n == 0:
                    continue
                if W.known.get(e.name, 0) < e.n:
                    waits.append((e, e.n))
            for e, n in waits:
                self._merge(W, e, n)
            W.ops.append((None, [(e, n * e.unit) for e, n in waits], None))

    def runner(self, sems):
        for e in self.E.values():
            e.sem = sems[e.name]

        def run(engobj, E):
            fuse = E.name in ("act", "dve", "pool")
            for fn, waits, inc_e in E.ops:
                if fn is None or not fuse or not waits:
                    for e, v in waits:
                        engobj.wait_ge(e.sem, v)
                    if fn is None:
                        continue
                    fn(engobj).then_inc(inc_e.sem, inc_e.unit)
                else:
                    for e, v in waits[:-1]:
                        engobj.wait_ge(e.sem, v)
                    ins = fn(engobj)
                    ins._wait_ge(waits[-1][0].sem, waits[-1][1])
                    ins.then_inc(inc_e.sem, inc_e.unit)
        return run


class Tile:
    def __init__(self, t):
        self.t = t
        self.r = Res()

    def __getitem__(self, k):
        return self.t[k]


class _V:
    def __init__(self, t, i):
        self.t = t
        self.i = i
        self.r = t.r

    def __getitem__(self, k):
        return self.t.t[(k[0], self.i) + tuple(k[1:])]


class Scope:
    def __init__(self, kb):
        self.kb = kb
        self.st = ExitStack()
        self.pools = {}

    def __enter__(self):
        self.st.__enter__()
        return self

    def __exit__(self, *a):
        self.kb.mk.barrier()
        return self.st.__exit__(*a)

    def sb(self, name, shape, dt=F32):
        self.kb.uid += 1
        return Tile(self.st.enter_context(self.kb.nc.sbuf_tensor("%s_%d" % (name, self.kb.uid), list(shape), dt)))

    def pool(self, name, n, shape, dt=F32):
        self.pools[name] = [[self.sb(name + str(i), shape, dt) for i in range(n)], 0]

    def get(self, name):
        p = self.pools[name]
        t = p[0][p[1] % len(p[0])]
        p[1] += 1
        return t


class KB:
    def __init__(self, T, PAST, with_sample=True, nlayers=L):
        self.T = T
        self.PAST = PAST
        self.with_sample = with_sample
        self.nl = nlayers
        self.nc = bass.Bass("TRN2", target_bir_lowering=False)
        self.mk = MK()
        self.st = ExitStack()
        self.uid = 0
        self.psp = {}
        self.phases = "ABCD"

    def sb(self, name, shape, dt=F32):
        return Tile(self.st.enter_context(self.nc.sbuf_tensor(name, list(shape), dt)))

    def din(self, name, shape, dt=F32):
        return self.nc.dram_tensor(name, list(shape), dt, kind="ExternalInput").ap()

    def dout(self, name, shape, dt=F32):
        return self.nc.dram_tensor(name, list(shape), dt, kind="ExternalOutput").ap()

    def ps(self, pool="g"):
        p = self.psp[pool]
        t = p[0][p[1] % len(p[0])]
        p[1] += 1
        return t

    @staticmethod
    def _rs(xs):
        return [x.r if isinstance(x, (Tile, _V)) else x for x in xs]

    def tt(self, eng, out, in0, in1, op, R, W):
        self.mk.op(eng, lambda e: e.tensor_tensor(out=out, in0=in0, in1=in1, op=op), self._rs(R), self._rs(W))

    def ts(self, eng, out, in0, s1, op0, R, W, s2=None, op1=None):
        if op1 is None:
            self.mk.op(eng, lambda e: e.tensor_scalar(out=out, in0=in0, scalar1=s1, scalar2=None, op0=op0),
                       self._rs(R), self._rs(W))
        else:
            self.mk.op(eng, lambda e: e.tensor_scalar(out=out, in0=in0, scalar1=s1, scalar2=s2, op0=op0, op1=op1),
                       self._rs(R), self._rs(W))

    def stt(self, out, in0, scalar, in1, op0, op1, R, W):
        self.mk.op("dve", lambda e: e.scalar_tensor_tensor(out=out, in0=in0, scalar=scalar, in1=in1, op0=op0, op1=op1),
                   self._rs(R), self._rs(W))

    def cp(self, eng, out, in_, R, W):
        if eng == "act":
            self.mk.op(eng, lambda e: e.copy(out=out, in_=in_), self._rs(R), self._rs(W))
        else:
            self.mk.op(eng, lambda e: e.tensor_copy(out=out, in_=in_), self._rs(R), self._rs(W))

    def act(self, out, in_, func, R, W, bias=0.0, scale=1.0):
        self.mk.op("act", lambda e: e.activation(out=out, in_=in_, func=func, bias=bias, scale=scale),
                   self._rs(R), self._rs(W))

    def mm(self, out, lhsT, rhs, start, stop, R, W):
        self.mk.op("pe", lambda e: e.matmul(out, lhsT=lhsT, rhs=rhs, start=start, stop=stop),
                   self._rs(R), self._rs(W))

    def tr(self, out, in_, ident, R, W):
        self.mk.op("pe", lambda e: e.transpose(out, in_, ident), self._rs(R), self._rs(W))

    def recip(self, out, in_, R, W):
        self.mk.op("dve", lambda e: e.reciprocal(out=out, in_=in_), self._rs(R), self._rs(W))

    def memset(self, eng, ap, val, W):
        self.mk.op(eng, lambda e: e.memset(ap, val), [], self._rs(W))

    def scan(self, out, d0, d1, R, W):
        self.mk.op("dve", lambda e: e.tensor_tensor_scan(out=out, data0=d0, data1=d1, initial=0.0,
                                                           op0=ALU.mult, op1=ALU.add), self._rs(R), self._rs(W))

    def dma(self, out, in_, R, W, **kw):
        self.mk.dma(out, in_, self._rs(R), self._rs(W), **kw)

    def pvc(self, name, c=0, rows=slice(0, 128)):
        o = PVO[name] + c
        return self.PV[rows, o:o + 1]

    def cst(self, name, w=128, rows=slice(0, 128), c0=0):
        o = CSO[name] + c0
        return self.CST[rows, o:o + w]

    def load_w(self, src3, col_ranges):
        for (sc, w, dc) in col_ranges:
            o = 0
            while o < w:
                ww = min(64, w - o)
                stg = self.get_ws()
                self.dma(stg[:, :, 0:ww], src3[:, :, sc + o:sc + o + ww], [], [stg])
                self.cp("pool", self.WB[:, :, dc + o:dc + o + ww], stg[:, :, 0:ww], [stg], [self.WB])
                o += ww

    def get_ws(self):
        t = self.WS[self.ws_i % len(self.WS)]
        self.ws_i += 1
        return t

    def proj_fm(self, ps_ap, wcol, ncols, xt, t0, n, pst):
        for k in range(8):
            self.mm(ps_ap, self.WB[:, k, wcol:wcol + ncols], xt[:, k, t0:t0 + n], k == 0, k == 7,
                    [self.WB, xt], [pst])

    def ln_fm(self, sc, Vt, n, gname, bname, out_bf=None, out_f32=None):
        VB = sc.get("lnvb")
        VQ = sc.get("lnvq")
        self.cp("act", VB[:, :, 0:n], Vt[:, :, 0:n], [Vt], [VB])
        self.act(VQ[:, :, 0:n], Vt[:, :, 0:n], AF.Square, [Vt], [VQ])
        p1 = self.ps()
        p2 = self.ps()
        for c in range(8):
            self.mm(p1[:, 0:n], self.ONESB[:, :], VB[:, c, 0:n], c == 0, c == 7, [self.CB, VB], [p1])
        for c in range(8):
            self.mm(p2[:, 0:n], self.ONESB[:, :], VQ[:, c, 0:n], c == 0, c == 7, [self.CB, VQ], [p2])
        ME = sc.get("lnt")
        MS = sc.get("lnt")
        VA = sc.get("lnt")
        RS = sc.get("lnt")
        self.ts("dve", ME[:, 0:n], p1[:, 0:n], 1.0 / D_MODEL, ALU.mult, [p1], [ME])
        self.tt("pool", MS[:, 0:n], ME[:, 0:n], ME[:, 0:n], ALU.mult, [ME], [MS])
        self.stt(VA[:, 0:n], p2[:, 0:n], 1.0 / D_MODEL, MS[:, 0:n], ALU.mult, ALU.subtract, [p2, MS], [VA])
        self.act(RS[:, 0:n], VA[:, 0:n], AF.Ln, [VA], [RS], bias=self.EPS[:, 0:1], scale=1.0)
        self.act(RS[:, 0:n], RS[:, 0:n], AF.Exp, [RS], [RS], scale=-0.5)
        for c in range(8):
            Dd = sc.get("lnd")
            self.tt("pool", Dd[:, 0:n], Vt[:, c, 0:n], ME[:, 0:n], ALU.subtract, [Vt, ME], [Dd])
            self.tt("dve", Dd[:, 0:n], Dd[:, 0:n], RS[:, 0:n], ALU.mult, [Dd, RS], [Dd])
            if out_bf is not None:
                ap, tl = out_bf(c)
                self.act(ap, Dd[:, 0:n], AF.Identity, [Dd, self.PV], [tl],
                         bias=self.pvc(bname, c), scale=self.pvc(gname, c))
            if out_f32 is not None:
                ap, tl = out_f32(c)
                self.act(ap, Dd[:, 0:n], AF.Identity, [Dd, self.PV], [tl],
                         bias=self.pvc(bname, c), scale=self.pvc(gname, c))

    def build(self):
        nc, mk = self.nc, self.mk
        T, PAST = self.T, self.PAST
        NB = T // 128
        d = {}
        d["xp"] = self.din("xp", [T, D_MODEL])
        d["w_in"] = self.din("w_in", [L, D_MODEL, N_IN])
        d["w_out"] = self.din("w_out", [L, D_MODEL, D_MODEL])
        d["w2"] = self.din("rwkv_w2", [L, 64, 256])
        d["a2"] = self.din("rwkv_a2", [L, 64, 256])
        d["pv"] = self.din("pv", [128, NPV])
        d["cst"] = self.din("cst", [128, NCST])
        o = {}
        o["y_p"] = self.dout("y_p", [T, D_MODEL])
        o["fk_p"] = self.dout("fk_p", [L, T, 512])
        o["fv_p"] = self.dout("fv_p", [L, T, 512])
        o["fl_p"] = self.dout("fl_p", [L, T, 8])
        o["sh_p"] = self.dout("sh_p", [L, 896])
        o["rw_p"] = self.dout("rw_p", [L, 4, 64, 64])
        o["cv_p"] = self.dout("cv_p", [L, 3, 768])
        o["gd_p"] = self.dout("gd_p", [L, 4, 64, 64])
        if self.with_sample:
            d["xs"] = self.din("xs", [NS * TS, D_MODEL])
            d["ckT"] = self.din("ckT", [L, NS, 8, 64, PAST])
            d["cv"] = self.din("cv", [L, NS, PAST, 512])
            d["clf"] = self.din("clf", [L, NS, 8, PAST])
            d["st_sh"] = self.din("st_sh", [L, NS, 896])
            d["st_rw"] = self.din("st_rw", [L, NS, 4, 64, 64])
            d["st_cv"] = self.din("st_cv", [L, NS, 3, 768])
            d["st_gd"] = self.din("st_gd", [L, NS, 4, 64, 64])
            o["y_s"] = self.dout("y_s", [NS * TS, D_MODEL])
            o["fk_s"] = self.dout("fk_s", [L, NS, TS, 512])
            o["fv_s"] = self.dout("fv_s", [L, NS, TS, 512])
            o["fl_s"] = self.dout("fl_s", [L, NS, TS, 8])
            o["sh_s"] = self.dout("sh_s", [L, NS, 896])
            o["rw_s"] = self.dout("rw_s", [L, NS, 4, 64, 64])
            o["cv_s"] = self.dout("cv_s", [L, NS, 3, 768])
            o["gd_s"] = self.dout("gd_s", [L, NS, 4, 64, 64])
        self.d, self.o = d, o

        with self.st:
            self.XT = self.sb("XT", [128, 8, T], BF)
            self.OT = self.sb("OT", [128, 8, T], BF)
            self.XTs = self.sb("XTs", [128, 8, NS * TS], BF)
            self.OTs = self.sb("OTs", [128, 8, NS * TS], BF)
            self.PV = self.sb("PV", [128, NPV])
            self.CST = self.sb("CST", [128, NCST])
            self.CB = self.sb("CB", [128, 6, 128], BF)
            self.WB = self.sb("WB", [128, 8, 1024], BF)
            self.WS = [self.sb("WS%d" % i, [128, 8, 64]) for i in range(2)]
            self.ws_i = 0
            self.EPS = self.sb("EPS", [128, 4])
            self.PD = self.sb("PD", [128, L, 12])
            g = [Tile(self.st.enter_context(nc.psum_tensor("PG%d" % i, [128, 512], F32))) for i in range(5)]
            a = [Tile(self.st.enter_context(nc.psum_tensor("PA%d" % i, [128, 512], F32))) for i in range(2)]
            tb = [Tile(self.st.enter_context(nc.psum_tensor("PT%d" % i, [128, 1024], BF))) for i in range(1)]
            for t_ in g + a + tb:
                t_.r.psum = True
            self.psp = {"g": [g, 0], "a": [a, 0], "tb": [tb, 0]}

            self.IDB = self.CB[:, 0, :]
            self.ONESBD = self.CB[:, 1, :]
            self.ONESB = self.CB[:, 2, :]
            self.TRIB = self.CB[:, 3, :]
            self.MSUB = self.CB[:, 4, :]
            self.MIUB = self.CB[:, 5, :]

            self.dma(self.PV[:, :], d["pv"], [], [self.PV])
            self.dma(self.CST[:, :], d["cst"], [], [self.CST])
            for i, nm in enumerate(("ident", "onesbd", "ones", "tri", "msu", "miu")):
                self.cp("pool", self.CB[:, i, :], self.cst(nm), [self.CST], [self.CB])
            self.memset("pool", self.EPS[:, 0:1], LN_EPS, [self.EPS])
            self.memset("pool", self.EPS[:, 1:2], RWKV_GN_EPS, [self.EPS])
            self.memset("pool", self.EPS[:, 2:3], GDN_NORM_EPS, [self.EPS])
            self.memset("pool", self.EPS[:, 3:4], L2_EPS, [self.EPS])
            for l in range(self.nl):
                for c in range(2):
                    self.ts("pool", self.PD[:, l, c:c + 1], self.pvc("w0%d" % l, c), -1.0, ALU.mult, [self.PV], [self.PD])
                    self.ts("pool", self.PD[:, l, 2 + c:3 + c], self.pvc("a0%d" % l, c), -1.0, ALU.mult, [self.PV], [self.PD])
                    self.ts("pool", self.PD[:, l, 4 + c:5 + c], self.pvc("ka%d" % l, c), -1.0, ALU.mult, [self.PV], [self.PD],
                            s2=1.0, op1=ALU.add)
                self.ts("pool", self.PD[:, l, 6:7], self.pvc("bf%d" % l), -1.0, ALU.mult, [self.PV], [self.PD])
                self.act(self.PD[:, l, 7:8], self.pvc("alog%d" % l), AF.Exp, [self.PV], [self.PD])
                self.ts("pool", self.PD[:, l, 7:8], self.PD[:, l, 7:8], -1.0, ALU.mult, [self.PD], [self.PD])
            mk.barrier()

            self.phase0()
            for l in range(self.nl):
                self.layer(l)
            mk.barrier()

            sems = {}
            for nm in mk.E:
                sems[nm] = self.st.enter_context(nc.semaphore("s_" + nm))
            run = mk.runner(sems)
            with nc.Block() as block:
                @block.tensor
                def _(e):
                    run(e, mk.E["pe"])

                @block.vector
                def _(e):
                    run(e, mk.E["dve"])

                @block.scalar
                def _(e):
                    run(e, mk.E["act"])

                @block.gpsimd
                def _(e):
                    run(e, mk.E["pool"])

                @block.sync
                def _(e):
                    run(e, mk.E["sp"])
        return nc

    def ln_pools(self, sc, n):
        sc.pool("v", 1, [128, 8, n])
        sc.pool("lnvb", 1, [128, 8, n], BF)
        sc.pool("lnvq", 1, [128, 8, n], BF)
        sc.pool("lnt", 4, [128, n])
        sc.pool("lnd", 3, [128, n])

    def phase0(self):
        n = 256
        with Scope(self) as sc:
            sc.pool("xin", 1, [128, 2, D_MODEL])
            self.ln_pools(sc, n)
            segs = [(self.d["xp"], self.XT, self.T)]
            if self.with_sample:
                segs.append((self.d["xs"], self.XTs, NS * TS))
            for (xd, XT, TT) in segs:
                for t0 in range(0, TT, n):
                    nn = min(n, TT - t0)
                    XI = sc.get("xin")
                    nb = (nn + 127) // 128
                    bw = min(128, nn)
                    self.dma(XI[0:bw, 0:nb, :], xd[t0:t0 + nn, :].rearrange("(b p) f -> p b f", p=bw), [], [XI])
                    Vt = sc.get("v")
                    for c in range(8):
                        pp = self.ps()
                        for bb in range(nb):
                            self.tr(pp[:, bb * 128:bb * 128 + bw], XI[0:bw, bb, c * 128:(c + 1) * 128],
                                    self.cst("ident", bw, slice(0, bw)), [XI, self.CST], [pp])
                        self.cp("act" if c % 2 else "dve", Vt[:, c, 0:nn], pp[:, 0:nn], [pp], [Vt])
                    self.ln_fm(sc, Vt, nn, "ln_in_g", "ln_in_b",
                               out_bf=lambda c, t0=t0, nn=nn, XT=XT: (XT[:, c, t0:t0 + nn], XT))

    def phaseD(self, l):
        last = (l == L - 1)
        w3 = self.d["w_out"][l].rearrange("(k p) n -> p k n", p=128)
        self.load_w(w3, [(0, 1024, 0)])
        n = 256
        with Scope(self) as sc:
            self.ln_pools(sc, n)
            if last:
                sc.pool("yf", 1, [128, 8, n])
                sc.pool("yt", 2, [128, D_MODEL])
            segs = [(self.XT, self.OT, self.T, self.o["y_p"])]
            if self.with_sample:
                segs.append((self.XTs, self.OTs, NS * TS, self.o["y_s"]))
            for (XT, OT, TT, yd) in segs:
                for t0 in range(0, TT, n):
                    nn = min(n, TT - t0)
                    Vt = sc.get("v")
                    for c in range(8):
                        pp = self.ps()
                        for k in range(8):
                            self.mm(pp[:, 0:nn], self.WB[:, k, c * 128:(c + 1) * 128], OT[:, k, t0:t0 + nn],
                                    k == 0, k == 7, [self.WB, OT], [pp])
                        self.stt(Vt[:, c, 0:nn], XT[:, c, t0:t0 + nn], ALPHA, pp[:, 0:nn], ALU.mult, ALU.add,
                                 [XT, pp], [Vt])
                    if not last:
                        self.ln_fm(sc, Vt, nn, "lng%d" % l, "lnb%d" % l,
                                   out_bf=lambda c, t0=t0, nn=nn, XT=XT: (XT[:, c, t0:t0 + nn], XT))
                    else:
                        YF = sc.get("yf")
                        self.ln_fm(sc, Vt, nn, "lng%d" % l, "lnb%d" % l,
                                   out_f32=lambda c, nn=nn, YF=YF: (YF[:, c, 0:nn], YF))
                        bw = min(128, nn)
                        for bb in range((nn + 127) // 128):
                            YT = sc.get("yt")
                            for half in range(2):
                                pp = self.ps()
                                for cc in range(4):
                                    c = half * 4 + cc
                                    self.tr(pp[0:bw, cc * 128:(cc + 1) * 128], YF[:, c, bb * 128:bb * 128 + bw],
                                            self.cst("ident"), [YF, self.CST], [pp])
                                self.cp("act" if half else "dve", YT[0:bw, half * 512:(half + 1) * 512], pp[0:bw, :], [pp], [YT])
                            self.dma(yd[t0 + bb * 128:t0 + bb * 128 + bw, :], YT[0:bw, :], [YT], [])

    def phaseB(self, l):
        T = self.T
        NT = T // 512
        NB = T // 128
        w3 = self.d["w_in"][l].rearrange("(k p) n -> p k n", p=128)
        with Scope(self) as so:
            HL3 = so.sb("HL3", [128, T], BF)
            NCK = so.sb("NCK", [128, NB, 8])
            if self.with_sample:
                nkb_s = self.PAST // 128
                self.HL3s = so.sb("HL3s", [128, NS, TS], BF)
                self.NCKs = so.sb("NCKs", [128, NS, nkb_s + 1, 8])
            with Scope(self) as sc:
                WF = sc.sb("WF", [128, 8, 72], BF)
                LFT = sc.sb("LFT", [128, NB, 8])
                CAR = sc.sb("CAR", [128, 2])
                sc.pool("t", 6, [128, 512])
                sc.pool("tb", 3, [128, 512], BF)
                self.memset("pool", WF[:, :, :], 0.0, [WF])
                self.memset("pool", HL3[:, :], 0.0, [HL3])
                self.memset("pool", CAR[:, :], 0.0, [CAR])
                stg = self.get_ws()
                self.dma(stg[:, :, 0:8], w3[:, :, OFF["b_f"]:OFF["b_f"] + 8], [], [stg])
                for g in (0, 32, 64):
                    self.cp("pool", WF[:, :, g:g + 8], stg[:, :, 0:8], [stg], [WF])
                ones_b = self.cst("ones", 1, slice(0, 72)).to_broadcast([72, 512])
                for tt in range(NT):
                    t0 = tt * 512
                    pp = self.ps()
                    for k in range(8):
                        self.mm(pp[0:72, :], WF[:, k, :], self.XT[:, k, t0:t0 + 512], k == 0, k == 7, [WF, self.XT], [pp])
                    LS = sc.get("t")
                    self.act(LS[0:72, :], pp[0:72, :], AF.Exp, [pp, self.PD], [LS], bias=self.PD[0:72, l, 6:7], scale=-1.0)
                    self.act(LS[0:72, :], LS[0:72, :], AF.Ln, [LS], [LS], bias=1.0, scale=1.0)
                    CUMN = sc.get("t")
                    self.mk.op("dve", lambda e, CUMN=CUMN, LS=LS, tt=tt: e.tensor_tensor_scan(
                        out=CUMN[0:72, :], data0=ones_b, data1=LS[0:72, :], initial=CAR[0:72, tt % 2:tt % 2 + 1],
                        op0=ALU.mult, op1=ALU.add), self._rs([self.CST, LS, CAR]), self._rs([CUMN]))
                    self.cp("pool", CAR[0:72, (tt + 1) % 2:(tt + 1) % 2 + 1], CUMN[0:72, 511:512], [CUMN], [CAR])
                    HI = sc.get("tb")
                    self.ts("dve", HI[0:72, :], CUMN[0:72, :], -1.0, ALU.mult, [CUMN], [HI])
                    self.cp("pool", HL3[0:8, t0:t0 + 512], HI[0:8, :], [HI], [HL3])
                    R1 = sc.get("t")
                    self.stt(R1[0:72, :], CUMN[0:72, :], -1.0, HI[0:72, :], ALU.mult, ALU.subtract, [CUMN, HI], [R1])
                    MI = sc.get("tb")
                    self.cp("pool", MI[0:72, :], R1[0:72, :], [R1], [MI])
                    self.cp("pool", HL3[32:40, t0:t0 + 512], MI[32:40, :], [MI], [HL3])
                    R2 = sc.get("t")
                    self.tt("dve", R2[64:72, :], R1[64:72, :], MI[64:72, :], ALU.subtract, [R1, MI], [R2])
                    self.cp("pool", HL3[64:72, t0:t0 + 512], R2[64:72, :], [R2], [HL3])
                    p1 = self.ps()
                    p2 = self.ps()
                    for bb in range(4):
                        self.tr(p1[:, bb * 8:(bb + 1) * 8], CUMN[0:8, bb * 128:(bb + 1) * 128],
                                self.cst("ident", 8, slice(0, 8)), [CUMN, self.CST], [p1])
                        self.tr(p2[:, bb * 8:(bb + 1) * 8], LS[0:8, bb * 128:(bb + 1) * 128],
                                self.cst("ident", 8, slice(0, 8)), [LS, self.CST], [p2])
                    self.cp("dve", NCK[:, tt * 4:tt * 4 + 4, :], p1[:, 0:32].rearrange("p (b h) -> p b h", h=8), [p1], [NCK])
                    self.ts("dve", LFT[:, tt * 4:tt * 4 + 4, :], p2[:, 0:32].rearrange("p (b h) -> p b h", h=8), -1.0, ALU.mult, [p2], [LFT])
                for b0 in range(0, NB, 8):
                    self.dma(self.o["fl_p"][l, b0 * 128:(b0 + 8) * 128 if b0 + 8 <= NB else NB * 128, :].rearrange("(b p) h -> p b h", p=128),
                             LFT[:, b0:min(b0 + 8, NB), :], [LFT], [])
                if self.with_sample:
                    self.fox_sample_setup(l, sc, WF, so)
            for h in range(8 if BSTAGE >= 1 else 0):
                self.fox_head(l, h, so, HL3, NCK, w3)

    def fox_head(self, l, h, so, HL3, NCK, w3):
        T = self.T
        NT = T // 512
        NB = T // 128
        hp, hh = h // 2, h % 2
        self.load_w(w3, [(OFF["b_q"] + h * 64, 64, 0), (OFF["b_k"] + h * 64, 64, 64),
                         (OFF["b_v"] + h * 64, 64, 128), (OFF["b_z"] + h * 64, 64, 192)])
        with Scope(self) as sc:
            QA = sc.sb("QA", [128, T], BF)
            KA = sc.sb("KA", [128, T], BF)
            VA = sc.sb("VA", [128, NB, 128], BF)
            sc.pool("kvo", 1, [128, 4, 128])
            sc.pool("pt", 3, [128, 512], BF)
            sc.pool("t", 5, [128, 256])
            if "m" not in os.environ.get("SKIP", ""):
                self.memset("pool", QA[:, :], 0.0, [QA])
                self.memset("pool", KA[:, :], 0.0, [KA])
                self.memset("pool", KA[64:67, :], 1.0, [KA])
                self.memset("pool", VA[:, :, 64:128], 1.0, [VA])
            for i, g in enumerate((0, 32, 64)):
                if os.environ.get("NOSB2SB"):
                    continue
                self.dma(QA[64 + i:65 + i, :], HL3[g + h:g + h + 1, :], [HL3], [QA])
            SK = os.environ.get("SKIP", "")
            for tt in range(NT):
                t0 = tt * 512
                if "q" not in SK:
                    pq = self.ps()
                    self.proj_fm(pq[0:64, :], 0, 64, self.XT, t0, 512, pq)
                    self.act(QA[0:64, t0:t0 + 512], pq[0:64, :], AF.Identity, [pq], [QA], scale=0.125)
                if "k" not in SK:
                    pk = self.ps()
                    self.proj_fm(pk[0:64, :], 64, 64, self.XT, t0, 512, pk)
                    self.cp("dve", KA[0:64, t0:t0 + 512], pk[0:64, :], [pk], [KA])
                if "t" in SK:
                    continue
                KVO = sc.get("kvo")
                pkv = self.ps()
                for b in range(4):
                    for k in range(8):
                        self.mm(pkv[:, b * 128:(b + 1) * 128], self.XT[:, k, t0 + b * 128:t0 + (b + 1) * 128],
                                self.WB[:, k, 64:192], k == 0, k == 7, [self.XT, self.WB], [pkv])
                self.cp("act", KVO[:, :, :], pkv[:, :].rearrange("p (b c) -> p b c", c=128), [pkv], [KVO])
                if "v" not in SK:
                    self.cp("dve", VA[:, tt * 4:(tt + 1) * 4, 0:64], pkv[:, :].rearrange("p (b c) -> p b c", c=128)[:, :, 64:128], [pkv], [VA])
                if not os.environ.get("NOKVOUT"):
                    self.dma(self.o["fk_p"][l, t0:t0 + 512, h * 64:(h + 1) * 64].rearrange("(b p) c -> p b c", p=128),
                             KVO[:, :, 0:64], [KVO], [])
                    self.dma(self.o["fv_p"][l, t0:t0 + 512, h * 64:(h + 1) * 64].rearrange("(b p) c -> p b c", p=128),
                             KVO[:, :, 64:128], [KVO], [])
            for qt in range(NT if BSTAGE >= 2 else 0):
                q0 = qt * 512
                acc = self.ps("a")
                nkb = 4 * qt + 4
                for kb in range(nkb):
                    c0 = max(0, kb * 128 - q0)
                    sp = self.ps()
                    self.mm(sp[:, c0:512], KA[:, kb * 128:(kb + 1) * 128], QA[:, q0 + c0:q0 + 512], True, True, [KA, QA], [sp])
                    pt = sc.get("pt")
                    self.act(pt[:, c0:512], sp[:, c0:512], AF.Exp, [sp, NCK], [pt], bias=NCK[:, kb, h:h + 1], scale=1.0)
                    if kb * 128 >= q0:
                        self.tt("pool", pt[:, c0:c0 + 128], pt[:, c0:c0 + 128], self.TRIB, ALU.mult, [pt, self.CB], [pt])
                    self.mm(acc[:, c0:512], VA[:, kb, :], pt[:, c0:512], kb == 0, kb == nkb - 1, [VA, pt], [acc])
                for hc in (0, 256):
                    RC = sc.get("t")
                    self.recip(RC[0:64, :], acc[64:128, hc:hc + 256], [acc], [RC])
                    ON = sc.get("t")
                    self.tt("dve", ON[0:64, :], acc[0:64, hc:hc + 256], RC[0:64, :], ALU.mult, [acc, RC], [ON])
                    pz = self.ps()
                    self.proj_fm(pz[0:64, 0:256], 192, 64, self.XT, q0 + hc, 256, pz)
                    E = sc.get("t")
                    self.act(E[0:64, :], pz[0:64, 0:256], AF.Exp, [pz], [E], scale=-1.0)
                    self.act(E[0:64, :], E[0:64, :], AF.Ln, [E], [E], bias=1.0)
                    R_ = sc.get("t")
                    self.act(R_[0:64, :], E[0:64, :], AF.Exp, [E], [R_], scale=-1.0)
                    self.tt("dve", R_[0:64, :], pz[0:64, 0:256], R_[0:64, :], ALU.mult, [pz, R_], [R_])
                    self.tt("dve", self.OT[hh * 64:(hh + 1) * 64, 2 + hp, q0 + hc:q0 + hc + 256], ON[0:64, :], R_[0:64, :], ALU.mult,
                            [ON, R_], [self.OT])
        if self.with_sample:
            self.fox_sample_head(l, h)

    def fox_sample_setup(self, l, sc, WF, so):
        PAST = self.PAST
        nkb = PAST // 128
        W = PAST + TS
        LSs = sc.sb("LSs", [128, W])
        CUMs = sc.sb("CUMs", [128, W])
        LFs = sc.sb("LFs", [TS, 8])
        self.memset("pool", LSs[:, :], 0.0, [LSs])
        self.memset("pool", self.HL3s[:, :, :], 0.0, [self.HL3s])
        ones_b = self.cst("ones", 1, slice(0, 72)).to_broadcast([72, W])
        for s in range(NS):
            for g in (0, 32, 64):
                self.dma(LSs[g:g + 8, 0:PAST], self.d["clf"][l, s], [], [LSs])
            self.ts("dve", LSs[0:72, 0:PAST], LSs[0:72, 0:PAST], -1.0, ALU.mult, [LSs], [LSs])
            pp = self.ps()
            for k in range(8):
                self.mm(pp[0:72, 0:TS], WF[:, k, :], self.XTs[:, k, s * TS:(s + 1) * TS], k == 0, k == 7, [WF, self.XTs], [pp])
            E = sc.get("t")
            self.act(E[0:72, 0:TS], pp[0:72, 0:TS], AF.Exp, [pp, self.PD], [E], bias=self.PD[0:72, l, 6:7], scale=-1.0)
            self.act(LSs[0:72, PAST:W], E[0:72, 0:TS], AF.Ln, [E], [LSs], bias=1.0, scale=1.0)
            self.scan(CUMs[0:72, :], ones_b, LSs[0:72, :], [self.CST, LSs], [CUMs])
            HI = sc.get("tb")
            self.ts("dve", HI[0:72, 0:TS], CUMs[0:72, PAST:W], -1.0, ALU.mult, [CUMs], [HI])
            self.cp("pool", self.HL3s[0:8, s, :], HI[0:8, 0:TS], [HI], [self.HL3s])
            R1 = sc.get("t")
            self.stt(R1[0:72, 0:TS], CUMs[0:72, PAST:W], -1.0, HI[0:72, 0:TS], ALU.mult, ALU.subtract, [CUMs, HI], [R1])
            MI = sc.get("tb")
            self.cp("pool", MI[0:72, 0:TS], R1[0:72, 0:TS], [R1], [MI])
            self.cp("pool", self.HL3s[32:40, s, :], MI[32:40, 0:TS], [MI], [self.HL3s])
            R2 = sc.get("t")
            self.tt("dve", R2[64:72, 0:TS], R1[64:72, 0:TS], MI[64:72, 0:TS], ALU.subtract, [R1, MI], [R2])
            self.cp("pool", self.HL3s[64:72, s, :], R2[64:72, 0:TS], [R2], [self.HL3s])
            p1 = self.ps()
            for kb in range(nkb):
                self.tr(p1[:, kb * 8:(kb + 1) * 8], CUMs[0:8, kb * 128:(kb + 1) * 128], self.cst("ident", 8, slice(0, 8)),
                        [CUMs, self.CST], [p1])
            self.tr(p1[0:TS, nkb * 8:(nkb + 1) * 8], CUMs[0:8, PAST:W], self.cst("ident", 8, slice(0, 8)), [CUMs, self.CST], [p1])
            self.cp("dve", self.NCKs[:, s, 0:nkb, :], p1[:, 0:nkb * 8].rearrange("p (b h) -> p b h", h=8), [p1], [self.NCKs])
            self.cp("dve", self.NCKs[0:TS, s, nkb, :], p1[0:TS, nkb * 8:(nkb + 1) * 8], [p1], [self.NCKs])
            p2 = self.ps()
            self.tr(p2[0:TS, 0:8], LSs[0:8, PAST:W], self.cst("ident", 8, slice(0, 8)), [LSs, self.CST], [p2])
            self.ts("dve", LFs[:, :], p2[0:TS, 0:8], -1.0, ALU.mult, [p2], [LFs])
            self.dma(self.o["fl_s"][l, s], LFs[:, :], [LFs], [])

    def fox_sample_head(self, l, h):
        PAST = self.PAST
        nkb = PAST // 128
        W = PAST + TS
        hp, hh = h // 2, h % 2
        with Scope(self) as sc:
            KAs = sc.sb("KAs", [128, W], BF)
            QAs = sc.sb("QAs", [128, TS], BF)
            VAs = sc.sb("VAs", [128, nkb + 1, 128], BF)
            sc.pool("kst", 2, [64, PAST])
            sc.pool("vst", 2, [128, nkb, 64])
            sc.pool("kvo", 2, [TS, 128])
            sc.pool("pt", 3, [128, TS], BF)
            sc.pool("t", 5, [64, TS])
            self.memset("pool", KAs[:, :], 0.0, [KAs])
            self.memset("pool", KAs[64:67, :], 1.0, [KAs])
            self.memset("pool", QAs[:, :], 0.0, [QAs])
            self.memset("pool", VAs[:, :, :], 0.0, [VAs])
            self.memset("pool", VAs[:, :, 64:128], 1.0, [VAs])
            for s in range(NS):
                s0 = s * TS
                KS = sc.get("kst")
                self.dma(KS[:, :], self.d["ckT"][l, s, h], [], [KS])
                self.cp("pool", KAs[0:64, 0:PAST], KS[:, :], [KS], [KAs])
                VS = sc.get("vst")
                self.dma(VS[:, :, :], self.d["cv"][l, s][:, h * 64:(h + 1) * 64].rearrange("(b p) c -> p b c", p=128), [], [VS])
                self.cp("pool", VAs[:, 0:nkb, 0:64], VS[:, :, :], [VS], [VAs])
                for i, g in enumerate((0, 32, 64)):
                    self.dma(QAs[64 + i:65 + i, :], self.HL3s[g + h:g + h + 1, s, :], [self.HL3s], [QAs])
                pq = self.ps()
                self.proj_fm(pq[0:64, 0:TS], 0, 64, self.XTs, s0, TS, pq)
                self.act(QAs[0:64, :], pq[0:64, 0:TS], AF.Identity, [pq], [QAs], scale=0.125)
                pk = self.ps()
                self.proj_fm(pk[0:64, 0:TS], 64, 64, self.XTs, s0, TS, pk)
                self.cp("dve", KAs[0:64, PAST:W], pk[0:64, 0:TS], [pk], [KAs])
                pkv = self.ps()
                for k in range(8):
                    self.mm(pkv[0:TS, 0:128], self.XTs[:, k, s0:s0 + TS], self.WB[:, k, 64:192], k == 0, k == 7,
                            [self.XTs, self.WB], [pkv])
                KVO = sc.get("kvo")
                self.cp("act", KVO[:, :], pkv[0:TS, 0:128], [pkv], [KVO])
                self.cp("pool", VAs[0:TS, nkb, 0:64], KVO[:, 64:128], [KVO], [VAs])
                self.dma(self.o["fk_s"][l, s, :, h * 64:(h + 1) * 64], KVO[:, 0:64], [KVO], [])
                self.dma(self.o["fv_s"][l, s, :, h * 64:(h + 1) * 64], KVO[:, 64:128], [KVO], [])
                acc = self.ps("a")
                for kb in range(nkb + 1):
                    kw = 128 if kb < nkb else TS
                    sp = self.ps()
                    self.mm(sp[0:kw, 0:TS], KAs[:, kb * 128:kb * 128 + kw], QAs[:, :], True, True, [KAs, QAs], [sp])
                    pt = sc.get("pt")
                    self.act(pt[0:kw, :], sp[0:kw, 0:TS], AF.Exp, [sp, self.NCKs], [pt], bias=self.NCKs[0:kw, s, kb, h:h + 1], scale=1.0)
                    if kb == nkb:
                        self.tt("pool", pt[0:TS, :], pt[0:TS, :], self.CB[0:TS, 3, 0:TS], ALU.mult, [pt, self.CB], [pt])
                    self.mm(acc[:, 0:TS], VAs[0:kw, kb, :], pt[0:kw, :], kb == 0, kb == nkb, [VAs, pt], [acc])
                RC = sc.get("t")
                self.recip(RC[:, :], acc[64:128, 0:TS], [acc], [RC])
                ON = sc.get("t")
                self.tt("dve", ON[:, :], acc[0:64, 0:TS], RC[:, :], ALU.mult, [acc, RC], [ON])
                pz = self.ps()
                self.proj_fm(pz[0:64, 0:TS], 192, 64, self.XTs, s0, TS, pz)
                E = sc.get("t")
                self.act(E[:, :], pz[0:64, 0:TS], AF.Exp, [pz], [E], scale=-1.0)
                self.act(E[:, :], E[:, :], AF.Ln, [E], [E], bias=1.0)
                R_ = sc.get("t")
                self.act(R_[:, :], E[:, :], AF.Exp, [E], [R_], scale=-1.0)
                self.tt("dve", R_[:, :], pz[0:64, 0:TS], R_[:, :], ALU.mult, [pz, R_], [R_])
                self.tt("dve", self.OTs[hh * 64:(hh + 1) * 64, 2 + hp, s0:s0 + TS], ON[:, :], R_[:, :], ALU.mult,
                        [ON, R_], [self.OTs])

    def delta_alloc(self, sc, n, rw):
        d = {}
        d["SC"] = [[sc.sb("SC%d_%d" % (i, hh), [128, 4 if rw else 2, 128], BF) for hh in range(2)] for i in range(2)]
        d["A"] = [sc.sb("DA%d" % i, [128, 2, 128], BF) for i in range(3)]
        d["X"] = [sc.sb("DX%d" % i, [128, 2, 128], BF) for i in range(3)]
        d["P"] = [sc.sb("DP%d" % i, [128, 2, 128], BF) for i in range(3)]
        d["TM"] = [sc.sb("TM%d" % i, [128, 4, 128], BF) for i in range(2)]
        d["VZ"] = [sc.sb("VZ%d" % i, [128, 2, 128], BF) for i in range(2)]
        d["WT"] = [sc.sb("WT%d" % i, [128, 128], BF) for i in range(2)]
        d["YB"] = [sc.sb("YB%d" % i, [128, 128], BF) for i in range(2)]
        d["UT"] = [sc.sb("UT%d" % i, [128, 128]) for i in range(2)]
        d["UP"] = sc.sb("UP", [128, 128], BF)
        d["UZ"] = sc.sb("UZ", [128, 2, 128], BF)
        d["H"] = sc.sb("H", [128, 128])
        d["HB"] = sc.sb("HB", [128, 128], BF)
        d["HB2"] = sc.sb("HB2", [128, 128], BF)
        d["hbi"] = 0
        d["i"] = 0
        for t in d["VZ"] + [d["UZ"], d["UP"], d["HB2"]]:
            self.memset("pool", t[:, :] if len(t.t.shape) == 2 else t[:, :, :], 0.0, [t])
        return d

    def delta_group(self, d, G, C, g0, masks, score_mms, tm_srcs, rw, Rf, OA, dec_ap, dec_res):
        if os.environ.get("SKIPDELTA"):
            return
        gi = d["i"]
        d["i"] += 1
        nm = 4 if rw else 2
        SC = d["SC"][gi % 2]
        for hh in range(2):
            pp = self.ps()
            rows = slice(hh * 64, hh * 64 + 64)
            for (Lf, R2, ncol, col0) in score_mms:
                if G == 128:
                    self.mm(pp[0:G, col0 * 128:(col0 + ncol) * 128].rearrange("p (m c) -> p m c", c=128)[:, :, 0:G],
                            Lf[rows, g0:g0 + G], R2[rows, 0:ncol, g0:g0 + G], True, True, [Lf, R2], [pp])
                else:
                    for m_ in range(ncol):
                        self.mm(pp[0:G, (col0 + m_) * 128:(col0 + m_) * 128 + G],
                                Lf[rows, g0:g0 + G], R2[rows, m_, g0:g0 + G], True, True, [Lf, R2], [pp])
            map_, mres = masks(hh)
            self.tt("dve", SC[hh][0:G, 0:nm, 0:G], pp[0:G, 0:nm * 128].rearrange("p (m c) -> p m c", c=128)[:, :, 0:G],
                    map_, ALU.mult, [pp] + mres, [SC[hh]])
        TM = d["TM"][gi % 2]
        VZ = d["VZ"][gi % 2]
        pt = self.ps("tb")
        for q, Ft in enumerate(tm_srcs):
            self.tr(pt[0:G, q * 128:(q + 1) * 128], Ft[:, g0:g0 + G], self.IDB, [Ft, self.CB], [pt])
        nq = len(tm_srcs)
        self.cp("act", TM[0:G, 0:nq, :], pt[0:G, 0:nq * 128].rearrange("p (q c) -> p q c", c=128), [pt], [TM])
        if rw:
            for hh in range(2):
                self.cp("pool", VZ[0:G, hh, hh * 64:hh * 64 + 64], TM[0:G, 1, hh * 64:hh * 64 + 64], [TM], [VZ])
        if rw:
            YB = d["YB"][gi % 2]
            py = self.ps()
            for hh in range(2):
                self.mm(py[0:G, hh * 64:hh * 64 + 64], SC[hh][0:G, 2, 0:G], TM[0:G, 1, hh * 64:hh * 64 + 64], True, True,
                        [SC[hh], TM], [py])
            self.cp("dve", YB[0:G, :], py[0:G, 0:128], [py], [YB])
            usrc, ures = YB, YB
        pt2 = self.ps("tb")
        for hh in range(2):
            self.tr(pt2[0:G, hh * 128:hh * 128 + G], SC[hh][0:G, 0, 0:G], self.IDB[0:G, 0:G], [SC[hh], self.CB], [pt2])
        A = d["A"][0]
        self.cp("dve", A[0:G, :, 0:G], pt2[0:G, 0:256].rearrange("p (h c) -> p h c", c=128)[:, :, 0:G], [pt2], [A])
        P = d["P"][0]
        for hh in range(2):
            self.tt("pool", P[0:G, hh, 0:G], SC[hh][0:G, 0, 0:G], self.IDB[0:G, 0:G], ALU.add, [SC[hh], self.CB], [P])
        nsteps = int(round(math.log2(C))) - 1
        Xc = None
        ai, xi, pi = 0, 0, 0
        for i in range(nsteps):
            last = (i == nsteps - 1)
            pa = self.ps()
            for hh in range(2):
                xl = SC[hh][0:G, 0, 0:G] if Xc is None else Xc[0:G, hh, 0:G]
                xr = SC[hh] if Xc is None else Xc
                self.mm(pa[0:G, hh * 128:hh * 128 + G], xl, A[0:G, hh, 0:G], True, True, [xr, A], [pa])
            if not last:
                px = self.ps()
                for hh in range(2):
                    xl = SC[hh][0:G, 0, 0:G] if Xc is None else Xc[0:G, hh, 0:G]
                    xr = SC[hh] if Xc is None else Xc
                    self.mm(px[0:G, hh * 128:hh * 128 + G], A[0:G, hh, 0:G], xl, True, True, [xr, A], [px])
            ai += 1
            An = d["A"][ai % 3]
            self.cp("act", An[0:G, :, 0:G], pa[0:G, 0:256].rearrange("p (h c) -> p h c", c=128)[:, :, 0:G], [pa], [An])
            if not last:
                xi += 1
                Xn = d["X"][xi % 3]
                self.cp("dve", Xn[0:G, :, 0:G], px[0:G, 0:256].rearrange("p (h c) -> p h c", c=128)[:, :, 0:G], [px], [Xn])
                Xc = Xn
            A = An
            pq = self.ps()
            for hh in range(2):
                self.mm(pq[0:G, hh * 128:hh * 128 + G], A[0:G, hh, 0:G], P[0:G, hh, 0:G], True, True, [A, P], [pq])
            pi += 1
            Pn = d["P"][pi % 3]
            self.tt("dve", Pn[0:G, :, 0:G], pq[0:G, 0:256].rearrange("p (h c) -> p h c", c=128)[:, :, 0:G], P[0:G, :, 0:G],
                    ALU.add, [pq, P], [Pn])
            P = Pn
        WT = d["WT"][gi % 2]
        pw = self.ps()
        for hh in range(2):
            self.mm(pw[:, hh * 128:hh * 128 + G], TM[0:G, 0, :], P[0:G, hh, 0:G], True, True, [TM, P], [pw])
        self.cp("act", WT[0:64, 0:G], pw[0:64, 0:G], [pw], [WT])
        self.cp("act", WT[64:128, 0:G], pw[64:128, 128:128 + G], [pw], [WT])
        UT = d["UT"][gi % 2]
        pu = self.ps()
        for hh in range(2):
            if rw:
                rhs = YB[0:G, hh * 64:hh * 64 + 64]
                rr = YB
            else:
                rhs = TM[0:G, 1, hh * 64:hh * 64 + 64]
                rr = TM
            self.mm(pu[0:G, hh * 64:hh * 64 + 64], P[0:G, hh, 0:G], rhs, True, True, [P, rr], [pu])
        self.cp("dve", UT[0:G, :], pu[0:G, 0:128], [pu], [UT])
        H, UP, UZ = d["H"], d["UP"], d["UZ"]
        for ci in range(G // C):
            HB = d["HB"] if d["hbi"] % 2 == 0 else d["HB2"]
            HBn = d["HB2"] if d["hbi"] % 2 == 0 else d["HB"]
            d["hbi"] += 1
            cs = slice(ci * C, ci * C + C)
            tc0 = g0 + ci * C
            pU = self.ps()
            self.mm(pU[0:G, 0:128], WT[:, 0:G], HB[:, :], True, True, [WT, HB], [pU])
            if rw:
                self.tt("dve", UP[cs, :], pU[cs, 0:128], UT[cs, :], ALU.add, [pU, UT], [UP])
            else:
                self.tt("dve", UP[cs, :], UT[cs, :], pU[cs, 0:128], ALU.subtract, [pU, UT], [UP])
            pS = self.ps()
            self.mm(pS[:, 0:128], TM[cs, 2, :], UP[cs, :], True, not rw, [TM, UP], [pS])
            if rw:
                self.mm(pS[:, 0:128], TM[cs, 3, :], TM[cs, 1, :], False, True, [TM], [pS])
            dcol = dec_ap(tc0 + C - 1)
            for hh in range(2):
                rows = slice(hh * 64, hh * 64 + 64)
                cols = slice(hh * 64, hh * 64 + 64)
                self.stt(HBn[rows, cols], H[rows, cols], dcol[rows, :], pS[rows, cols], ALU.mult, ALU.add,
                         [H, pS] + dec_res, [HBn])
            for hh in range(2):
                self.cp("pool", UZ[cs, hh, hh * 64:hh * 64 + 64], UP[cs, hh * 64:hh * 64 + 64], [UP], [UZ])
            po = self.ps("a")
            self.mm(po[:, 0:C], HB[:, :], Rf[:, tc0:tc0 + C], True, False, [HB, Rf], [po])
            for hh in range(2):
                lastmm = (not rw) and hh == 1
                self.mm(po[:, 0:C], UZ[0:G, hh, :], SC[hh][0:G, 1, ci * C:ci * C + C], False, lastmm, [UZ, SC[hh]], [po])
            if rw:
                for hh in range(2):
                    self.mm(po[:, 0:C], VZ[0:G, hh, :], SC[hh][0:G, 3, ci * C:ci * C + C], False, hh == 1, [VZ, SC[hh]], [po])
            self.cp("act", OA[:, tc0:tc0 + C], po[:, 0:C], [po], [OA])
            for hh in range(2):
                rows = slice(hh * 64, hh * 64 + 64)
                cols = slice(hh * 64, hh * 64 + 64)
                self.stt(H[rows, cols], H[rows, cols], dcol[rows, :], pS[rows, cols], ALU.mult, ALU.add,
                         [H, pS] + dec_res, [H])

    def silu_mul(self, sc, n, zap, zres, Dn, out_ap, out_res, extra_scale=None, fpool="f"):
        E = sc.get(fpool)
        self.act(E[:, 0:n], zap, AF.Exp, zres, [E], scale=-1.0)
        self.act(E[:, 0:n], E[:, 0:n], AF.Ln, [E], [E], bias=1.0)
        R_ = sc.get(fpool)
        self.act(R_[:, 0:n], E[:, 0:n], AF.Exp, [E], [R_], scale=-1.0)
        self.tt("pool", R_[:, 0:n], R_[:, 0:n], zap, ALU.mult, [R_] + zres, [R_])
        if extra_scale is not None:
            self.stt(out_ap, Dn[:, 0:n], extra_scale, R_[:, 0:n], ALU.mult, ALU.mult, [Dn, R_, self.PV], out_res)
        else:
            self.tt("dve", out_ap, Dn[:, 0:n], R_[:, 0:n], ALU.mult, [Dn, R_], out_res)

    def zip_run(self, fns):
        if len(fns) == 1 or os.environ.get("NOZIP"):
            for f_ in fns:
                f_()
            return
        import threading
        n = len(fns)
        alive = [True] * n
        turn = [0]
        cv = threading.Condition()
        errs = []
        mk = self.mk

        def nxt(i):
            j = (i + 1) % n
            for _ in range(n):
                if alive[j]:
                    return j
                j = (j + 1) % n
            return -1

        def yp(i):
            with cv:
                turn[0] = nxt(i)
                cv.notify_all()
                while turn[0] != i:
                    cv.wait()

        def worker(i):
            with cv:
                while turn[0] != i:
                    cv.wait()
            try:
                mk.tl.hook = lambda: yp(i)
                fns[i]()
            except BaseException as e:
                errs.append(e)
            finally:
                mk.tl.hook = None
                with cv:
                    alive[i] = False
                    turn[0] = nxt(i)
                    cv.notify_all()
        ths = [threading.Thread(target=worker, args=(i,)) for i in range(n)]
        for t in ths:
            t.start()
        for t in ths:
            t.join()
        if errs:
            raise errs[0]

    def phaseA(self, l, pr):
        w3 = self.d["w_in"][l].rearrange("(k p) n -> p k n", p=128)
        a = OFF["a_sh"]
        self.load_w(w3, [(a + pr * 128, 128, 0), (a + 256 + pr * 128, 128, 128), (a + 512 + pr * 128, 128, 256),
                         (a + 768, 128, 384), (OFF["a_z"] + pr * 128, 128, 512)])
        n, G, C = 128, 128, 64
        NL = 2
        with Scope(self) as sc:
            W2B = sc.sb("W2B", [128, 128], BF)
            stg = self.get_ws()
            self.dma(stg[0:64, 0:2, :], self.d["w2"][l][:, pr * 128:(pr + 1) * 128].rearrange("p (a c) -> p a c", c=64), [], [stg])
            self.dma(stg[64:128, 0:2, :], self.d["a2"][l][:, pr * 128:(pr + 1) * 128].rearrange("p (a c) -> p a c", c=64), [], [stg])
            self.cp("pool", W2B[:, :].rearrange("p (a c) -> p a c", c=64), stg[:, 0:2, :], [stg], [W2B])
            LAST = sc.sb("LAST", [128, 4])
            M4 = sc.sb("M4", [128, 4, 128], BF)
            lanes = []
            for i in range(NL):
                ln = dict(i=i, PB=sc.sb("PB%d" % i, [128, 5, n + 1]), BR=sc.sb("BR%d" % i, [128, 2, n], BF),
                          OA=sc.sb("OA%d" % i, [128, n]), BON=sc.sb("BON%d" % i, [128, n]), PCt=sc.sb("PCt%d" % i, [128, n]),
                          AT=sc.sb("AT%d" % i, [128, n], BF), KTt=sc.sb("KTt%d" % i, [128, n], BF),
                          ADF=sc.sb("ADF%d" % i, [128, n], BF), KDF=sc.sb("KDF%d" % i, [128, n], BF),
                          VB=sc.sb("VB%d" % i, [128, n], BF), f="f%d" % i, b="b%d" % i)
                sc.pool(ln["f"], 12, [128, n])
                sc.pool(ln["b"], 5, [128, n], BF)
                lanes.append(ln)
            for m_, nm_ in enumerate(("msu", "miu", "msu", "miu")):
                self.cp("pool", M4[:, m_, :], self.cst(nm_), [self.CST], [M4])
            dd = self.delta_alloc(sc, n, True)
            self.rwkv_seq(l, pr, sc, dd, dict(n=n, G=G, C=C, nt=self.T // n, XT=self.XT, OT=self.OT, x0=0,
                                               prompt=True, o_sh=self.o["sh_p"][l], o_rw=self.o["rw_p"][l]),
                          lanes, LAST, M4, W2B)
            for s in range(NS if self.with_sample else 0):
                self.rwkv_seq(l, pr, sc, dd, dict(n=TS, G=TS, C=TS, nt=1, XT=self.XTs, OT=self.OTs, x0=s * TS,
                                                   prompt=False, s=s, o_sh=self.o["sh_s"][l, s], o_rw=self.o["rw_s"][l, s]),
                              lanes[s % NL:s % NL + 1], LAST, M4, W2B)

    def rwkv_seq(self, l, pr, sc, dd, sq, lanes, LAST, M4, W2B):
        n = sq["n"]
        H, HB = dd["H"], dd["HB"]
        dd["hbi"] = 0
        self.memset("pool", H[:, :], 0.0, [H])
        if sq["prompt"]:
            self.memset("pool", HB[:, :], 0.0, [HB])
            self.memset("pool", LAST[:, :], 0.0, [LAST])
        else:
            s = sq["s"]
            cols_ = (pr * 128, 256 + pr * 128, 512 + pr * 128, 768)
            for j in range(4):
                self.dma(LAST[:, j:j + 1], self.d["st_sh"][l, s, cols_[j]:cols_[j] + 128].rearrange("(p o) -> p o", o=1), [], [LAST])
            for hh in range(2):
                self.dma(H[hh * 64:hh * 64 + 64, hh * 64:hh * 64 + 64], self.d["st_rw"][l, s, 2 * pr + hh], [], [H])
            self.cp("pool", HB[:, :], H[:, :], [H], [HB])
        NL = len(lanes)
        for tt0 in range(0, sq["nt"], NL):
            tts = list(range(tt0, min(sq["nt"], tt0 + NL)))
            for i, tt in enumerate(tts):
                self.rwkv_proj(sq, lanes[i], tt, LAST)
            self.zip_run([(lambda i=i, tt=tt: self.rwkv_pre(l, pr, sc, sq, lanes[i], W2B)) for i, tt in enumerate(tts)])
            for i, tt in enumerate(tts):
                ln = lanes[i]
                for g in range(n // sq["G"]):
                    self.delta_group(dd, sq["G"], sq["C"], g * sq["G"],
                                     masks=lambda hh, G=sq["G"]: (M4[0:G, :, 0:G], [M4]),
                                     score_mms=[(ln["AT"], ln["BR"], 2, 0), (ln["KTt"], ln["BR"], 2, 2)],
                                     tm_srcs=[_V(ln["BR"], 0), ln["VB"], ln["ADF"], ln["KDF"]],
                                     rw=True, Rf=_V(ln["BR"], 1), OA=ln["OA"],
                                     dec_ap=lambda col, ln=ln: ln["PCt"][:, col:col + 1], dec_res=[ln["PCt"]])
            self.zip_run([(lambda i=i, tt=tt: self.rwkv_post(l, pr, sc, sq, lanes[i], tt)) for i, tt in enumerate(tts)])
        o_sh, o_rw = sq["o_sh"], sq["o_rw"]
        cols = (pr * 128, 256 + pr * 128, 512 + pr * 128, 768)
        for j in range(4 if pr == 0 else 3):
            self.dma(o_sh[cols[j]:cols[j] + 128].rearrange("(p o) -> p o", o=1), LAST[:, j:j + 1], [LAST], [])
        HT = sc.get(lanes[0]["f"])
        for hh in range(2):
            pt = self.ps()
            rows = slice(hh * 64, hh * 64 + 64)
            self.tr(pt[0:64, 0:64], H[rows, hh * 64:hh * 64 + 64], self.cst("ident", 64, rows, hh * 64),
                    [H, self.CST], [pt])
            self.cp("dve", HT[0:64, hh * 64:hh * 64 + 64], pt[0:64, 0:64], [pt], [HT])
        for hh in range(2):
            self.dma(o_rw[2 * pr + hh], HT[0:64, hh * 64:hh * 64 + 64], [HT], [])

    def rwkv_proj(self, sq, ln, tt, LAST):
        n = sq["n"]
        t0 = sq["x0"] + tt * n
        PB = ln["PB"]
        self.cp("pool", PB[:, 0:4, 0], LAST[:, :], [LAST], [PB])
        for j in range(5):
            pp = self.ps()
            self.proj_fm(pp[:, 0:n], j * 128, 128, sq["XT"], t0, n, pp)
            self.cp("act" if j % 2 == 0 else "dve", PB[:, j, 1:n + 1], pp[:, 0:n], [pp], [PB])
        self.cp("pool", LAST[:, :], PB[:, 0:4, n], [PB], [LAST])

    def rwkv_pre(self, l, pr, sc, sq, ln, W2B):
        n, C = sq["n"], sq["C"]
        PB, BR, BON, PCt = ln["PB"], ln["BR"], ln["BON"], ln["PCt"]
        AT, KTt, ADF, KDF, VB = ln["AT"], ln["KTt"], ln["ADF"], ln["KDF"], ln["VB"]
        f = lambda: sc.get(ln["f"])
        b = lambda: sc.get(ln["b"])
        mu_idx = (pr, 2 + pr, 4 + pr, 6)
        for j in range(4):
            Dt = f()
            self.tt("pool", Dt[:, 0:n], PB[:, j, 0:n], PB[:, j, 1:n + 1], ALU.subtract, [PB], [Dt])
            self.stt(PB[:, j, 1:n + 1], Dt[:, 0:n], self.pvc("mu%d" % l, mu_idx[j]), PB[:, j, 1:n + 1], ALU.mult, ALU.add,
                     [Dt, PB, self.PV], [PB])
        r_, k_, v_ = PB[:, 0, 1:n + 1], PB[:, 1, 1:n + 1], PB[:, 2, 1:n + 1]
        E = f()
        self.act(E[0:64, 0:n], PB[0:64, 3, 1:n + 1], AF.Exp, [PB], [E], scale=-2.0)
        self.act(E[0:64, 0:n], E[0:64, 0:n], AF.Ln, [E], [E], bias=1.0)
        self.act(E[0:64, 0:n], E[0:64, 0:n], AF.Exp, [E], [E], scale=-1.0)
        TH = b()
        self.ts("dve", TH[0:64, 0:n], E[0:64, 0:n], 2.0, ALU.mult, [E], [TH], s2=-1.0, op1=ALU.add)
        self.cp("pool", TH[64:128, 0:n], PB[64:128, 3, 1:n + 1], [PB], [TH])
        pw = self.ps()
        self.mm(pw[:, 0:n], W2B[0:64, :], TH[0:64, 0:n], True, True, [W2B, TH], [pw])
        pa = self.ps()
        self.mm(pa[:, 0:n], W2B[64:128, :], TH[64:128, 0:n], True, True, [W2B, TH], [pa])
        SG = f()
        self.act(SG[:, 0:n], pw[:, 0:n], AF.Exp, [pw, self.PD], [SG], bias=self.PD[:, l, pr:pr + 1], scale=-1.0)
        self.act(SG[:, 0:n], SG[:, 0:n], AF.Ln, [SG], [SG], bias=1.0)
        self.act(SG[:, 0:n], SG[:, 0:n], AF.Exp, [SG], [SG], scale=-1.0)
        AA = f()
        self.act(AA[:, 0:n], pa[:, 0:n], AF.Exp, [pa, self.PD], [AA], bias=self.PD[:, l, 2 + pr:3 + pr], scale=-1.0)
        self.act(AA[:, 0:n], AA[:, 0:n], AF.Ln, [AA], [AA], bias=1.0)
        self.act(AA[:, 0:n], AA[:, 0:n], AF.Exp, [AA], [AA], scale=-1.0)
        KKu = f()
        self.ts("dve", KKu[:, 0:n], k_, self.pvc("kk%d" % l, pr), ALU.mult, [PB, self.PV], [KKu])
        KQ = b()
        self.act(KQ[:, 0:n], KKu[:, 0:n], AF.Square, [KKu], [KQ])
        pss = self.ps()
        self.mm(pss[:, 0:n], self.ONESBD, KQ[:, 0:n], True, True, [self.CB, KQ], [pss])
        RN = f()
        self.act(RN[:, 0:n], pss[:, 0:n], AF.Ln, [pss], [RN])
        self.act(RN[:, 0:n], RN[:, 0:n], AF.Exp, [RN], [RN], scale=-0.5)
        KKn = f()
        self.stt(KKn[:, 0:n], RN[:, 0:n], 1e12, KKu[:, 0:n], ALU.min, ALU.mult, [RN, KKu], [KKn])
        KM = f()
        self.ts("dve", KM[:, 0:n], AA[:, 0:n], self.pvc("ka%d" % l, pr), ALU.mult, [AA, self.PV, self.PD], [KM],
                s2=self.PD[:, l, 4 + pr:5 + pr], op1=ALU.add)
        self.tt("pool", KM[:, 0:n], KM[:, 0:n], k_, ALU.mult, [KM, PB], [KM])
        RK = f()
        self.tt("pool", RK[:, 0:n], r_, KM[:, 0:n], ALU.mult, [PB, KM], [RK])
        RKB = b()
        self.ts("dve", RKB[:, 0:n], RK[:, 0:n], self.pvc("rk%d" % l, pr), ALU.mult, [RK, self.PV], [RKB])
        pbs = self.ps()
        self.mm(pbs[:, 0:n], self.ONESBD, RKB[:, 0:n], True, True, [self.CB, RKB], [pbs])
        self.tt("dve", BON[:, 0:n], pbs[:, 0:n], v_, ALU.mult, [pbs, PB], [BON])
        LW = f()
        self.ts("dve", LW[:, 0:n], SG[:, 0:n], -math.exp(-0.5), ALU.mult, [SG], [LW])
        LC = f()
        self.scan(LC[:, 0:n], self.cst("reset", n), LW[:, 0:n], [self.CST, LW], [LC])
        LCX = f()
        self.tt("pool", LCX[:, 0:n], LC[:, 0:n], LW[:, 0:n], ALU.subtract, [LC, LW], [LCX])
        Pm = f()
        self.act(Pm[:, 0:n], LC[:, 0:n], AF.Exp, [LC], [Pm])
        self.act(LCX[:, 0:n], LCX[:, 0:n], AF.Exp, [LCX], [LCX])
        PINV = f()
        self.act(PINV[:, 0:n], LC[:, 0:n], AF.Exp, [LC], [PINV], scale=-1.0)
        KA_ = f()
        self.tt("pool", KA_[:, 0:n], KKn[:, 0:n], AA[:, 0:n], ALU.mult, [KKn, AA], [KA_])
        self.tt("dve", BR[:, 0, 0:n], KKn[:, 0:n], LCX[:, 0:n], ALU.mult, [KKn, LCX], [BR])
        self.tt("dve", BR[:, 1, 0:n], r_, Pm[:, 0:n], ALU.mult, [PB, Pm], [BR])
        self.stt(AT[:, 0:n], KA_[:, 0:n], -1.0, PINV[:, 0:n], ALU.mult, ALU.mult, [KA_, PINV], [AT])
        self.tt("pool", KTt[:, 0:n], KM[:, 0:n], PINV[:, 0:n], ALU.mult, [KM, PINV], [KTt])
        LCD = f()
        nch = n // C
        lc3 = LC[:, 0:n].rearrange("p (c t) -> p c t", t=C)
        self.tt("pool", LCD[:, 0:n].rearrange("p (c t) -> p c t", t=C), lc3[:, :, C - 1:C].to_broadcast([128, nch, C]), lc3,
                ALU.subtract, [LC], [LCD])
        self.act(LCD[:, 0:n], LCD[:, 0:n], AF.Exp, [LCD], [LCD])
        self.stt(ADF[:, 0:n], KA_[:, 0:n], -1.0, LCD[:, 0:n], ALU.mult, ALU.mult, [KA_, LCD], [ADF])
        self.tt("pool", KDF[:, 0:n], KM[:, 0:n], LCD[:, 0:n], ALU.mult, [KM, LCD], [KDF])
        self.cp("pool", VB[:, 0:n], v_, [PB], [VB])
        self.cp("pool", PCt[:, 0:n], Pm[:, 0:n], [Pm], [PCt])

    def rwkv_post(self, l, pr, sc, sq, ln, tt):
        n = sq["n"]
        t0 = sq["x0"] + tt * n
        PB, OA, BON = ln["PB"], ln["OA"], ln["BON"]
        f = lambda: sc.get(ln["f"])
        b = lambda: sc.get(ln["b"])
        z_ = PB[:, 4, 1:n + 1]
        OB = b()
        self.cp("act", OB[:, 0:n], OA[:, 0:n], [OA], [OB])
        OQ = b()
        self.act(OQ[:, 0:n], OA[:, 0:n], AF.Square, [OA], [OQ])
        p1 = self.ps()
        self.mm(p1[:, 0:n], self.ONESBD, OB[:, 0:n], True, True, [self.CB, OB], [p1])
        p2 = self.ps()
        self.mm(p2[:, 0:n], self.ONESBD, OQ[:, 0:n], True, True, [self.CB, OQ], [p2])
        ME = f()
        self.ts("dve", ME[:, 0:n], p1[:, 0:n], 1.0 / 64, ALU.mult, [p1], [ME])
        MS = f()
        self.tt("pool", MS[:, 0:n], ME[:, 0:n], ME[:, 0:n], ALU.mult, [ME], [MS])
        VA_ = f()
        self.stt(VA_[:, 0:n], p2[:, 0:n], 1.0 / 64, MS[:, 0:n], ALU.mult, ALU.subtract, [p2, MS], [VA_])
        self.act(VA_[:, 0:n], VA_[:, 0:n], AF.Ln, [VA_, self.EPS], [VA_], bias=self.EPS[:, 1:2])
        self.act(VA_[:, 0:n], VA_[:, 0:n], AF.Exp, [VA_], [VA_], scale=-0.5)
        Dn = f()
        self.tt("pool", Dn[:, 0:n], OA[:, 0:n], ME[:, 0:n], ALU.subtract, [OA, ME], [Dn])
        self.tt("dve", Dn[:, 0:n], Dn[:, 0:n], VA_[:, 0:n], ALU.mult, [Dn, VA_], [Dn])
        self.act(Dn[:, 0:n], Dn[:, 0:n], AF.Identity, [Dn, self.PV], [Dn], bias=self.pvc("gnb%d" % l, pr),
                 scale=self.pvc("gng%d" % l, pr))
        self.tt("pool", Dn[:, 0:n], Dn[:, 0:n], BON[:, 0:n], ALU.add, [Dn, BON], [Dn])
        self.silu_mul(sc, n, z_, [PB], Dn, sq["OT"][:, pr, t0:t0 + n], [sq["OT"]], fpool=ln["f"])

    def phaseC(self, l, pr):
        w3 = self.d["w_in"][l].rearrange("(k p) n -> p k n", p=128)
        c = OFF["c_qkv"]
        self.memset("pool", self.WB[:, :, 512:576], 0.0, [self.WB])
        self.load_w(w3, [(c + pr * 128, 128, 0), (c + 256 + pr * 128, 128, 128), (c + 512 + pr * 128, 128, 256),
                         (OFF["c_z"] + pr * 128, 128, 384), (OFF["c_b"], 4, 512), (OFF["c_a"], 4, 544)])
        n, G, C = 256, 128, 64
        with Scope(self) as sc:
            PC = sc.sb("PC", [128, 3, n + 3])
            KQ = sc.sb("KQ", [128, 2, n], BF)
            OA = sc.sb("OA", [128, n])
            EGCb = sc.sb("EGCb", [128, n])
            GC = sc.sb("GC", [128, n])
            GCT = sc.sb("GCT", [128, 64])
            DMM = [[sc.sb("DMM%d_%d" % (i, hh), [128, 2, 128]) for hh in range(2)] for i in range(2)]
            NMSU = sc.sb("NMSU", [128, 128])
            self.gd_long = [sc.sb("GL%d" % i, [128, n]) for i in range(4)]
            sc.pool("f", 9, [128, n])
            sc.pool("b", 8, [128, n], BF)
            sc.pool("df", 2, [128, 2, 128])
            self.ts("pool", NMSU[:, :], self.cst("msu"), -1.0, ALU.mult, [self.CST], [NMSU])
            dd = self.delta_alloc(sc, n, False)
            self.gdn_seq(l, pr, sc, dd, dict(n=n, G=G, C=C, nt=self.T // n, XT=self.XT, OT=self.OT, x0=0, prompt=True,
                                              o_cv=self.o["cv_p"][l], o_gd=self.o["gd_p"][l]),
                         PC, KQ, OA, EGCb, GC, GCT, DMM, NMSU)
            for s in range(NS if self.with_sample else 0):
                self.gdn_seq(l, pr, sc, dd, dict(n=TS, G=TS, C=TS, nt=1, XT=self.XTs, OT=self.OTs, x0=s * TS, prompt=False,
                                                  s=s, o_cv=self.o["cv_s"][l, s], o_gd=self.o["gd_s"][l, s]),
                             PC, KQ, OA, EGCb, GC, GCT, DMM, NMSU)

    def gdn_seq(self, l, pr, sc, dd, sq, PC, KQ, OA, EGCb, GC, GCT, DMM, NMSU):
        n, G, C = sq["n"], sq["G"], sq["C"]
        XT, OT = sq["XT"], sq["OT"]
        H, HB = dd["H"], dd["HB"]
        dd["hbi"] = 0
        self.memset("pool", H[:, :], 0.0, [H])
        self.memset("pool", PC[:, :, :], 0.0, [PC])
        if sq["prompt"]:
            self.memset("pool", HB[:, :], 0.0, [HB])
        else:
            s = sq["s"]
            for j in range(3):
                jj = 2 * j + pr
                self.dma(PC[:, j, 0:3], self.d["st_cv"][l, s][:, jj * 128:(jj + 1) * 128].rearrange("i p -> p i"), [], [PC],
                         allow_slow_non_contiguous=True)
            for hh in range(2):
                self.dma(H[hh * 64:hh * 64 + 64, hh * 64:hh * 64 + 64], self.d["st_gd"][l, s, 2 * pr + hh], [], [H])
            self.cp("pool", HB[:, :], H[:, :], [H], [HB])
        for tt in range(sq["nt"]):
            t0 = sq["x0"] + tt * n
            f = lambda: sc.get("f")
            b = lambda: sc.get("b")
            if tt > 0:
                CR = f()
                self.cp("pool", CR[:, 0:9].rearrange("p (j i) -> p j i", i=3), PC[:, :, n:n + 3], [PC], [CR])
                self.cp("pool", PC[:, :, 0:3], CR[:, 0:9].rearrange("p (j i) -> p j i", i=3), [CR], [PC])
            for j in range(3):
                pp = self.ps()
                self.proj_fm(pp[:, 0:n], j * 128, 128, XT, t0, n, pp)
                self.cp("act" if j % 2 == 0 else "dve", PC[:, j, 3:n + 3], pp[:, 0:n], [pp], [PC])
            PZ = self.gd_long[3]
            pp = self.ps()
            self.proj_fm(pp[:, 0:n], 384, 128, XT, t0, n, pp)
            self.cp("act", PZ[:, 0:n], pp[:, 0:n], [pp], [PZ])
            BA = f()
            pp = self.ps()
            self.proj_fm(pp[0:64, 0:n], 512, 64, XT, t0, n, pp)
            self.cp("dve", BA[0:64, 0:n], pp[0:64, 0:n], [pp], [BA])
            Y = []
            for j in range(3):
                jj = 2 * j + pr
                Yj = self.gd_long[j]
                self.ts("dve", Yj[:, 0:n], PC[:, j, 3:n + 3], self.pvc("cw%d" % l, 3 * 6 + jj), ALU.mult, [PC, self.PV], [Yj])
                for i in (2, 1, 0):
                    self.stt(Yj[:, 0:n], PC[:, j, i:i + n], self.pvc("cw%d" % l, i * 6 + jj), Yj[:, 0:n], ALU.mult, ALU.add,
                             [PC, self.PV, Yj], [Yj])
                E = f()
                self.act(E[:, 0:n], Yj[:, 0:n], AF.Exp, [Yj], [E], scale=-1.0)
                self.act(E[:, 0:n], E[:, 0:n], AF.Ln, [E], [E], bias=1.0)
                self.act(E[:, 0:n], E[:, 0:n], AF.Exp, [E], [E], scale=-1.0)
                self.tt("pool", Yj[:, 0:n], Yj[:, 0:n], E[:, 0:n], ALU.mult, [Yj, E], [Yj])
                Y.append(Yj)
            q_, k_, v_ = Y
            for (src_, scl) in ((q_, 0.125), (k_, 1.0)):
                Q2 = b()
                self.act(Q2[:, 0:n], src_[:, 0:n], AF.Square, [src_], [Q2])
                pss = self.ps()
                self.mm(pss[:, 0:n], self.ONESBD, Q2[:, 0:n], True, True, [self.CB, Q2], [pss])
                RN = f()
                self.act(RN[:, 0:n], pss[:, 0:n], AF.Ln, [pss, self.EPS], [RN], bias=self.EPS[:, 3:4])
                self.act(RN[:, 0:n], RN[:, 0:n], AF.Exp, [RN], [RN], scale=-0.5)
                self.stt(src_[:, 0:n], src_[:, 0:n], scl, RN[:, 0:n], ALU.mult, ALU.mult, [src_, RN], [src_])
            BG = f()
            self.act(BG[0:64, 0:n], BA[0:64, 0:n], AF.Exp, [BA], [BG], scale=-1.0)
            self.act(BG[0:64, 0:n], BG[0:64, 0:n], AF.Ln, [BG], [BG], bias=1.0)
            self.act(BG[0:64, 0:n], BG[0:64, 0:n], AF.Exp, [BG], [BG], scale=-1.0)
            SP = f()
            self.act(SP[0:64, 0:n], BA[0:64, 0:n], AF.Exp, [BA, self.PV], [SP], bias=self.pvc("dtb%d" % l, 0, slice(0, 64)))
            self.act(SP[0:64, 0:n], SP[0:64, 0:n], AF.Ln, [SP], [SP], bias=1.0)
            self.ts("dve", SP[0:64, 0:n], SP[0:64, 0:n], self.PD[0:64, l, 7:8], ALU.mult, [SP, self.PD], [SP])
            self.scan(GC[0:64, 0:n], self.cst("reset", n, slice(0, 64)), SP[0:64, 0:n], [self.CST, SP], [GC])
            EGC = f()
            self.act(EGC[0:64, 0:n], GC[0:64, 0:n], AF.Exp, [GC], [EGC])
            EGD = f()
            nch = n // C
            gc3 = GC[0:64, 0:n].rearrange("p (c t) -> p c t", t=C)
            self.tt("pool", EGD[0:64, 0:n].rearrange("p (c t) -> p c t", t=C), gc3[:, :, C - 1:C].to_broadcast([64, nch, C]), gc3,
                    ALU.subtract, [GC], [EGD])
            self.act(EGD[0:64, 0:n], EGD[0:64, 0:n], AF.Exp, [EGD], [EGD])
            pb_ = self.ps()
            self.mm(pb_[:, 0:n], self.cst("selb", 128, slice(0, 64), pr * 128), BG[0:64, 0:n], True, True, [self.CST, BG], [pb_])
            BETb = f()
            self.cp("act", BETb[:, 0:n], pb_[:, 0:n], [pb_], [BETb])
            pg_ = self.ps()
            self.mm(pg_[:, 0:n], self.cst("selg", 128, slice(0, 64), pr * 128), EGC[0:64, 0:n], True, True, [self.CST, EGC], [pg_])
            self.cp("act", EGCb[:, 0:n], pg_[:, 0:n], [pg_], [EGCb])
            pd_ = self.ps()
            self.mm(pd_[:, 0:n], self.cst("selg", 128, slice(0, 64), pr * 128), EGD[0:64, 0:n], True, True, [self.CST, EGD], [pd_])
            KD = b()
            self.tt("dve", KD[:, 0:n], k_[:, 0:n], pd_[:, 0:n], ALU.mult, [k_, pd_], [KD])
            KBf = f()
            self.tt("pool", KBf[:, 0:n], k_[:, 0:n], BETb[:, 0:n], ALU.mult, [k_, BETb], [KBf])
            self.cp("pool", KQ[:, 0, 0:n], KBf[:, 0:n], [KBf], [KQ])
            self.cp("pool", KQ[:, 1, 0:n], q_[:, 0:n], [q_], [KQ])
            QG = b()
            self.tt("dve", QG[:, 0:n], q_[:, 0:n], EGCb[:, 0:n], ALU.mult, [q_, EGCb], [QG])
            KBG = b()
            self.tt("dve", KBG[:, 0:n], KBf[:, 0:n], EGCb[:, 0:n], ALU.mult, [KBf, EGCb], [KBG])
            VBt = b()
            self.tt("pool", VBt[:, 0:n], v_[:, 0:n], BETb[:, 0:n], ALU.mult, [v_, BETb], [VBt])
            Kf = b()
            self.cp("pool", Kf[:, 0:n], k_[:, 0:n], [k_], [Kf])
            for g in range(n // G):
                g0 = g * G
                gi = dd["i"]
                DMg = DMM[gi % 2]
                pT = self.ps()
                self.tr(pT[0:G, 0:64], GC[0:64, g0:g0 + G], self.cst("ident", 64, slice(0, 64)), [GC, self.CST], [pT])
                self.cp("dve", GCT[0:G, :], pT[0:G, 0:64], [pT], [GCT])
                pR = self.ps()
                for hh in range(2):
                    h = 2 * pr + hh
                    self.mm(pR[0:G, hh * 128:hh * 128 + G], self.cst("selh", G, slice(0, 64), h * 128), GC[0:64, g0:g0 + G],
                            True, True, [self.CST, GC], [pR])
                DF = sc.get("df")
                for hh in range(2):
                    h = 2 * pr + hh
                    self.stt(DF[0:G, hh, 0:G], pR[0:G, hh * 128:hh * 128 + G], GCT[0:G, 32 + h:33 + h], self.cst("miu", G, slice(0, G)),
                             ALU.subtract, ALU.mult, [pR, GCT, self.CST], [DF])
                self.act(DF[0:G, :, 0:G], DF[0:G, :, 0:G], AF.Exp, [DF], [DF])
                for hh in range(2):
                    self.tt("pool", DMg[hh][0:G, 0, 0:G], DF[0:G, hh, 0:G], NMSU[0:G, 0:G], ALU.mult, [DF, NMSU], [DMg[hh]])
                    self.tt("pool", DMg[hh][0:G, 1, 0:G], DF[0:G, hh, 0:G], self.cst("miu", G, slice(0, G)), ALU.mult,
                            [DF, self.CST], [DMg[hh]])
                self.delta_group(dd, G, C, g0,
                                 masks=lambda hh, DMg=DMg: (DMg[hh][0:G, :, 0:G], [DMg[hh]]),
                                 score_mms=[(Kf, KQ, 2, 0)],
                                 tm_srcs=[KBG, VBt, KD],
                                 rw=False, Rf=QG, OA=OA,
                                 dec_ap=lambda col: EGCb[:, col:col + 1], dec_res=[EGCb])
            OQ = b()
            self.act(OQ[:, 0:n], OA[:, 0:n], AF.Square, [OA], [OQ])
            p2 = self.ps()
            self.mm(p2[:, 0:n], self.ONESBD, OQ[:, 0:n], True, True, [self.CB, OQ], [p2])
            RS = f()
            self.act(RS[:, 0:n], p2[:, 0:n], AF.Ln, [p2, self.EPS], [RS], bias=self.EPS[:, 2:3], scale=1.0 / 64)
            self.act(RS[:, 0:n], RS[:, 0:n], AF.Exp, [RS], [RS], scale=-0.5)
            Dn = f()
            self.tt("dve", Dn[:, 0:n], OA[:, 0:n], RS[:, 0:n], ALU.mult, [OA, RS], [Dn])
            self.silu_mul(sc, n, PZ[:, 0:n], [PZ], Dn, OT[:, 6 + pr, t0:t0 + n], [OT], extra_scale=self.pvc("gnorm%d" % l))
        o_cv, o_gd = sq["o_cv"], sq["o_gd"]
        for j in range(3):
            jj = 2 * j + pr
            self.dma(o_cv[:, jj * 128:(jj + 1) * 128].rearrange("i p -> p i"), PC[:, j, n:n + 3], [PC], [],
                     allow_slow_non_contiguous=True)
        for hh in range(2):
            self.dma(o_gd[2 * pr + hh], H[hh * 64:hh * 64 + 64, hh * 64:hh * 64 + 64], [H], [])

    def layer(self, l):
        if not getattr(self, "ot_zeroed", False):
            self.memset("pool", self.OT[:, :, :], 0.0, [self.OT])
            self.memset("pool", self.OTs[:, :, :], 0.0, [self.OTs])
            self.ot_zeroed = True
        if "A" in self.phases:
            for pr in range(2):
                self.phaseA(l, pr)
        if "C" in self.phases:
            for pr in range(2):
                self.phaseC(l, pr)
        if "B" in self.phases:
            self.phaseB(l)
        self.phaseD(l)


_CACHE = {}


def get_nc(T, PAST, **kw):
    key = (T, PAST, tuple(sorted(kw.items())))
    if key not in _CACHE:
        kb = KB(T, PAST, **kw)
        _CACHE[key] = (kb.build(), kb)
    return _CACHE[key]


def kernel(**inp):
    inp = {k: np.asarray(v) for k, v in inp.items()}
    B, T, _ = inp["x_prompt"].shape
    PAST = inp["cache_fox_k"].shape[2]
    nc, kb = get_nc(T, PAST, with_sample=not os.environ.get("NOSAMPLE"))
    pv = make_pv(inp)
    cst = make_cst()
    in_maps = []
    for c in range(NCORE):
        m = {"xp": np.ascontiguousarray(inp["x_prompt"][c]), "w_in": inp["w_in"], "w_out": inp["w_out"],
             "rwkv_w2": inp["rwkv_w2"], "rwkv_a2": inp["rwkv_a2"], "pv": pv, "cst": cst}
        if kb.with_sample:
            ss = slice(c * NS, (c + 1) * NS)
            m["xs"] = np.ascontiguousarray(inp["x_sample"][ss]).reshape(NS * TS, D_MODEL)
            m["ckT"] = np.ascontiguousarray(inp["cache_fox_k"][:, ss].transpose(0, 1, 3, 4, 2))
            m["cv"] = np.ascontiguousarray(inp["cache_fox_v"][:, ss]).reshape(L, NS, PAST, 512)
            m["clf"] = np.ascontiguousarray(inp["cache_fox_logf"][:, ss].transpose(0, 1, 3, 2))
            m["st_sh"] = np.ascontiguousarray(inp["state_rwkv_shift"][:, ss])
            m["st_rw"] = np.ascontiguousarray(inp["state_rwkv_wkv"][:, ss].transpose(0, 1, 2, 4, 3))
            m["st_cv"] = np.ascontiguousarray(inp["state_gdn_conv"][:, ss])
            m["st_gd"] = np.ascontiguousarray(inp["state_gdn_wkv"][:, ss])
        in_maps.append(m)
    if os.environ.get("KTRACE"):
        res = run_bass_kernel_spmd(nc, in_maps, core_ids=list(range(NCORE)), trace=True)
        print("EXEC_TIME_NS", res.exec_time_ns)
    else:
        res = run_bass_kernel_spmd(nc, in_maps, core_ids=list(range(NCORE)))
    R = res.results
    SB = NCORE * NS

    def gp(name, shape_tail, axis_layer=True):
        if name not in R[0]:
            return None
        return np.stack([np.asarray(R[c][name]) for c in range(NCORE)], axis=1)

    y_p = np.stack([R[c]["y_p"] for c in range(NCORE)])
    fk_p = gp("fk_p", None).reshape(L, NCORE, T, 8, 64)
    fv_p = gp("fv_p", None).reshape(L, NCORE, T, 8, 64)
    fl_p = gp("fl_p", None)
    sh_p = gp("sh_p", None)
    rw_p = gp("rw_p", None)
    cv_p = gp("cv_p", None)
    gd_p = gp("gd_p", None)

    def gs(name, tail):
        if name not in R[0]:
            return np.zeros((L, SB) + tail, np.float32)
        a = np.stack([np.asarray(R[c][name]) for c in range(NCORE)], axis=1)
        return a.reshape((L, SB) + tail)
    if "y_s" in R[0]:
        y_s = np.stack([R[c]["y_s"] for c in range(NCORE)]).reshape(SB, TS, D_MODEL)
    else:
        y_s = np.zeros((SB, TS, D_MODEL), np.float32)
    fk_s = gs("fk_s", (TS, 8, 64))
    fv_s = gs("fv_s", (TS, 8, 64))
    fl_s = gs("fl_s", (TS, 8))
    sh_s = gs("sh_s", (896,))
    rw_s = gs("rw_s", (4, 64, 64))
    cv_s = gs("cv_s", (3, 768))
    gd_s = gs("gd_s", (4, 64, 64))
    return (y_p, y_s, fk_p, fv_p, fl_p, sh_p, rw_p, cv_p, gd_p, fk_s, fv_s, fl_s, sh_s, rw_s, cv_s, gd_s)
```

```python
import math
from contextlib import ExitStack
import numpy as np
import concourse.bass as bass
import concourse.mybir as mybir
from concourse.bass_utils import run_bass_kernel_spmd

F32 = mybir.dt.float32
BF = mybir.dt.bfloat16
AF = mybir.ActivationFunctionType
ALU = mybir.AluOpType

D_MODEL = 1024
L = 4
N_IN = 4240
OFF = dict(a_sh=0, a_z=896, b_q=1152, b_k=1664, b_v=2176, b_f=2688, b_z=2696,
           c_qkv=3208, c_b=3976, c_a=3980, c_z=3984)
ALPHA = (2 * L) ** 0.25
LN_EPS = 1e-5
RWKV_GN_EPS = 64e-5
GDN_NORM_EPS = 1e-6
L2_EPS = 1e-6
import os
NCORE = int(os.environ.get("KCORES", "8"))
BSTAGE = int(os.environ.get("BSTAGE", "9"))
NS = 4
TS = 16


def pv_layout():
    off = {}
    n = 0

    def add(name, w):
        nonlocal n
        off[name] = n
        n += w
    add("ln_in_g", 8)
    add("ln_in_b", 8)
    for l in range(L):
        for nm, w in (("lng", 8), ("lnb", 8), ("mu", 7), ("w0", 2), ("a0", 2), ("kk", 2), ("ka", 2),
                      ("rk", 2), ("gng", 2), ("gnb", 2), ("cw", 24), ("gnorm", 1), ("bf", 1),
                      ("alog", 1), ("dtb", 1)):
            add("%s%d" % (nm, l), w)
    return off, n


PVO, NPV = pv_layout()


def cst_layout():
    off = {}
    n = 0
    for nm, w in (("ident", 128), ("onesbd", 128), ("ones", 128), ("tri", 128), ("msu", 128), ("miu", 128),
                  ("reset", 256), ("selb", 256), ("selg", 256), ("selh", 512)):
        off[nm] = n
        n += w
    return off, n


CSO, NCST = cst_layout()


def make_cst():
    c = np.zeros((128, NCST), np.float32)
    p = np.arange(128)[:, None]
    f = np.arange(128)[None, :]
    c[:, CSO["ident"]:CSO["ident"] + 128] = (p == f)
    c[:, CSO["onesbd"]:CSO["onesbd"] + 128] = (p // 64 == f // 64)
    c[:, CSO["ones"]:CSO["ones"] + 128] = 1.0
    c[:, CSO["tri"]:CSO["tri"] + 128] = (p <= f)
    c[:, CSO["msu"]:CSO["msu"] + 128] = (p < f) & (p // 64 == f // 64)
    c[:, CSO["miu"]:CSO["miu"] + 128] = (p <= f) & (p // 64 == f // 64)
    r = np.ones(256, np.float32)
    r[::64] = 0
    c[:, CSO["reset"]:CSO["reset"] + 256] = r[None, :]
    selb = np.zeros((128, 2, 128), np.float32)
    selg = np.zeros((128, 2, 128), np.float32)
    for pr in range(2):
        for hh in range(2):
            selb[2 * pr + hh, pr, hh * 64:(hh + 1) * 64] = 1
            selg[32 + 2 * pr + hh, pr, hh * 64:(hh + 1) * 64] = 1
    c[:, CSO["selb"]:CSO["selb"] + 256] = selb.reshape(128, 256)
    c[:, CSO["selg"]:CSO["selg"] + 256] = selg.reshape(128, 256)
    selh = np.zeros((128, 4, 128), np.float32)
    for h in range(4):
        selh[32 + h, h, :] = 1
    c[:, CSO["selh"]:CSO["selh"] + 512] = selh.reshape(128, 512)
    return c


def make_pv(inp):
    pv = np.zeros((128, NPV), np.float32)

    def put(name, vec, w):
        pv[:, PVO[name]:PVO[name] + w] = np.asarray(vec, np.float32).reshape(w, 128).T
    put("ln_in_g", inp["ln_in_g"], 8)
    put("ln_in_b", inp["ln_in_b"], 8)
    for l in range(L):
        put("lng%d" % l, inp["ln_post_g"][l], 8)
        put("lnb%d" % l, inp["ln_post_b"][l], 8)
        put("mu%d" % l, inp["rwkv_mu"][l], 7)
        put("w0%d" % l, inp["rwkv_w0"][l], 2)
        put("a0%d" % l, inp["rwkv_a0"][l], 2)
        put("kk%d" % l, inp["rwkv_k_k"][l], 2)
        put("ka%d" % l, inp["rwkv_k_a"][l], 2)
        put("rk%d" % l, np.asarray(inp["rwkv_r_k"][l]).reshape(256), 2)
        put("gng%d" % l, inp["rwkv_gn_g"][l], 2)
        put("gnb%d" % l, inp["rwkv_gn_b"][l], 2)
        cw = np.asarray(inp["gdn_conv_w"][l], np.float32)
        for i in range(4):
            pv[:, PVO["cw%d" % l] + i * 6:PVO["cw%d" % l] + i * 6 + 6] = cw[i].reshape(6, 128).T
        gn = np.asarray(inp["gdn_norm_g"][l], np.float32)
        pv[:, PVO["gnorm%d" % l]] = np.concatenate([gn, gn])
        bfv = np.asarray(inp["fox_b_f"][l], np.float32)
        for g in (0, 32, 64):
            pv[g:g + 8, PVO["bf%d" % l]] = bfv
        pv[32:36, PVO["alog%d" % l]] = np.asarray(inp["gdn_a_log"][l], np.float32)
        pv[32:36, PVO["dtb%d" % l]] = np.asarray(inp["gdn_dt_bias"][l], np.float32)
    return pv


class Res:
    __slots__ = ("w", "r", "psum")

    def __init__(self):
        self.w = None
        self.r = {}
        self.psum = False


class Eng:
    def __init__(self, name, unit):
        self.name = name
        self.unit = unit
        self.n = 0
        self.known = {}
        self.hist = {}
        self.ops = []
        self.sem = None


class MK:
    NDQ = 12
    CE = ("pe", "act", "dve", "pool", "sp")

    def __init__(self):
        self.E = {}
        for nm in self.CE:
            self.E[nm] = Eng(nm, 1)
        for i in range(self.NDQ):
            self.E["dq%d" % i] = Eng("dq%d" % i, 16)
        self.dq_rr = 0
        self.n_wait = 0
        self.n_ins = 0
        import threading
        self.tl = threading.local()

    def _need(self, W, reads, writes, is_dma=False):
        toks = {}

        def add(tok, same_ok):
            if tok is None:
                return
            e, n = tok
            if same_ok and e is W and W.name == "pe" and not is_dma:
                return
            if W.known.get(e.name, 0) >= n:
                return
            if toks.get(e.name, (None, 0))[1] < n:
                toks[e.name] = (e, n)

        for r in reads:
            add(r.w, False)
            if r.psum:
                for t in r.r.values():
                    if t[0] is not W:
                        add(t, True)
        for r in writes:
            add(r.w, True)
            for t in r.r.values():
                add(t, True)
        return list(toks.values())

    def _merge(self, W, e, n):
        snap = e.hist.get(n)
        if snap:
            for k, v in snap.items():
                if W.known.get(k, 0) < v:
                    W.known[k] = v
        if W.known.get(e.name, 0) < n:
            W.known[e.name] = n

    def _record(self, te, tn, reads, writes, snap):
        te.hist[tn] = snap
        tok = (te, tn)
        for r in reads:
            r.r[te.name] = tok
        for r in writes:
            r.w = tok
            r.r = {}

    def op(self, eng, fn, reads=(), writes=()):
        W = self.E[eng]
        waits = self._need(W, reads, writes)
        for e, n in waits:
            self._merge(W, e, n)
        W.n += 1
        n = W.n
        snap = dict(W.known)
        snap[W.name] = n
        self._record(W, n, reads, writes, snap)
        W.ops.append((fn, [(e, n_ * e.unit) for e, n_ in waits], W))
        self.n_wait += len(waits)
        self.n_ins += 1
        hk = getattr(self.tl, "hook", None)
        if hk:
            hk()

    def dma(self, out, in_, reads=(), writes=(), queue="sp", **kw):
        Q = self.E[queue]
        k = self.dq_rr
        self.dq_rr = (self.dq_rr + 1) % self.NDQ
        Dq = self.E["dq%d" % k]
        waits = self._need(Q, reads, writes, is_dma=True)
        if Dq.n > 0 and Q.known.get(Dq.name, 0) < Dq.n:
            waits = [w for w in waits if w[0] is not Dq] + [(Dq, Dq.n)]
        for e, n in waits:
            self._merge(Q, e, n)
        Dq.n += 1
        n = Dq.n
        snap = dict(Q.known)
        snap[Dq.name] = n
        self._record(Dq, n, reads, writes, snap)

        def fn(eng, out=out, in_=in_, kw=kw):
            return eng.dma_start(out=out, in_=in_, **kw)
        Q.ops.append((fn, [(e, n_ * e.unit) for e, n_ in waits], Dq))
        self.n_wait += len(waits)
        self.n_ins += 1
        hk = getattr(self.tl, "hook", None)
        if hk:
            hk()

    def barrier(self):
        for nm in self.CE:
            W = self.E[nm]
            waits = []
            for e in self.E.values():
                if e is W or e.n == 0:
                    continue
                if W.known.get(e.name, 0) < e.n:
                    waits.append((e, e.n))
            for e, n in waits:
                self._merge(W, e, n)
            W.ops.append((None, [(e, n * e.unit) for e, n in waits], None))

    def runner(self, sems):
        for e in self.E.values():
            e.sem = sems[e.name]

        def run(engobj, E):
            fuse = E.name in ("act", "dve", "pool")
            for fn, waits, inc_e in E.ops:
                if fn is None or not fuse or not waits:
                    for e, v in waits:
                        engobj.wait_ge(e.sem, v)
                    if fn is None:
                        continue
                    fn(engobj).then_inc(inc_e.sem, inc_e.unit)
                else:
                    for e, v in waits[:-1]:
                        engobj.wait_ge(e.sem, v)
                    ins = fn(engobj)
                    ins._wait_ge(waits[-1][0].sem, waits[-1][1])
                    ins.then_inc(inc_e.sem, inc_e.unit)
        return run


class Tile:
    def __init__(self, t):
        self.t = t
        self.r = Res()

    def __getitem__(self, k):
        return self.t[k]


class _V:
    def __init__(self, t, i):
        self.t = t
        self.i = i
        self.r = t.r

    def __getitem__(self, k):
        return self.t.t[(k[0], self.i) + tuple(k[1:])]


class Scope:
    def __init__(self, kb):
        self.kb = kb
        self.st = ExitStack()
        self.pools = {}

    def __enter__(self):
        self.st.__enter__()
        return self

    def __exit__(self, *a):
        self.kb.mk.barrier()
        return self.st.__exit__(*a)

    def sb(self, name, shape, dt=F32):
        self.kb.uid += 1
        return Tile(self.st.enter_context(self.kb.nc.sbuf_tensor("%s_%d" % (name, self.kb.uid), list(shape), dt)))

    def pool(self, name, n, shape, dt=F32):
        self.pools[name] = [[self.sb(name + str(i), shape, dt) for i in range(n)], 0]

    def get(self, name):
        p = self.pools[name]
        t = p[0][p[1] % len(p[0])]
        p[1] += 1
        return t


class KB:
    def __init__(self, T, PAST, with_sample=True, nlayers=L):
        self.T = T
        self.PAST = PAST
        self.with_sample = with_sample
        self.nl = nlayers
        self.nc = bass.Bass("TRN2", target_bir_lowering=False)
        self.mk = MK()
        self.st = ExitStack()
        self.uid = 0
        self.psp = {}
        self.phases = "ABCD"

    def sb(self, name, shape, dt=F32):
        return Tile(self.st.enter_context(self.nc.sbuf_tensor(name, list(shape), dt)))

    def din(self, name, shape, dt=F32):
        return self.nc.dram_tensor(name, list(shape), dt, kind="ExternalInput").ap()

    def dout(self, name, shape, dt=F32):
        return self.nc.dram_tensor(name, list(shape), dt, kind="ExternalOutput").ap()

    def ps(self, pool="g"):
        p = self.psp[pool]
        t = p[0][p[1] % len(p[0])]
        p[1] += 1
        return t

    @staticmethod
    def _rs(xs):
        return [x.r if isinstance(x, (Tile, _V)) else x for x in xs]

    def tt(self, eng, out, in0, in1, op, R, W):
        self.mk.op(eng, lambda e: e.tensor_tensor(out=out, in0=in0, in1=in1, op=op), self._rs(R), self._rs(W))

    def ts(self, eng, out, in0, s1, op0, R, W, s2=None, op1=None):
        if op1 is None:
            self.mk.op(eng, lambda e: e.tensor_scalar(out=out, in0=in0, scalar1=s1, scalar2=None, op0=op0),
                       self._rs(R), self._rs(W))
        else:
            self.mk.op(eng, lambda e: e.tensor_scalar(out=out, in0=in0, scalar1=s1, scalar2=s2, op0=op0, op1=op1),
                       self._rs(R), self._rs(W))

    def stt(self, out, in0, scalar, in1, op0, op1, R, W):
        self.mk.op("dve", lambda e: e.scalar_tensor_tensor(out=out, in0=in0, scalar=scalar, in1=in1, op0=op0, op1=op1),
                   self._rs(R), self._rs(W))

    def cp(self, eng, out, in_, R, W):
        if eng == "act":
            self.mk.op(eng, lambda e: e.copy(out=out, in_=in_), self._rs(R), self._rs(W))
        else:
            self.mk.op(eng, lambda e: e.tensor_copy(out=out, in_=in_), self._rs(R), self._rs(W))

    def act(self, out, in_, func, R, W, bias=0.0, scale=1.0):
        self.mk.op("act", lambda e: e.activation(out=out, in_=in_, func=func, bias=bias, scale=scale),
                   self._rs(R), self._rs(W))

    def mm(self, out, lhsT, rhs, start, stop, R, W):
        self.mk.op("pe", lambda e: e.matmul(out, lhsT=lhsT, rhs=rhs, start=start, stop=stop),
                   self._rs(R), self._rs(W))

    def tr(self, out, in_, ident, R, W):
        self.mk.op("pe", lambda e: e.transpose(out, in_, ident), self._rs(R), self._rs(W))

    def recip(self, out, in_, R, W):
        self.mk.op("dve", lambda e: e.reciprocal(out=out, in_=in_), self._rs(R), self._rs(W))

    def memset(self, eng, ap, val, W):
        self.mk.op(eng, lambda e: e.memset(ap, val), [], self._rs(W))

    def scan(self, out, d0, d1, R, W):
        self.mk.op("dve", lambda e: e.tensor_tensor_scan(out=out, data0=d0, data1=d1, initial=0.0,
                                                           op0=ALU.mult, op1=ALU.add), self._rs(R), self._rs(W))

    def dma(self, out, in_, R, W, **kw):
        self.mk.dma(out, in_, self._rs(R), self._rs(W), **kw)

    def pvc(self, name, c=0, rows=slice(0, 128)):
        o = PVO[name] + c
        return self.PV[rows, o:o + 1]

    def cst(self, name, w=128, rows=slice(0, 128), c0=0):
        o = CSO[name] + c0
        return self.CST[rows, o:o + w]

    def load_w(self, src3, col_ranges):
        for (sc, w, dc) in col_ranges:
            o = 0
            while o < w:
                ww = min(64, w - o)
                stg = self.get_ws()
                self.dma(stg[:, :, 0:ww], src3[:, :, sc + o:sc + o + ww], [], [stg])
                self.cp("pool", self.WB[:, :, dc + o:dc + o + ww], stg[:, :, 0:ww], [stg], [self.WB])
                o += ww

    def get_ws(self):
        t = self.WS[self.ws_i % len(self.WS)]
        self.ws_i += 1
        return t

    def proj_fm(self, ps_ap, wcol, ncols, xt, t0, n, pst):
        for k in range(8):
            self.mm(ps_ap, self.WB[:, k, wcol:wcol + ncols], xt[:, k, t0:t0 + n], k == 0, k == 7,
                    [self.WB, xt], [pst])

    def ln_fm(self, sc, Vt, n, gname, bname, out_bf=None, out_f32=None):
        VB = sc.get("lnvb")
        VQ = sc.get("lnvq")
        self.cp("act", VB[:, :, 0:n], Vt[:, :, 0:n], [Vt], [VB])
        self.act(VQ[:, :, 0:n], Vt[:, :, 0:n], AF.Square, [Vt], [VQ])
        p1 = self.ps()
        p2 = self.ps()
        for c in range(8):
            self.mm(p1[:, 0:n], self.ONESB[:, :], VB[:, c, 0:n], c == 0, c == 7, [self.CB, VB], [p1])
        for c in range(8):
            self.mm(p2[:, 0:n], self.ONESB[:, :], VQ[:, c, 0:n], c == 0, c == 7, [self.CB, VQ], [p2])
        ME = sc.get("lnt")
        MS = sc.get("lnt")
        VA = sc.get("lnt")
        RS = sc.get("lnt")
        self.ts("dve", ME[:, 0:n], p1[:, 0:n], 1.0 / D_MODEL, ALU.mult, [p1], [ME])
        self.tt("pool", MS[:, 0:n], ME[:, 0:n], ME[:, 0:n], ALU.mult, [ME], [MS])
        self.stt(VA[:, 0:n], p2[:, 0:n], 1.0 / D_MODEL, MS[:, 0:n], ALU.mult, ALU.subtract, [p2, MS], [VA])
        self.act(RS[:, 0:n], VA[:, 0:n], AF.Ln, [VA], [RS], bias=self.EPS[:, 0:1], scale=1.0)
        self.act(RS[:, 0:n], RS[:, 0:n], AF.Exp, [RS], [RS], scale=-0.5)
        for c in range(8):
            Dd = sc.get("lnd")
            self.tt("pool", Dd[:, 0:n], Vt[:, c, 0:n], ME[:, 0:n], ALU.subtract, [Vt, ME], [Dd])
            self.tt("dve", Dd[:, 0:n], Dd[:, 0:n], RS[:, 0:n], ALU.mult, [Dd, RS], [Dd])
            if out_bf is not None:
                ap, tl = out_bf(c)
                self.act(ap, Dd[:, 0:n], AF.Identity, [Dd, self.PV], [tl],
                         bias=self.pvc(bname, c), scale=self.pvc(gname, c))
            if out_f32 is not None:
                ap, tl = out_f32(c)
                self.act(ap, Dd[:, 0:n], AF.Identity, [Dd, self.PV], [tl],
                         bias=self.pvc(bname, c), scale=self.pvc(gname, c))

    def build(self):
        nc, mk = self.nc, self.mk
        T, PAST = self.T, self.PAST
        NB = T // 128
        d = {}
        d["xp"] = self.din("xp", [T, D_MODEL])
        d["w_in"] = self.din("w_in", [L, D_MODEL, N_IN])
        d["w_out"] = self.din("w_out", [L, D_MODEL, D_MODEL])
        d["w2"] = self.din("rwkv_w2", [L, 64, 256])
        d["a2"] = self.din("rwkv_a2", [L, 64, 256])
        d["pv"] = self.din("pv", [128, NPV])
        d["cst"] = self.din("cst", [128, NCST])
        o = {}
        o["y_p"] = self.dout("y_p", [T, D_MODEL])
        o["fk_p"] = self.dout("fk_p", [L, T, 512])
        o["fv_p"] = self.dout("fv_p", [L, T, 512])
        o["fl_p"] = self.dout("fl_p", [L, T, 8])
        o["sh_p"] = self.dout("sh_p", [L, 896])
        o["rw_p"] = self.dout("rw_p", [L, 4, 64, 64])
        o["cv_p"] = self.dout("cv_p", [L, 3, 768])
        o["gd_p"] = self.dout("gd_p", [L, 4, 64, 64])
        if self.with_sample:
            d["xs"] = self.din("xs", [NS * TS, D_MODEL])
            d["ckT"] = self.din("ckT", [L, NS, 8, 64, PAST])
            d["cv"] = self.din("cv", [L, NS, PAST, 512])
            d["clf"] = self.din("clf", [L, NS, 8, PAST])
            d["st_sh"] = self.din("st_sh", [L, NS, 896])
            d["st_rw"] = self.din("st_rw", [L, NS, 4, 64, 64])
            d["st_cv"] = self.din("st_cv", [L, NS, 3, 768])
            d["st_gd"] = self.din("st_gd", [L, NS, 4, 64, 64])
            o["y_s"] = self.dout("y_s", [NS * TS, D_MODEL])
            o["fk_s"] = self.dout("fk_s", [L, NS, TS, 512])
            o["fv_s"] = self.dout("fv_s", [L, NS, TS, 512])
            o["fl_s"] = self.dout("fl_s", [L, NS, TS, 8])
            o["sh_s"] = self.dout("sh_s", [L, NS, 896])
            o["rw_s"] = self.dout("rw_s", [L, NS, 4, 64, 64])
            o["cv_s"] = self.dout("cv_s", [L, NS, 3, 768])
            o["gd_s"] = self.dout("gd_s", [L, NS, 4, 64, 64])
        self.d, self.o = d, o

        with self.st:
            self.XT = self.sb("XT", [128, 8, T], BF)
            self.OT = self.sb("OT", [128, 8, T], BF)
            self.XTs = self.sb("XTs", [128, 8, NS * TS], BF)
            self.OTs = self.sb("OTs", [128, 8, NS * TS], BF)
            self.PV = self.sb("PV", [128, NPV])
            self.CST = self.sb("CST", [128, NCST])
            self.CB = self.sb("CB", [128, 6, 128], BF)
            self.WB = self.sb("WB", [128, 8, 1024], BF)
            self.WS = [self.sb("WS%d" % i, [128, 8, 64]) for i in range(2)]
            self.ws_i = 0
            self.EPS = self.sb("EPS", [128, 4])
            self.PD = self.sb("PD", [128, L, 12])
            g = [Tile(self.st.enter_context(nc.psum_tensor("PG%d" % i, [128, 512], F32))) for i in range(5)]
            a = [Tile(self.st.enter_context(nc.psum_tensor("PA%d" % i, [128, 512], F32))) for i in range(2)]
            tb = [Tile(self.st.enter_context(nc.psum_tensor("PT%d" % i, [128, 1024], BF))) for i in range(1)]
            for t_ in g + a + tb:
                t_.r.psum = True
            self.psp = {"g": [g, 0], "a": [a, 0], "tb": [tb, 0]}

            self.IDB = self.CB[:, 0, :]
            self.ONESBD = self.CB[:, 1, :]
            self.ONESB = self.CB[:, 2, :]
            self.TRIB = self.CB[:, 3, :]
            self.MSUB = self.CB[:, 4, :]
            self.MIUB = self.CB[:, 5, :]

            self.dma(self.PV[:, :], d["pv"], [], [self.PV])
            self.dma(self.CST[:, :], d["cst"], [], [self.CST])
            for i, nm in enumerate(("ident", "onesbd", "ones", "tri", "msu", "miu")):
                self.cp("pool", self.CB[:, i, :], self.cst(nm), [self.CST], [self.CB])
            self.memset("pool", self.EPS[:, 0:1], LN_EPS, [self.EPS])
            self.memset("pool", self.EPS[:, 1:2], RWKV_GN_EPS, [self.EPS])
            self.memset("pool", self.EPS[:, 2:3], GDN_NORM_EPS, [self.EPS])
            self.memset("pool", self.EPS[:, 3:4], L2_EPS, [self.EPS])
            for l in range(self.nl):
                for c in range(2):
                    self.ts("pool", self.PD[:, l, c:c + 1], self.pvc("w0%d" % l, c), -1.0, ALU.mult, [self.PV], [self.PD])
                    self.ts("pool", self.PD[:, l, 2 + c:3 + c], self.pvc("a0%d" % l, c), -1.0, ALU.mult, [self.PV], [self.PD])
                    self.ts("pool", self.PD[:, l, 4 + c:5 + c], self.pvc("ka%d" % l, c), -1.0, ALU.mult, [self.PV], [self.PD],
                            s2=1.0, op1=ALU.add)
                self.ts("pool", self.PD[:, l, 6:7], self.pvc("bf%d" % l), -1.0, ALU.mult, [self.PV], [self.PD])
                self.act(self.PD[:, l, 7:8], self.pvc("alog%d" % l), AF.Exp, [self.PV], [self.PD])
                self.ts("pool", self.PD[:, l, 7:8], self.PD[:, l, 7:8], -1.0, ALU.mult, [self.PD], [self.PD])
            mk.barrier()

            self.phase0()
            for l in range(self.nl):
                self.layer(l)
            mk.barrier()

            sems = {}
            for nm in mk.E:
                sems[nm] = self.st.enter_context(nc.semaphore("s_" + nm))
            run = mk.runner(sems)
            with nc.Block() as block:
                @block.tensor
                def _(e):
                    run(e, mk.E["pe"])

                @block.vector
                def _(e):
                    run(e, mk.E["dve"])

                @block.scalar
                def _(e):
                    run(e, mk.E["act"])

                @block.gpsimd
                def _(e):
                    run(e, mk.E["pool"])

                @block.sync
                def _(e):
                    run(e, mk.E["sp"])
        return nc

    def ln_pools(self, sc, n):
        sc.pool("v", 1, [128, 8, n])
        sc.pool("lnvb", 1, [128, 8, n], BF)
        sc.pool("lnvq", 1, [128, 8, n], BF)
        sc.pool("lnt", 4, [128, n])
        sc.pool("lnd", 3, [128, n])

    def phase0(self):
        n = 256
        with Scope(self) as sc:
            sc.pool("xin", 1, [128, 2, D_MODEL])
            self.ln_pools(sc, n)
            segs = [(self.d["xp"], self.XT, self.T)]
            if self.with_sample:
                segs.append((self.d["xs"], self.XTs, NS * TS))
            for (xd, XT, TT) in segs:
                for t0 in range(0, TT, n):
                    nn = min(n, TT - t0)
                    XI = sc.get("xin")
                    nb = (nn + 127) // 128
                    bw = min(128, nn)
                    self.dma(XI[0:bw, 0:nb, :], xd[t0:t0 + nn, :].rearrange("(b p) f -> p b f", p=bw), [], [XI])
                    Vt = sc.get("v")
                    for c in range(8):
                        pp = self.ps()
                        for bb in range(nb):
                            self.tr(pp[:, bb * 128:bb * 128 + bw], XI[0:bw, bb, c * 128:(c + 1) * 128],
                                    self.cst("ident", bw, slice(0, bw)), [XI, self.CST], [pp])
                        self.cp("act" if c % 2 else "dve", Vt[:, c, 0:nn], pp[:, 0:nn], [pp], [Vt])
                    self.ln_fm(sc, Vt, nn, "ln_in_g", "ln_in_b",
                               out_bf=lambda c, t0=t0, nn=nn, XT=XT: (XT[:, c, t0:t0 + nn], XT))

    def phaseD(self, l):
        last = (l == L - 1)
        w3 = self.d["w_out"][l].rearrange("(k p) n -> p k n", p=128)
        self.load_w(w3, [(0, 1024, 0)])
        n = 256
        with Scope(self) as sc:
            self.ln_pools(sc, n)
            if last:
                sc.pool("yf", 1, [128, 8, n])
                sc.pool("yt", 2, [128, D_MODEL])
            segs = [(self.XT, self.OT, self.T, self.o["y_p"])]
            if self.with_sample:
                segs.append((self.XTs, self.OTs, NS * TS, self.o["y_s"]))
            for (XT, OT, TT, yd) in segs:
                for t0 in range(0, TT, n):
                    nn = min(n, TT - t0)
                    Vt = sc.get("v")
                    for c in range(8):
                        pp = self.ps()
                        for k in range(8):
                            self.mm(pp[:, 0:nn], self.WB[:, k, c * 128:(c + 1) * 128], OT[:, k, t0:t0 + nn],
                                    k == 0, k == 7, [self.WB, OT], [pp])
                        self.stt(Vt[:, c, 0:nn], XT[:, c, t0:t0 + nn], ALPHA, pp[:, 0:nn], ALU.mult, ALU.add,
                                 [XT, pp], [Vt])
                    if not last:
                        self.ln_fm(sc, Vt, nn, "lng%d" % l, "lnb%d" % l,
                                   out_bf=lambda c, t0=t0, nn=nn, XT=XT: (XT[:, c, t0:t0 + nn], XT))
                    else:
                        YF = sc.get("yf")
                        self.ln_fm(sc, Vt, nn, "lng%d" % l, "lnb%d" % l,
                                   out_f32=lambda c, nn=nn, YF=YF: (YF[:, c, 0:nn], YF))
                        bw = min(128, nn)
                        for bb in range((nn + 127) // 128):
                            YT = sc.get("yt")
                            for half in range(2):
                                pp = self.ps()
                                for cc in range(4):
                                    c = half * 4 + cc
                                    self.tr(pp[0:bw, cc * 128:(cc + 1) * 128], YF[:, c, bb * 128:bb * 128 + bw],
                                            self.cst("ident"), [YF, self.CST], [pp])
                                self.cp("act" if half else "dve", YT[0:bw, half * 512:(half + 1) * 512], pp[0:bw, :], [pp], [YT])
                            self.dma(yd[t0 + bb * 128:t0 + bb * 128 + bw, :], YT[0:bw, :], [YT], [])

    def phaseB(self, l):
        T = self.T
        NT = T // 512
        NB = T // 128
        w3 = self.d["w_in"][l].rearrange("(k p) n -> p k n", p=128)
        with Scope(self) as so:
            HL3 = so.sb("HL3", [128, T], BF)
            NCK = so.sb("NCK", [128, NB, 8])
            if self.with_sample:
                nkb_s = self.PAST // 128
                self.HL3s = so.sb("HL3s", [128, NS, TS], BF)
                self.NCKs = so.sb("NCKs", [128, NS, nkb_s + 1, 8])
            with Scope(self) as sc:
                WF = sc.sb("WF", [128, 8, 72], BF)
                LFT = sc.sb("LFT", [128, NB, 8])
                CAR = sc.sb("CAR", [128, 2])
                sc.pool("t", 6, [128, 512])
                sc.pool("tb", 3, [128, 512], BF)
                self.memset("pool", WF[:, :, :], 0.0, [WF])
                self.memset("pool", HL3[:, :], 0.0, [HL3])
                self.memset("pool", CAR[:, :], 0.0, [CAR])
                stg = self.get_ws()
                self.dma(stg[:, :, 0:8], w3[:, :, OFF["b_f"]:OFF["b_f"] + 8], [], [stg])
                for g in (0, 32, 64):
                    self.cp("pool", WF[:, :, g:g + 8], stg[:, :, 0:8], [stg], [WF])
                ones_b = self.cst("ones", 1, slice(0, 72)).to_broadcast([72, 512])
                for tt in range(NT):
                    t0 = tt * 512
                    pp = self.ps()
                    for k in range(8):
                        self.mm(pp[0:72, :], WF[:, k, :], self.XT[:, k, t0:t0 + 512], k == 0, k == 7, [WF, self.XT], [pp])
                    LS = sc.get("t")
                    self.act(LS[0:72, :], pp[0:72, :], AF.Exp, [pp, self.PD], [LS], bias=self.PD[0:72, l, 6:7], scale=-1.0)
                    self.act(LS[0:72, :], LS[0:72, :], AF.Ln, [LS], [LS], bias=1.0, scale=1.0)
                    CUMN = sc.get("t")
                    self.mk.op("dve", lambda e, CUMN=CUMN, LS=LS, tt=tt: e.tensor_tensor_scan(
                        out=CUMN[0:72, :], data0=ones_b, data1=LS[0:72, :], initial=CAR[0:72, tt % 2:tt % 2 + 1],
                        op0=ALU.mult, op1=ALU.add), self._rs([self.CST, LS, CAR]), self._rs([CUMN]))
                    self.cp("pool", CAR[0:72, (tt + 1) % 2:(tt + 1) % 2 + 1], CUMN[0:72, 511:512], [CUMN], [CAR])
                    HI = sc.get("tb")
                    self.ts("dve", HI[0:72, :], CUMN[0:72, :], -1.0, ALU.mult, [CUMN], [HI])
                    self.cp("pool", HL3[0:8, t0:t0 + 512], HI[0:8, :], [HI], [HL3])
                    R1 = sc.get("t")
                    self.stt(R1[0:72, :], CUMN[0:72, :], -1.0, HI[0:72, :], ALU.mult, ALU.subtract, [CUMN, HI], [R1])
                    MI = sc.get("tb")
                    self.cp("pool", MI[0:72, :], R1[0:72, :], [R1], [MI])
                    self.cp("pool", HL3[32:40, t0:t0 + 512], MI[32:40, :], [MI], [HL3])
                    R2 = sc.get("t")
                    self.tt("dve", R2[64:72, :], R1[64:72, :], MI[64:72, :], ALU.subtract, [R1, MI], [R2])
                    self.cp("pool", HL3[64:72, t0:t0 + 512], R2[64:72, :], [R2], [HL3])
                    p1 = self.ps()
                    p2 = self.ps()
                    for bb in range(4):
                        self.tr(p1[:, bb * 8:(bb + 1) * 8], CUMN[0:8, bb * 128:(bb + 1) * 128],
                                self.cst("ident", 8, slice(0, 8)), [CUMN, self.CST], [p1])
                        self.tr(p2[:, bb * 8:(bb + 1) * 8], LS[0:8, bb * 128:(bb + 1) * 128],
                                self.cst("ident", 8, slice(0, 8)), [LS, self.CST], [p2])
                    self.cp("dve", NCK[:, tt * 4:tt * 4 + 4, :], p1[:, 0:32].rearrange("p (b h) -> p b h", h=8), [p1], [NCK])
                    self.ts("dve", LFT[:, tt * 4:tt * 4 + 4, :], p2[:, 0:32].rearrange("p (b h) -> p b h", h=8), -1.0, ALU.mult, [p2], [LFT])
                for b0 in range(0, NB, 8):
                    self.dma(self.o["fl_p"][l, b0 * 128:(b0 + 8) * 128 if b0 + 8 <= NB else NB * 128, :].rearrange("(b p) h -> p b h", p=128),
                             LFT[:, b0:min(b0 + 8, NB), :], [LFT], [])
                if self.with_sample:
                    self.fox_sample_setup(l, sc, WF, so)
            for h in range(8 if BSTAGE >= 1 else 0):
                self.fox_head(l, h, so, HL3, NCK, w3)

    def fox_head(self, l, h, so, HL3, NCK, w3):
        T = self.T
        NT = T // 512
        NB = T // 128
        hp, hh = h // 2, h % 2
        self.load_w(w3, [(OFF["b_q"] + h * 64, 64, 0), (OFF["b_k"] + h * 64, 64, 64),
                         (OFF["b_v"] + h * 64, 64, 128), (OFF["b_z"] + h * 64, 64, 192)])
        with Scope(self) as sc:
            QA = sc.sb("QA", [128, T], BF)
            KA = sc.sb("KA", [128, T], BF)
            VA = sc.sb("VA", [128, NB, 128], BF)
            sc.pool("kvo", 1, [128, 4, 128])
            sc.pool("pt", 3, [128, 512], BF)
            sc.pool("t", 5, [128, 256])
            if "m" not in os.environ.get("SKIP", ""):
                self.memset("pool", QA[:, :], 0.0, [QA])
                self.memset("pool", KA[:, :], 0.0, [KA])
                self.memset("pool", KA[64:67, :], 1.0, [KA])
                self.memset("pool", VA[:, :, 64:128], 1.0, [VA])
            for i, g in enumerate((0, 32, 64)):
                if os.environ.get("NOSB2SB"):
                    continue
                self.dma(QA[64 + i:65 + i, :], HL3[g + h:g + h + 1, :], [HL3], [QA])
            SK = os.environ.get("SKIP", "")
            for tt in range(NT):
                t0 = tt * 512
                if "q" not in SK:
                    pq = self.ps()
                    self.proj_fm(pq[0:64, :], 0, 64, self.XT, t0, 512, pq)
                    self.act(QA[0:64, t0:t0 + 512], pq[0:64, :], AF.Identity, [pq], [QA], scale=0.125)
                if "k" not in SK:
                    pk = self.ps()
                    self.proj_fm(pk[0:64, :], 64, 64, self.XT, t0, 512, pk)
                    self.cp("dve", KA[0:64, t0:t0 + 512], pk[0:64, :], [pk], [KA])
                if "t" in SK:
                    continue
                KVO = sc.get("kvo")
                pkv = self.ps()
                for b in range(4):
                    for k in range(8):
                        self.mm(pkv[:, b * 128:(b + 1) * 128], self.XT[:, k, t0 + b * 128:t0 + (b + 1) * 128],
                                self.WB[:, k, 64:192], k == 0, k == 7, [self.XT, self.WB], [pkv])
                self.cp("act", KVO[:, :, :], pkv[:, :].rearrange("p (b c) -> p b c", c=128), [pkv], [KVO])
                if "v" not in SK:
                    self.cp("dve", VA[:, tt * 4:(tt + 1) * 4, 0:64], pkv[:, :].rearrange("p (b c) -> p b c", c=128)[:, :, 64:128], [pkv], [VA])
                if not os.environ.get("NOKVOUT"):
                    self.dma(self.o["fk_p"][l, t0:t0 + 512, h * 64:(h + 1) * 64].rearrange("(b p) c -> p b c", p=128),
                             KVO[:, :, 0:64], [KVO], [])
                    self.dma(self.o["fv_p"][l, t0:t0 + 512, h * 64:(h + 1) * 64].rearrange("(b p) c -> p b c", p=128),
                             KVO[:, :, 64:128], [KVO], [])
            for qt in range(NT if BSTAGE >= 2 else 0):
                q0 = qt * 512
                acc = self.ps("a")
                nkb = 4 * qt + 4
                sps = {}

                def issue(kb_):
                    c0_ = max(0, kb_ * 128 - q0)
                    sp_ = self.ps()
                    self.mm(sp_[:, c0_:512], KA[:, kb_ * 128:(kb_ + 1) * 128], QA[:, q0 + c0_:q0 + 512], True, True, [KA, QA], [sp_])
                    sps[kb_] = sp_
                for kb_ in range(min(2, nkb)):
                    issue(kb_)
                for kb in range(nkb):
                    c0 = max(0, kb * 128 - q0)
                    if kb + 2 < nkb:
                        issue(kb + 2)
                    sp = sps.pop(kb)
                    pt = sc.get("pt")
                    self.act(pt[:, c0:512], sp[:, c0:512], AF.Exp, [sp, NCK], [pt], bias=NCK[:, kb, h:h + 1], scale=1.0)
                    if kb * 128 >= q0:
                        self.tt("pool", pt[:, c0:c0 + 128], pt[:, c0:c0 + 128], self.TRIB, ALU.mult, [pt, self.CB], [pt])
                    self.mm(acc[:, c0:512], VA[:, kb, :], pt[:, c0:512], kb == 0, kb == nkb - 1, [VA, pt], [acc])
                for hc in (0, 256):
                    RC = sc.get("t")
                    self.recip(RC[0:64, :], acc[64:128, hc:hc + 256], [acc], [RC])
                    ON = sc.get("t")
                    self.tt("dve", ON[0:64, :], acc[0:64, hc:hc + 256], RC[0:64, :], ALU.mult, [acc, RC], [ON])
                    pz = self.ps()
                    self.proj_fm(pz[0:64, 0:256], 192, 64, self.XT, q0 + hc, 256, pz)
                    E = sc.get("t")
                    self.act(E[0:64, :], pz[0:64, 0:256], AF.Exp, [pz], [E], scale=-1.0)
                    self.act(E[0:64, :], E[0:64, :], AF.Ln, [E], [E], bias=1.0)
                    R_ = sc.get("t")
                    self.act(R_[0:64, :], E[0:64, :], AF.Exp, [E], [R_], scale=-1.0)
                    self.tt("dve", R_[0:64, :], pz[0:64, 0:256], R_[0:64, :], ALU.mult, [pz, R_], [R_])
                    self.tt("dve", self.OT[hh * 64:(hh + 1) * 64, 2 + hp, q0 + hc:q0 + hc + 256], ON[0:64, :], R_[0:64, :], ALU.mult,
                            [ON, R_], [self.OT])
        if self.with_sample:
            self.fox_sample_head(l, h)

    def fox_sample_setup(self, l, sc, WF, so):
        PAST = self.PAST
        nkb = PAST // 128
        W = PAST + TS
        LSs = sc.sb("LSs", [128, W])
        CUMs = sc.sb("CUMs", [128, W])
        LFs = sc.sb("LFs", [TS, 8])
        self.memset("pool", LSs[:, :], 0.0, [LSs])
        self.memset("pool", self.HL3s[:, :, :], 0.0, [self.HL3s])
        ones_b = self.cst("ones", 1, slice(0, 72)).to_broadcast([72, W])
        for s in range(NS):
            for g in (0, 32, 64):
                self.dma(LSs[g:g + 8, 0:PAST], self.d["clf"][l, s], [], [LSs])
            self.ts("dve", LSs[0:72, 0:PAST], LSs[0:72, 0:PAST], -1.0, ALU.mult, [LSs], [LSs])
            pp = self.ps()
            for k in range(8):
                self.mm(pp[0:72, 0:TS], WF[:, k, :], self.XTs[:, k, s * TS:(s + 1) * TS], k == 0, k == 7, [WF, self.XTs], [pp])
            E = sc.get("t")
            self.act(E[0:72, 0:TS], pp[0:72, 0:TS], AF.Exp, [pp, self.PD], [E], bias=self.PD[0:72, l, 6:7], scale=-1.0)
            self.act(LSs[0:72, PAST:W], E[0:72, 0:TS], AF.Ln, [E], [LSs], bias=1.0, scale=1.0)
            self.scan(CUMs[0:72, :], ones_b, LSs[0:72, :], [self.CST, LSs], [CUMs])
            HI = sc.get("tb")
            self.ts("dve", HI[0:72, 0:TS], CUMs[0:72, PAST:W], -1.0, ALU.mult, [CUMs], [HI])
            self.cp("pool", self.HL3s[0:8, s, :], HI[0:8, 0:TS], [HI], [self.HL3s])
            R1 = sc.get("t")
            self.stt(R1[0:72, 0:TS], CUMs[0:72, PAST:W], -1.0, HI[0:72, 0:TS], ALU.mult, ALU.subtract, [CUMs, HI], [R1])
            MI = sc.get("tb")
            self.cp("pool", MI[0:72, 0:TS], R1[0:72, 0:TS], [R1], [MI])
            self.cp("pool", self.HL3s[32:40, s, :], MI[32:40, 0:TS], [MI], [self.HL3s])
            R2 = sc.get("t")
            self.tt("dve", R2[64:72, 0:TS], R1[64:72, 0:TS], MI[64:72, 0:TS], ALU.subtract, [R1, MI], [R2])
            self.cp("pool", self.HL3s[64:72, s, :], R2[64:72, 0:TS], [R2], [self.HL3s])
            p1 = self.ps()
            for kb in range(nkb):
                self.tr(p1[:, kb * 8:(kb + 1) * 8], CUMs[0:8, kb * 128:(kb + 1) * 128], self.cst("ident", 8, slice(0, 8)),
                        [CUMs, self.CST], [p1])
            self.tr(p1[0:TS, nkb * 8:(nkb + 1) * 8], CUMs[0:8, PAST:W], self.cst("ident", 8, slice(0, 8)), [CUMs, self.CST], [p1])
            self.cp("dve", self.NCKs[:, s, 0:nkb, :], p1[:, 0:nkb * 8].rearrange("p (b h) -> p b h", h=8), [p1], [self.NCKs])
            self.cp("dve", self.NCKs[0:TS, s, nkb, :], p1[0:TS, nkb * 8:(nkb + 1) * 8], [p1], [self.NCKs])
            p2 = self.ps()
            self.tr(p2[0:TS, 0:8], LSs[0:8, PAST:W], self.cst("ident", 8, slice(0, 8)), [LSs, self.CST], [p2])
            self.ts("dve", LFs[:, :], p2[0:TS, 0:8], -1.0, ALU.mult, [p2], [LFs])
            self.dma(self.o["fl_s"][l, s], LFs[:, :], [LFs], [])

    def fox_sample_head(self, l, h):
        PAST = self.PAST
        nkb = PAST // 128
        W = PAST + TS
        hp, hh = h // 2, h % 2
        with Scope(self) as sc:
            KAs = sc.sb("KAs", [128, W], BF)
            QAs = sc.sb("QAs", [128, TS], BF)
            VAs = sc.sb("VAs", [128, nkb + 1, 128], BF)
            sc.pool("kst", 2, [64, PAST])
            sc.pool("vst", 2, [128, nkb, 64])
            sc.pool("kvo", 2, [TS, 128])
            sc.pool("pt", 3, [128, TS], BF)
            sc.pool("t", 5, [64, TS])
            self.memset("pool", KAs[:, :], 0.0, [KAs])
            self.memset("pool", KAs[64:67, :], 1.0, [KAs])
            self.memset("pool", QAs[:, :], 0.0, [QAs])
            self.memset("pool", VAs[:, :, :], 0.0, [VAs])
            self.memset("pool", VAs[:, :, 64:128], 1.0, [VAs])
            for s in range(NS):
                s0 = s * TS
                KS = sc.get("kst")
                self.dma(KS[:, :], self.d["ckT"][l, s, h], [], [KS])
                self.cp("pool", KAs[0:64, 0:PAST], KS[:, :], [KS], [KAs])
                VS = sc.get("vst")
                self.dma(VS[:, :, :], self.d["cv"][l, s][:, h * 64:(h + 1) * 64].rearrange("(b p) c -> p b c", p=128), [], [VS])
                self.cp("pool", VAs[:, 0:nkb, 0:64], VS[:, :, :], [VS], [VAs])
                for i, g in enumerate((0, 32, 64)):
                    self.dma(QAs[64 + i:65 + i, :], self.HL3s[g + h:g + h + 1, s, :], [self.HL3s], [QAs])
                pq = self.ps()
                self.proj_fm(pq[0:64, 0:TS], 0, 64, self.XTs, s0, TS, pq)
                self.act(QAs[0:64, :], pq[0:64, 0:TS], AF.Identity, [pq], [QAs], scale=0.125)
                pk = self.ps()
                self.proj_fm(pk[0:64, 0:TS], 64, 64, self.XTs, s0, TS, pk)
                self.cp("dve", KAs[0:64, PAST:W], pk[0:64, 0:TS], [pk], [KAs])
                pkv = self.ps()
                for k in range(8):
                    self.mm(pkv[0:TS, 0:128], self.XTs[:, k, s0:s0 + TS], self.WB[:, k, 64:192], k == 0, k == 7,
                            [self.XTs, self.WB], [pkv])
                KVO = sc.get("kvo")
                self.cp("act", KVO[:, :], pkv[0:TS, 0:128], [pkv], [KVO])
                self.cp("pool", VAs[0:TS, nkb, 0:64], KVO[:, 64:128], [KVO], [VAs])
                self.dma(self.o["fk_s"][l, s, :, h * 64:(h + 1) * 64], KVO[:, 0:64], [KVO], [])
                self.dma(self.o["fv_s"][l, s, :, h * 64:(h + 1) * 64], KVO[:, 64:128], [KVO], [])
                acc = self.ps("a")
                sps = {}

                def issue(kb_):
                    kw_ = 128 if kb_ < nkb else TS
                    sp_ = self.ps()
                    self.mm(sp_[0:kw_, 0:TS], KAs[:, kb_ * 128:kb_ * 128 + kw_], QAs[:, :], True, True, [KAs, QAs], [sp_])
                    sps[kb_] = sp_
                for kb_ in range(min(2, nkb + 1)):
                    issue(kb_)
                for kb in range(nkb + 1):
                    kw = 128 if kb < nkb else TS
                    if kb + 2 < nkb + 1:
                        issue(kb + 2)
                    sp = sps.pop(kb)
                    pt = sc.get("pt")
                    self.act(pt[0:kw, :], sp[0:kw, 0:TS], AF.Exp, [sp, self.NCKs], [pt], bias=self.NCKs[0:kw, s, kb, h:h + 1], scale=1.0)
                    if kb == nkb:
                        self.tt("pool", pt[0:TS, :], pt[0:TS, :], self.CB[0:TS, 3, 0:TS], ALU.mult, [pt, self.CB], [pt])
                    self.mm(acc[:, 0:TS], VAs[0:kw, kb, :], pt[0:kw, :], kb == 0, kb == nkb, [VAs, pt], [acc])
                RC = sc.get("t")
                self.recip(RC[:, :], acc[64:128, 0:TS], [acc], [RC])
                ON = sc.get("t")
                self.tt("dve", ON[:, :], acc[0:64, 0:TS], RC[:, :], ALU.mult, [acc, RC], [ON])
                pz = self.ps()
                self.proj_fm(pz[0:64, 0:TS], 192, 64, self.XTs, s0, TS, pz)
                E = sc.get("t")
                self.act(E[:, :], pz[0:64, 0:TS], AF.Exp, [pz], [E], scale=-1.0)
                self.act(E[:, :], E[:, :], AF.Ln, [E], [E], bias=1.0)
                R_ = sc.get("t")
                self.act(R_[:, :], E[:, :], AF.Exp, [E], [R_], scale=-1.0)
                self.tt("dve", R_[:, :], pz[0:64, 0:TS], R_[:, :], ALU.mult, [pz, R_], [R_])
                self.tt("dve", self.OTs[hh * 64:(hh + 1) * 64, 2 + hp, s0:s0 + TS], ON[:, :], R_[:, :], ALU.mult,
                        [ON, R_], [self.OTs])

    def delta_alloc(self, sc, n, rw):
        d = {}
        d["SC"] = [[sc.sb("SC%d_%d" % (i, hh), [128, 4 if rw else 2, 128], BF) for hh in range(2)] for i in range(2)]
        d["A"] = [sc.sb("DA%d" % i, [128, 2, 128], BF) for i in range(3)]
        d["X"] = [sc.sb("DX%d" % i, [128, 2, 128], BF) for i in range(3)]
        d["P"] = [sc.sb("DP%d" % i, [128, 2, 128], BF) for i in range(3)]
        d["TM"] = [sc.sb("TM%d" % i, [128, 4, 128], BF) for i in range(2)]
        d["VZ"] = [sc.sb("VZ%d" % i, [128, 2, 128], BF) for i in range(2)]
        d["WT"] = [sc.sb("WT%d" % i, [128, 128], BF) for i in range(2)]
        d["YB"] = [sc.sb("YB%d" % i, [128, 128], BF) for i in range(2)]
        d["UT"] = [sc.sb("UT%d" % i, [128, 128]) for i in range(2)]
        d["UP"] = sc.sb("UP", [128, 128], BF)
        d["UZ"] = sc.sb("UZ", [128, 2, 128], BF)
        d["H"] = sc.sb("H", [128, 128])
        d["HB"] = sc.sb("HB", [128, 128], BF)
        d["HB2"] = sc.sb("HB2", [128, 128], BF)
        d["hbi"] = 0
        d["i"] = 0
        for t in d["VZ"] + [d["UZ"], d["UP"], d["HB2"]]:
            self.memset("pool", t[:, :] if len(t.t.shape) == 2 else t[:, :, :], 0.0, [t])
        return d

    def delta_group(self, d, G, C, g0, masks, score_mms, tm_srcs, rw, Rf, OA, dec_ap, dec_res):
        if os.environ.get("SKIPDELTA"):
            return
        gi = d["i"]
        d["i"] += 1
        nm = 4 if rw else 2
        SC = d["SC"][gi % 2]
        for hh in range(2):
            pp = self.ps()
            rows = slice(hh * 64, hh * 64 + 64)
            for (Lf, R2, ncol, col0) in score_mms:
                if G == 128:
                    self.mm(pp[0:G, col0 * 128:(col0 + ncol) * 128].rearrange("p (m c) -> p m c", c=128)[:, :, 0:G],
                            Lf[rows, g0:g0 + G], R2[rows, 0:ncol, g0:g0 + G], True, True, [Lf, R2], [pp])
                else:
                    for m_ in range(ncol):
                        self.mm(pp[0:G, (col0 + m_) * 128:(col0 + m_) * 128 + G],
                                Lf[rows, g0:g0 + G], R2[rows, m_, g0:g0 + G], True, True, [Lf, R2], [pp])
            map_, mres = masks(hh)
            self.tt("dve", SC[hh][0:G, 0:nm, 0:G], pp[0:G, 0:nm * 128].rearrange("p (m c) -> p m c", c=128)[:, :, 0:G],
                    map_, ALU.mult, [pp] + mres, [SC[hh]])
        TM = d["TM"][gi % 2]
        VZ = d["VZ"][gi % 2]
        pt = self.ps("tb")
        for q, Ft in enumerate(tm_srcs):
            self.tr(pt[0:G, q * 128:(q + 1) * 128], Ft[:, g0:g0 + G], self.IDB, [Ft, self.CB], [pt])
        nq = len(tm_srcs)
        self.cp("act", TM[0:G, 0:nq, :], pt[0:G, 0:nq * 128].rearrange("p (q c) -> p q c", c=128), [pt], [TM])
        if rw:
            for hh in range(2):
                self.cp("pool", VZ[0:G, hh, hh * 64:hh * 64 + 64], TM[0:G, 1, hh * 64:hh * 64 + 64], [TM], [VZ])
        if rw:
            YB = d["YB"][gi % 2]
            py = self.ps()
            for hh in range(2):
                self.mm(py[0:G, hh * 64:hh * 64 + 64], SC[hh][0:G, 2, 0:G], TM[0:G, 1, hh * 64:hh * 64 + 64], True, True,
                        [SC[hh], TM], [py])
            self.cp("dve", YB[0:G, :], py[0:G, 0:128], [py], [YB])
            usrc, ures = YB, YB
        pt2 = self.ps("tb")
        for hh in range(2):
            self.tr(pt2[0:G, hh * 128:hh * 128 + G], SC[hh][0:G, 0, 0:G], self.IDB[0:G, 0:G], [SC[hh], self.CB], [pt2])
        A = d["A"][0]
        self.cp("dve", A[0:G, :, 0:G], pt2[0:G, 0:256].rearrange("p (h c) -> p h c", c=128)[:, :, 0:G], [pt2], [A])
        P = d["P"][0]
        for hh in range(2):
            self.tt("pool", P[0:G, hh, 0:G], SC[hh][0:G, 0, 0:G], self.IDB[0:G, 0:G], ALU.add, [SC[hh], self.CB], [P])
        nsteps = int(round(math.log2(C))) - 1
        Xc = None
        ai, xi, pi = 0, 0, 0
        for i in range(nsteps):
            last = (i == nsteps - 1)
            pa = self.ps()
            for hh in range(2):
                xl = SC[hh][0:G, 0, 0:G] if Xc is None else Xc[0:G, hh, 0:G]
                xr = SC[hh] if Xc is None else Xc
                self.mm(pa[0:G, hh * 128:hh * 128 + G], xl, A[0:G, hh, 0:G], True, True, [xr, A], [pa])
            if not last:
                px = self.ps()
                for hh in range(2):
                    xl = SC[hh][0:G, 0, 0:G] if Xc is None else Xc[0:G, hh, 0:G]
                    xr = SC[hh] if Xc is None else Xc
                    self.mm(px[0:G, hh * 128:hh * 128 + G], A[0:G, hh, 0:G], xl, True, True, [xr, A], [px])
            ai += 1
            An = d["A"][ai % 3]
            self.cp("act", An[0:G, :, 0:G], pa[0:G, 0:256].rearrange("p (h c) -> p h c", c=128)[:, :, 0:G], [pa], [An])
            if not last:
                xi += 1
                Xn = d["X"][xi % 3]
                self.cp("dve", Xn[0:G, :, 0:G], px[0:G, 0:256].rearrange("p (h c) -> p h c", c=128)[:, :, 0:G], [px], [Xn])
                Xc = Xn
            A = An
            pq = self.ps()
            for hh in range(2):
                self.mm(pq[0:G, hh * 128:hh * 128 + G], A[0:G, hh, 0:G], P[0:G, hh, 0:G], True, True, [A, P], [pq])
            pi += 1
            Pn = d["P"][pi % 3]
            self.tt("dve", Pn[0:G, :, 0:G], pq[0:G, 0:256].rearrange("p (h c) -> p h c", c=128)[:, :, 0:G], P[0:G, :, 0:G],
                    ALU.add, [pq, P], [Pn])
            P = Pn
        WT = d["WT"][gi % 2]
        pw = self.ps()
        for hh in range(2):
            self.mm(pw[:, hh * 128:hh * 128 + G], TM[0:G, 0, :], P[0:G, hh, 0:G], True, True, [TM, P], [pw])
        self.cp("act", WT[0:64, 0:G], pw[0:64, 0:G], [pw], [WT])
        self.cp("act", WT[64:128, 0:G], pw[64:128, 128:128 + G], [pw], [WT])
        UT = d["UT"][gi % 2]
        pu = self.ps()
        for hh in range(2):
            if rw:
                rhs = YB[0:G, hh * 64:hh * 64 + 64]
                rr = YB
            else:
                rhs = TM[0:G, 1, hh * 64:hh * 64 + 64]
                rr = TM
            self.mm(pu[0:G, hh * 64:hh * 64 + 64], P[0:G, hh, 0:G], rhs, True, True, [P, rr], [pu])
        self.cp("dve", UT[0:G, :], pu[0:G, 0:128], [pu], [UT])
        H, UP, UZ = d["H"], d["UP"], d["UZ"]
        for ci in range(G // C):
            HB = d["HB"] if d["hbi"] % 2 == 0 else d["HB2"]
            HBn = d["HB2"] if d["hbi"] % 2 == 0 else d["HB"]
            d["hbi"] += 1
            cs = slice(ci * C, ci * C + C)
            tc0 = g0 + ci * C
            pU = self.ps()
            self.mm(pU[0:G, 0:128], WT[:, 0:G], HB[:, :], True, True, [WT, HB], [pU])
            if rw:
                self.tt("dve", UP[cs, :], pU[cs, 0:128], UT[cs, :], ALU.add, [pU, UT], [UP])
            else:
                self.tt("dve", UP[cs, :], UT[cs, :], pU[cs, 0:128], ALU.subtract, [pU, UT], [UP])
            pS = self.ps()
            self.mm(pS[:, 0:128], TM[cs, 2, :], UP[cs, :], True, not rw, [TM, UP], [pS])
            if rw:
                self.mm(pS[:, 0:128], TM[cs, 3, :], TM[cs, 1, :], False, True, [TM], [pS])
            dcol = dec_ap(tc0 + C - 1)
            for hh in range(2):
                rows = slice(hh * 64, hh * 64 + 64)
                cols = slice(hh * 64, hh * 64 + 64)
                self.stt(HBn[rows, cols], H[rows, cols], dcol[rows, :], pS[rows, cols], ALU.mult, ALU.add,
                         [H, pS] + dec_res, [HBn])
            for hh in range(2):
                self.cp("pool", UZ[cs, hh, hh * 64:hh * 64 + 64], UP[cs, hh * 64:hh * 64 + 64], [UP], [UZ])
            po = self.ps("a")
            self.mm(po[:, 0:C], HB[:, :], Rf[:, tc0:tc0 + C], True, False, [HB, Rf], [po])
            for hh in range(2):
                lastmm = (not rw) and hh == 1
                self.mm(po[:, 0:C], UZ[0:G, hh, :], SC[hh][0:G, 1, ci * C:ci * C + C], False, lastmm, [UZ, SC[hh]], [po])
            if rw:
                for hh in range(2):
                    self.mm(po[:, 0:C], VZ[0:G, hh, :], SC[hh][0:G, 3, ci * C:ci * C + C], False, hh == 1, [VZ, SC[hh]], [po])
            self.cp("act", OA[:, tc0:tc0 + C], po[:, 0:C], [po], [OA])
            for hh in range(2):
                rows = slice(hh * 64, hh * 64 + 64)
                cols = slice(hh * 64, hh * 64 + 64)
                self.stt(H[rows, cols], H[rows, cols], dcol[rows, :], pS[rows, cols], ALU.mult, ALU.add,
                         [H, pS] + dec_res, [H])

    def silu_mul(self, sc, n, zap, zres, Dn, out_ap, out_res, extra_scale=None, fpool="f"):
        E = sc.get(fpool)
        self.act(E[:, 0:n], zap, AF.Exp, zres, [E], scale=-1.0)
        self.act(E[:, 0:n], E[:, 0:n], AF.Ln, [E], [E], bias=1.0)
        R_ = sc.get(fpool)
        self.act(R_[:, 0:n], E[:, 0:n], AF.Exp, [E], [R_], scale=-1.0)
        self.tt("pool", R_[:, 0:n], R_[:, 0:n], zap, ALU.mult, [R_] + zres, [R_])
        if extra_scale is not None:
            self.stt(out_ap, Dn[:, 0:n], extra_scale, R_[:, 0:n], ALU.mult, ALU.mult, [Dn, R_, self.PV], out_res)
        else:
            self.tt("dve", out_ap, Dn[:, 0:n], R_[:, 0:n], ALU.mult, [Dn, R_], out_res)

    def zip_run(self, fns):
        if len(fns) == 1 or os.environ.get("NOZIP"):
            for f_ in fns:
                f_()
            return
        import threading
        n = len(fns)
        alive = [True] * n
        turn = [0]
        cv = threading.Condition()
        errs = []
        mk = self.mk

        def nxt(i):
            j = (i + 1) % n
            for _ in range(n):
                if alive[j]:
                    return j
                j = (j + 1) % n
            return -1

        def yp(i):
            with cv:
                turn[0] = nxt(i)
                cv.notify_all()
                while turn[0] != i:
                    cv.wait()

        def worker(i):
            with cv:
                while turn[0] != i:
                    cv.wait()
            try:
                mk.tl.hook = lambda: yp(i)
                fns[i]()
            except BaseException as e:
                errs.append(e)
            finally:
                mk.tl.hook = None
                with cv:
                    alive[i] = False
                    turn[0] = nxt(i)
                    cv.notify_all()
        ths = [threading.Thread(target=worker, args=(i,)) for i in range(n)]
        for t in ths:
            t.start()
        for t in ths:
            t.join()
        if errs:
            raise errs[0]

    def phaseA(self, l, pr):
        w3 = self.d["w_in"][l].rearrange("(k p) n -> p k n", p=128)
        a = OFF["a_sh"]
        self.load_w(w3, [(a + pr * 128, 128, 0), (a + 256 + pr * 128, 128, 128), (a + 512 + pr * 128, 128, 256),
                         (a + 768, 128, 384), (OFF["a_z"] + pr * 128, 128, 512)])
        n, G, C = 128, 128, 64
        NL = 2
        with Scope(self) as sc:
            W2B = sc.sb("W2B", [128, 128], BF)
            stg = self.get_ws()
            self.dma(stg[0:64, 0:2, :], self.d["w2"][l][:, pr * 128:(pr + 1) * 128].rearrange("p (a c) -> p a c", c=64), [], [stg])
            self.dma(stg[64:128, 0:2, :], self.d["a2"][l][:, pr * 128:(pr + 1) * 128].rearrange("p (a c) -> p a c", c=64), [], [stg])
            self.cp("pool", W2B[:, :].rearrange("p (a c) -> p a c", c=64), stg[:, 0:2, :], [stg], [W2B])
            LAST = sc.sb("LAST", [128, 4])
            M4 = sc.sb("M4", [128, 4, 128], BF)
            lanes = []
            for i in range(NL):
                ln = dict(i=i, PB=sc.sb("PB%d" % i, [128, 5, n + 1]), BR=sc.sb("BR%d" % i, [128, 2, n], BF),
                          OA=sc.sb("OA%d" % i, [128, n]), BON=sc.sb("BON%d" % i, [128, n]), PCt=sc.sb("PCt%d" % i, [128, n]),
                          AT=sc.sb("AT%d" % i, [128, n], BF), KTt=sc.sb("KTt%d" % i, [128, n], BF),
                          ADF=sc.sb("ADF%d" % i, [128, n], BF), KDF=sc.sb("KDF%d" % i, [128, n], BF),
                          VB=sc.sb("VB%d" % i, [128, n], BF), f="f%d" % i, b="b%d" % i)
                sc.pool(ln["f"], 12, [128, n])
                sc.pool(ln["b"], 5, [128, n], BF)
                lanes.append(ln)
            for m_, nm_ in enumerate(("msu", "miu", "msu", "miu")):
                self.cp("pool", M4[:, m_, :], self.cst(nm_), [self.CST], [M4])
            dd = self.delta_alloc(sc, n, True)
            self.rwkv_seq(l, pr, sc, dd, dict(n=n, G=G, C=C, nt=self.T // n, XT=self.XT, OT=self.OT, x0=0,
                                               prompt=True, o_sh=self.o["sh_p"][l], o_rw=self.o["rw_p"][l]),
                          lanes, LAST, M4, W2B)
            for s in range(NS if self.with_sample else 0):
                self.rwkv_seq(l, pr, sc, dd, dict(n=TS, G=TS, C=TS, nt=1, XT=self.XTs, OT=self.OTs, x0=s * TS,
                                                   prompt=False, s=s, o_sh=self.o["sh_s"][l, s], o_rw=self.o["rw_s"][l, s]),
                              lanes[s % NL:s % NL + 1], LAST, M4, W2B)

    def rwkv_seq(self, l, pr, sc, dd, sq, lanes, LAST, M4, W2B):
        n = sq["n"]
        H, HB = dd["H"], dd["HB"]
        dd["hbi"] = 0
        self.memset("pool", H[:, :], 0.0, [H])
        if sq["prompt"]:
            self.memset("pool", HB[:, :], 0.0, [HB])
            self.memset("pool", LAST[:, :], 0.0, [LAST])
        else:
            s = sq["s"]
            cols_ = (pr * 128, 256 + pr * 128, 512 + pr * 128, 768)
            for j in range(4):
                self.dma(LAST[:, j:j + 1], self.d["st_sh"][l, s, cols_[j]:cols_[j] + 128].rearrange("(p o) -> p o", o=1), [], [LAST])
            for hh in range(2):
                self.dma(H[hh * 64:hh * 64 + 64, hh * 64:hh * 64 + 64], self.d["st_rw"][l, s, 2 * pr + hh], [], [H])
            self.cp("pool", HB[:, :], H[:, :], [H], [HB])
        NL = len(lanes)
        for tt0 in range(0, sq["nt"], NL):
            tts = list(range(tt0, min(sq["nt"], tt0 + NL)))
            for i, tt in enumerate(tts):
                self.rwkv_proj(sq, lanes[i], tt, LAST)
            self.zip_run([(lambda i=i, tt=tt: self.rwkv_pre(l, pr, sc, sq, lanes[i], W2B)) for i, tt in enumerate(tts)])
            for i, tt in enumerate(tts):
                ln = lanes[i]
                for g in range(n // sq["G"]):
                    self.delta_group(dd, sq["G"], sq["C"], g * sq["G"],
                                     masks=lambda hh, G=sq["G"]: (M4[0:G, :, 0:G], [M4]),
                                     score_mms=[(ln["AT"], ln["BR"], 2, 0), (ln["KTt"], ln["BR"], 2, 2)],
                                     tm_srcs=[_V(ln["BR"], 0), ln["VB"], ln["ADF"], ln["KDF"]],
                                     rw=True, Rf=_V(ln["BR"], 1), OA=ln["OA"],
                                     dec_ap=lambda col, ln=ln: ln["PCt"][:, col:col + 1], dec_res=[ln["PCt"]])
            self.zip_run([(lambda i=i, tt=tt: self.rwkv_post(l, pr, sc, sq, lanes[i], tt)) for i, tt in enumerate(tts)])
        o_sh, o_rw = sq["o_sh"], sq["o_rw"]
        cols = (pr * 128, 256 + pr * 128, 512 + pr * 128, 768)
        for j in range(4 if pr == 0 else 3):
            self.dma(o_sh[cols[j]:cols[j] + 128].rearrange("(p o) -> p o", o=1), LAST[:, j:j + 1], [LAST], [])
        HT = sc.get(lanes[0]["f"])
        for hh in range(2):
            pt = self.ps()
            rows = slice(hh * 64, hh * 64 + 64)
            self.tr(pt[0:64, 0:64], H[rows, hh * 64:hh * 64 + 64], self.cst("ident", 64, rows, hh * 64),
                    [H, self.CST], [pt])
            self.cp("dve", HT[0:64, hh * 64:hh * 64 + 64], pt[0:64, 0:64], [pt], [HT])
        for hh in range(2):
            self.dma(o_rw[2 * pr + hh], HT[0:64, hh * 64:hh * 64 + 64], [HT], [])

    def rwkv_proj(self, sq, ln, tt, LAST):
        n = sq["n"]
        t0 = sq["x0"] + tt * n
        PB = ln["PB"]
        self.cp("pool", PB[:, 0:4, 0], LAST[:, :], [LAST], [PB])
        for j in range(5):
            pp = self.ps()
            self.proj_fm(pp[:, 0:n], j * 128, 128, sq["XT"], t0, n, pp)
            self.cp("act" if j % 2 == 0 else "dve", PB[:, j, 1:n + 1], pp[:, 0:n], [pp], [PB])
        self.cp("pool", LAST[:, :], PB[:, 0:4, n], [PB], [LAST])

    def rwkv_pre(self, l, pr, sc, sq, ln, W2B):
        n, C = sq["n"], sq["C"]
        PB, BR, BON, PCt = ln["PB"], ln["BR"], ln["BON"], ln["PCt"]
        AT, KTt, ADF, KDF, VB = ln["AT"], ln["KTt"], ln["ADF"], ln["KDF"], ln["VB"]
        f = lambda: sc.get(ln["f"])
        b = lambda: sc.get(ln["b"])
        mu_idx = (pr, 2 + pr, 4 + pr, 6)
        for j in range(4):
            Dt = f()
            self.tt("pool", Dt[:, 0:n], PB[:, j, 0:n], PB[:, j, 1:n + 1], ALU.subtract, [PB], [Dt])
            self.stt(PB[:, j, 1:n + 1], Dt[:, 0:n], self.pvc("mu%d" % l, mu_idx[j]), PB[:, j, 1:n + 1], ALU.mult, ALU.add,
                     [Dt, PB, self.PV], [PB])
        r_, k_, v_ = PB[:, 0, 1:n + 1], PB[:, 1, 1:n + 1], PB[:, 2, 1:n + 1]
        E = f()
        self.act(E[0:64, 0:n], PB[0:64, 3, 1:n + 1], AF.Exp, [PB], [E], scale=-2.0)
        self.act(E[0:64, 0:n], E[0:64, 0:n], AF.Ln, [E], [E], bias=1.0)
        self.act(E[0:64, 0:n], E[0:64, 0:n], AF.Exp, [E], [E], scale=-1.0)
        TH = b()
        self.ts("dve", TH[0:64, 0:n], E[0:64, 0:n], 2.0, ALU.mult, [E], [TH], s2=-1.0, op1=ALU.add)
        self.cp("pool", TH[64:128, 0:n], PB[64:128, 3, 1:n + 1], [PB], [TH])
        pw = self.ps()
        self.mm(pw[:, 0:n], W2B[0:64, :], TH[0:64, 0:n], True, True, [W2B, TH], [pw])
        pa = self.ps()
        self.mm(pa[:, 0:n], W2B[64:128, :], TH[64:128, 0:n], True, True, [W2B, TH], [pa])
        SG = f()
        self.act(SG[:, 0:n], pw[:, 0:n], AF.Exp, [pw, self.PD], [SG], bias=self.PD[:, l, pr:pr + 1], scale=-1.0)
        self.act(SG[:, 0:n], SG[:, 0:n], AF.Ln, [SG], [SG], bias=1.0)
        self.act(SG[:, 0:n], SG[:, 0:n], AF.Exp, [SG], [SG], scale=-1.0)
        AA = f()
        self.act(AA[:, 0:n], pa[:, 0:n], AF.Exp, [pa, self.PD], [AA], bias=self.PD[:, l, 2 + pr:3 + pr], scale=-1.0)
        self.act(AA[:, 0:n], AA[:, 0:n], AF.Ln, [AA], [AA], bias=1.0)
        self.act(AA[:, 0:n], AA[:, 0:n], AF.Exp, [AA], [AA], scale=-1.0)
        KKu = f()
        self.ts("dve", KKu[:, 0:n], k_, self.pvc("kk%d" % l, pr), ALU.mult, [PB, self.PV], [KKu])
        KQ = b()
        self.act(KQ[:, 0:n], KKu[:, 0:n], AF.Square, [KKu], [KQ])
        pss = self.ps()
        self.mm(pss[:, 0:n], self.ONESBD, KQ[:, 0:n], True, True, [self.CB, KQ], [pss])
        RN = f()
        self.act(RN[:, 0:n], pss[:, 0:n], AF.Ln, [pss], [RN])
        self.act(RN[:, 0:n], RN[:, 0:n], AF.Exp, [RN], [RN], scale=-0.5)
        KKn = f()
        self.stt(KKn[:, 0:n], RN[:, 0:n], 1e12, KKu[:, 0:n], ALU.min, ALU.mult, [RN, KKu], [KKn])
        KM = f()
        self.ts("dve", KM[:, 0:n], AA[:, 0:n], self.pvc("ka%d" % l, pr), ALU.mult, [AA, self.PV, self.PD], [KM],
                s2=self.PD[:, l, 4 + pr:5 + pr], op1=ALU.add)
        self.tt("pool", KM[:, 0:n], KM[:, 0:n], k_, ALU.mult, [KM, PB], [KM])
        RK = f()
        self.tt("pool", RK[:, 0:n], r_, KM[:, 0:n], ALU.mult, [PB, KM], [RK])
        RKB = b()
        self.ts("dve", RKB[:, 0:n], RK[:, 0:n], self.pvc("rk%d" % l, pr), ALU.mult, [RK, self.PV], [RKB])
        pbs = self.ps()
        self.mm(pbs[:, 0:n], self.ONESBD, RKB[:, 0:n], True, True, [self.CB, RKB], [pbs])
        self.tt("dve", BON[:, 0:n], pbs[:, 0:n], v_, ALU.mult, [pbs, PB], [BON])
        LW = f()
        self.ts("dve", LW[:, 0:n], SG[:, 0:n], -math.exp(-0.5), ALU.mult, [SG], [LW])
        LC = f()
        self.scan(LC[:, 0:n], self.cst("reset", n), LW[:, 0:n], [self.CST, LW], [LC])
        LCX = f()
        self.tt("pool", LCX[:, 0:n], LC[:, 0:n], LW[:, 0:n], ALU.subtract, [LC, LW], [LCX])
        Pm = f()
        self.act(Pm[:, 0:n], LC[:, 0:n], AF.Exp, [LC], [Pm])
        self.act(LCX[:, 0:n], LCX[:, 0:n], AF.Exp, [LCX], [LCX])
        PINV = f()
        self.act(PINV[:, 0:n], LC[:, 0:n], AF.Exp, [LC], [PINV], scale=-1.0)
        KA_ = f()
        self.tt("pool", KA_[:, 0:n], KKn[:, 0:n], AA[:, 0:n], ALU.mult, [KKn, AA], [KA_])
        self.tt("dve", BR[:, 0, 0:n], KKn[:, 0:n], LCX[:, 0:n], ALU.mult, [KKn, LCX], [BR])
        self.tt("dve", BR[:, 1, 0:n], r_, Pm[:, 0:n], ALU.mult, [PB, Pm], [BR])
        self.stt(AT[:, 0:n], KA_[:, 0:n], -1.0, PINV[:, 0:n], ALU.mult, ALU.mult, [KA_, PINV], [AT])
        self.tt("pool", KTt[:, 0:n], KM[:, 0:n], PINV[:, 0:n], ALU.mult, [KM, PINV], [KTt])
        LCD = f()
        nch = n // C
        lc3 = LC[:, 0:n].rearrange("p (c t) -> p c t", t=C)
        self.tt("pool", LCD[:, 0:n].rearrange("p (c t) -> p c t", t=C), lc3[:, :, C - 1:C].to_broadcast([128, nch, C]), lc3,
                ALU.subtract, [LC], [LCD])
        self.act(LCD[:, 0:n], LCD[:, 0:n], AF.Exp, [LCD], [LCD])
        self.stt(ADF[:, 0:n], KA_[:, 0:n], -1.0, LCD[:, 0:n], ALU.mult, ALU.mult, [KA_, LCD], [ADF])
        self.tt("pool", KDF[:, 0:n], KM[:, 0:n], LCD[:, 0:n], ALU.mult, [KM, LCD], [KDF])
        self.cp("pool", VB[:, 0:n], v_, [PB], [VB])
        self.cp("pool", PCt[:, 0:n], Pm[:, 0:n], [Pm], [PCt])

    def rwkv_post(self, l, pr, sc, sq, ln, tt):
        n = sq["n"]
        t0 = sq["x0"] + tt * n
        PB, OA, BON = ln["PB"], ln["OA"], ln["BON"]
        f = lambda: sc.get(ln["f"])
        b = lambda: sc.get(ln["b"])
        z_ = PB[:, 4, 1:n + 1]
        OB = b()
        self.cp("act", OB[:, 0:n], OA[:, 0:n], [OA], [OB])
        OQ = b()
        self.act(OQ[:, 0:n], OA[:, 0:n], AF.Square, [OA], [OQ])
        p1 = self.ps()
        self.mm(p1[:, 0:n], self.ONESBD, OB[:, 0:n], True, True, [self.CB, OB], [p1])
        p2 = self.ps()
        self.mm(p2[:, 0:n], self.ONESBD, OQ[:, 0:n], True, True, [self.CB, OQ], [p2])
        ME = f()
        self.ts("dve", ME[:, 0:n], p1[:, 0:n], 1.0 / 64, ALU.mult, [p1], [ME])
        MS = f()
        self.tt("pool", MS[:, 0:n], ME[:, 0:n], ME[:, 0:n], ALU.mult, [ME], [MS])
        VA_ = f()
        self.stt(VA_[:, 0:n], p2[:, 0:n], 1.0 / 64, MS[:, 0:n], ALU.mult, ALU.subtract, [p2, MS], [VA_])
        self.act(VA_[:, 0:n], VA_[:, 0:n], AF.Ln, [VA_, self.EPS], [VA_], bias=self.EPS[:, 1:2])
        self.act(VA_[:, 0:n], VA_[:, 0:n], AF.Exp, [VA_], [VA_], scale=-0.5)
        Dn = f()
        self.tt("pool", Dn[:, 0:n], OA[:, 0:n], ME[:, 0:n], ALU.subtract, [OA, ME], [Dn])
        self.tt("dve", Dn[:, 0:n], Dn[:, 0:n], VA_[:, 0:n], ALU.mult, [Dn, VA_], [Dn])
        self.act(Dn[:, 0:n], Dn[:, 0:n], AF.Identity, [Dn, self.PV], [Dn], bias=self.pvc("gnb%d" % l, pr),
                 scale=self.pvc("gng%d" % l, pr))
        self.tt("pool", Dn[:, 0:n], Dn[:, 0:n], BON[:, 0:n], ALU.add, [Dn, BON], [Dn])
        self.silu_mul(sc, n, z_, [PB], Dn, sq["OT"][:, pr, t0:t0 + n], [sq["OT"]], fpool=ln["f"])

    def phaseC(self, l, pr):
        w3 = self.d["w_in"][l].rearrange("(k p) n -> p k n", p=128)
        c = OFF["c_qkv"]
        self.memset("pool", self.WB[:, :, 512:576], 0.0, [self.WB])
        self.load_w(w3, [(c + pr * 128, 128, 0), (c + 256 + pr * 128, 128, 128), (c + 512 + pr * 128, 128, 256),
                         (OFF["c_z"] + pr * 128, 128, 384), (OFF["c_b"], 4, 512), (OFF["c_a"], 4, 544)])
        n, G, C = 256, 128, 64
        with Scope(self) as sc:
            PC = sc.sb("PC", [128, 3, n + 3])
            KQ = sc.sb("KQ", [128, 2, n], BF)
            OA = sc.sb("OA", [128, n])
            EGCb = sc.sb("EGCb", [128, n])
            GC = sc.sb("GC", [128, n])
            GCT = sc.sb("GCT", [128, 64])
            DMM = [[sc.sb("DMM%d_%d" % (i, hh), [128, 2, 128]) for hh in range(2)] for i in range(2)]
            NMSU = sc.sb("NMSU", [128, 128])
            self.gd_long = [sc.sb("GL%d" % i, [128, n]) for i in range(4)]
            sc.pool("f", 9, [128, n])
            sc.pool("b", 8, [128, n], BF)
            sc.pool("df", 2, [128, 2, 128])
            self.ts("pool", NMSU[:, :], self.cst("msu"), -1.0, ALU.mult, [self.CST], [NMSU])
            dd = self.delta_alloc(sc, n, False)
            self.gdn_seq(l, pr, sc, dd, dict(n=n, G=G, C=C, nt=self.T // n, XT=self.XT, OT=self.OT, x0=0, prompt=True,
                                              o_cv=self.o["cv_p"][l], o_gd=self.o["gd_p"][l]),
                         PC, KQ, OA, EGCb, GC, GCT, DMM, NMSU)
            for s in range(NS if self.with_sample else 0):
                self.gdn_seq(l, pr, sc, dd, dict(n=TS, G=TS, C=TS, nt=1, XT=self.XTs, OT=self.OTs, x0=s * TS, prompt=False,
                                                  s=s, o_cv=self.o["cv_s"][l, s], o_gd=self.o["gd_s"][l, s]),
                             PC, KQ, OA, EGCb, GC, GCT, DMM, NMSU)

    def gdn_seq(self, l, pr, sc, dd, sq, PC, KQ, OA, EGCb, GC, GCT, DMM, NMSU):
        n, G, C = sq["n"], sq["G"], sq["C"]
        XT, OT = sq["XT"], sq["OT"]
        H, HB = dd["H"], dd["HB"]
        dd["hbi"] = 0
        self.memset("pool", H[:, :], 0.0, [H])
        self.memset("pool", PC[:, :, :], 0.0, [PC])
        if sq["prompt"]:
            self.memset("pool", HB[:, :], 0.0, [HB])
        else:
            s = sq["s"]
            for j in range(3):
                jj = 2 * j + pr
                self.dma(PC[:, j, 0:3], self.d["st_cv"][l, s][:, jj * 128:(jj + 1) * 128].rearrange("i p -> p i"), [], [PC],
                         allow_slow_non_contiguous=True)
            for hh in range(2):
                self.dma(H[hh * 64:hh * 64 + 64, hh * 64:hh * 64 + 64], self.d["st_gd"][l, s, 2 * pr + hh], [], [H])
            self.cp("pool", HB[:, :], H[:, :], [H], [HB])
        for tt in range(sq["nt"]):
            t0 = sq["x0"] + tt * n
            f = lambda: sc.get("f")
            b = lambda: sc.get("b")
            if tt > 0:
                CR = f()
                self.cp("pool", CR[:, 0:9].rearrange("p (j i) -> p j i", i=3), PC[:, :, n:n + 3], [PC], [CR])
                self.cp("pool", PC[:, :, 0:3], CR[:, 0:9].rearrange("p (j i) -> p j i", i=3), [CR], [PC])
            for j in range(3):
                pp = self.ps()
                self.proj_fm(pp[:, 0:n], j * 128, 128, XT, t0, n, pp)
                self.cp("act" if j % 2 == 0 else "dve", PC[:, j, 3:n + 3], pp[:, 0:n], [pp], [PC])
            PZ = self.gd_long[3]
            pp = self.ps()
            self.proj_fm(pp[:, 0:n], 384, 128, XT, t0, n, pp)
            self.cp("act", PZ[:, 0:n], pp[:, 0:n], [pp], [PZ])
            BA = f()
            pp = self.ps()
            self.proj_fm(pp[0:64, 0:n], 512, 64, XT, t0, n, pp)
            self.cp("dve", BA[0:64, 0:n], pp[0:64, 0:n], [pp], [BA])
            Y = []
            for j in range(3):
                jj = 2 * j + pr
                Yj = self.gd_long[j]
                self.ts("dve", Yj[:, 0:n], PC[:, j, 3:n + 3], self.pvc("cw%d" % l, 3 * 6 + jj), ALU.mult, [PC, self.PV], [Yj])
                for i in (2, 1, 0):
                    self.stt(Yj[:, 0:n], PC[:, j, i:i + n], self.pvc("cw%d" % l, i * 6 + jj), Yj[:, 0:n], ALU.mult, ALU.add,
                             [PC, self.PV, Yj], [Yj])
                E = f()
                self.act(E[:, 0:n], Yj[:, 0:n], AF.Exp, [Yj], [E], scale=-1.0)
                self.act(E[:, 0:n], E[:, 0:n], AF.Ln, [E], [E], bias=1.0)
                self.act(E[:, 0:n], E[:, 0:n], AF.Exp, [E], [E], scale=-1.0)
                self.tt("pool", Yj[:, 0:n], Yj[:, 0:n], E[:, 0:n], ALU.mult, [Yj, E], [Yj])
                Y.append(Yj)
            q_, k_, v_ = Y
            for (src_, scl) in ((q_, 0.125), (k_, 1.0)):
                Q2 = b()
                self.act(Q2[:, 0:n], src_[:, 0:n], AF.Square, [src_], [Q2])
                pss = self.ps()
                self.mm(pss[:, 0:n], self.ONESBD, Q2[:, 0:n], True, True, [self.CB, Q2], [pss])
                RN = f()
                self.act(RN[:, 0:n], pss[:, 0:n], AF.Ln, [pss, self.EPS], [RN], bias=self.EPS[:, 3:4])
                self.act(RN[:, 0:n], RN[:, 0:n], AF.Exp, [RN], [RN], scale=-0.5)
                self.stt(src_[:, 0:n], src_[:, 0:n], scl, RN[:, 0:n], ALU.mult, ALU.mult, [src_, RN], [src_])
            BG = f()
            self.act(BG[0:64, 0:n], BA[0:64, 0:n], AF.Exp, [BA], [BG], scale=-1.0)
            self.act(BG[0:64, 0:n], BG[0:64, 0:n], AF.Ln, [BG], [BG], bias=1.0)
            self.act(BG[0:64, 0:n], BG[0:64, 0:n], AF.Exp, [BG], [BG], scale=-1.0)
            SP = f()
            self.act(SP[0:64, 0:n], BA[0:64, 0:n], AF.Exp, [BA, self.PV], [SP], bias=self.pvc("dtb%d" % l, 0, slice(0, 64)))
            self.act(SP[0:64, 0:n], SP[0:64, 0:n], AF.Ln, [SP], [SP], bias=1.0)
            self.ts("dve", SP[0:64, 0:n], SP[0:64, 0:n], self.PD[0:64, l, 7:8], ALU.mult, [SP, self.PD], [SP])
            self.scan(GC[0:64, 0:n], self.cst("reset", n, slice(0, 64)), SP[0:64, 0:n], [self.CST, SP], [GC])
            EGC = f()
            self.act(EGC[0:64, 0:n], GC[0:64, 0:n], AF.Exp, [GC], [EGC])
            EGD = f()
            nch = n // C
            gc3 = GC[0:64, 0:n].rearrange("p (c t) -> p c t", t=C)
            self.tt("pool", EGD[0:64, 0:n].rearrange("p (c t) -> p c t", t=C), gc3[:, :, C - 1:C].to_broadcast([64, nch, C]), gc3,
                    ALU.subtract, [GC], [EGD])
            self.act(EGD[0:64, 0:n], EGD[0:64, 0:n], AF.Exp, [EGD], [EGD])
            pb_ = self.ps()
            self.mm(pb_[:, 0:n], self.cst("selb", 128, slice(0, 64), pr * 128), BG[0:64, 0:n], True, True, [self.CST, BG], [pb_])
            BETb = f()
            self.cp("act", BETb[:, 0:n], pb_[:, 0:n], [pb_], [BETb])
            pg_ = self.ps()
            self.mm(pg_[:, 0:n], self.cst("selg", 128, slice(0, 64), pr * 128), EGC[0:64, 0:n], True, True, [self.CST, EGC], [pg_])
            self.cp("act", EGCb[:, 0:n], pg_[:, 0:n], [pg_], [EGCb])
            pd_ = self.ps()
            self.mm(pd_[:, 0:n], self.cst("selg", 128, slice(0, 64), pr * 128), EGD[0:64, 0:n], True, True, [self.CST, EGD], [pd_])
            KD = b()
            self.tt("dve", KD[:, 0:n], k_[:, 0:n], pd_[:, 0:n], ALU.mult, [k_, pd_], [KD])
            KBf = f()
            self.tt("pool", KBf[:, 0:n], k_[:, 0:n], BETb[:, 0:n], ALU.mult, [k_, BETb], [KBf])
            self.cp("pool", KQ[:, 0, 0:n], KBf[:, 0:n], [KBf], [KQ])
            self.cp("pool", KQ[:, 1, 0:n], q_[:, 0:n], [q_], [KQ])
            QG = b()
            self.tt("dve", QG[:, 0:n], q_[:, 0:n], EGCb[:, 0:n], ALU.mult, [q_, EGCb], [QG])
            KBG = b()
            self.tt("dve", KBG[:, 0:n], KBf[:, 0:n], EGCb[:, 0:n], ALU.mult, [KBf, EGCb], [KBG])
            VBt = b()
            self.tt("pool", VBt[:, 0:n], v_[:, 0:n], BETb[:, 0:n], ALU.mult, [v_, BETb], [VBt])
            Kf = b()
            self.cp("pool", Kf[:, 0:n], k_[:, 0:n], [k_], [Kf])
            for g in range(n // G):
                g0 = g * G
                gi = dd["i"]
                DMg = DMM[gi % 2]
                pT = self.ps()
                self.tr(pT[0:G, 0:64], GC[0:64, g0:g0 + G], self.cst("ident", 64, slice(0, 64)), [GC, self.CST], [pT])
                self.cp("dve", GCT[0:G, :], pT[0:G, 0:64], [pT], [GCT])
                pR = self.ps()
                for hh in range(2):
                    h = 2 * pr + hh
                    self.mm(pR[0:G, hh * 128:hh * 128 + G], self.cst("selh", G, slice(0, 64), h * 128), GC[0:64, g0:g0 + G],
                            True, True, [self.CST, GC], [pR])
                DF = sc.get("df")
                for hh in range(2):
                    h = 2 * pr + hh
                    self.stt(DF[0:G, hh, 0:G], pR[0:G, hh * 128:hh * 128 + G], GCT[0:G, 32 + h:33 + h], self.cst("miu", G, slice(0, G)),
                             ALU.subtract, ALU.mult, [pR, GCT, self.CST], [DF])
                self.act(DF[0:G, :, 0:G], DF[0:G, :, 0:G], AF.Exp, [DF], [DF])
                for hh in range(2):
                    self.tt("pool", DMg[hh][0:G, 0, 0:G], DF[0:G, hh, 0:G], NMSU[0:G, 0:G], ALU.mult, [DF, NMSU], [DMg[hh]])
                    self.tt("pool", DMg[hh][0:G, 1, 0:G], DF[0:G, hh, 0:G], self.cst("miu", G, slice(0, G)), ALU.mult,
                            [DF, self.CST], [DMg[hh]])
                self.delta_group(dd, G, C, g0,
                                 masks=lambda hh, DMg=DMg: (DMg[hh][0:G, :, 0:G], [DMg[hh]]),
                                 score_mms=[(Kf, KQ, 2, 0)],
                                 tm_srcs=[KBG, VBt, KD],
                                 rw=False, Rf=QG, OA=OA,
                                 dec_ap=lambda col: EGCb[:, col:col + 1], dec_res=[EGCb])
            OQ = b()
            self.act(OQ[:, 0:n], OA[:, 0:n], AF.Square, [OA], [OQ])
            p2 = self.ps()
            self.mm(p2[:, 0:n], self.ONESBD, OQ[:, 0:n], True, True, [self.CB, OQ], [p2])
            RS = f()
            self.act(RS[:, 0:n], p2[:, 0:n], AF.Ln, [p2, self.EPS], [RS], bias=self.EPS[:, 2:3], scale=1.0 / 64)
            self.act(RS[:, 0:n], RS[:, 0:n], AF.Exp, [RS], [RS], scale=-0.5)
            Dn = f()
            self.tt("dve", Dn[:, 0:n], OA[:, 0:n], RS[:, 0:n], ALU.mult, [OA, RS], [Dn])
            self.silu_mul(sc, n, PZ[:, 0:n], [PZ], Dn, OT[:, 6 + pr, t0:t0 + n], [OT], extra_scale=self.pvc("gnorm%d" % l))
        o_cv, o_gd = sq["o_cv"], sq["o_gd"]
        for j in range(3):
            jj = 2 * j + pr
            self.dma(o_cv[:, jj * 128:(jj + 1) * 128].rearrange("i p -> p i"), PC[:, j, n:n + 3], [PC], [],
                     allow_slow_non_contiguous=True)
        for hh in range(2):
            self.dma(o_gd[2 * pr + hh], H[hh * 64:hh * 64 + 64, hh * 64:hh * 64 + 64], [H], [])

    def layer(self, l):
        if not getattr(self, "ot_zeroed", False):
            self.memset("pool", self.OT[:, :, :], 0.0, [self.OT])
            self.memset("pool", self.OTs[:, :, :], 0.0, [self.OTs])
            self.ot_zeroed = True
        if "A" in self.phases:
            for pr in range(2):
                self.phaseA(l, pr)
        if "C" in self.phases:
            for pr in range(2):
                self.phaseC(l, pr)
        if "B" in self.phases:
            self.phaseB(l)
        self.phaseD(l)


_CACHE = {}


def get_nc(T, PAST, **kw):
    key = (T, PAST, tuple(sorted(kw.items())))
    if key not in _CACHE:
        kb = KB(T, PAST, **kw)
        _CACHE[key] = (kb.build(), kb)
    return _CACHE[key]


def kernel(**inp):
    inp = {k: np.asarray(v) for k, v in inp.items()}
    B, T, _ = inp["x_prompt"].shape
    PAST = inp["cache_fox_k"].shape[2]
    nc, kb = get_nc(T, PAST, with_sample=not os.environ.get("NOSAMPLE"))
    pv = make_pv(inp)
    cst = make_cst()
    in_maps = []
    for c in range(NCORE):
        m = {"xp": np.ascontiguousarray(inp["x_prompt"][c]), "w_in": inp["w_in"], "w_out": inp["w_out"],
             "rwkv_w2": inp["rwkv_w2"], "rwkv_a2": inp["rwkv_a2"], "pv": pv, "cst": cst}
        if kb.with_sample:
            ss = slice(c * NS, (c + 1) * NS)
            m["xs"] = np.ascontiguousarray(inp["x_sample"][ss]).reshape(NS * TS, D_MODEL)
            m["ckT"] = np.ascontiguousarray(inp["cache_fox_k"][:, ss].transpose(0, 1, 3, 4, 2))
            m["cv"] = np.ascontiguousarray(inp["cache_fox_v"][:, ss]).reshape(L, NS, PAST, 512)
            m["clf"] = np.ascontiguousarray(inp["cache_fox_logf"][:, ss].transpose(0, 1, 3, 2))
            m["st_sh"] = np.ascontiguousarray(inp["state_rwkv_shift"][:, ss])
            m["st_rw"] = np.ascontiguousarray(inp["state_rwkv_wkv"][:, ss].transpose(0, 1, 2, 4, 3))
            m["st_cv"] = np.ascontiguousarray(inp["state_gdn_conv"][:, ss])
            m["st_gd"] = np.ascontiguousarray(inp["state_gdn_wkv"][:, ss])
        in_maps.append(m)
    if os.environ.get("KTRACE"):
        res = run_bass_kernel_spmd(nc, in_maps, core_ids=list(range(NCORE)), trace=True)
        print("EXEC_TIME_NS", res.exec_time_ns)
    else:
        res = run_bass_kernel_spmd(nc, in_maps, core_ids=list(range(NCORE)))
    R = res.results
    SB = NCORE * NS

    def gp(name, shape_tail, axis_layer=True):
        if name not in R[0]:
            return None
        return np.stack([np.asarray(R[c][name]) for c in range(NCORE)], axis=1)

    y_p = np.stack([R[c]["y_p"] for c in range(NCORE)])
    fk_p = gp("fk_p", None).reshape(L, NCORE, T, 8, 64)
    fv_p = gp("fv_p", None).reshape(L, NCORE, T, 8, 64)
    fl_p = gp("fl_p", None)
    sh_p = gp("sh_p", None)
    rw_p = gp("rw_p", None)
    cv_p = gp("cv_p", None)
    gd_p = gp("gd_p", None)

    def gs(name, tail):
        if name not in R[0]:
            return np.zeros((L, SB) + tail, np.float32)
        a = np.stack([np.asarray(R[c][name]) for c in range(NCORE)], axis=1)
        return a.reshape((L, SB) + tail)
    if "y_s" in R[0]:
        y_s = np.stack([R[c]["y_s"] for c in range(NCORE)]).reshape(SB, TS, D_MODEL)
    else:
        y_s = np.zeros((SB, TS, D_MODEL), np.float32)
    fk_s = gs("fk_s", (TS, 8, 64))
    fv_s = gs("fv_s", (TS, 8, 64))
    fl_s = gs("fl_s", (TS, 8))
    sh_s = gs("sh_s", (896,))
    rw_s = gs("rw_s", (4, 64, 64))
    cv_s = gs("cv_s", (3, 768))
    gd_s = gs("gd_s", (4, 64, 64))
    return (y_p, y_s, fk_p, fv_p, fl_p, sh_p, rw_p, cv_p, gd_p, fk_s, fv_s, fl_s, sh_s, rw_s, cv_s, gd_s)
```

```python
import math
from contextlib import ExitStack
import numpy as np
import concourse.bass as bass
import concourse.mybir as mybir
from concourse.bass_utils import run_bass_kernel_spmd

F32 = mybir.dt.float32
BF = mybir.dt.bfloat16
AF = mybir.ActivationFunctionType
ALU = mybir.AluOpType

D_MODEL = 1024
L = 4
N_IN = 4240
OFF = dict(a_sh=0, a_z=896, b_q=1152, b_k=1664, b_v=2176, b_f=2688, b_z=2696,
           c_qkv=3208, c_b=3976, c_a=3980, c_z=3984)
ALPHA = (2 * L) ** 0.25
LN_EPS = 1e-5
RWKV_GN_EPS = 64e-5
GDN_NORM_EPS = 1e-6
L2_EPS = 1e-6
import os
NCORE = int(os.environ.get("KCORES", "8"))
BSTAGE = int(os.environ.get("BSTAGE", "9"))
NS = 4
TS = 16


def pv_layout():
    off = {}
    n = 0

    def add(name, w):
        nonlocal n
        off[name] = n
        n += w
    add("ln_in_g", 8)
    add("ln_in_b", 8)
    for l in range(L):
        for nm, w in (("lng", 8), ("lnb", 8), ("mu", 7), ("w0", 2), ("a0", 2), ("kk", 2), ("ka", 2),
                      ("rk", 2), ("gng", 2), ("gnb", 2), ("cw", 24), ("gnorm", 1), ("bf", 1),
                      ("alog", 1), ("dtb", 1)):
            add("%s%d" % (nm, l), w)
    return off, n


PVO, NPV = pv_layout()


def cst_layout():
    off = {}
    n = 0
    for nm, w in (("ident", 128), ("onesbd", 128), ("ones", 128), ("tri", 128), ("msu", 128), ("miu", 128),
                  ("reset", 256), ("selb", 256), ("selg", 256), ("selh", 512)):
        off[nm] = n
        n += w
    return off, n


CSO, NCST = cst_layout()


def make_cst():
    c = np.zeros((128, NCST), np.float32)
    p = np.arange(128)[:, None]
    f = np.arange(128)[None, :]
    c[:, CSO["ident"]:CSO["ident"] + 128] = (p == f)
    c[:, CSO["onesbd"]:CSO["onesbd"] + 128] = (p // 64 == f // 64)
    c[:, CSO["ones"]:CSO["ones"] + 128] = 1.0
    c[:, CSO["tri"]:CSO["tri"] + 128] = (p <= f)
    c[:, CSO["msu"]:CSO["msu"] + 128] = (p < f) & (p // 64 == f // 64)
    c[:, CSO["miu"]:CSO["miu"] + 128] = (p <= f) & (p // 64 == f // 64)
    r = np.ones(256, np.float32)
    r[::64] = 0
    c[:, CSO["reset"]:CSO["reset"] + 256] = r[None, :]
    selb = np.zeros((128, 2, 128), np.float32)
    selg = np.zeros((128, 2, 128), np.float32)
    for pr in range(2):
        for hh in range(2):
            selb[2 * pr + hh, pr, hh * 64:(hh + 1) * 64] = 1
            selg[32 + 2 * pr + hh, pr, hh * 64:(hh + 1) * 64] = 1
    c[:, CSO["selb"]:CSO["selb"] + 256] = selb.reshape(128, 256)
    c[:, CSO["selg"]:CSO["selg"] + 256] = selg.reshape(128, 256)
    selh = np.zeros((128, 4, 128), np.float32)
    for h in range(4):
        selh[32 + h, h, :] = 1
    c[:, CSO["selh"]:CSO["selh"] + 512] = selh.reshape(128, 512)
    return c


def make_pv(inp):
    pv = np.zeros((128, NPV), np.float32)

    def put(name, vec, w):
        pv[:, PVO[name]:PVO[name] + w] = np.asarray(vec, np.float32).reshape(w, 128).T
    put("ln_in_g", inp["ln_in_g"], 8)
    put("ln_in_b", inp["ln_in_b"], 8)
    for l in range(L):
        put("lng%d" % l, inp["ln_post_g"][l], 8)
        put("lnb%d" % l, inp["ln_post_b"][l], 8)
        put("mu%d" % l, inp["rwkv_mu"][l], 7)
        put("w0%d" % l, inp["rwkv_w0"][l], 2)
        put("a0%d" % l, inp["rwkv_a0"][l], 2)
        put("kk%d" % l, inp["rwkv_k_k"][l], 2)
        put("ka%d" % l, inp["rwkv_k_a"][l], 2)
        put("rk%d" % l, np.asarray(inp["rwkv_r_k"][l]).reshape(256), 2)
        put("gng%d" % l, inp["rwkv_gn_g"][l], 2)
        put("gnb%d" % l, inp["rwkv_gn_b"][l], 2)
        cw = np.asarray(inp["gdn_conv_w"][l], np.float32)
        for i in range(4):
            pv[:, PVO["cw%d" % l] + i * 6:PVO["cw%d" % l] + i * 6 + 6] = cw[i].reshape(6, 128).T
        gn = np.asarray(inp["gdn_norm_g"][l], np.float32)
        pv[:, PVO["gnorm%d" % l]] = np.concatenate([gn, gn])
        bfv = np.asarray(inp["fox_b_f"][l], np.float32)
        for g in (0, 32, 64):
            pv[g:g + 8, PVO["bf%d" % l]] = bfv
        pv[32:36, PVO["alog%d" % l]] = np.asarray(inp["gdn_a_log"][l], np.float32)
        pv[32:36, PVO["dtb%d" % l]] = np.asarray(inp["gdn_dt_bias"][l], np.float32)
    return pv


class Res:
    __slots__ = ("w", "r", "psum")

    def __init__(self):
        self.w = None
        self.r = {}
        self.psum = False


class Eng:
    def __init__(self, name, unit):
        self.name = name
        self.unit = unit
        self.n = 0
        self.known = {}
        self.hist = {}
        self.ops = []
        self.sem = None


class MK:
    NDQ = 12
    CE = ("pe", "act", "dve", "pool", "sp")

    def __init__(self):
        self.E = {}
        for nm in self.CE:
            self.E[nm] = Eng(nm, 1)
        for i in range(self.NDQ):
            self.E["dq%d" % i] = Eng("dq%d" % i, 16)
        self.dq_rr = 0
        self.n_wait = 0
        self.n_ins = 0
        import threading
        self.tl = threading.local()

    def _need(self, W, reads, writes, is_dma=False):
        toks = {}

        def add(tok, same_ok):
            if tok is None:
                return
            e, n = tok
            if same_ok and e is W and W.name == "pe" and not is_dma:
                return
            if W.known.get(e.name, 0) >= n:
                return
            if toks.get(e.name, (None, 0))[1] < n:
                toks[e.name] = (e, n)

        for r in reads:
            add(r.w, False)
            if r.psum:
                for t in r.r.values():
                    if t[0] is not W:
                        add(t, True)
        for r in writes:
            add(r.w, True)
            for t in r.r.values():
                add(t, True)
        return list(toks.values())

    def _merge(self, W, e, n):
        snap = e.hist.get(n)
        if snap:
            for k, v in snap.items():
                if W.known.get(k, 0) < v:
                    W.known[k] = v
        if W.known.get(e.name, 0) < n:
            W.known[e.name] = n

    def _record(self, te, tn, reads, writes, snap):
        te.hist[tn] = snap
        tok = (te, tn)
        for r in reads:
            r.r[te.name] = tok
        for r in writes:
            r.w = tok
            r.r = {}

    def op(self, eng, fn, reads=(), writes=()):
        W = self.E[eng]
        waits = self._need(W, reads, writes)
        for e, n in waits:
            self._merge(W, e, n)
        W.n += 1
        n = W.n
        snap = dict(W.known)
        snap[W.name] = n
        self._record(W, n, reads, writes, snap)
        W.ops.append((fn, [(e, n_ * e.unit) for e, n_ in waits], W))
        self.n_wait += len(waits)
        self.n_ins += 1
        hk = getattr(self.tl, "hook", None)
        if hk:
            hk()

    def dma(self, out, in_, reads=(), writes=(), queue="sp", **kw):
        Q = self.E[queue]
        k = self.dq_rr
        self.dq_rr = (self.dq_rr + 1) % self.NDQ
        Dq = self.E["dq%d" % k]
        waits = self._need(Q, reads, writes, is_dma=True)
        if Dq.n > 0 and Q.known.get(Dq.name, 0) < Dq.n:
            waits = [w for w in waits if w[0] is not Dq] + [(Dq, Dq.n)]
        for e, n in waits:
            self._merge(Q, e, n)
        Dq.n += 1
        n = Dq.n
        snap = dict(Q.known)
        snap[Dq.name] = n
        self._record(Dq, n, reads, writes, snap)

        def fn(eng, out=out, in_=in_, kw=kw):
            return eng.dma_start(out=out, in_=in_, **kw)
        Q.ops.append((fn, [(e, n_ * e.unit) for e, n_ in waits], Dq))
        self.n_wait += len(waits)
        self.n_ins += 1
        hk = getattr(self.tl, "hook", None)
        if hk:
            hk()

    def barrier(self):
        for nm in self.CE:
            W = self.E[nm]
            waits = []
            for e in self.E.values():
                if e is W or e.n == 0:
                    continue
                if W.known.get(e.name, 0) < e.n:
                    waits.append((e, e.n))
            for e, n in waits:
                self._merge(W, e, n)
            W.ops.append((None, [(e, n * e.unit) for e, n in waits], None))

    def runner(self, sems):
        for e in self.E.values():
            e.sem = sems[e.name]

        def run(engobj, E):
            fuse = E.name in ("act", "dve", "pool")
            for fn, waits, inc_e in E.ops:
                if fn is None or not fuse or not waits:
                    for e, v in waits:
                        engobj.wait_ge(e.sem, v)
                    if fn is None:
                        continue
                    fn(engobj).then_inc(inc_e.sem, inc_e.unit)
                else:
                    for e, v in waits[:-1]:
                        engobj.wait_ge(e.sem, v)
                    ins = fn(engobj)
                    ins._wait_ge(waits[-1][0].sem, waits[-1][1])
                    ins.then_inc(inc_e.sem, inc_e.unit)
        return run


class Tile:
    def __init__(self, t):
        self.t = t
        self.r = Res()

    def __getitem__(self, k):
        return self.t[k]


class _V:
    def __init__(self, t, i):
        self.t = t
        self.i = i
        self.r = t.r

    def __getitem__(self, k):
        return self.t.t[(k[0], self.i) + tuple(k[1:])]


class _B:
    def __init__(self, t):
        self.t = t
        self.r = t.r
        self.ap = t.t[:, :].bitcast(BF)

    def __getitem__(self, k):
        return self.ap[k]


class Scope:
    def __init__(self, kb):
        self.kb = kb
        self.st = ExitStack()
        self.pools = {}

    def __enter__(self):
        self.st.__enter__()
        return self

    def __exit__(self, *a):
        self.kb.mk.barrier()
        return self.st.__exit__(*a)

    def sb(self, name, shape, dt=F32):
        self.kb.uid += 1
        return Tile(self.st.enter_context(self.kb.nc.sbuf_tensor("%s_%d" % (name, self.kb.uid), list(shape), dt)))

    def pool(self, name, n, shape, dt=F32):
        self.pools[name] = [[self.sb(name + str(i), shape, dt) for i in range(n)], 0]

    def get(self, name):
        p = self.pools[name]
        t = p[0][p[1] % len(p[0])]
        p[1] += 1
        return t


class KB:
    def __init__(self, T, PAST, with_sample=True, nlayers=L):
        self.T = T
        self.PAST = PAST
        self.with_sample = with_sample
        self.nl = nlayers
        self.nc = bass.Bass("TRN2", target_bir_lowering=False)
        self.mk = MK()
        self.st = ExitStack()
        self.uid = 0
        self.psp = {}
        self.phases = "ABCD"

    def sb(self, name, shape, dt=F32):
        return Tile(self.st.enter_context(self.nc.sbuf_tensor(name, list(shape), dt)))

    def din(self, name, shape, dt=F32):
        return self.nc.dram_tensor(name, list(shape), dt, kind="ExternalInput").ap()

    def dout(self, name, shape, dt=F32):
        return self.nc.dram_tensor(name, list(shape), dt, kind="ExternalOutput").ap()

    def psb(self, pool):
        t = self.ps(pool)
        v = _B(t)
        return v

    def ps(self, pool="g"):
        p = self.psp[pool]
        t = p[0][p[1] % len(p[0])]
        p[1] += 1
        return t

    @staticmethod
    def _rs(xs):
        return [x.r if isinstance(x, (Tile, _V, _B)) else x for x in xs]

    def tt(self, eng, out, in0, in1, op, R, W):
        self.mk.op(eng, lambda e: e.tensor_tensor(out=out, in0=in0, in1=in1, op=op), self._rs(R), self._rs(W))

    def ts(self, eng, out, in0, s1, op0, R, W, s2=None, op1=None):
        if op1 is None:
            self.mk.op(eng, lambda e: e.tensor_scalar(out=out, in0=in0, scalar1=s1, scalar2=None, op0=op0),
                       self._rs(R), self._rs(W))
        else:
            self.mk.op(eng, lambda e: e.tensor_scalar(out=out, in0=in0, scalar1=s1, scalar2=s2, op0=op0, op1=op1),
                       self._rs(R), self._rs(W))

    def stt(self, out, in0, scalar, in1, op0, op1, R, W):
        self.mk.op("dve", lambda e: e.scalar_tensor_tensor(out=out, in0=in0, scalar=scalar, in1=in1, op0=op0, op1=op1),
                   self._rs(R), self._rs(W))

    def cp(self, eng, out, in_, R, W):
        if eng == "act":
            self.mk.op(eng, lambda e: e.copy(out=out, in_=in_), self._rs(R), self._rs(W))
        else:
            self.mk.op(eng, lambda e: e.tensor_copy(out=out, in_=in_), self._rs(R), self._rs(W))

    def act(self, out, in_, func, R, W, bias=0.0, scale=1.0):
        self.mk.op("act", lambda e: e.activation(out=out, in_=in_, func=func, bias=bias, scale=scale),
                   self._rs(R), self._rs(W))

    def mm(self, out, lhsT, rhs, start, stop, R, W):
        self.mk.op("pe", lambda e: e.matmul(out, lhsT=lhsT, rhs=rhs, start=start, stop=stop),
                   self._rs(R), self._rs(W))

    def tr(self, out, in_, ident, R, W):
        self.mk.op("pe", lambda e: e.transpose(out, in_, ident), self._rs(R), self._rs(W))

    def recip(self, out, in_, R, W):
        self.mk.op("dve", lambda e: e.reciprocal(out=out, in_=in_), self._rs(R), self._rs(W))

    def memset(self, eng, ap, val, W):
        self.mk.op(eng, lambda e: e.memset(ap, val), [], self._rs(W))

    def scan(self, out, d0, d1, R, W):
        self.mk.op("dve", lambda e: e.tensor_tensor_scan(out=out, data0=d0, data1=d1, initial=0.0,
                                                           op0=ALU.mult, op1=ALU.add), self._rs(R), self._rs(W))

    def dma(self, out, in_, R, W, **kw):
        self.mk.dma(out, in_, self._rs(R), self._rs(W), **kw)

    def pvc(self, name, c=0, rows=slice(0, 128)):
        o = PVO[name] + c
        return self.PV[rows, o:o + 1]

    def cst(self, name, w=128, rows=slice(0, 128), c0=0):
        o = CSO[name] + c0
        return self.CST[rows, o:o + w]

    def load_w(self, src3, col_ranges):
        for (sc, w, dc) in col_ranges:
            o = 0
            while o < w:
                ww = min(64, w - o)
                stg = self.get_ws()
                self.dma(stg[:, :, 0:ww], src3[:, :, sc + o:sc + o + ww], [], [stg])
                self.cp("pool", self.WB[:, :, dc + o:dc + o + ww], stg[:, :, 0:ww], [stg], [self.WB])
                o += ww

    def get_ws(self):
        t = self.WS[self.ws_i % len(self.WS)]
        self.ws_i += 1
        return t

    def proj_fm(self, ps_ap, wcol, ncols, xt, t0, n, pst):
        for k in range(8):
            self.mm(ps_ap, self.WB[:, k, wcol:wcol + ncols], xt[:, k, t0:t0 + n], k == 0, k == 7,
                    [self.WB, xt], [pst])

    def ln_fm(self, sc, Vt, n, gname, bname, out_bf=None, out_f32=None):
        VB = sc.get("lnvb")
        VQ = sc.get("lnvq")
        self.cp("act", VB[:, :, 0:n], Vt[:, :, 0:n], [Vt], [VB])
        self.act(VQ[:, :, 0:n], Vt[:, :, 0:n], AF.Square, [Vt], [VQ])
        p1 = self.ps()
        p2 = self.ps()
        for c in range(8):
            self.mm(p1[:, 0:n], self.ONESB[:, :], VB[:, c, 0:n], c == 0, c == 7, [self.CB, VB], [p1])
        for c in range(8):
            self.mm(p2[:, 0:n], self.ONESB[:, :], VQ[:, c, 0:n], c == 0, c == 7, [self.CB, VQ], [p2])
        ME = sc.get("lnt")
        MS = sc.get("lnt")
        VA = sc.get("lnt")
        RS = sc.get("lnt")
        self.ts("dve", ME[:, 0:n], p1[:, 0:n], 1.0 / D_MODEL, ALU.mult, [p1], [ME])
        self.tt("pool", MS[:, 0:n], ME[:, 0:n], ME[:, 0:n], ALU.mult, [ME], [MS])
        self.stt(VA[:, 0:n], p2[:, 0:n], 1.0 / D_MODEL, MS[:, 0:n], ALU.mult, ALU.subtract, [p2, MS], [VA])
        self.act(RS[:, 0:n], VA[:, 0:n], AF.Ln, [VA], [RS], bias=self.EPS[:, 0:1], scale=1.0)
        self.act(RS[:, 0:n], RS[:, 0:n], AF.Exp, [RS], [RS], scale=-0.5)
        for c in range(8):
            Dd = sc.get("lnd")
            self.tt("pool", Dd[:, 0:n], Vt[:, c, 0:n], ME[:, 0:n], ALU.subtract, [Vt, ME], [Dd])
            self.tt("dve", Dd[:, 0:n], Dd[:, 0:n], RS[:, 0:n], ALU.mult, [Dd, RS], [Dd])
            if out_bf is not None:
                ap, tl = out_bf(c)
                self.act(ap, Dd[:, 0:n], AF.Identity, [Dd, self.PV], [tl],
                         bias=self.pvc(bname, c), scale=self.pvc(gname, c))
            if out_f32 is not None:
                ap, tl = out_f32(c)
                self.act(ap, Dd[:, 0:n], AF.Identity, [Dd, self.PV], [tl],
                         bias=self.pvc(bname, c), scale=self.pvc(gname, c))

    def build(self):
        nc, mk = self.nc, self.mk
        T, PAST = self.T, self.PAST
        NB = T // 128
        d = {}
        d["xp"] = self.din("xp", [T, D_MODEL])
        d["w_in"] = self.din("w_in", [L, D_MODEL, N_IN])
        d["w_out"] = self.din("w_out", [L, D_MODEL, D_MODEL])
        d["w2"] = self.din("rwkv_w2", [L, 64, 256])
        d["a2"] = self.din("rwkv_a2", [L, 64, 256])
        d["pv"] = self.din("pv", [128, NPV])
        d["cst"] = self.din("cst", [128, NCST])
        o = {}
        o["y_p"] = self.dout("y_p", [T, D_MODEL])
        o["fk_p"] = self.dout("fk_p", [L, T, 512])
        o["fv_p"] = self.dout("fv_p", [L, T, 512])
        o["fl_p"] = self.dout("fl_p", [L, T, 8])
        o["sh_p"] = self.dout("sh_p", [L, 896])
        o["rw_p"] = self.dout("rw_p", [L, 4, 64, 64])
        o["cv_p"] = self.dout("cv_p", [L, 3, 768])
        o["gd_p"] = self.dout("gd_p", [L, 4, 64, 64])
        if self.with_sample:
            d["xs"] = self.din("xs", [NS * TS, D_MODEL])
            d["ckT"] = self.din("ckT", [L, NS, 8, 64, PAST])
            d["cv"] = self.din("cv", [L, NS, PAST, 512])
            d["clf"] = self.din("clf", [L, NS, 8, PAST])
            d["st_sh"] = self.din("st_sh", [L, NS, 896])
            d["st_rw"] = self.din("st_rw", [L, NS, 4, 64, 64])
            d["st_cv"] = self.din("st_cv", [L, NS, 3, 768])
            d["st_gd"] = self.din("st_gd", [L, NS, 4, 64, 64])
            o["y_s"] = self.dout("y_s", [NS * TS, D_MODEL])
            o["fk_s"] = self.dout("fk_s", [L, NS, TS, 512])
            o["fv_s"] = self.dout("fv_s", [L, NS, TS, 512])
            o["fl_s"] = self.dout("fl_s", [L, NS, TS, 8])
            o["sh_s"] = self.dout("sh_s", [L, NS, 896])
            o["rw_s"] = self.dout("rw_s", [L, NS, 4, 64, 64])
            o["cv_s"] = self.dout("cv_s", [L, NS, 3, 768])
            o["gd_s"] = self.dout("gd_s", [L, NS, 4, 64, 64])
        self.d, self.o = d, o

        with self.st:
            self.XT = self.sb("XT", [128, 8, T], BF)
            self.OT = self.sb("OT", [128, 8, T], BF)
            self.XTs = self.sb("XTs", [128, 8, NS * TS], BF)
            self.OTs = self.sb("OTs", [128, 8, NS * TS], BF)
            self.PV = self.sb("PV", [128, NPV])
            self.CST = self.sb("CST", [128, NCST])
            self.CB = self.sb("CB", [128, 6, 128], BF)
            self.WB = self.sb("WB", [128, 8, 1024], BF)
            self.WS = [self.sb("WS%d" % i, [128, 8, 64]) for i in range(2)]
            self.ws_i = 0
            self.EPS = self.sb("EPS", [128, 4])
            self.PD = self.sb("PD", [128, L, 12])
            bk = [Tile(self.st.enter_context(nc.psum_tensor("PB%d" % i, [128, 512], F32))) for i in range(8)]
            for t_ in bk:
                t_.r.psum = True
            self.psp = {"g": [bk[0:5], 0], "a": [bk[5:7], 0], "z0": [[bk[0], bk[1], bk[7]], 0], "z1": [[bk[2], bk[3], bk[4]], 0]}

            self.IDB = self.CB[:, 0, :]
            self.ONESBD = self.CB[:, 1, :]
            self.ONESB = self.CB[:, 2, :]
            self.TRIB = self.CB[:, 3, :]
            self.MSUB = self.CB[:, 4, :]
            self.MIUB = self.CB[:, 5, :]

            self.dma(self.PV[:, :], d["pv"], [], [self.PV])
            self.dma(self.CST[:, :], d["cst"], [], [self.CST])
            for i, nm in enumerate(("ident", "onesbd", "ones", "tri", "msu", "miu")):
                self.cp("pool", self.CB[:, i, :], self.cst(nm), [self.CST], [self.CB])
            self.memset("pool", self.EPS[:, 0:1], LN_EPS, [self.EPS])
            self.memset("pool", self.EPS[:, 1:2], RWKV_GN_EPS, [self.EPS])
            self.memset("pool", self.EPS[:, 2:3], GDN_NORM_EPS, [self.EPS])
            self.memset("pool", self.EPS[:, 3:4], L2_EPS, [self.EPS])
            for l in range(self.nl):
                for c in range(2):
                    self.ts("pool", self.PD[:, l, c:c + 1], self.pvc("w0%d" % l, c), -1.0, ALU.mult, [self.PV], [self.PD])
                    self.ts("pool", self.PD[:, l, 2 + c:3 + c], self.pvc("a0%d" % l, c), -1.0, ALU.mult, [self.PV], [self.PD])
                    self.ts("pool", self.PD[:, l, 4 + c:5 + c], self.pvc("ka%d" % l, c), -1.0, ALU.mult, [self.PV], [self.PD],
                            s2=1.0, op1=ALU.add)
                self.ts("pool", self.PD[:, l, 6:7], self.pvc("bf%d" % l), -1.0, ALU.mult, [self.PV], [self.PD])
                self.act(self.PD[:, l, 7:8], self.pvc("alog%d" % l), AF.Exp, [self.PV], [self.PD])
                self.ts("pool", self.PD[:, l, 7:8], self.PD[:, l, 7:8], -1.0, ALU.mult, [self.PD], [self.PD])
            mk.barrier()

            self.phase0()
            for l in range(self.nl):
                self.layer(l)
            mk.barrier()

            sems = {}
            for nm in mk.E:
                sems[nm] = self.st.enter_context(nc.semaphore("s_" + nm))
            run = mk.runner(sems)
            with nc.Block() as block:
                @block.tensor
                def _(e):
                    run(e, mk.E["pe"])

                @block.vector
                def _(e):
                    run(e, mk.E["dve"])

                @block.scalar
                def _(e):
                    run(e, mk.E["act"])

                @block.gpsimd
                def _(e):
                    run(e, mk.E["pool"])

                @block.sync
                def _(e):
                    run(e, mk.E["sp"])
        return nc

    def ln_pools(self, sc, n):
        sc.pool("v", 1, [128, 8, n])
        sc.pool("lnvb", 1, [128, 8, n], BF)
        sc.pool("lnvq", 1, [128, 8, n], BF)
        sc.pool("lnt", 4, [128, n])
        sc.pool("lnd", 3, [128, n])

    def phase0(self):
        n = 256
        with Scope(self) as sc:
            sc.pool("xin", 1, [128, 2, D_MODEL])
            self.ln_pools(sc, n)
            segs = [(self.d["xp"], self.XT, self.T)]
            if self.with_sample:
                segs.append((self.d["xs"], self.XTs, NS * TS))
            for (xd, XT, TT) in segs:
                for t0 in range(0, TT, n):
                    nn = min(n, TT - t0)
                    XI = sc.get("xin")
                    nb = (nn + 127) // 128
                    bw = min(128, nn)
                    self.dma(XI[0:bw, 0:nb, :], xd[t0:t0 + nn, :].rearrange("(b p) f -> p b f", p=bw), [], [XI])
                    Vt = sc.get("v")
                    for c in range(8):
                        pp = self.ps()
                        for bb in range(nb):
                            self.tr(pp[:, bb * 128:bb * 128 + bw], XI[0:bw, bb, c * 128:(c + 1) * 128],
                                    self.cst("ident", bw, slice(0, bw)), [XI, self.CST], [pp])
                        self.cp("act" if c % 2 else "dve", Vt[:, c, 0:nn], pp[:, 0:nn], [pp], [Vt])
                    self.ln_fm(sc, Vt, nn, "ln_in_g", "ln_in_b",
                               out_bf=lambda c, t0=t0, nn=nn, XT=XT: (XT[:, c, t0:t0 + nn], XT))

    def phaseD(self, l):
        last = (l == L - 1)
        w3 = self.d["w_out"][l].rearrange("(k p) n -> p k n", p=128)
        self.load_w(w3, [(0, 1024, 0)])
        n = 256
        with Scope(self) as sc:
            self.ln_pools(sc, n)
            if last:
                sc.pool("yf", 1, [128, 8, n])
                sc.pool("yt", 2, [128, D_MODEL])
            segs = [(self.XT, self.OT, self.T, self.o["y_p"])]
            if self.with_sample:
                segs.append((self.XTs, self.OTs, NS * TS, self.o["y_s"]))
            for (XT, OT, TT, yd) in segs:
                for t0 in range(0, TT, n):
                    nn = min(n, TT - t0)
                    Vt = sc.get("v")
                    for c in range(8):
                        pp = self.ps()
                        for k in range(8):
                            self.mm(pp[:, 0:nn], self.WB[:, k, c * 128:(c + 1) * 128], OT[:, k, t0:t0 + nn],
                                    k == 0, k == 7, [self.WB, OT], [pp])
                        self.stt(Vt[:, c, 0:nn], XT[:, c, t0:t0 + nn], ALPHA, pp[:, 0:nn], ALU.mult, ALU.add,
                                 [XT, pp], [Vt])
                    if not last:
                        self.ln_fm(sc, Vt, nn, "lng%d" % l, "lnb%d" % l,
                                   out_bf=lambda c, t0=t0, nn=nn, XT=XT: (XT[:, c, t0:t0 + nn], XT))
                    else:
                        YF = sc.get("yf")
                        self.ln_fm(sc, Vt, nn, "lng%d" % l, "lnb%d" % l,
                                   out_f32=lambda c, nn=nn, YF=YF: (YF[:, c, 0:nn], YF))
                        bw = min(128, nn)
                        for bb in range((nn + 127) // 128):
                            YT = sc.get("yt")
                            for half in range(2):
                                pp = self.ps()
                                for cc in range(4):
                                    c = half * 4 + cc
                                    self.tr(pp[0:bw, cc * 128:(cc + 1) * 128], YF[:, c, bb * 128:bb * 128 + bw],
                                            self.cst("ident"), [YF, self.CST], [pp])
                                self.cp("act" if half else "dve", YT[0:bw, half * 512:(half + 1) * 512], pp[0:bw, :], [pp], [YT])
                            self.dma(yd[t0 + bb * 128:t0 + bb * 128 + bw, :], YT[0:bw, :], [YT], [])

    def phaseB(self, l):
        T = self.T
        NT = T // 512
        NB = T // 128
        w3 = self.d["w_in"][l].rearrange("(k p) n -> p k n", p=128)
        with Scope(self) as so:
            HL3 = so.sb("HL3", [128, T], BF)
            NCK = so.sb("NCK", [128, NB, 8])
            if self.with_sample:
                nkb_s = self.PAST // 128
                self.HL3s = so.sb("HL3s", [128, NS, TS], BF)
                self.NCKs = so.sb("NCKs", [128, NS, nkb_s + 1, 8])
            with Scope(self) as sc:
                WF = sc.sb("WF", [128, 8, 72], BF)
                LFT = sc.sb("LFT", [128, NB, 8])
                CAR = sc.sb("CAR", [128, 2])
                sc.pool("t", 6, [128, 512])
                sc.pool("tb", 3, [128, 512], BF)
                self.memset("pool", WF[:, :, :], 0.0, [WF])
                self.memset("pool", HL3[:, :], 0.0, [HL3])
                self.memset("pool", CAR[:, :], 0.0, [CAR])
                stg = self.get_ws()
                self.dma(stg[:, :, 0:8], w3[:, :, OFF["b_f"]:OFF["b_f"] + 8], [], [stg])
                for g in (0, 32, 64):
                    self.cp("pool", WF[:, :, g:g + 8], stg[:, :, 0:8], [stg], [WF])
                ones_b = self.cst("ones", 1, slice(0, 72)).to_broadcast([72, 512])
                for tt in range(NT):
                    t0 = tt * 512
                    pp = self.ps()
                    for k in range(8):
                        self.mm(pp[0:72, :], WF[:, k, :], self.XT[:, k, t0:t0 + 512], k == 0, k == 7, [WF, self.XT], [pp])
                    LS = sc.get("t")
                    self.act(LS[0:72, :], pp[0:72, :], AF.Exp, [pp, self.PD], [LS], bias=self.PD[0:72, l, 6:7], scale=-1.0)
                    self.act(LS[0:72, :], LS[0:72, :], AF.Ln, [LS], [LS], bias=1.0, scale=1.0)
                    CUMN = sc.get("t")
                    self.mk.op("dve", lambda e, CUMN=CUMN, LS=LS, tt=tt: e.tensor_tensor_scan(
                        out=CUMN[0:72, :], data0=ones_b, data1=LS[0:72, :], initial=CAR[0:72, tt % 2:tt % 2 + 1],
                        op0=ALU.mult, op1=ALU.add), self._rs([self.CST, LS, CAR]), self._rs([CUMN]))
                    self.cp("pool", CAR[0:72, (tt + 1) % 2:(tt + 1) % 2 + 1], CUMN[0:72, 511:512], [CUMN], [CAR])
                    HI = sc.get("tb")
                    self.ts("dve", HI[0:72, :], CUMN[0:72, :], -1.0, ALU.mult, [CUMN], [HI])
                    self.cp("pool", HL3[0:8, t0:t0 + 512], HI[0:8, :], [HI], [HL3])
                    R1 = sc.get("t")
                    self.stt(R1[0:72, :], CUMN[0:72, :], -1.0, HI[0:72, :], ALU.mult, ALU.subtract, [CUMN, HI], [R1])
                    MI = sc.get("tb")
                    self.cp("pool", MI[0:72, :], R1[0:72, :], [R1], [MI])
                    self.cp("pool", HL3[32:40, t0:t0 + 512], MI[32:40, :], [MI], [HL3])
                    R2 = sc.get("t")
                    self.tt("dve", R2[64:72, :], R1[64:72, :], MI[64:72, :], ALU.subtract, [R1, MI], [R2])
                    self.cp("pool", HL3[64:72, t0:t0 + 512], R2[64:72, :], [R2], [HL3])
                    p1 = self.ps()
                    p2 = self.ps()
                    for bb in range(4):
                        self.tr(p1[:, bb * 8:(bb + 1) * 8], CUMN[0:8, bb * 128:(bb + 1) * 128],
                                self.cst("ident", 8, slice(0, 8)), [CUMN, self.CST], [p1])
                        self.tr(p2[:, bb * 8:(bb + 1) * 8], LS[0:8, bb * 128:(bb + 1) * 128],
                                self.cst("ident", 8, slice(0, 8)), [LS, self.CST], [p2])
                    self.cp("dve", NCK[:, tt * 4:tt * 4 + 4, :], p1[:, 0:32].rearrange("p (b h) -> p b h", h=8), [p1], [NCK])
                    self.ts("dve", LFT[:, tt * 4:tt * 4 + 4, :], p2[:, 0:32].rearrange("p (b h) -> p b h", h=8), -1.0, ALU.mult, [p2], [LFT])
                for b0 in range(0, NB, 8):
                    self.dma(self.o["fl_p"][l, b0 * 128:(b0 + 8) * 128 if b0 + 8 <= NB else NB * 128, :].rearrange("(b p) h -> p b h", p=128),
                             LFT[:, b0:min(b0 + 8, NB), :], [LFT], [])
                if self.with_sample:
                    self.fox_sample_setup(l, sc, WF, so)
            for h in range(8 if BSTAGE >= 1 else 0):
                self.fox_head(l, h, so, HL3, NCK, w3)

    def fox_head(self, l, h, so, HL3, NCK, w3):
        T = self.T
        NT = T // 512
        NB = T // 128
        hp, hh = h // 2, h % 2
        self.load_w(w3, [(OFF["b_q"] + h * 64, 64, 0), (OFF["b_k"] + h * 64, 64, 64),
                         (OFF["b_v"] + h * 64, 64, 128), (OFF["b_z"] + h * 64, 64, 192)])
        with Scope(self) as sc:
            QA = sc.sb("QA", [128, T], BF)
            KA = sc.sb("KA", [128, T], BF)
            VA = sc.sb("VA", [128, NB, 128], BF)
            sc.pool("kvo", 1, [128, 4, 128])
            sc.pool("pt", 3, [128, 512], BF)
            sc.pool("t", 5, [128, 256])
            if "m" not in os.environ.get("SKIP", ""):
                self.memset("pool", QA[:, :], 0.0, [QA])
                self.memset("pool", KA[:, :], 0.0, [KA])
                self.memset("pool", KA[64:67, :], 1.0, [KA])
                self.memset("pool", VA[:, :, 64:128], 1.0, [VA])
            for i, g in enumerate((0, 32, 64)):
                if os.environ.get("NOSB2SB"):
                    continue
                self.dma(QA[64 + i:65 + i, :], HL3[g + h:g + h + 1, :], [HL3], [QA])
            SK = os.environ.get("SKIP", "")
            for tt in range(NT):
                t0 = tt * 512
                if "q" not in SK:
                    pq = self.ps()
                    self.proj_fm(pq[0:64, :], 0, 64, self.XT, t0, 512, pq)
                    self.act(QA[0:64, t0:t0 + 512], pq[0:64, :], AF.Identity, [pq], [QA], scale=0.125)
                if "k" not in SK:
                    pk = self.ps()
                    self.proj_fm(pk[0:64, :], 64, 64, self.XT, t0, 512, pk)
                    self.cp("dve", KA[0:64, t0:t0 + 512], pk[0:64, :], [pk], [KA])
                if "t" in SK:
                    continue
                KVO = sc.get("kvo")
                pkv = self.ps()
                for b in range(4):
                    for k in range(8):
                        self.mm(pkv[:, b * 128:(b + 1) * 128], self.XT[:, k, t0 + b * 128:t0 + (b + 1) * 128],
                                self.WB[:, k, 64:192], k == 0, k == 7, [self.XT, self.WB], [pkv])
                self.cp("act", KVO[:, :, :], pkv[:, :].rearrange("p (b c) -> p b c", c=128), [pkv], [KVO])
                if "v" not in SK:
                    self.cp("dve", VA[:, tt * 4:(tt + 1) * 4, 0:64], pkv[:, :].rearrange("p (b c) -> p b c", c=128)[:, :, 64:128], [pkv], [VA])
                if not os.environ.get("NOKVOUT"):
                    self.dma(self.o["fk_p"][l, t0:t0 + 512, h * 64:(h + 1) * 64].rearrange("(b p) c -> p b c", p=128),
                             KVO[:, :, 0:64], [KVO], [])
                    self.dma(self.o["fv_p"][l, t0:t0 + 512, h * 64:(h + 1) * 64].rearrange("(b p) c -> p b c", p=128),
                             KVO[:, :, 64:128], [KVO], [])
            for qt in range(NT if BSTAGE >= 2 else 0):
                q0 = qt * 512
                acc = self.ps("a")
                nkb = 4 * qt + 4
                sps = {}

                def issue(kb_):
                    c0_ = max(0, kb_ * 128 - q0)
                    sp_ = self.ps()
                    self.mm(sp_[:, c0_:512], KA[:, kb_ * 128:(kb_ + 1) * 128], QA[:, q0 + c0_:q0 + 512], True, True, [KA, QA], [sp_])
                    sps[kb_] = sp_
                for kb_ in range(min(2, nkb)):
                    issue(kb_)
                for kb in range(nkb):
                    c0 = max(0, kb * 128 - q0)
                    if kb + 2 < nkb:
                        issue(kb + 2)
                    sp = sps.pop(kb)
                    pt = sc.get("pt")
                    self.act(pt[:, c0:512], sp[:, c0:512], AF.Exp, [sp, NCK], [pt], bias=NCK[:, kb, h:h + 1], scale=1.0)
                    if kb * 128 >= q0:
                        self.tt("pool", pt[:, c0:c0 + 128], pt[:, c0:c0 + 128], self.TRIB, ALU.mult, [pt, self.CB], [pt])
                    self.mm(acc[:, c0:512], VA[:, kb, :], pt[:, c0:512], kb == 0, kb == nkb - 1, [VA, pt], [acc])
                for hc in (0, 256):
                    RC = sc.get("t")
                    self.recip(RC[0:64, :], acc[64:128, hc:hc + 256], [acc], [RC])
                    ON = sc.get("t")
                    self.tt("dve", ON[0:64, :], acc[0:64, hc:hc + 256], RC[0:64, :], ALU.mult, [acc, RC], [ON])
                    pz = self.ps()
                    self.proj_fm(pz[0:64, 0:256], 192, 64, self.XT, q0 + hc, 256, pz)
                    E = sc.get("t")
                    self.act(E[0:64, :], pz[0:64, 0:256], AF.Exp, [pz], [E], scale=-1.0)
                    self.act(E[0:64, :], E[0:64, :], AF.Ln, [E], [E], bias=1.0)
                    R_ = sc.get("t")
                    self.act(R_[0:64, :], E[0:64, :], AF.Exp, [E], [R_], scale=-1.0)
                    self.tt("dve", R_[0:64, :], pz[0:64, 0:256], R_[0:64, :], ALU.mult, [pz, R_], [R_])
                    self.tt("dve", self.OT[hh * 64:(hh + 1) * 64, 2 + hp, q0 + hc:q0 + hc + 256], ON[0:64, :], R_[0:64, :], ALU.mult,
                            [ON, R_], [self.OT])
        if self.with_sample:
            self.fox_sample_head(l, h)

    def fox_sample_setup(self, l, sc, WF, so):
        PAST = self.PAST
        nkb = PAST // 128
        W = PAST + TS
        LSs = sc.sb("LSs", [128, W])
        CUMs = sc.sb("CUMs", [128, W])
        LFs = sc.sb("LFs", [TS, 8])
        self.memset("pool", LSs[:, :], 0.0, [LSs])
        self.memset("pool", self.HL3s[:, :, :], 0.0, [self.HL3s])
        ones_b = self.cst("ones", 1, slice(0, 72)).to_broadcast([72, W])
        for s in range(NS):
            for g in (0, 32, 64):
                self.dma(LSs[g:g + 8, 0:PAST], self.d["clf"][l, s], [], [LSs])
            self.ts("dve", LSs[0:72, 0:PAST], LSs[0:72, 0:PAST], -1.0, ALU.mult, [LSs], [LSs])
            pp = self.ps()
            for k in range(8):
                self.mm(pp[0:72, 0:TS], WF[:, k, :], self.XTs[:, k, s * TS:(s + 1) * TS], k == 0, k == 7, [WF, self.XTs], [pp])
            E = sc.get("t")
            self.act(E[0:72, 0:TS], pp[0:72, 0:TS], AF.Exp, [pp, self.PD], [E], bias=self.PD[0:72, l, 6:7], scale=-1.0)
            self.act(LSs[0:72, PAST:W], E[0:72, 0:TS], AF.Ln, [E], [LSs], bias=1.0, scale=1.0)
            self.scan(CUMs[0:72, :], ones_b, LSs[0:72, :], [self.CST, LSs], [CUMs])
            HI = sc.get("tb")
            self.ts("dve", HI[0:72, 0:TS], CUMs[0:72, PAST:W], -1.0, ALU.mult, [CUMs], [HI])
            self.cp("pool", self.HL3s[0:8, s, :], HI[0:8, 0:TS], [HI], [self.HL3s])
            R1 = sc.get("t")
            self.stt(R1[0:72, 0:TS], CUMs[0:72, PAST:W], -1.0, HI[0:72, 0:TS], ALU.mult, ALU.subtract, [CUMs, HI], [R1])
            MI = sc.get("tb")
            self.cp("pool", MI[0:72, 0:TS], R1[0:72, 0:TS], [R1], [MI])
            self.cp("pool", self.HL3s[32:40, s, :], MI[32:40, 0:TS], [MI], [self.HL3s])
            R2 = sc.get("t")
            self.tt("dve", R2[64:72, 0:TS], R1[64:72, 0:TS], MI[64:72, 0:TS], ALU.subtract, [R1, MI], [R2])
            self.cp("pool", self.HL3s[64:72, s, :], R2[64:72, 0:TS], [R2], [self.HL3s])
            p1 = self.ps()
            for kb in range(nkb):
                self.tr(p1[:, kb * 8:(kb + 1) * 8], CUMs[0:8, kb * 128:(kb + 1) * 128], self.cst("ident", 8, slice(0, 8)),
                        [CUMs, self.CST], [p1])
            self.tr(p1[0:TS, nkb * 8:(nkb + 1) * 8], CUMs[0:8, PAST:W], self.cst("ident", 8, slice(0, 8)), [CUMs, self.CST], [p1])
            self.cp("dve", self.NCKs[:, s, 0:nkb, :], p1[:, 0:nkb * 8].rearrange("p (b h) -> p b h", h=8), [p1], [self.NCKs])
            self.cp("dve", self.NCKs[0:TS, s, nkb, :], p1[0:TS, nkb * 8:(nkb + 1) * 8], [p1], [self.NCKs])
            p2 = self.ps()
            self.tr(p2[0:TS, 0:8], LSs[0:8, PAST:W], self.cst("ident", 8, slice(0, 8)), [LSs, self.CST], [p2])
            self.ts("dve", LFs[:, :], p2[0:TS, 0:8], -1.0, ALU.mult, [p2], [LFs])
            self.dma(self.o["fl_s"][l, s], LFs[:, :], [LFs], [])

    def fox_sample_head(self, l, h):
        PAST = self.PAST
        nkb = PAST // 128
        W = PAST + TS
        hp, hh = h // 2, h % 2
        with Scope(self) as sc:
            KAs = sc.sb("KAs", [128, W], BF)
            QAs = sc.sb("QAs", [128, TS], BF)
            VAs = sc.sb("VAs", [128, nkb + 1, 128], BF)
            sc.pool("kst", 2, [64, PAST])
            sc.pool("vst", 2, [128, nkb, 64])
            sc.pool("kvo", 2, [TS, 128])
            sc.pool("pt", 3, [128, TS], BF)
            sc.pool("t", 5, [64, TS])
            self.memset("pool", KAs[:, :], 0.0, [KAs])
            self.memset("pool", KAs[64:67, :], 1.0, [KAs])
            self.memset("pool", QAs[:, :], 0.0, [QAs])
            self.memset("pool", VAs[:, :, :], 0.0, [VAs])
            self.memset("pool", VAs[:, :, 64:128], 1.0, [VAs])
            for s in range(NS):
                s0 = s * TS
                KS = sc.get("kst")
                self.dma(KS[:, :], self.d["ckT"][l, s, h], [], [KS])
                self.cp("pool", KAs[0:64, 0:PAST], KS[:, :], [KS], [KAs])
                VS = sc.get("vst")
                self.dma(VS[:, :, :], self.d["cv"][l, s][:, h * 64:(h + 1) * 64].rearrange("(b p) c -> p b c", p=128), [], [VS])
                self.cp("pool", VAs[:, 0:nkb, 0:64], VS[:, :, :], [VS], [VAs])
                for i, g in enumerate((0, 32, 64)):
                    self.dma(QAs[64 + i:65 + i, :], self.HL3s[g + h:g + h + 1, s, :], [self.HL3s], [QAs])
                pq = self.ps()
                self.proj_fm(pq[0:64, 0:TS], 0, 64, self.XTs, s0, TS, pq)
                self.act(QAs[0:64, :], pq[0:64, 0:TS], AF.Identity, [pq], [QAs], scale=0.125)
                pk = self.ps()
                self.proj_fm(pk[0:64, 0:TS], 64, 64, self.XTs, s0, TS, pk)
                self.cp("dve", KAs[0:64, PAST:W], pk[0:64, 0:TS], [pk], [KAs])
                pkv = self.ps()
                for k in range(8):
                    self.mm(pkv[0:TS, 0:128], self.XTs[:, k, s0:s0 + TS], self.WB[:, k, 64:192], k == 0, k == 7,
                            [self.XTs, self.WB], [pkv])
                KVO = sc.get("kvo")
                self.cp("act", KVO[:, :], pkv[0:TS, 0:128], [pkv], [KVO])
                self.cp("pool", VAs[0:TS, nkb, 0:64], KVO[:, 64:128], [KVO], [VAs])
                self.dma(self.o["fk_s"][l, s, :, h * 64:(h + 1) * 64], KVO[:, 0:64], [KVO], [])
                self.dma(self.o["fv_s"][l, s, :, h * 64:(h + 1) * 64], KVO[:, 64:128], [KVO], [])
                acc = self.ps("a")
                sps = {}

                def issue(kb_):
                    kw_ = 128 if kb_ < nkb else TS
                    sp_ = self.ps()
                    self.mm(sp_[0:kw_, 0:TS], KAs[:, kb_ * 128:kb_ * 128 + kw_], QAs[:, :], True, True, [KAs, QAs], [sp_])
                    sps[kb_] = sp_
                for kb_ in range(min(2, nkb + 1)):
                    issue(kb_)
                for kb in range(nkb + 1):
                    kw = 128 if kb < nkb else TS
                    if kb + 2 < nkb + 1:
                        issue(kb + 2)
                    sp = sps.pop(kb)
                    pt = sc.get("pt")
                    self.act(pt[0:kw, :], sp[0:kw, 0:TS], AF.Exp, [sp, self.NCKs], [pt], bias=self.NCKs[0:kw, s, kb, h:h + 1], scale=1.0)
                    if kb == nkb:
                        self.tt("pool", pt[0:TS, :], pt[0:TS, :], self.CB[0:TS, 3, 0:TS], ALU.mult, [pt, self.CB], [pt])
                    self.mm(acc[:, 0:TS], VAs[0:kw, kb, :], pt[0:kw, :], kb == 0, kb == nkb, [VAs, pt], [acc])
                RC = sc.get("t")
                self.recip(RC[:, :], acc[64:128, 0:TS], [acc], [RC])
                ON = sc.get("t")
                self.tt("dve", ON[:, :], acc[0:64, 0:TS], RC[:, :], ALU.mult, [acc, RC], [ON])
                pz = self.ps()
                self.proj_fm(pz[0:64, 0:TS], 192, 64, self.XTs, s0, TS, pz)
                E = sc.get("t")
                self.act(E[:, :], pz[0:64, 0:TS], AF.Exp, [pz], [E], scale=-1.0)
                self.act(E[:, :], E[:, :], AF.Ln, [E], [E], bias=1.0)
                R_ = sc.get("t")
                self.act(R_[:, :], E[:, :], AF.Exp, [E], [R_], scale=-1.0)
                self.tt("dve", R_[:, :], pz[0:64, 0:TS], R_[:, :], ALU.mult, [pz, R_], [R_])
                self.tt("dve", self.OTs[hh * 64:(hh + 1) * 64, 2 + hp, s0:s0 + TS], ON[:, :], R_[:, :], ALU.mult,
                        [ON, R_], [self.OTs])

    def delta_alloc(self, sc, n, rw):
        d = {}
        d["SC"] = [[sc.sb("SC%d_%d" % (i, hh), [128, 4 if rw else 2, 128], BF) for hh in range(2)] for i in range(2)]
        d["A"] = [[sc.sb("DA%d_%d" % (p_, i), [128, 2, 128], BF) for i in range(2)] for p_ in range(2)]
        d["X"] = [[sc.sb("DX%d_%d" % (p_, i), [128, 2, 128], BF) for i in range(2)] for p_ in range(2)]
        d["P"] = [[sc.sb("DP%d_%d" % (p_, i), [128, 2, 128], BF) for i in range(2)] for p_ in range(2)]
        d["TM"] = [sc.sb("TM%d" % i, [128, 4, 128], BF) for i in range(2)]
        d["VZ"] = [sc.sb("VZ%d" % i, [128, 2, 128], BF) for i in range(2)]
        d["WT"] = [sc.sb("WT%d" % i, [128, 128], BF) for i in range(2)]
        d["YB"] = [sc.sb("YB%d" % i, [128, 128], BF) for i in range(2)]
        d["UT"] = [sc.sb("UT%d" % i, [128, 128]) for i in range(2)]
        d["UP"] = sc.sb("UP", [128, 128], BF)
        d["UZ"] = sc.sb("UZ", [128, 2, 128], BF)
        d["H"] = sc.sb("H", [128, 128])
        d["HB"] = sc.sb("HB", [128, 128], BF)
        d["HB2"] = sc.sb("HB2", [128, 128], BF)
        d["hbi"] = 0
        d["i"] = 0
        for t in d["VZ"] + [d["UZ"], d["UP"], d["HB2"]]:
            self.memset("pool", t[:, :] if len(t.t.shape) == 2 else t[:, :, :], 0.0, [t])
        return d

    def delta_group(self, d, G, C, g0, masks, score_mms, tm_srcs, rw, Rf, OA, dec_ap, dec_res):
        gi = d["i"]
        d["i"] += 1
        ctx = self.delta_pre(d, gi, G, C, g0, masks, score_mms, tm_srcs, rw)
        self.delta_chunks(d, ctx, G, C, g0, rw, Rf, OA, dec_ap, dec_res)

    def delta_pre(self, d, gi, G, C, g0, masks, score_mms, tm_srcs, rw):
        nm = 4 if rw else 2
        zp = "z%d" % (gi % 2)
        SC = d["SC"][gi % 2]
        for hh in range(2):
            pp = self.ps(zp)
            rows = slice(hh * 64, hh * 64 + 64)
            for (Lf, R2, ncol, col0) in score_mms:
                if G == 128:
                    self.mm(pp[0:G, col0 * 128:(col0 + ncol) * 128].rearrange("p (m c) -> p m c", c=128)[:, :, 0:G],
                            Lf[rows, g0:g0 + G], R2[rows, 0:ncol, g0:g0 + G], True, True, [Lf, R2], [pp])
                else:
                    for m_ in range(ncol):
                        self.mm(pp[0:G, (col0 + m_) * 128:(col0 + m_) * 128 + G],
                                Lf[rows, g0:g0 + G], R2[rows, m_, g0:g0 + G], True, True, [Lf, R2], [pp])
            map_, mres = masks(hh)
            self.tt("dve", SC[hh][0:G, 0:nm, 0:G], pp[0:G, 0:nm * 128].rearrange("p (m c) -> p m c", c=128)[:, :, 0:G],
                    map_, ALU.mult, [pp] + mres, [SC[hh]])
        TM = d["TM"][gi % 2]
        VZ = d["VZ"][gi % 2]
        pt = self.psb(zp)
        for q, Ft in enumerate(tm_srcs):
            self.tr(pt[0:G, q * 128:(q + 1) * 128], Ft[:, g0:g0 + G], self.IDB, [Ft, self.CB], [pt])
        nq = len(tm_srcs)
        self.cp("act", TM[0:G, 0:nq, :], pt[0:G, 0:nq * 128].rearrange("p (q c) -> p q c", c=128), [pt], [TM])
        if rw:
            for hh in range(2):
                self.cp("pool", VZ[0:G, hh, hh * 64:hh * 64 + 64], TM[0:G, 1, hh * 64:hh * 64 + 64], [TM], [VZ])
        if rw:
            YB = d["YB"][gi % 2]
            py = self.ps(zp)
            for hh in range(2):
                self.mm(py[0:G, hh * 64:hh * 64 + 64], SC[hh][0:G, 2, 0:G], TM[0:G, 1, hh * 64:hh * 64 + 64], True, True,
                        [SC[hh], TM], [py])
            self.cp("dve", YB[0:G, :], py[0:G, 0:128], [py], [YB])
            usrc, ures = YB, YB
        pt2 = self.psb(zp)
        for hh in range(2):
            self.tr(pt2[0:G, hh * 128:hh * 128 + G], SC[hh][0:G, 0, 0:G], self.IDB[0:G, 0:G], [SC[hh], self.CB], [pt2])
        A = d["A"][gi % 2][0]
        self.cp("dve", A[0:G, :, 0:G], pt2[0:G, 0:256].rearrange("p (h c) -> p h c", c=128)[:, :, 0:G], [pt2], [A])
        P = d["P"][gi % 2][0]
        for hh in range(2):
            self.tt("pool", P[0:G, hh, 0:G], SC[hh][0:G, 0, 0:G], self.IDB[0:G, 0:G], ALU.add, [SC[hh], self.CB], [P])
        nsteps = int(round(math.log2(C))) - 1
        Xc = None
        ai, xi, pi = 0, 0, 0
        for i in range(nsteps):
            last = (i == nsteps - 1)
            pa = self.ps(zp)
            for hh in range(2):
                xl = SC[hh][0:G, 0, 0:G] if Xc is None else Xc[0:G, hh, 0:G]
                xr = SC[hh] if Xc is None else Xc
                self.mm(pa[0:G, hh * 128:hh * 128 + G], xl, A[0:G, hh, 0:G], True, True, [xr, A], [pa])
            if not last:
                px = self.ps(zp)
                for hh in range(2):
                    xl = SC[hh][0:G, 0, 0:G] if Xc is None else Xc[0:G, hh, 0:G]
                    xr = SC[hh] if Xc is None else Xc
                    self.mm(px[0:G, hh * 128:hh * 128 + G], A[0:G, hh, 0:G], xl, True, True, [xr, A], [px])
            ai += 1
            An = d["A"][gi % 2][ai % 2]
            self.cp("act", An[0:G, :, 0:G], pa[0:G, 0:256].rearrange("p (h c) -> p h c", c=128)[:, :, 0:G], [pa], [An])
            if not last:
                xi += 1
                Xn = d["X"][gi % 2][xi % 2]
                self.cp("dve", Xn[0:G, :, 0:G], px[0:G, 0:256].rearrange("p (h c) -> p h c", c=128)[:, :, 0:G], [px], [Xn])
                Xc = Xn
            A = An
            pq = self.ps(zp)
            for hh in range(2):
                self.mm(pq[0:G, hh * 128:hh * 128 + G], A[0:G, hh, 0:G], P[0:G, hh, 0:G], True, True, [A, P], [pq])
            pi += 1
            Pn = d["P"][gi % 2][pi % 2]
            self.tt("dve", Pn[0:G, :, 0:G], pq[0:G, 0:256].rearrange("p (h c) -> p h c", c=128)[:, :, 0:G], P[0:G, :, 0:G],
                    ALU.add, [pq, P], [Pn])
            P = Pn
        WT = d["WT"][gi % 2]
        pw = self.ps(zp)
        for hh in range(2):
            self.mm(pw[:, hh * 128:hh * 128 + G], TM[0:G, 0, :], P[0:G, hh, 0:G], True, True, [TM, P], [pw])
        self.cp("act", WT[0:64, 0:G], pw[0:64, 0:G], [pw], [WT])
        self.cp("act", WT[64:128, 0:G], pw[64:128, 128:128 + G], [pw], [WT])
        UT = d["UT"][gi % 2]
        pu = self.ps(zp)
        for hh in range(2):
            if rw:
                rhs = YB[0:G, hh * 64:hh * 64 + 64]
                rr = YB
            else:
                rhs = TM[0:G, 1, hh * 64:hh * 64 + 64]
                rr = TM
            self.mm(pu[0:G, hh * 64:hh * 64 + 64], P[0:G, hh, 0:G], rhs, True, True, [P, rr], [pu])
        self.cp("dve", UT[0:G, :], pu[0:G, 0:128], [pu], [UT])
        return dict(SC=SC, TM=TM, VZ=VZ, WT=WT, UT=UT)

    def delta_chunks(self, d, ctx, G, C, g0, rw, Rf, OA, dec_ap, dec_res):
        SC, TM, VZ, WT, UT = ctx["SC"], ctx["TM"], ctx["VZ"], ctx["WT"], ctx["UT"]
        H, UP, UZ = d["H"], d["UP"], d["UZ"]
        for ci in range(G // C):
            HB = d["HB"] if d["hbi"] % 2 == 0 else d["HB2"]
            HBn = d["HB2"] if d["hbi"] % 2 == 0 else d["HB"]
            d["hbi"] += 1
            cs = slice(ci * C, ci * C + C)
            tc0 = g0 + ci * C
            pU = self.ps()
            self.mm(pU[0:G, 0:128], WT[:, 0:G], HB[:, :], True, True, [WT, HB], [pU])
            if rw:
                self.tt("dve", UP[cs, :], pU[cs, 0:128], UT[cs, :], ALU.add, [pU, UT], [UP])
            else:
                self.tt("dve", UP[cs, :], UT[cs, :], pU[cs, 0:128], ALU.subtract, [pU, UT], [UP])
            pS = self.ps()
            self.mm(pS[:, 0:128], TM[cs, 2, :], UP[cs, :], True, not rw, [TM, UP], [pS])
            if rw:
                self.mm(pS[:, 0:128], TM[cs, 3, :], TM[cs, 1, :], False, True, [TM], [pS])
            dcol = dec_ap(tc0 + C - 1)
            for hh in range(2):
                rows = slice(hh * 64, hh * 64 + 64)
                cols = slice(hh * 64, hh * 64 + 64)
                self.stt(HBn[rows, cols], H[rows, cols], dcol[rows, :], pS[rows, cols], ALU.mult, ALU.add,
                         [H, pS] + dec_res, [HBn])
            for hh in range(2):
                self.cp("pool", UZ[cs, hh, hh * 64:hh * 64 + 64], UP[cs, hh * 64:hh * 64 + 64], [UP], [UZ])
            po = self.ps("a")
            self.mm(po[:, 0:C], HB[:, :], Rf[:, tc0:tc0 + C], True, False, [HB, Rf], [po])
            for hh in range(2):
                lastmm = (not rw) and hh == 1
                self.mm(po[:, 0:C], UZ[0:G, hh, :], SC[hh][0:G, 1, ci * C:ci * C + C], False, lastmm, [UZ, SC[hh]], [po])
            if rw:
                for hh in range(2):
                    self.mm(po[:, 0:C], VZ[0:G, hh, :], SC[hh][0:G, 3, ci * C:ci * C + C], False, hh == 1, [VZ, SC[hh]], [po])
            self.cp("act", OA[:, tc0:tc0 + C], po[:, 0:C], [po], [OA])
            for hh in range(2):
                rows = slice(hh * 64, hh * 64 + 64)
                cols = slice(hh * 64, hh * 64 + 64)
                self.stt(H[rows, cols], H[rows, cols], dcol[rows, :], pS[rows, cols], ALU.mult, ALU.add,
                         [H, pS] + dec_res, [H])

    def silu_mul(self, sc, n, zap, zres, Dn, out_ap, out_res, extra_scale=None, fpool="f"):
        E = sc.get(fpool)
        self.act(E[:, 0:n], zap, AF.Exp, zres, [E], scale=-1.0)
        self.act(E[:, 0:n], E[:, 0:n], AF.Ln, [E], [E], bias=1.0)
        R_ = sc.get(fpool)
        self.act(R_[:, 0:n], E[:, 0:n], AF.Exp, [E], [R_], scale=-1.0)
        self.tt("pool", R_[:, 0:n], R_[:, 0:n], zap, ALU.mult, [R_] + zres, [R_])
        if extra_scale is not None:
            self.stt(out_ap, Dn[:, 0:n], extra_scale, R_[:, 0:n], ALU.mult, ALU.mult, [Dn, R_, self.PV], out_res)
        else:
            self.tt("dve", out_ap, Dn[:, 0:n], R_[:, 0:n], ALU.mult, [Dn, R_], out_res)

    def zip_run(self, fns):
        if len(fns) == 1 or os.environ.get("NOZIP"):
            for f_ in fns:
                f_()
            return
        import threading
        n = len(fns)
        alive = [True] * n
        turn = [0]
        cv = threading.Condition()
        errs = []
        mk = self.mk

        def nxt(i):
            j = (i + 1) % n
            for _ in range(n):
                if alive[j]:
                    return j
                j = (j + 1) % n
            return -1

        def yp(i):
            with cv:
                turn[0] = nxt(i)
                cv.notify_all()
                while turn[0] != i:
                    cv.wait()

        def worker(i):
            with cv:
                while turn[0] != i:
                    cv.wait()
            try:
                mk.tl.hook = lambda: yp(i)
                fns[i]()
            except BaseException as e:
                errs.append(e)
            finally:
                mk.tl.hook = None
                with cv:
                    alive[i] = False
                    turn[0] = nxt(i)
                    cv.notify_all()
        ths = [threading.Thread(target=worker, args=(i,)) for i in range(n)]
        for t in ths:
            t.start()
        for t in ths:
            t.join()
        if errs:
            raise errs[0]

    def phaseA(self, l, pr):
        w3 = self.d["w_in"][l].rearrange("(k p) n -> p k n", p=128)
        a = OFF["a_sh"]
        self.load_w(w3, [(a + pr * 128, 128, 0), (a + 256 + pr * 128, 128, 128), (a + 512 + pr * 128, 128, 256),
                         (a + 768, 128, 384), (OFF["a_z"] + pr * 128, 128, 512)])
        n, G, C = 128, 128, 64
        NL = 2
        with Scope(self) as sc:
            W2B = sc.sb("W2B", [128, 128], BF)
            stg = self.get_ws()
            self.dma(stg[0:64, 0:2, :], self.d["w2"][l][:, pr * 128:(pr + 1) * 128].rearrange("p (a c) -> p a c", c=64), [], [stg])
            self.dma(stg[64:128, 0:2, :], self.d["a2"][l][:, pr * 128:(pr + 1) * 128].rearrange("p (a c) -> p a c", c=64), [], [stg])
            self.cp("pool", W2B[:, :].rearrange("p (a c) -> p a c", c=64), stg[:, 0:2, :], [stg], [W2B])
            LAST = sc.sb("LAST", [128, 4])
            M4 = sc.sb("M4", [128, 4, 128], BF)
            lanes = []
            for i in range(NL):
                ln = dict(i=i, PB=sc.sb("PB%d" % i, [128, 5, n + 1]), BR=sc.sb("BR%d" % i, [128, 2, n], BF),
                          OA=sc.sb("OA%d" % i, [128, n]), BON=sc.sb("BON%d" % i, [128, n]), PCt=sc.sb("PCt%d" % i, [128, n]),
                          AT=sc.sb("AT%d" % i, [128, n], BF), KTt=sc.sb("KTt%d" % i, [128, n], BF),
                          ADF=sc.sb("ADF%d" % i, [128, n], BF), KDF=sc.sb("KDF%d" % i, [128, n], BF),
                          VB=sc.sb("VB%d" % i, [128, n], BF), f="f%d" % i, b="b%d" % i)
                sc.pool(ln["f"], 12, [128, n])
                sc.pool(ln["b"], 5, [128, n], BF)
                lanes.append(ln)
            for m_, nm_ in enumerate(("msu", "miu", "msu", "miu")):
                self.cp("pool", M4[:, m_, :], self.cst(nm_), [self.CST], [M4])
            dd = self.delta_alloc(sc, n, True)
            self.rwkv_seq(l, pr, sc, dd, dict(n=n, G=G, C=C, nt=self.T // n, XT=self.XT, OT=self.OT, x0=0,
                                               prompt=True, o_sh=self.o["sh_p"][l], o_rw=self.o["rw_p"][l]),
                          lanes, LAST, M4, W2B)
            for s in range(NS if self.with_sample else 0):
                self.rwkv_seq(l, pr, sc, dd, dict(n=TS, G=TS, C=TS, nt=1, XT=self.XTs, OT=self.OTs, x0=s * TS,
                                                   prompt=False, s=s, o_sh=self.o["sh_s"][l, s], o_rw=self.o["rw_s"][l, s]),
                              lanes[s % NL:s % NL + 1], LAST, M4, W2B)

    def rwkv_seq(self, l, pr, sc, dd, sq, lanes, LAST, M4, W2B):
        n = sq["n"]
        H, HB = dd["H"], dd["HB"]
        dd["hbi"] = 0
        self.memset("pool", H[:, :], 0.0, [H])
        if sq["prompt"]:
            self.memset("pool", HB[:, :], 0.0, [HB])
            self.memset("pool", LAST[:, :], 0.0, [LAST])
        else:
            s = sq["s"]
            cols_ = (pr * 128, 256 + pr * 128, 512 + pr * 128, 768)
            for j in range(4):
                self.dma(LAST[:, j:j + 1], self.d["st_sh"][l, s, cols_[j]:cols_[j] + 128].rearrange("(p o) -> p o", o=1), [], [LAST])
            for hh in range(2):
                self.dma(H[hh * 64:hh * 64 + 64, hh * 64:hh * 64 + 64], self.d["st_rw"][l, s, 2 * pr + hh], [], [H])
            self.cp("pool", HB[:, :], H[:, :], [H], [HB])
        NL = len(lanes)
        for tt0 in range(0, sq["nt"], NL):
            tts = list(range(tt0, min(sq["nt"], tt0 + NL)))
            for i, tt in enumerate(tts):
                self.rwkv_proj(sq, lanes[i], tt, LAST)
            self.zip_run([(lambda i=i, tt=tt: self.rwkv_pre(l, pr, sc, sq, lanes[i], W2B)) for i, tt in enumerate(tts)])
            G_, C_ = sq["G"], sq["C"]
            ctxs = [None] * len(tts)
            gis = []
            for i in range(len(tts)):
                gis.append(dd["i"])
                dd["i"] += 1

            def mkpre(i):
                ln = lanes[i]

                def run():
                    ctxs[i] = self.delta_pre(dd, gis[i], G_, C_, 0,
                                             masks=lambda hh: (M4[0:G_, :, 0:G_], [M4]),
                                             score_mms=[(ln["AT"], ln["BR"], 2, 0), (ln["KTt"], ln["BR"], 2, 2)],
                                             tm_srcs=[_V(ln["BR"], 0), ln["VB"], ln["ADF"], ln["KDF"]], rw=True)
                return run
            self.zip_run([mkpre(i) for i in range(len(tts))])
            for i, tt in enumerate(tts):
                ln = lanes[i]
                self.delta_chunks(dd, ctxs[i], G_, C_, 0, True, _V(ln["BR"], 1), ln["OA"],
                                  lambda col, ln=ln: ln["PCt"][:, col:col + 1], [ln["PCt"]])
            self.zip_run([(lambda i=i, tt=tt: self.rwkv_post(l, pr, sc, sq, lanes[i], tt)) for i, tt in enumerate(tts)])
        o_sh, o_rw = sq["o_sh"], sq["o_rw"]
        cols = (pr * 128, 256 + pr * 128, 512 + pr * 128, 768)
        for j in range(4 if pr == 0 else 3):
            self.dma(o_sh[cols[j]:cols[j] + 128].rearrange("(p o) -> p o", o=1), LAST[:, j:j + 1], [LAST], [])
        HT = sc.get(lanes[0]["f"])
        for hh in range(2):
            pt = self.ps()
            rows = slice(hh * 64, hh * 64 + 64)
            self.tr(pt[0:64, 0:64], H[rows, hh * 64:hh * 64 + 64], self.cst("ident", 64, rows, hh * 64),
                    [H, self.CST], [pt])
            self.cp("dve", HT[0:64, hh * 64:hh * 64 + 64], pt[0:64, 0:64], [pt], [HT])
        for hh in range(2):
            self.dma(o_rw[2 * pr + hh], HT[0:64, hh * 64:hh * 64 + 64], [HT], [])

    def rwkv_proj(self, sq, ln, tt, LAST):
        n = sq["n"]
        t0 = sq["x0"] + tt * n
        PB = ln["PB"]
        self.cp("pool", PB[:, 0:4, 0], LAST[:, :], [LAST], [PB])
        for j in range(5):
            pp = self.ps()
            self.proj_fm(pp[:, 0:n], j * 128, 128, sq["XT"], t0, n, pp)
            self.cp("act" if j % 2 == 0 else "dve", PB[:, j, 1:n + 1], pp[:, 0:n], [pp], [PB])
        self.cp("pool", LAST[:, :], PB[:, 0:4, n], [PB], [LAST])

    def rwkv_pre(self, l, pr, sc, sq, ln, W2B):
        n, C = sq["n"], sq["C"]
        PB, BR, BON, PCt = ln["PB"], ln["BR"], ln["BON"], ln["PCt"]
        AT, KTt, ADF, KDF, VB = ln["AT"], ln["KTt"], ln["ADF"], ln["KDF"], ln["VB"]
        f = lambda: sc.get(ln["f"])
        b = lambda: sc.get(ln["b"])
        mu_idx = (pr, 2 + pr, 4 + pr, 6)
        for j in range(4):
            Dt = f()
            self.tt("pool", Dt[:, 0:n], PB[:, j, 0:n], PB[:, j, 1:n + 1], ALU.subtract, [PB], [Dt])
            self.stt(PB[:, j, 1:n + 1], Dt[:, 0:n], self.pvc("mu%d" % l, mu_idx[j]), PB[:, j, 1:n + 1], ALU.mult, ALU.add,
                     [Dt, PB, self.PV], [PB])
        r_, k_, v_ = PB[:, 0, 1:n + 1], PB[:, 1, 1:n + 1], PB[:, 2, 1:n + 1]
        E = f()
        self.act(E[0:64, 0:n], PB[0:64, 3, 1:n + 1], AF.Exp, [PB], [E], scale=-2.0)
        self.act(E[0:64, 0:n], E[0:64, 0:n], AF.Ln, [E], [E], bias=1.0)
        self.act(E[0:64, 0:n], E[0:64, 0:n], AF.Exp, [E], [E], scale=-1.0)
        TH = b()
        self.ts("dve", TH[0:64, 0:n], E[0:64, 0:n], 2.0, ALU.mult, [E], [TH], s2=-1.0, op1=ALU.add)
        self.cp("pool", TH[64:128, 0:n], PB[64:128, 3, 1:n + 1], [PB], [TH])
        pw = self.ps()
        self.mm(pw[:, 0:n], W2B[0:64, :], TH[0:64, 0:n], True, True, [W2B, TH], [pw])
        pa = self.ps()
        self.mm(pa[:, 0:n], W2B[64:128, :], TH[64:128, 0:n], True, True, [W2B, TH], [pa])
        SG = f()
        self.act(SG[:, 0:n], pw[:, 0:n], AF.Exp, [pw, self.PD], [SG], bias=self.PD[:, l, pr:pr + 1], scale=-1.0)
        self.act(SG[:, 0:n], SG[:, 0:n], AF.Ln, [SG], [SG], bias=1.0)
        self.act(SG[:, 0:n], SG[:, 0:n], AF.Exp, [SG], [SG], scale=-1.0)
        AA = f()
        self.act(AA[:, 0:n], pa[:, 0:n], AF.Exp, [pa, self.PD], [AA], bias=self.PD[:, l, 2 + pr:3 + pr], scale=-1.0)
        self.act(AA[:, 0:n], AA[:, 0:n], AF.Ln, [AA], [AA], bias=1.0)
        self.act(AA[:, 0:n], AA[:, 0:n], AF.Exp, [AA], [AA], scale=-1.0)
        KKu = f()
        self.ts("dve", KKu[:, 0:n], k_, self.pvc("kk%d" % l, pr), ALU.mult, [PB, self.PV], [KKu])
        KQ = b()
        self.act(KQ[:, 0:n], KKu[:, 0:n], AF.Square, [KKu], [KQ])
        pss = self.ps()
        self.mm(pss[:, 0:n], self.ONESBD, KQ[:, 0:n], True, True, [self.CB, KQ], [pss])
        RN = f()
        self.act(RN[:, 0:n], pss[:, 0:n], AF.Ln, [pss], [RN])
        self.act(RN[:, 0:n], RN[:, 0:n], AF.Exp, [RN], [RN], scale=-0.5)
        KKn = f()
        self.stt(KKn[:, 0:n], RN[:, 0:n], 1e12, KKu[:, 0:n], ALU.min, ALU.mult, [RN, KKu], [KKn])
        KM = f()
        self.ts("dve", KM[:, 0:n], AA[:, 0:n], self.pvc("ka%d" % l, pr), ALU.mult, [AA, self.PV, self.PD], [KM],
                s2=self.PD[:, l, 4 + pr:5 + pr], op1=ALU.add)
        self.tt("pool", KM[:, 0:n], KM[:, 0:n], k_, ALU.mult, [KM, PB], [KM])
        RK = f()
        self.tt("pool", RK[:, 0:n], r_, KM[:, 0:n], ALU.mult, [PB, KM], [RK])
        RKB = b()
        self.ts("dve", RKB[:, 0:n], RK[:, 0:n], self.pvc("rk%d" % l, pr), ALU.mult, [RK, self.PV], [RKB])
        pbs = self.ps()
        self.mm(pbs[:, 0:n], self.ONESBD, RKB[:, 0:n], True, True, [self.CB, RKB], [pbs])
        self.tt("dve", BON[:, 0:n], pbs[:, 0:n], v_, ALU.mult, [pbs, PB], [BON])
        LW = f()
        self.ts("dve", LW[:, 0:n], SG[:, 0:n], -math.exp(-0.5), ALU.mult, [SG], [LW])
        LC = f()
        self.scan(LC[:, 0:n], self.cst("reset", n), LW[:, 0:n], [self.CST, LW], [LC])
        LCX = f()
        self.tt("pool", LCX[:, 0:n], LC[:, 0:n], LW[:, 0:n], ALU.subtract, [LC, LW], [LCX])
        Pm = f()
        self.act(Pm[:, 0:n], LC[:, 0:n], AF.Exp, [LC], [Pm])
        self.act(LCX[:, 0:n], LCX[:, 0:n], AF.Exp, [LCX], [LCX])
        PINV = f()
        self.act(PINV[:, 0:n], LC[:, 0:n], AF.Exp, [LC], [PINV], scale=-1.0)
        KA_ = f()
        self.tt("pool", KA_[:, 0:n], KKn[:, 0:n], AA[:, 0:n], ALU.mult, [KKn, AA], [KA_])
        self.tt("dve", BR[:, 0, 0:n], KKn[:, 0:n], LCX[:, 0:n], ALU.mult, [KKn, LCX], [BR])
        self.tt("dve", BR[:, 1, 0:n], r_, Pm[:, 0:n], ALU.mult, [PB, Pm], [BR])
        self.stt(AT[:, 0:n], KA_[:, 0:n], -1.0, PINV[:, 0:n], ALU.mult, ALU.mult, [KA_, PINV], [AT])
        self.tt("pool", KTt[:, 0:n], KM[:, 0:n], PINV[:, 0:n], ALU.mult, [KM, PINV], [KTt])
        LCD = f()
        nch = n // C
        lc3 = LC[:, 0:n].rearrange("p (c t) -> p c t", t=C)
        self.tt("pool", LCD[:, 0:n].rearrange("p (c t) -> p c t", t=C), lc3[:, :, C - 1:C].to_broadcast([128, nch, C]), lc3,
                ALU.subtract, [LC], [LCD])
        self.act(LCD[:, 0:n], LCD[:, 0:n], AF.Exp, [LCD], [LCD])
        self.stt(ADF[:, 0:n], KA_[:, 0:n], -1.0, LCD[:, 0:n], ALU.mult, ALU.mult, [KA_, LCD], [ADF])
        self.tt("pool", KDF[:, 0:n], KM[:, 0:n], LCD[:, 0:n], ALU.mult, [KM, LCD], [KDF])
        self.cp("pool", VB[:, 0:n], v_, [PB], [VB])
        self.cp("pool", PCt[:, 0:n], Pm[:, 0:n], [Pm], [PCt])

    def rwkv_post(self, l, pr, sc, sq, ln, tt):
        n = sq["n"]
        t0 = sq["x0"] + tt * n
        PB, OA, BON = ln["PB"], ln["OA"], ln["BON"]
        f = lambda: sc.get(ln["f"])
        b = lambda: sc.get(ln["b"])
        z_ = PB[:, 4, 1:n + 1]
        OB = b()
        self.cp("act", OB[:, 0:n], OA[:, 0:n], [OA], [OB])
        OQ = b()
        self.act(OQ[:, 0:n], OA[:, 0:n], AF.Square, [OA], [OQ])
        p1 = self.ps()
        self.mm(p1[:, 0:n], self.ONESBD, OB[:, 0:n], True, True, [self.CB, OB], [p1])
        p2 = self.ps()
        self.mm(p2[:, 0:n], self.ONESBD, OQ[:, 0:n], True, True, [self.CB, OQ], [p2])
        ME = f()
        self.ts("dve", ME[:, 0:n], p1[:, 0:n], 1.0 / 64, ALU.mult, [p1], [ME])
        MS = f()
        self.tt("pool", MS[:, 0:n], ME[:, 0:n], ME[:, 0:n], ALU.mult, [ME], [MS])
        VA_ = f()
        self.stt(VA_[:, 0:n], p2[:, 0:n], 1.0 / 64, MS[:, 0:n], ALU.mult, ALU.subtract, [p2, MS], [VA_])
        self.act(VA_[:, 0:n], VA_[:, 0:n], AF.Ln, [VA_, self.EPS], [VA_], bias=self.EPS[:, 1:2])
        self.act(VA_[:, 0:n], VA_[:, 0:n], AF.Exp, [VA_], [VA_], scale=-0.5)
        Dn = f()
        self.tt("pool", Dn[:, 0:n], OA[:, 0:n], ME[:, 0:n], ALU.subtract, [OA, ME], [Dn])
        self.tt("dve", Dn[:, 0:n], Dn[:, 0:n], VA_[:, 0:n], ALU.mult, [Dn, VA_], [Dn])
        self.act(Dn[:, 0:n], Dn[:, 0:n], AF.Identity, [Dn, self.PV], [Dn], bias=self.pvc("gnb%d" % l, pr),
                 scale=self.pvc("gng%d" % l, pr))
        self.tt("pool", Dn[:, 0:n], Dn[:, 0:n], BON[:, 0:n], ALU.add, [Dn, BON], [Dn])
        self.silu_mul(sc, n, z_, [PB], Dn, sq["OT"][:, pr, t0:t0 + n], [sq["OT"]], fpool=ln["f"])

    def phaseC(self, l, pr):
        w3 = self.d["w_in"][l].rearrange("(k p) n -> p k n", p=128)
        c = OFF["c_qkv"]
        self.memset("pool", self.WB[:, :, 512:576], 0.0, [self.WB])
        self.load_w(w3, [(c + pr * 128, 128, 0), (c + 256 + pr * 128, 128, 128), (c + 512 + pr * 128, 128, 256),
                         (OFF["c_z"] + pr * 128, 128, 384), (OFF["c_b"], 4, 512), (OFF["c_a"], 4, 544)])
        n, G, C = 256, 128, 64
        with Scope(self) as sc:
            PC = sc.sb("PC", [128, 3, n + 3])
            KQ = sc.sb("KQ", [128, 2, n], BF)
            OA = sc.sb("OA", [128, n])
            EGCb = sc.sb("EGCb", [128, n])
            GC = sc.sb("GC", [128, n])
            GCT = [sc.sb("GCT%d" % i_, [128, 64]) for i_ in range(2)]
            DMM = [[sc.sb("DMM%d_%d" % (i, hh), [128, 2, 128]) for hh in range(2)] for i in range(2)]
            NMSU = sc.sb("NMSU", [128, 128])
            self.gd_long = [sc.sb("GL%d" % i, [128, n]) for i in range(4)]
            sc.pool("f", 9, [128, n])
            sc.pool("b", 8, [128, n], BF)
            sc.pool("df", 2, [128, 2, 128])
            self.ts("pool", NMSU[:, :], self.cst("msu"), -1.0, ALU.mult, [self.CST], [NMSU])
            dd = self.delta_alloc(sc, n, False)
            self.gdn_seq(l, pr, sc, dd, dict(n=n, G=G, C=C, nt=self.T // n, XT=self.XT, OT=self.OT, x0=0, prompt=True,
                                              o_cv=self.o["cv_p"][l], o_gd=self.o["gd_p"][l]),
                         PC, KQ, OA, EGCb, GC, GCT, DMM, NMSU)
            for s in range(NS if self.with_sample else 0):
                self.gdn_seq(l, pr, sc, dd, dict(n=TS, G=TS, C=TS, nt=1, XT=self.XTs, OT=self.OTs, x0=s * TS, prompt=False,
                                                  s=s, o_cv=self.o["cv_s"][l, s], o_gd=self.o["gd_s"][l, s]),
                             PC, KQ, OA, EGCb, GC, GCT, DMM, NMSU)

    def gdn_seq(self, l, pr, sc, dd, sq, PC, KQ, OA, EGCb, GC, GCT, DMM, NMSU):
        n, G, C = sq["n"], sq["G"], sq["C"]
        XT, OT = sq["XT"], sq["OT"]
        H, HB = dd["H"], dd["HB"]
        dd["hbi"] = 0
        self.memset("pool", H[:, :], 0.0, [H])
        self.memset("pool", PC[:, :, :], 0.0, [PC])
        if sq["prompt"]:
            self.memset("pool", HB[:, :], 0.0, [HB])
        else:
            s = sq["s"]
            for j in range(3):
                jj = 2 * j + pr
                self.dma(PC[:, j, 0:3], self.d["st_cv"][l, s][:, jj * 128:(jj + 1) * 128].rearrange("i p -> p i"), [], [PC],
                         allow_slow_non_contiguous=True)
            for hh in range(2):
                self.dma(H[hh * 64:hh * 64 + 64, hh * 64:hh * 64 + 64], self.d["st_gd"][l, s, 2 * pr + hh], [], [H])
            self.cp("pool", HB[:, :], H[:, :], [H], [HB])
        for tt in range(sq["nt"]):
            t0 = sq["x0"] + tt * n
            f = lambda: sc.get("f")
            b = lambda: sc.get("b")
            if tt > 0:
                CR = f()
                self.cp("pool", CR[:, 0:9].rearrange("p (j i) -> p j i", i=3), PC[:, :, n:n + 3], [PC], [CR])
                self.cp("pool", PC[:, :, 0:3], CR[:, 0:9].rearrange("p (j i) -> p j i", i=3), [CR], [PC])
            for j in range(3):
                pp = self.ps()
                self.proj_fm(pp[:, 0:n], j * 128, 128, XT, t0, n, pp)
                self.cp("act" if j % 2 == 0 else "dve", PC[:, j, 3:n + 3], pp[:, 0:n], [pp], [PC])
            PZ = self.gd_long[3]
            pp = self.ps()
            self.proj_fm(pp[:, 0:n], 384, 128, XT, t0, n, pp)
            self.cp("act", PZ[:, 0:n], pp[:, 0:n], [pp], [PZ])
            BA = f()
            pp = self.ps()
            self.proj_fm(pp[0:64, 0:n], 512, 64, XT, t0, n, pp)
            self.cp("dve", BA[0:64, 0:n], pp[0:64, 0:n], [pp], [BA])
            Y = []
            for j in range(3):
                jj = 2 * j + pr
                Yj = self.gd_long[j]
                self.ts("dve", Yj[:, 0:n], PC[:, j, 3:n + 3], self.pvc("cw%d" % l, 3 * 6 + jj), ALU.mult, [PC, self.PV], [Yj])
                for i in (2, 1, 0):
                    self.stt(Yj[:, 0:n], PC[:, j, i:i + n], self.pvc("cw%d" % l, i * 6 + jj), Yj[:, 0:n], ALU.mult, ALU.add,
                             [PC, self.PV, Yj], [Yj])
                E = f()
                self.act(E[:, 0:n], Yj[:, 0:n], AF.Exp, [Yj], [E], scale=-1.0)
                self.act(E[:, 0:n], E[:, 0:n], AF.Ln, [E], [E], bias=1.0)
                self.act(E[:, 0:n], E[:, 0:n], AF.Exp, [E], [E], scale=-1.0)
                self.tt("pool", Yj[:, 0:n], Yj[:, 0:n], E[:, 0:n], ALU.mult, [Yj, E], [Yj])
                Y.append(Yj)
            q_, k_, v_ = Y
            for (src_, scl) in ((q_, 0.125), (k_, 1.0)):
                Q2 = b()
                self.act(Q2[:, 0:n], src_[:, 0:n], AF.Square, [src_], [Q2])
                pss = self.ps()
                self.mm(pss[:, 0:n], self.ONESBD, Q2[:, 0:n], True, True, [self.CB, Q2], [pss])
                RN = f()
                self.act(RN[:, 0:n], pss[:, 0:n], AF.Ln, [pss, self.EPS], [RN], bias=self.EPS[:, 3:4])
                self.act(RN[:, 0:n], RN[:, 0:n], AF.Exp, [RN], [RN], scale=-0.5)
                self.stt(src_[:, 0:n], src_[:, 0:n], scl, RN[:, 0:n], ALU.mult, ALU.mult, [src_, RN], [src_])
            BG = f()
            self.act(BG[0:64, 0:n], BA[0:64, 0:n], AF.Exp, [BA], [BG], scale=-1.0)
            self.act(BG[0:64, 0:n], BG[0:64, 0:n], AF.Ln, [BG], [BG], bias=1.0)
            self.act(BG[0:64, 0:n], BG[0:64, 0:n], AF.Exp, [BG], [BG], scale=-1.0)
            SP = f()
            self.act(SP[0:64, 0:n], BA[0:64, 0:n], AF.Exp, [BA, self.PV], [SP], bias=self.pvc("dtb%d" % l, 0, slice(0, 64)))
            self.act(SP[0:64, 0:n], SP[0:64, 0:n], AF.Ln, [SP], [SP], bias=1.0)
            self.ts("dve", SP[0:64, 0:n], SP[0:64, 0:n], self.PD[0:64, l, 7:8], ALU.mult, [SP, self.PD], [SP])
            self.scan(GC[0:64, 0:n], self.cst("reset", n, slice(0, 64)), SP[0:64, 0:n], [self.CST, SP], [GC])
            EGC = f()
            self.act(EGC[0:64, 0:n], GC[0:64, 0:n], AF.Exp, [GC], [EGC])
            EGD = f()
            nch = n // C
            gc3 = GC[0:64, 0:n].rearrange("p (c t) -> p c t", t=C)
            self.tt("pool", EGD[0:64, 0:n].rearrange("p (c t) -> p c t", t=C), gc3[:, :, C - 1:C].to_broadcast([64, nch, C]), gc3,
                    ALU.subtract, [GC], [EGD])
            self.act(EGD[0:64, 0:n], EGD[0:64, 0:n], AF.Exp, [EGD], [EGD])
            pb_ = self.ps()
            self.mm(pb_[:, 0:n], self.cst("selb", 128, slice(0, 64), pr * 128), BG[0:64, 0:n], True, True, [self.CST, BG], [pb_])
            BETb = f()
            self.cp("act", BETb[:, 0:n], pb_[:, 0:n], [pb_], [BETb])
            pg_ = self.ps()
            self.mm(pg_[:, 0:n], self.cst("selg", 128, slice(0, 64), pr * 128), EGC[0:64, 0:n], True, True, [self.CST, EGC], [pg_])
            self.cp("act", EGCb[:, 0:n], pg_[:, 0:n], [pg_], [EGCb])
            pd_ = self.ps()
            self.mm(pd_[:, 0:n], self.cst("selg", 128, slice(0, 64), pr * 128), EGD[0:64, 0:n], True, True, [self.CST, EGD], [pd_])
            KD = b()
            self.tt("dve", KD[:, 0:n], k_[:, 0:n], pd_[:, 0:n], ALU.mult, [k_, pd_], [KD])
            KBf = f()
            self.tt("pool", KBf[:, 0:n], k_[:, 0:n], BETb[:, 0:n], ALU.mult, [k_, BETb], [KBf])
            self.cp("pool", KQ[:, 0, 0:n], KBf[:, 0:n], [KBf], [KQ])
            self.cp("pool", KQ[:, 1, 0:n], q_[:, 0:n], [q_], [KQ])
            QG = b()
            self.tt("dve", QG[:, 0:n], q_[:, 0:n], EGCb[:, 0:n], ALU.mult, [q_, EGCb], [QG])
            KBG = b()
            self.tt("dve", KBG[:, 0:n], KBf[:, 0:n], EGCb[:, 0:n], ALU.mult, [KBf, EGCb], [KBG])
            VBt = b()
            self.tt("pool", VBt[:, 0:n], v_[:, 0:n], BETb[:, 0:n], ALU.mult, [v_, BETb], [VBt])
            Kf = b()
            self.cp("pool", Kf[:, 0:n], k_[:, 0:n], [k_], [Kf])
            ng = n // G
            gis = []
            for g in range(ng):
                gis.append(dd["i"])
                dd["i"] += 1
            ctxs = [None] * ng

            def mkpre(g):
                g0 = g * G
                gi = gis[g]
                DMg = DMM[gi % 2]
                GCTg = GCT[gi % 2]

                def run():
                    pT = self.ps("z%d" % (gi % 2))
                    self.tr(pT[0:G, 0:64], GC[0:64, g0:g0 + G], self.cst("ident", 64, slice(0, 64)), [GC, self.CST], [pT])
                    self.cp("dve", GCTg[0:G, :], pT[0:G, 0:64], [pT], [GCTg])
                    pR = self.ps("z%d" % (gi % 2))
                    for hh in range(2):
                        h = 2 * pr + hh
                        self.mm(pR[0:G, hh * 128:hh * 128 + G], self.cst("selh", G, slice(0, 64), h * 128), GC[0:64, g0:g0 + G],
                                True, True, [self.CST, GC], [pR])
                    DF = sc.get("df")
                    for hh in range(2):
                        h = 2 * pr + hh
                        self.stt(DF[0:G, hh, 0:G], pR[0:G, hh * 128:hh * 128 + G], GCTg[0:G, 32 + h:33 + h],
                                 self.cst("miu", G, slice(0, G)), ALU.subtract, ALU.mult, [pR, GCTg, self.CST], [DF])
                    self.act(DF[0:G, :, 0:G], DF[0:G, :, 0:G], AF.Exp, [DF], [DF])
                    for hh in range(2):
                        self.tt("pool", DMg[hh][0:G, 0, 0:G], DF[0:G, hh, 0:G], NMSU[0:G, 0:G], ALU.mult, [DF, NMSU], [DMg[hh]])
                        self.tt("pool", DMg[hh][0:G, 1, 0:G], DF[0:G, hh, 0:G], self.cst("miu", G, slice(0, G)), ALU.mult,
                                [DF, self.CST], [DMg[hh]])
                    ctxs[g] = self.delta_pre(dd, gi, G, C, g0,
                                             masks=lambda hh: (DMg[hh][0:G, :, 0:G], [DMg[hh]]),
                                             score_mms=[(Kf, KQ, 2, 0)], tm_srcs=[KBG, VBt, KD], rw=False)
                return run
            self.zip_run([mkpre(g) for g in range(ng)])
            for g in range(ng):
                self.delta_chunks(dd, ctxs[g], G, C, g * G, False, QG, OA,
                                  lambda col: EGCb[:, col:col + 1], [EGCb])
            OQ = b()
            self.act(OQ[:, 0:n], OA[:, 0:n], AF.Square, [OA], [OQ])
            p2 = self.ps()
            self.mm(p2[:, 0:n], self.ONESBD, OQ[:, 0:n], True, True, [self.CB, OQ], [p2])
            RS = f()
            self.act(RS[:, 0:n], p2[:, 0:n], AF.Ln, [p2, self.EPS], [RS], bias=self.EPS[:, 2:3], scale=1.0 / 64)
            self.act(RS[:, 0:n], RS[:, 0:n], AF.Exp, [RS], [RS], scale=-0.5)
            Dn = f()
            self.tt("dve", Dn[:, 0:n], OA[:, 0:n], RS[:, 0:n], ALU.mult, [OA, RS], [Dn])
            self.silu_mul(sc, n, PZ[:, 0:n], [PZ], Dn, OT[:, 6 + pr, t0:t0 + n], [OT], extra_scale=self.pvc("gnorm%d" % l))
        o_cv, o_gd = sq["o_cv"], sq["o_gd"]
        for j in range(3):
            jj = 2 * j + pr
            self.dma(o_cv[:, jj * 128:(jj + 1) * 128].rearrange("i p -> p i"), PC[:, j, n:n + 3], [PC], [],
                     allow_slow_non_contiguous=True)
        for hh in range(2):
            self.dma(o_gd[2 * pr + hh], H[hh * 64:hh * 64 + 64, hh * 64:hh * 64 + 64], [H], [])

    def layer(self, l):
        if not getattr(self, "ot_zeroed", False):
            self.memset("pool", self.OT[:, :, :], 0.0, [self.OT])
            self.memset("pool", self.OTs[:, :, :], 0.0, [self.OTs])
            self.ot_zeroed = True
        if "A" in self.phases:
            for pr in range(2):
                self.phaseA(l, pr)
        if "C" in self.phases:
            for pr in range(2):
                self.phaseC(l, pr)
        if "B" in self.phases:
            self.phaseB(l)
        self.phaseD(l)


_CACHE = {}


def get_nc(T, PAST, **kw):
    key = (T, PAST, tuple(sorted(kw.items())))
    if key not in _CACHE:
        kb = KB(T, PAST, **kw)
        _CACHE[key] = (kb.build(), kb)
    return _CACHE[key]


def kernel(**inp):
    inp = {k: np.asarray(v) for k, v in inp.items()}
    B, T, _ = inp["x_prompt"].shape
    PAST = inp["cache_fox_k"].shape[2]
    nc, kb = get_nc(T, PAST, with_sample=not os.environ.get("NOSAMPLE"))
    pv = make_pv(inp)
    cst = make_cst()
    in_maps = []
    for c in range(NCORE):
        m = {"xp": np.ascontiguousarray(inp["x_prompt"][c]), "w_in": inp["w_in"], "w_out": inp["w_out"],
             "rwkv_w2": inp["rwkv_w2"], "rwkv_a2": inp["rwkv_a2"], "pv": pv, "cst": cst}
        if kb.with_sample:
            ss = slice(c * NS, (c + 1) * NS)
            m["xs"] = np.ascontiguousarray(inp["x_sample"][ss]).reshape(NS * TS, D_MODEL)
            m["ckT"] = np.ascontiguousarray(inp["cache_fox_k"][:, ss].transpose(0, 1, 3, 4, 2))
            m["cv"] = np.ascontiguousarray(inp["cache_fox_v"][:, ss]).reshape(L, NS, PAST, 512)
            m["clf"] = np.ascontiguousarray(inp["cache_fox_logf"][:, ss].transpose(0, 1, 3, 2))
            m["st_sh"] = np.ascontiguousarray(inp["state_rwkv_shift"][:, ss])
            m["st_rw"] = np.ascontiguousarray(inp["state_rwkv_wkv"][:, ss].transpose(0, 1, 2, 4, 3))
            m["st_cv"] = np.ascontiguousarray(inp["state_gdn_conv"][:, ss])
            m["st_gd"] = np.ascontiguousarray(inp["state_gdn_wkv"][:, ss])
        in_maps.append(m)
    if os.environ.get("KTRACE"):
        res = run_bass_kernel_spmd(nc, in_maps, core_ids=list(range(NCORE)), trace=True)
        print("EXEC_TIME_NS", res.exec_time_ns)
    else:
        res = run_bass_kernel_spmd(nc, in_maps, core_ids=list(range(NCORE)))
    R = res.results
    SB = NCORE * NS

    def gp(name, shape_tail, axis_layer=True):
        if name not in R[0]:
            return None
        return np.stack([np.asarray(R[c][name]) for c in range(NCORE)], axis=1)

    y_p = np.stack([R[c]["y_p"] for c in range(NCORE)])
    fk_p = gp("fk_p", None).reshape(L, NCORE, T, 8, 64)
    fv_p = gp("fv_p", None).reshape(L, NCORE, T, 8, 64)
    fl_p = gp("fl_p", None)
    sh_p = gp("sh_p", None)
    rw_p = gp("rw_p", None)
    cv_p = gp("cv_p", None)
    gd_p = gp("gd_p", None)

    def gs(name, tail):
        if name not in R[0]:
            return np.zeros((L, SB) + tail, np.float32)
        a = np.stack([np.asarray(R[c][name]) for c in range(NCORE)], axis=1)
        return a.reshape((L, SB) + tail)
    if "y_s" in R[0]:
        y_s = np.stack([R[c]["y_s"] for c in range(NCORE)]).reshape(SB, TS, D_MODEL)
    else:
        y_s = np.zeros((SB, TS, D_MODEL), np.float32)
    fk_s = gs("fk_s", (TS, 8, 64))
    fv_s = gs("fv_s", (TS, 8, 64))
    fl_s = gs("fl_s", (TS, 8))
    sh_s = gs("sh_s", (896,))
    rw_s = gs("rw_s", (4, 64, 64))
    cv_s = gs("cv_s", (3, 768))
    gd_s = gs("gd_s", (4, 64, 64))
    return (y_p, y_s, fk_p, fv_p, fl_p, sh_p, rw_p, cv_p, gd_p, fk_s, fv_s, fl_s, sh_s, rw_s, cv_s, gd_s)
```

```python
import math
from contextlib import ExitStack
import numpy as np
import concourse.bass as bass
import concourse.mybir as mybir
from concourse.bass_utils import run_bass_kernel_spmd

F32 = mybir.dt.float32
BF = mybir.dt.bfloat16
AF = mybir.ActivationFunctionType
ALU = mybir.AluOpType

D_MODEL = 1024
L = 4
N_IN = 4240
OFF = dict(a_sh=0, a_z=896, b_q=1152, b_k=1664, b_v=2176, b_f=2688, b_z=2696,
           c_qkv=3208, c_b=3976, c_a=3980, c_z=3984)
ALPHA = (2 * L) ** 0.25
LN_EPS = 1e-5
RWKV_GN_EPS = 64e-5
GDN_NORM_EPS = 1e-6
L2_EPS = 1e-6
import os
NCORE = int(os.environ.get("KCORES", "8"))
BSTAGE = int(os.environ.get("BSTAGE", "9"))
NS = 4
TS = 16


def pv_layout():
    off = {}
    n = 0

    def add(name, w):
        nonlocal n
        off[name] = n
        n += w
    add("ln_in_g", 8)
    add("ln_in_b", 8)
    for l in range(L):
        for nm, w in (("lng", 8), ("lnb", 8), ("mu", 7), ("w0", 2), ("a0", 2), ("kk", 2), ("ka", 2),
                      ("rk", 2), ("gng", 2), ("gnb", 2), ("cw", 24), ("gnorm", 1), ("bf", 1),
                      ("alog", 1), ("dtb", 1)):
            add("%s%d" % (nm, l), w)
    return off, n


PVO, NPV = pv_layout()


def cst_layout():
    off = {}
    n = 0
    for nm, w in (("ident", 128), ("onesbd", 128), ("ones", 128), ("tri", 128), ("msu", 128), ("miu", 128),
                  ("reset", 256), ("selb", 256), ("selg", 256), ("selh", 512)):
        off[nm] = n
        n += w
    return off, n


CSO, NCST = cst_layout()


def make_cst():
    c = np.zeros((128, NCST), np.float32)
    p = np.arange(128)[:, None]
    f = np.arange(128)[None, :]
    c[:, CSO["ident"]:CSO["ident"] + 128] = (p == f)
    c[:, CSO["onesbd"]:CSO["onesbd"] + 128] = (p // 64 == f // 64)
    c[:, CSO["ones"]:CSO["ones"] + 128] = 1.0
    c[:, CSO["tri"]:CSO["tri"] + 128] = (p <= f)
    c[:, CSO["msu"]:CSO["msu"] + 128] = (p < f) & (p // 64 == f // 64)
    c[:, CSO["miu"]:CSO["miu"] + 128] = (p <= f) & (p // 64 == f // 64)
    r = np.ones(256, np.float32)
    r[::64] = 0
    c[:, CSO["reset"]:CSO["reset"] + 256] = r[None, :]
    selb = np.zeros((128, 2, 128), np.float32)
    selg = np.zeros((128, 2, 128), np.float32)
    for pr in range(2):
        for hh in range(2):
            selb[2 * pr + hh, pr, hh * 64:(hh + 1) * 64] = 1
            selg[32 + 2 * pr + hh, pr, hh * 64:(hh + 1) * 64] = 1
    c[:, CSO["selb"]:CSO["selb"] + 256] = selb.reshape(128, 256)
    c[:, CSO["selg"]:CSO["selg"] + 256] = selg.reshape(128, 256)
    selh = np.zeros((128, 4, 128), np.float32)
    for h in range(4):
        selh[32 + h, h, :] = 1
    c[:, CSO["selh"]:CSO["selh"] + 512] = selh.reshape(128, 512)
    return c


def make_pv(inp):
    pv = np.zeros((128, NPV), np.float32)

    def put(name, vec, w):
        pv[:, PVO[name]:PVO[name] + w] = np.asarray(vec, np.float32).reshape(w, 128).T
    put("ln_in_g", inp["ln_in_g"], 8)
    put("ln_in_b", inp["ln_in_b"], 8)
    for l in range(L):
        put("lng%d" % l, inp["ln_post_g"][l], 8)
        put("lnb%d" % l, inp["ln_post_b"][l], 8)
        put("mu%d" % l, inp["rwkv_mu"][l], 7)
        put("w0%d" % l, inp["rwkv_w0"][l], 2)
        put("a0%d" % l, inp["rwkv_a0"][l], 2)
        put("kk%d" % l, inp["rwkv_k_k"][l], 2)
        put("ka%d" % l, inp["rwkv_k_a"][l], 2)
        put("rk%d" % l, np.asarray(inp["rwkv_r_k"][l]).reshape(256), 2)
        put("gng%d" % l, inp["rwkv_gn_g"][l], 2)
        put("gnb%d" % l, inp["rwkv_gn_b"][l], 2)
        cw = np.asarray(inp["gdn_conv_w"][l], np.float32)
        for i in range(4):
            pv[:, PVO["cw%d" % l] + i * 6:PVO["cw%d" % l] + i * 6 + 6] = cw[i].reshape(6, 128).T
        gn = np.asarray(inp["gdn_norm_g"][l], np.float32)
        pv[:, PVO["gnorm%d" % l]] = np.concatenate([gn, gn])
        bfv = np.asarray(inp["fox_b_f"][l], np.float32)
        for g in (0, 32, 64):
            pv[g:g + 8, PVO["bf%d" % l]] = bfv
        pv[32:36, PVO["alog%d" % l]] = np.asarray(inp["gdn_a_log"][l], np.float32)
        pv[32:36, PVO["dtb%d" % l]] = np.asarray(inp["gdn_dt_bias"][l], np.float32)
    return pv


class Res:
    __slots__ = ("w", "r", "psum")

    def __init__(self):
        self.w = None
        self.r = {}
        self.psum = False


class Eng:
    def __init__(self, name, unit):
        self.name = name
        self.unit = unit
        self.n = 0
        self.known = {}
        self.hist = {}
        self.ops = []
        self.sem = None


class MK:
    NDQ = 12
    CE = ("pe", "act", "dve", "pool", "sp")

    def __init__(self):
        self.E = {}
        for nm in self.CE:
            self.E[nm] = Eng(nm, 1)
        for i in range(self.NDQ):
            self.E["dq%d" % i] = Eng("dq%d" % i, 16)
        self.dq_rr = 0
        self.n_wait = 0
        self.n_ins = 0
        import threading
        self.tl = threading.local()

    def _need(self, W, reads, writes, is_dma=False):
        toks = {}

        def add(tok, same_ok):
            if tok is None:
                return
            e, n = tok
            if same_ok and e is W and W.name == "pe" and not is_dma:
                return
            if W.known.get(e.name, 0) >= n:
                return
            if toks.get(e.name, (None, 0))[1] < n:
                toks[e.name] = (e, n)

        for r in reads:
            add(r.w, False)
            if r.psum:
                for t in r.r.values():
                    if t[0] is not W:
                        add(t, True)
        for r in writes:
            add(r.w, True)
            for t in r.r.values():
                add(t, True)
        return list(toks.values())

    def _merge(self, W, e, n):
        snap = e.hist.get(n)
        if snap:
            for k, v in snap.items():
                if W.known.get(k, 0) < v:
                    W.known[k] = v
        if W.known.get(e.name, 0) < n:
            W.known[e.name] = n

    def _record(self, te, tn, reads, writes, snap):
        te.hist[tn] = snap
        tok = (te, tn)
        for r in reads:
            r.r[te.name] = tok
        for r in writes:
            r.w = tok
            r.r = {}

    def op(self, eng, fn, reads=(), writes=()):
        W = self.E[eng]
        waits = self._need(W, reads, writes)
        for e, n in waits:
            self._merge(W, e, n)
        W.n += 1
        n = W.n
        snap = dict(W.known)
        snap[W.name] = n
        self._record(W, n, reads, writes, snap)
        W.ops.append((fn, [(e, n_ * e.unit) for e, n_ in waits], W))
        self.n_wait += len(waits)
        self.n_ins += 1
        hk = getattr(self.tl, "hook", None)
        if hk:
            hk()

    def dma(self, out, in_, reads=(), writes=(), queue="sp", **kw):
        Q = self.E[queue]
        k = self.dq_rr
        self.dq_rr = (self.dq_rr + 1) % self.NDQ
        Dq = self.E["dq%d" % k]
        waits = self._need(Q, reads, writes, is_dma=True)
        if Dq.n > 0 and Q.known.get(Dq.name, 0) < Dq.n:
            waits = [w for w in waits if w[0] is not Dq] + [(Dq, Dq.n)]
        for e, n in waits:
            self._merge(Q, e, n)
        Dq.n += 1
        n = Dq.n
        snap = dict(Q.known)
        snap[Dq.name] = n
        self._record(Dq, n, reads, writes, snap)

        def fn(eng, out=out, in_=in_, kw=kw):
            return eng.dma_start(out=out, in_=in_, **kw)
        Q.ops.append((fn, [(e, n_ * e.unit) for e, n_ in waits], Dq))
        self.n_wait += len(waits)
        self.n_ins += 1
        hk = getattr(self.tl, "hook", None)
        if hk:
            hk()

    def barrier(self):
        for nm in self.CE:
            W = self.E[nm]
            waits = []
            for e in self.E.values():
                if e is W or e.n == 0:
                    continue
                if W.known.get(e.name, 0) < e.n:
                    waits.append((e, e.n))
            for e, n in waits:
                self._merge(W, e, n)
            W.ops.append((None, [(e, n * e.unit) for e, n in waits], None))

    def runner(self, sems):
        for e in self.E.values():
            e.sem = sems[e.name]

        def run(engobj, E):
            fuse = E.name in ("act", "dve", "pool")
            for fn, waits, inc_e in E.ops:
                if fn is None or not fuse or not waits:
                    for e, v in waits:
                        engobj.wait_ge(e.sem, v)
                    if fn is None:
                        continue
                    fn(engobj).then_inc(inc_e.sem, inc_e.unit)
                else:
                    for e, v in waits[:-1]:
                        engobj.wait_ge(e.sem, v)
                    ins = fn(engobj)
                    ins._wait_ge(waits[-1][0].sem, waits[-1][1])
                    ins.then_inc(inc_e.sem, inc_e.unit)
        return run


class Tile:
    def __init__(self, t):
        self.t = t
        self.r = Res()

    def __getitem__(self, k):
        return self.t[k]


class _V:
    def __init__(self, t, i):
        self.t = t
        self.i = i
        self.r = t.r

    def __getitem__(self, k):
        return self.t.t[(k[0], self.i) + tuple(k[1:])]


class _B:
    def __init__(self, t):
        self.t = t
        self.r = t.r
        self.ap = t.t[:, :].bitcast(BF)

    def __getitem__(self, k):
        return self.ap[k]


class Scope:
    def __init__(self, kb):
        self.kb = kb
        self.st = ExitStack()
        self.pools = {}

    def __enter__(self):
        self.st.__enter__()
        return self

    def __exit__(self, *a):
        self.kb.mk.barrier()
        return self.st.__exit__(*a)

    def sb(self, name, shape, dt=F32):
        self.kb.uid += 1
        return Tile(self.st.enter_context(self.kb.nc.sbuf_tensor("%s_%d" % (name, self.kb.uid), list(shape), dt)))

    def pool(self, name, n, shape, dt=F32):
        self.pools[name] = [[self.sb(name + str(i), shape, dt) for i in range(n)], 0]

    def get(self, name):
        p = self.pools[name]
        t = p[0][p[1] % len(p[0])]
        p[1] += 1
        return t


class KB:
    def __init__(self, T, PAST, with_sample=True, nlayers=L):
        self.T = T
        self.PAST = PAST
        self.with_sample = with_sample
        self.nl = nlayers
        self.nc = bass.Bass("TRN2", target_bir_lowering=False)
        self.mk = MK()
        self.st = ExitStack()
        self.uid = 0
        self.psp = {}
        self.phases = "ABCD"

    def sb(self, name, shape, dt=F32):
        return Tile(self.st.enter_context(self.nc.sbuf_tensor(name, list(shape), dt)))

    def din(self, name, shape, dt=F32):
        return self.nc.dram_tensor(name, list(shape), dt, kind="ExternalInput").ap()

    def dout(self, name, shape, dt=F32):
        return self.nc.dram_tensor(name, list(shape), dt, kind="ExternalOutput").ap()

    def psb(self, pool):
        t = self.ps(pool)
        v = _B(t)
        return v

    def ps(self, pool="g"):
        p = self.psp[pool]
        t = p[0][p[1] % len(p[0])]
        p[1] += 1
        return t

    @staticmethod
    def _rs(xs):
        return [x.r if isinstance(x, (Tile, _V, _B)) else x for x in xs]

    def tt(self, eng, out, in0, in1, op, R, W):
        self.mk.op(eng, lambda e: e.tensor_tensor(out=out, in0=in0, in1=in1, op=op), self._rs(R), self._rs(W))

    def ts(self, eng, out, in0, s1, op0, R, W, s2=None, op1=None):
        if op1 is None:
            self.mk.op(eng, lambda e: e.tensor_scalar(out=out, in0=in0, scalar1=s1, scalar2=None, op0=op0),
                       self._rs(R), self._rs(W))
        else:
            self.mk.op(eng, lambda e: e.tensor_scalar(out=out, in0=in0, scalar1=s1, scalar2=s2, op0=op0, op1=op1),
                       self._rs(R), self._rs(W))

    def stt(self, out, in0, scalar, in1, op0, op1, R, W):
        self.mk.op("dve", lambda e: e.scalar_tensor_tensor(out=out, in0=in0, scalar=scalar, in1=in1, op0=op0, op1=op1),
                   self._rs(R), self._rs(W))

    def cp(self, eng, out, in_, R, W):
        if eng == "act":
            self.mk.op(eng, lambda e: e.copy(out=out, in_=in_), self._rs(R), self._rs(W))
        else:
            self.mk.op(eng, lambda e: e.tensor_copy(out=out, in_=in_), self._rs(R), self._rs(W))

    def act(self, out, in_, func, R, W, bias=0.0, scale=1.0):
        self.mk.op("act", lambda e: e.activation(out=out, in_=in_, func=func, bias=bias, scale=scale),
                   self._rs(R), self._rs(W))

    def mm(self, out, lhsT, rhs, start, stop, R, W):
        self.mk.op("pe", lambda e: e.matmul(out, lhsT=lhsT, rhs=rhs, start=start, stop=stop),
                   self._rs(R), self._rs(W))

    def tr(self, out, in_, ident, R, W):
        self.mk.op("pe", lambda e: e.transpose(out, in_, ident), self._rs(R), self._rs(W))

    def recip(self, out, in_, R, W):
        self.mk.op("dve", lambda e: e.reciprocal(out=out, in_=in_), self._rs(R), self._rs(W))

    def memset(self, eng, ap, val, W):
        self.mk.op(eng, lambda e: e.memset(ap, val), [], self._rs(W))

    def scan(self, out, d0, d1, R, W):
        self.mk.op("dve", lambda e: e.tensor_tensor_scan(out=out, data0=d0, data1=d1, initial=0.0,
                                                           op0=ALU.mult, op1=ALU.add), self._rs(R), self._rs(W))

    def dma(self, out, in_, R, W, **kw):
        self.mk.dma(out, in_, self._rs(R), self._rs(W), **kw)

    def pvc(self, name, c=0, rows=slice(0, 128)):
        o = PVO[name] + c
        return self.PV[rows, o:o + 1]

    def cst(self, name, w=128, rows=slice(0, 128), c0=0):
        o = CSO[name] + c0
        return self.CST[rows, o:o + w]

    def load_w(self, src3, col_ranges):
        for (sc, w, dc) in col_ranges:
            o = 0
            while o < w:
                ww = min(64, w - o)
                stg = self.get_ws()
                self.dma(stg[:, :, 0:ww], src3[:, :, sc + o:sc + o + ww], [], [stg])
                self.cp("pool", self.WB[:, :, dc + o:dc + o + ww], stg[:, :, 0:ww], [stg], [self.WB])
                o += ww

    def get_ws(self):
        t = self.WS[self.ws_i % len(self.WS)]
        self.ws_i += 1
        return t

    def proj_fm(self, ps_ap, wcol, ncols, xt, t0, n, pst):
        for k in range(8):
            self.mm(ps_ap, self.WB[:, k, wcol:wcol + ncols], xt[:, k, t0:t0 + n], k == 0, k == 7,
                    [self.WB, xt], [pst])

    def ln_fm(self, sc, Vt, n, gname, bname, out_bf=None, out_f32=None):
        VB = sc.get("lnvb")
        VQ = sc.get("lnvq")
        self.cp("act", VB[:, :, 0:n], Vt[:, :, 0:n], [Vt], [VB])
        self.act(VQ[:, :, 0:n], Vt[:, :, 0:n], AF.Square, [Vt], [VQ])
        p1 = self.ps()
        p2 = self.ps()
        for c in range(8):
            self.mm(p1[:, 0:n], self.ONESB[:, :], VB[:, c, 0:n], c == 0, c == 7, [self.CB, VB], [p1])
        for c in range(8):
            self.mm(p2[:, 0:n], self.ONESB[:, :], VQ[:, c, 0:n], c == 0, c == 7, [self.CB, VQ], [p2])
        ME = sc.get("lnt")
        MS = sc.get("lnt")
        VA = sc.get("lnt")
        RS = sc.get("lnt")
        self.ts("dve", ME[:, 0:n], p1[:, 0:n], 1.0 / D_MODEL, ALU.mult, [p1], [ME])
        self.tt("pool", MS[:, 0:n], ME[:, 0:n], ME[:, 0:n], ALU.mult, [ME], [MS])
        self.stt(VA[:, 0:n], p2[:, 0:n], 1.0 / D_MODEL, MS[:, 0:n], ALU.mult, ALU.subtract, [p2, MS], [VA])
        self.act(RS[:, 0:n], VA[:, 0:n], AF.Ln, [VA], [RS], bias=self.EPS[:, 0:1], scale=1.0)
        self.act(RS[:, 0:n], RS[:, 0:n], AF.Exp, [RS], [RS], scale=-0.5)
        for c in range(8):
            Dd = sc.get("lnd")
            self.tt("pool", Dd[:, 0:n], Vt[:, c, 0:n], ME[:, 0:n], ALU.subtract, [Vt, ME], [Dd])
            self.tt("dve", Dd[:, 0:n], Dd[:, 0:n], RS[:, 0:n], ALU.mult, [Dd, RS], [Dd])
            if out_bf is not None:
                ap, tl = out_bf(c)
                self.act(ap, Dd[:, 0:n], AF.Identity, [Dd, self.PV], [tl],
                         bias=self.pvc(bname, c), scale=self.pvc(gname, c))
            if out_f32 is not None:
                ap, tl = out_f32(c)
                self.act(ap, Dd[:, 0:n], AF.Identity, [Dd, self.PV], [tl],
                         bias=self.pvc(bname, c), scale=self.pvc(gname, c))

    def build(self):
        nc, mk = self.nc, self.mk
        T, PAST = self.T, self.PAST
        NB = T // 128
        d = {}
        d["xp"] = self.din("xp", [T, D_MODEL])
        d["w_in"] = self.din("w_in", [L, D_MODEL, N_IN])
        d["w_out"] = self.din("w_out", [L, D_MODEL, D_MODEL])
        d["w2"] = self.din("rwkv_w2", [L, 64, 256])
        d["a2"] = self.din("rwkv_a2", [L, 64, 256])
        d["pv"] = self.din("pv", [128, NPV])
        d["cst"] = self.din("cst", [128, NCST])
        o = {}
        o["y_p"] = self.dout("y_p", [T, D_MODEL])
        o["fk_p"] = self.dout("fk_p", [L, T, 512])
        o["fv_p"] = self.dout("fv_p", [L, T, 512])
        o["fl_p"] = self.dout("fl_p", [L, T, 8])
        o["sh_p"] = self.dout("sh_p", [L, 896])
        o["rw_p"] = self.dout("rw_p", [L, 4, 64, 64])
        o["cv_p"] = self.dout("cv_p", [L, 3, 768])
        o["gd_p"] = self.dout("gd_p", [L, 4, 64, 64])
        if self.with_sample:
            d["xs"] = self.din("xs", [NS * TS, D_MODEL])
            d["ckT"] = self.din("ckT", [L, NS, 8, 64, PAST])
            d["cv"] = self.din("cv", [L, NS, PAST, 512])
            d["clf"] = self.din("clf", [L, NS, 8, PAST])
            d["st_sh"] = self.din("st_sh", [L, NS, 896])
            d["st_rw"] = self.din("st_rw", [L, NS, 4, 64, 64])
            d["st_cv"] = self.din("st_cv", [L, NS, 3, 768])
            d["st_gd"] = self.din("st_gd", [L, NS, 4, 64, 64])
            o["y_s"] = self.dout("y_s", [NS * TS, D_MODEL])
            o["fk_s"] = self.dout("fk_s", [L, NS, TS, 512])
            o["fv_s"] = self.dout("fv_s", [L, NS, TS, 512])
            o["fl_s"] = self.dout("fl_s", [L, NS, TS, 8])
            o["sh_s"] = self.dout("sh_s", [L, NS, 896])
            o["rw_s"] = self.dout("rw_s", [L, NS, 4, 64, 64])
            o["cv_s"] = self.dout("cv_s", [L, NS, 3, 768])
            o["gd_s"] = self.dout("gd_s", [L, NS, 4, 64, 64])
        self.d, self.o = d, o

        with self.st:
            self.XT = self.sb("XT", [128, 8, T], BF)
            self.OT = self.sb("OT", [128, 8, T], BF)
            self.XTs = self.sb("XTs", [128, 8, NS * TS], BF)
            self.OTs = self.sb("OTs", [128, 8, NS * TS], BF)
            self.PV = self.sb("PV", [128, NPV])
            self.CST = self.sb("CST", [128, NCST])
            self.CB = self.sb("CB", [128, 6, 128], BF)
            self.WB = self.sb("WB", [128, 8, 1024], BF)
            self.WS = [self.sb("WS%d" % i, [128, 8, 64]) for i in range(2)]
            self.ws_i = 0
            self.EPS = self.sb("EPS", [128, 4])
            self.MNEG = self.sb("MNEG", [128, 128], BF)
            self.PD = self.sb("PD", [128, L, 12])
            bk = [Tile(self.st.enter_context(nc.psum_tensor("PB%d" % i, [128, 512], F32))) for i in range(8)]
            for t_ in bk:
                t_.r.psum = True
            self.psp = {"g": [bk[0:5], 0], "a": [bk[5:7], 0], "z0": [[bk[0], bk[1], bk[7]], 0], "z1": [[bk[2], bk[3], bk[4]], 0]}

            self.IDB = self.CB[:, 0, :]
            self.ONESBD = self.CB[:, 1, :]
            self.ONESB = self.CB[:, 2, :]
            self.TRIB = self.CB[:, 3, :]
            self.MSUB = self.CB[:, 4, :]
            self.MIUB = self.CB[:, 5, :]

            self.dma(self.PV[:, :], d["pv"], [], [self.PV])
            self.dma(self.CST[:, :], d["cst"], [], [self.CST])
            for i, nm in enumerate(("ident", "onesbd", "ones", "tri", "msu", "miu")):
                self.cp("pool", self.CB[:, i, :], self.cst(nm), [self.CST], [self.CB])
            self.ts("pool", self.MNEG[:, :], self.cst("tri"), -1.0, ALU.add, [self.CST], [self.MNEG], s2=30000.0, op1=ALU.mult)
            self.memset("pool", self.EPS[:, 0:1], LN_EPS, [self.EPS])
            self.memset("pool", self.EPS[:, 1:2], RWKV_GN_EPS, [self.EPS])
            self.memset("pool", self.EPS[:, 2:3], GDN_NORM_EPS, [self.EPS])
            self.memset("pool", self.EPS[:, 3:4], L2_EPS, [self.EPS])
            for l in range(self.nl):
                for c in range(2):
                    self.ts("pool", self.PD[:, l, c:c + 1], self.pvc("w0%d" % l, c), -1.0, ALU.mult, [self.PV], [self.PD])
                    self.ts("pool", self.PD[:, l, 2 + c:3 + c], self.pvc("a0%d" % l, c), -1.0, ALU.mult, [self.PV], [self.PD])
                    self.ts("pool", self.PD[:, l, 4 + c:5 + c], self.pvc("ka%d" % l, c), -1.0, ALU.mult, [self.PV], [self.PD],
                            s2=1.0, op1=ALU.add)
                self.ts("pool", self.PD[:, l, 6:7], self.pvc("bf%d" % l), -1.0, ALU.mult, [self.PV], [self.PD])
                self.act(self.PD[:, l, 7:8], self.pvc("alog%d" % l), AF.Exp, [self.PV], [self.PD])
                self.ts("pool", self.PD[:, l, 7:8], self.PD[:, l, 7:8], -1.0, ALU.mult, [self.PD], [self.PD])
            mk.barrier()

            self.phase0()
            for l in range(self.nl):
                self.layer(l)
            mk.barrier()

            sems = {}
            for nm in mk.E:
                sems[nm] = self.st.enter_context(nc.semaphore("s_" + nm))
            run = mk.runner(sems)
            with nc.Block() as block:
                @block.tensor
                def _(e):
                    run(e, mk.E["pe"])

                @block.vector
                def _(e):
                    run(e, mk.E["dve"])

                @block.scalar
                def _(e):
                    run(e, mk.E["act"])

                @block.gpsimd
                def _(e):
                    run(e, mk.E["pool"])

                @block.sync
                def _(e):
                    run(e, mk.E["sp"])
        return nc

    def ln_pools(self, sc, n):
        sc.pool("v", 1, [128, 8, n])
        sc.pool("lnvb", 1, [128, 8, n], BF)
        sc.pool("lnvq", 1, [128, 8, n], BF)
        sc.pool("lnt", 4, [128, n])
        sc.pool("lnd", 3, [128, n])

    def phase0(self):
        n = 256
        with Scope(self) as sc:
            sc.pool("xin", 1, [128, 2, D_MODEL])
            self.ln_pools(sc, n)
            segs = [(self.d["xp"], self.XT, self.T)]
            if self.with_sample:
                segs.append((self.d["xs"], self.XTs, NS * TS))
            for (xd, XT, TT) in segs:
                for t0 in range(0, TT, n):
                    nn = min(n, TT - t0)
                    XI = sc.get("xin")
                    nb = (nn + 127) // 128
                    bw = min(128, nn)
                    self.dma(XI[0:bw, 0:nb, :], xd[t0:t0 + nn, :].rearrange("(b p) f -> p b f", p=bw), [], [XI])
                    Vt = sc.get("v")
                    for c in range(8):
                        pp = self.ps()
                        for bb in range(nb):
                            self.tr(pp[:, bb * 128:bb * 128 + bw], XI[0:bw, bb, c * 128:(c + 1) * 128],
                                    self.cst("ident", bw, slice(0, bw)), [XI, self.CST], [pp])
                        self.cp("act" if c % 2 else "dve", Vt[:, c, 0:nn], pp[:, 0:nn], [pp], [Vt])
                    self.ln_fm(sc, Vt, nn, "ln_in_g", "ln_in_b",
                               out_bf=lambda c, t0=t0, nn=nn, XT=XT: (XT[:, c, t0:t0 + nn], XT))

    def phaseD(self, l):
        last = (l == L - 1)
        w3 = self.d["w_out"][l].rearrange("(k p) n -> p k n", p=128)
        self.load_w(w3, [(0, 1024, 0)])
        n = 256
        with Scope(self) as sc:
            self.ln_pools(sc, n)
            if last:
                sc.pool("yf", 1, [128, 8, n])
                sc.pool("yt", 2, [128, D_MODEL])
            segs = [(self.XT, self.OT, self.T, self.o["y_p"])]
            if self.with_sample:
                segs.append((self.XTs, self.OTs, NS * TS, self.o["y_s"]))
            for (XT, OT, TT, yd) in segs:
                for t0 in range(0, TT, n):
                    nn = min(n, TT - t0)
                    Vt = sc.get("v")
                    for c in range(8):
                        pp = self.ps()
                        for k in range(8):
                            self.mm(pp[:, 0:nn], self.WB[:, k, c * 128:(c + 1) * 128], OT[:, k, t0:t0 + nn],
                                    k == 0, k == 7, [self.WB, OT], [pp])
                        self.stt(Vt[:, c, 0:nn], XT[:, c, t0:t0 + nn], ALPHA, pp[:, 0:nn], ALU.mult, ALU.add,
                                 [XT, pp], [Vt])
                    if not last:
                        self.ln_fm(sc, Vt, nn, "lng%d" % l, "lnb%d" % l,
                                   out_bf=lambda c, t0=t0, nn=nn, XT=XT: (XT[:, c, t0:t0 + nn], XT))
                    else:
                        YF = sc.get("yf")
                        self.ln_fm(sc, Vt, nn, "lng%d" % l, "lnb%d" % l,
                                   out_f32=lambda c, nn=nn, YF=YF: (YF[:, c, 0:nn], YF))
                        bw = min(128, nn)
                        for bb in range((nn + 127) // 128):
                            YT = sc.get("yt")
                            for half in range(2):
                                pp = self.ps()
                                for cc in range(4):
                                    c = half * 4 + cc
                                    self.tr(pp[0:bw, cc * 128:(cc + 1) * 128], YF[:, c, bb * 128:bb * 128 + bw],
                                            self.cst("ident"), [YF, self.CST], [pp])
                                self.cp("act" if half else "dve", YT[0:bw, half * 512:(half + 1) * 512], pp[0:bw, :], [pp], [YT])
                            self.dma(yd[t0 + bb * 128:t0 + bb * 128 + bw, :], YT[0:bw, :], [YT], [])

    def phaseB(self, l):
        T = self.T
        NT = T // 512
        NB = T // 128
        w3 = self.d["w_in"][l].rearrange("(k p) n -> p k n", p=128)
        with Scope(self) as so:
            HL3 = so.sb("HL3", [128, T], BF)
            NCK = so.sb("NCK", [128, NB, 8])
            if self.with_sample:
                nkb_s = self.PAST // 128
                self.HL3s = so.sb("HL3s", [128, NS, TS], BF)
                self.NCKs = so.sb("NCKs", [128, NS, nkb_s + 1, 8])
            with Scope(self) as sc:
                WF = sc.sb("WF", [128, 8, 72], BF)
                LFT = sc.sb("LFT", [128, NB, 8])
                CAR = sc.sb("CAR", [128, 2])
                sc.pool("t", 6, [128, 512])
                sc.pool("tb", 3, [128, 512], BF)
                self.memset("pool", WF[:, :, :], 0.0, [WF])
                self.memset("pool", HL3[:, :], 0.0, [HL3])
                self.memset("pool", CAR[:, :], 0.0, [CAR])
                stg = self.get_ws()
                self.dma(stg[:, :, 0:8], w3[:, :, OFF["b_f"]:OFF["b_f"] + 8], [], [stg])
                for g in (0, 32, 64):
                    self.cp("pool", WF[:, :, g:g + 8], stg[:, :, 0:8], [stg], [WF])
                ones_b = self.cst("ones", 1, slice(0, 72)).to_broadcast([72, 512])
                for tt in range(NT):
                    t0 = tt * 512
                    pp = self.ps()
                    for k in range(8):
                        self.mm(pp[0:72, :], WF[:, k, :], self.XT[:, k, t0:t0 + 512], k == 0, k == 7, [WF, self.XT], [pp])
                    LS = sc.get("t")
                    self.act(LS[0:72, :], pp[0:72, :], AF.Exp, [pp, self.PD], [LS], bias=self.PD[0:72, l, 6:7], scale=-1.0)
                    self.act(LS[0:72, :], LS[0:72, :], AF.Ln, [LS], [LS], bias=1.0, scale=1.0)
                    CUMN = sc.get("t")
                    self.mk.op("dve", lambda e, CUMN=CUMN, LS=LS, tt=tt: e.tensor_tensor_scan(
                        out=CUMN[0:72, :], data0=ones_b, data1=LS[0:72, :], initial=CAR[0:72, tt % 2:tt % 2 + 1],
                        op0=ALU.mult, op1=ALU.add), self._rs([self.CST, LS, CAR]), self._rs([CUMN]))
                    self.cp("pool", CAR[0:72, (tt + 1) % 2:(tt + 1) % 2 + 1], CUMN[0:72, 511:512], [CUMN], [CAR])
                    HI = sc.get("tb")
                    self.ts("dve", HI[0:72, :], CUMN[0:72, :], -1.0, ALU.mult, [CUMN], [HI])
                    self.cp("pool", HL3[0:8, t0:t0 + 512], HI[0:8, :], [HI], [HL3])
                    R1 = sc.get("t")
                    self.stt(R1[0:72, :], CUMN[0:72, :], -1.0, HI[0:72, :], ALU.mult, ALU.subtract, [CUMN, HI], [R1])
                    MI = sc.get("tb")
                    self.cp("pool", MI[0:72, :], R1[0:72, :], [R1], [MI])
                    self.cp("pool", HL3[32:40, t0:t0 + 512], MI[32:40, :], [MI], [HL3])
                    R2 = sc.get("t")
                    self.tt("dve", R2[64:72, :], R1[64:72, :], MI[64:72, :], ALU.subtract, [R1, MI], [R2])
                    self.cp("pool", HL3[64:72, t0:t0 + 512], R2[64:72, :], [R2], [HL3])
                    p1 = self.ps()
                    p2 = self.ps()
                    for bb in range(4):
                        self.tr(p1[:, bb * 8:(bb + 1) * 8], CUMN[0:8, bb * 128:(bb + 1) * 128],
                                self.cst("ident", 8, slice(0, 8)), [CUMN, self.CST], [p1])
                        self.tr(p2[:, bb * 8:(bb + 1) * 8], LS[0:8, bb * 128:(bb + 1) * 128],
                                self.cst("ident", 8, slice(0, 8)), [LS, self.CST], [p2])
                    self.cp("dve", NCK[:, tt * 4:tt * 4 + 4, :], p1[:, 0:32].rearrange("p (b h) -> p b h", h=8), [p1], [NCK])
                    self.ts("dve", LFT[:, tt * 4:tt * 4 + 4, :], p2[:, 0:32].rearrange("p (b h) -> p b h", h=8), -1.0, ALU.mult, [p2], [LFT])
                for b0 in range(0, NB, 8):
                    self.dma(self.o["fl_p"][l, b0 * 128:(b0 + 8) * 128 if b0 + 8 <= NB else NB * 128, :].rearrange("(b p) h -> p b h", p=128),
                             LFT[:, b0:min(b0 + 8, NB), :], [LFT], [])
                if self.with_sample:
                    self.fox_sample_setup(l, sc, WF, so)
            for h in range(8 if BSTAGE >= 1 else 0):
                self.fox_head(l, h, so, HL3, NCK, w3)

    def fox_head(self, l, h, so, HL3, NCK, w3):
        T = self.T
        NT = T // 512
        NB = T // 128
        hp, hh = h // 2, h % 2
        self.load_w(w3, [(OFF["b_q"] + h * 64, 64, 0), (OFF["b_k"] + h * 64, 64, 64),
                         (OFF["b_v"] + h * 64, 64, 128), (OFF["b_z"] + h * 64, 64, 192)])
        with Scope(self) as sc:
            QA = sc.sb("QA", [128, T], BF)
            KA = sc.sb("KA", [128, T], BF)
            VA = sc.sb("VA", [128, NB, 128], BF)
            sc.pool("kvo", 1, [128, 4, 128])
            sc.pool("pt", 3, [128, 512], BF)
            sc.pool("t", 5, [128, 256])
            if "m" not in os.environ.get("SKIP", ""):
                self.memset("pool", QA[:, :], 0.0, [QA])
                self.memset("pool", KA[:, :], 0.0, [KA])
                self.memset("pool", KA[64:67, :], 1.0, [KA])
                self.memset("pool", VA[:, :, 64:128], 1.0, [VA])
            for i, g in enumerate((0, 32, 64)):
                if os.environ.get("NOSB2SB"):
                    continue
                self.dma(QA[64 + i:65 + i, :], HL3[g + h:g + h + 1, :], [HL3], [QA])
            SK = os.environ.get("SKIP", "")
            for tt in range(NT):
                t0 = tt * 512
                if "q" not in SK:
                    pq = self.ps()
                    self.proj_fm(pq[0:64, :], 0, 64, self.XT, t0, 512, pq)
                    self.act(QA[0:64, t0:t0 + 512], pq[0:64, :], AF.Identity, [pq], [QA], scale=0.125)
                if "k" not in SK:
                    pk = self.ps()
                    self.proj_fm(pk[0:64, :], 64, 64, self.XT, t0, 512, pk)
                    self.cp("dve", KA[0:64, t0:t0 + 512], pk[0:64, :], [pk], [KA])
                if "t" in SK:
                    continue
                KVO = sc.get("kvo")
                pkv = self.ps()
                for b in range(4):
                    for k in range(8):
                        self.mm(pkv[:, b * 128:(b + 1) * 128], self.XT[:, k, t0 + b * 128:t0 + (b + 1) * 128],
                                self.WB[:, k, 64:192], k == 0, k == 7, [self.XT, self.WB], [pkv])
                self.cp("act", KVO[:, :, :], pkv[:, :].rearrange("p (b c) -> p b c", c=128), [pkv], [KVO])
                if "v" not in SK:
                    self.cp("dve", VA[:, tt * 4:(tt + 1) * 4, 0:64], pkv[:, :].rearrange("p (b c) -> p b c", c=128)[:, :, 64:128], [pkv], [VA])
                if not os.environ.get("NOKVOUT"):
                    self.dma(self.o["fk_p"][l, t0:t0 + 512, h * 64:(h + 1) * 64].rearrange("(b p) c -> p b c", p=128),
                             KVO[:, :, 0:64], [KVO], [])
                    self.dma(self.o["fv_p"][l, t0:t0 + 512, h * 64:(h + 1) * 64].rearrange("(b p) c -> p b c", p=128),
                             KVO[:, :, 64:128], [KVO], [])
            for qt in range(NT if BSTAGE >= 2 else 0):
                q0 = qt * 512
                acc = self.ps("a")
                nkb = 4 * qt + 4
                sps = {}

                def issue(kb_):
                    c0_ = max(0, kb_ * 128 - q0)
                    sp_ = self.ps()
                    dg_ = kb_ * 128 >= q0
                    self.mm(sp_[:, c0_:512], KA[:, kb_ * 128:(kb_ + 1) * 128], QA[:, q0 + c0_:q0 + 512], True, not dg_, [KA, QA], [sp_])
                    if dg_:
                        self.mm(sp_[:, c0_:c0_ + 128], self.IDB, self.MNEG[:, :], False, True, [self.CB, self.MNEG], [sp_])
                    sps[kb_] = sp_
                for kb_ in range(min(2, nkb)):
                    issue(kb_)
                for kb in range(nkb):
                    c0 = max(0, kb * 128 - q0)
                    if kb + 2 < nkb:
                        issue(kb + 2)
                    sp = sps.pop(kb)
                    pt = sc.get("pt")
                    self.act(pt[:, c0:512], sp[:, c0:512], AF.Exp, [sp, NCK], [pt], bias=NCK[:, kb, h:h + 1], scale=1.0)
                    self.mm(acc[:, c0:512], VA[:, kb, :], pt[:, c0:512], kb == 0, kb == nkb - 1, [VA, pt], [acc])
                for hc in (0, 256):
                    RC = sc.get("t")
                    self.recip(RC[0:64, :], acc[64:128, hc:hc + 256], [acc], [RC])
                    ON = sc.get("t")
                    self.tt("dve", ON[0:64, :], acc[0:64, hc:hc + 256], RC[0:64, :], ALU.mult, [acc, RC], [ON])
                    pz = self.ps()
                    self.proj_fm(pz[0:64, 0:256], 192, 64, self.XT, q0 + hc, 256, pz)
                    E = sc.get("t")
                    self.act(E[0:64, :], pz[0:64, 0:256], AF.Exp, [pz], [E], scale=-1.0)
                    self.act(E[0:64, :], E[0:64, :], AF.Ln, [E], [E], bias=1.0)
                    R_ = sc.get("t")
                    self.act(R_[0:64, :], E[0:64, :], AF.Exp, [E], [R_], scale=-1.0)
                    self.tt("dve", R_[0:64, :], pz[0:64, 0:256], R_[0:64, :], ALU.mult, [pz, R_], [R_])
                    self.tt("dve", self.OT[hh * 64:(hh + 1) * 64, 2 + hp, q0 + hc:q0 + hc + 256], ON[0:64, :], R_[0:64, :], ALU.mult,
                            [ON, R_], [self.OT])
        if self.with_sample:
            self.fox_sample_head(l, h)

    def fox_sample_setup(self, l, sc, WF, so):
        PAST = self.PAST
        nkb = PAST // 128
        W = PAST + TS
        LSs = sc.sb("LSs", [128, W])
        CUMs = sc.sb("CUMs", [128, W])
        LFs = sc.sb("LFs", [TS, 8])
        self.memset("pool", LSs[:, :], 0.0, [LSs])
        self.memset("pool", self.HL3s[:, :, :], 0.0, [self.HL3s])
        ones_b = self.cst("ones", 1, slice(0, 72)).to_broadcast([72, W])
        for s in range(NS):
            for g in (0, 32, 64):
                self.dma(LSs[g:g + 8, 0:PAST], self.d["clf"][l, s], [], [LSs])
            self.ts("dve", LSs[0:72, 0:PAST], LSs[0:72, 0:PAST], -1.0, ALU.mult, [LSs], [LSs])
            pp = self.ps()
            for k in range(8):
                self.mm(pp[0:72, 0:TS], WF[:, k, :], self.XTs[:, k, s * TS:(s + 1) * TS], k == 0, k == 7, [WF, self.XTs], [pp])
            E = sc.get("t")
            self.act(E[0:72, 0:TS], pp[0:72, 0:TS], AF.Exp, [pp, self.PD], [E], bias=self.PD[0:72, l, 6:7], scale=-1.0)
            self.act(LSs[0:72, PAST:W], E[0:72, 0:TS], AF.Ln, [E], [LSs], bias=1.0, scale=1.0)
            self.scan(CUMs[0:72, :], ones_b, LSs[0:72, :], [self.CST, LSs], [CUMs])
            HI = sc.get("tb")
            self.ts("dve", HI[0:72, 0:TS], CUMs[0:72, PAST:W], -1.0, ALU.mult, [CUMs], [HI])
            self.cp("pool", self.HL3s[0:8, s, :], HI[0:8, 0:TS], [HI], [self.HL3s])
            R1 = sc.get("t")
            self.stt(R1[0:72, 0:TS], CUMs[0:72, PAST:W], -1.0, HI[0:72, 0:TS], ALU.mult, ALU.subtract, [CUMs, HI], [R1])
            MI = sc.get("tb")
            self.cp("pool", MI[0:72, 0:TS], R1[0:72, 0:TS], [R1], [MI])
            self.cp("pool", self.HL3s[32:40, s, :], MI[32:40, 0:TS], [MI], [self.HL3s])
            R2 = sc.get("t")
            self.tt("dve", R2[64:72, 0:TS], R1[64:72, 0:TS], MI[64:72, 0:TS], ALU.subtract, [R1, MI], [R2])
            self.cp("pool", self.HL3s[64:72, s, :], R2[64:72, 0:TS], [R2], [self.HL3s])
            p1 = self.ps()
            for kb in range(nkb):
                self.tr(p1[:, kb * 8:(kb + 1) * 8], CUMs[0:8, kb * 128:(kb + 1) * 128], self.cst("ident", 8, slice(0, 8)),
                        [CUMs, self.CST], [p1])
            self.tr(p1[0:TS, nkb * 8:(nkb + 1) * 8], CUMs[0:8, PAST:W], self.cst("ident", 8, slice(0, 8)), [CUMs, self.CST], [p1])
            self.cp("dve", self.NCKs[:, s, 0:nkb, :], p1[:, 0:nkb * 8].rearrange("p (b h) -> p b h", h=8), [p1], [self.NCKs])
            self.cp("dve", self.NCKs[0:TS, s, nkb, :], p1[0:TS, nkb * 8:(nkb + 1) * 8], [p1], [self.NCKs])
            p2 = self.ps()
            self.tr(p2[0:TS, 0:8], LSs[0:8, PAST:W], self.cst("ident", 8, slice(0, 8)), [LSs, self.CST], [p2])
            self.ts("dve", LFs[:, :], p2[0:TS, 0:8], -1.0, ALU.mult, [p2], [LFs])
            self.dma(self.o["fl_s"][l, s], LFs[:, :], [LFs], [])

    def fox_sample_head(self, l, h):
        PAST = self.PAST
        nkb = PAST // 128
        W = PAST + TS
        hp, hh = h // 2, h % 2
        with Scope(self) as sc:
            KAs = sc.sb("KAs", [128, W], BF)
            QAs = sc.sb("QAs", [128, TS], BF)
            VAs = sc.sb("VAs", [128, nkb + 1, 128], BF)
            sc.pool("kst", 2, [64, PAST])
            sc.pool("vst", 2, [128, nkb, 64])
            sc.pool("kvo", 2, [TS, 128])
            sc.pool("pt", 3, [128, TS], BF)
            sc.pool("t", 5, [64, TS])
            self.memset("pool", KAs[:, :], 0.0, [KAs])
            self.memset("pool", KAs[64:67, :], 1.0, [KAs])
            self.memset("pool", QAs[:, :], 0.0, [QAs])
            self.memset("pool", VAs[:, :, :], 0.0, [VAs])
            self.memset("pool", VAs[:, :, 64:128], 1.0, [VAs])
            for s in range(NS):
                s0 = s * TS
                KS = sc.get("kst")
                self.dma(KS[:, :], self.d["ckT"][l, s, h], [], [KS])
                self.cp("pool", KAs[0:64, 0:PAST], KS[:, :], [KS], [KAs])
                VS = sc.get("vst")
                self.dma(VS[:, :, :], self.d["cv"][l, s][:, h * 64:(h + 1) * 64].rearrange("(b p) c -> p b c", p=128), [], [VS])
                self.cp("pool", VAs[:, 0:nkb, 0:64], VS[:, :, :], [VS], [VAs])
                for i, g in enumerate((0, 32, 64)):
                    self.dma(QAs[64 + i:65 + i, :], self.HL3s[g + h:g + h + 1, s, :], [self.HL3s], [QAs])
                pq = self.ps()
                self.proj_fm(pq[0:64, 0:TS], 0, 64, self.XTs, s0, TS, pq)
                self.act(QAs[0:64, :], pq[0:64, 0:TS], AF.Identity, [pq], [QAs], scale=0.125)
                pk = self.ps()
                self.proj_fm(pk[0:64, 0:TS], 64, 64, self.XTs, s0, TS, pk)
                self.cp("dve", KAs[0:64, PAST:W], pk[0:64, 0:TS], [pk], [KAs])
                pkv = self.ps()
                for k in range(8):
                    self.mm(pkv[0:TS, 0:128], self.XTs[:, k, s0:s0 + TS], self.WB[:, k, 64:192], k == 0, k == 7,
                            [self.XTs, self.WB], [pkv])
                KVO = sc.get("kvo")
                self.cp("act", KVO[:, :], pkv[0:TS, 0:128], [pkv], [KVO])
                self.cp("pool", VAs[0:TS, nkb, 0:64], KVO[:, 64:128], [KVO], [VAs])
                self.dma(self.o["fk_s"][l, s, :, h * 64:(h + 1) * 64], KVO[:, 0:64], [KVO], [])
                self.dma(self.o["fv_s"][l, s, :, h * 64:(h + 1) * 64], KVO[:, 64:128], [KVO], [])
                acc = self.ps("a")
                sps = {}

                def issue(kb_):
                    kw_ = 128 if kb_ < nkb else TS
                    sp_ = self.ps()
                    self.mm(sp_[0:kw_, 0:TS], KAs[:, kb_ * 128:kb_ * 128 + kw_], QAs[:, :], True, kb_ < nkb, [KAs, QAs], [sp_])
                    if kb_ == nkb:
                        self.mm(sp_[0:TS, 0:TS], self.CB[0:TS, 0, 0:TS], self.MNEG[0:TS, 0:TS], False, True, [self.CB, self.MNEG], [sp_])
                    sps[kb_] = sp_
                for kb_ in range(min(2, nkb + 1)):
                    issue(kb_)
                for kb in range(nkb + 1):
                    kw = 128 if kb < nkb else TS
                    if kb + 2 < nkb + 1:
                        issue(kb + 2)
                    sp = sps.pop(kb)
                    pt = sc.get("pt")
                    self.act(pt[0:kw, :], sp[0:kw, 0:TS], AF.Exp, [sp, self.NCKs], [pt], bias=self.NCKs[0:kw, s, kb, h:h + 1], scale=1.0)
                    self.mm(acc[:, 0:TS], VAs[0:kw, kb, :], pt[0:kw, :], kb == 0, kb == nkb, [VAs, pt], [acc])
                RC = sc.get("t")
                self.recip(RC[:, :], acc[64:128, 0:TS], [acc], [RC])
                ON = sc.get("t")
                self.tt("dve", ON[:, :], acc[0:64, 0:TS], RC[:, :], ALU.mult, [acc, RC], [ON])
                pz = self.ps()
                self.proj_fm(pz[0:64, 0:TS], 192, 64, self.XTs, s0, TS, pz)
                E = sc.get("t")
                self.act(E[:, :], pz[0:64, 0:TS], AF.Exp, [pz], [E], scale=-1.0)
                self.act(E[:, :], E[:, :], AF.Ln, [E], [E], bias=1.0)
                R_ = sc.get("t")
                self.act(R_[:, :], E[:, :], AF.Exp, [E], [R_], scale=-1.0)
                self.tt("dve", R_[:, :], pz[0:64, 0:TS], R_[:, :], ALU.mult, [pz, R_], [R_])
                self.tt("dve", self.OTs[hh * 64:(hh + 1) * 64, 2 + hp, s0:s0 + TS], ON[:, :], R_[:, :], ALU.mult,
                        [ON, R_], [self.OTs])

    def delta_alloc(self, sc, n, rw):
        d = {}
        d["SC"] = [[sc.sb("SC%d_%d" % (i, hh), [128, 4 if rw else 2, 128], BF) for hh in range(2)] for i in range(2)]
        d["A"] = [[sc.sb("DA%d_%d" % (p_, i), [128, 2, 128], BF) for i in range(2)] for p_ in range(2)]
        d["X"] = [[sc.sb("DX%d_%d" % (p_, i), [128, 2, 128], BF) for i in range(2)] for p_ in range(2)]
        d["P"] = [[sc.sb("DP%d_%d" % (p_, i), [128, 2, 128], BF) for i in range(2)] for p_ in range(2)]
        d["TM"] = [sc.sb("TM%d" % i, [128, 4, 128], BF) for i in range(2)]
        d["VZ"] = [sc.sb("VZ%d" % i, [128, 2, 128], BF) for i in range(2)]
        d["WT"] = [sc.sb("WT%d" % i, [128, 128], BF) for i in range(2)]
        d["YB"] = [sc.sb("YB%d" % i, [128, 128], BF) for i in range(2)]
        d["UT"] = [sc.sb("UT%d" % i, [128, 128]) for i in range(2)]
        d["UP"] = sc.sb("UP", [128, 128], BF)
        d["UZ"] = sc.sb("UZ", [128, 2, 128], BF)
        d["H"] = sc.sb("H", [128, 128])
        d["HB"] = sc.sb("HB", [128, 128], BF)
        d["HB2"] = sc.sb("HB2", [128, 128], BF)
        d["hbi"] = 0
        d["i"] = 0
        for t in d["VZ"] + [d["UZ"], d["UP"], d["HB2"]]:
            self.memset("pool", t[:, :] if len(t.t.shape) == 2 else t[:, :, :], 0.0, [t])
        return d

    def delta_group(self, d, G, C, g0, masks, score_mms, tm_srcs, rw, Rf, OA, dec_ap, dec_res):
        gi = d["i"]
        d["i"] += 1
        ctx = self.delta_pre(d, gi, G, C, g0, masks, score_mms, tm_srcs, rw)
        self.delta_chunks(d, ctx, G, C, g0, rw, Rf, OA, dec_ap, dec_res)

    def delta_pre(self, d, gi, G, C, g0, masks, score_mms, tm_srcs, rw):
        nm = 4 if rw else 2
        zp = "z%d" % (gi % 2)
        SC = d["SC"][gi % 2]
        for hh in range(2):
            pp = self.ps(zp)
            rows = slice(hh * 64, hh * 64 + 64)
            for (Lf, R2, ncol, col0) in score_mms:
                if G == 128:
                    self.mm(pp[0:G, col0 * 128:(col0 + ncol) * 128].rearrange("p (m c) -> p m c", c=128)[:, :, 0:G],
                            Lf[rows, g0:g0 + G], R2[rows, 0:ncol, g0:g0 + G], True, True, [Lf, R2], [pp])
                else:
                    for m_ in range(ncol):
                        self.mm(pp[0:G, (col0 + m_) * 128:(col0 + m_) * 128 + G],
                                Lf[rows, g0:g0 + G], R2[rows, m_, g0:g0 + G], True, True, [Lf, R2], [pp])
            map_, mres = masks(hh)
            self.tt("dve", SC[hh][0:G, 0:nm, 0:G], pp[0:G, 0:nm * 128].rearrange("p (m c) -> p m c", c=128)[:, :, 0:G],
                    map_, ALU.mult, [pp] + mres, [SC[hh]])
        TM = d["TM"][gi % 2]
        VZ = d["VZ"][gi % 2]
        pt = self.psb(zp)
        for q, Ft in enumerate(tm_srcs):
            self.tr(pt[0:G, q * 128:(q + 1) * 128], Ft[:, g0:g0 + G], self.IDB, [Ft, self.CB], [pt])
        nq = len(tm_srcs)
        self.cp("act", TM[0:G, 0:nq, :], pt[0:G, 0:nq * 128].rearrange("p (q c) -> p q c", c=128), [pt], [TM])
        if rw:
            for hh in range(2):
                self.cp("pool", VZ[0:G, hh, hh * 64:hh * 64 + 64], TM[0:G, 1, hh * 64:hh * 64 + 64], [TM], [VZ])
        if rw:
            YB = d["YB"][gi % 2]
            py = self.ps(zp)
            for hh in range(2):
                self.mm(py[0:G, hh * 64:hh * 64 + 64], SC[hh][0:G, 2, 0:G], TM[0:G, 1, hh * 64:hh * 64 + 64], True, True,
                        [SC[hh], TM], [py])
            self.cp("dve", YB[0:G, :], py[0:G, 0:128], [py], [YB])
            usrc, ures = YB, YB
        pt2 = self.psb(zp)
        for hh in range(2):
            self.tr(pt2[0:G, hh * 128:hh * 128 + G], SC[hh][0:G, 0, 0:G], self.IDB[0:G, 0:G], [SC[hh], self.CB], [pt2])
        A = d["A"][gi % 2][0]
        self.cp("dve", A[0:G, :, 0:G], pt2[0:G, 0:256].rearrange("p (h c) -> p h c", c=128)[:, :, 0:G], [pt2], [A])
        P = d["P"][gi % 2][0]
        for hh in range(2):
            self.tt("pool", P[0:G, hh, 0:G], SC[hh][0:G, 0, 0:G], self.IDB[0:G, 0:G], ALU.add, [SC[hh], self.CB], [P])
        nsteps = int(round(math.log2(C))) - 1
        Xc = None
        ai, xi, pi = 0, 0, 0
        for i in range(nsteps):
            last = (i == nsteps - 1)
            pa = self.ps(zp)
            for hh in range(2):
                xl = SC[hh][0:G, 0, 0:G] if Xc is None else Xc[0:G, hh, 0:G]
                xr = SC[hh] if Xc is None else Xc
                self.mm(pa[0:G, hh * 128:hh * 128 + G], xl, A[0:G, hh, 0:G], True, True, [xr, A], [pa])
            if not last:
                px = self.ps(zp)
                for hh in range(2):
                    xl = SC[hh][0:G, 0, 0:G] if Xc is None else Xc[0:G, hh, 0:G]
                    xr = SC[hh] if Xc is None else Xc
                    self.mm(px[0:G, hh * 128:hh * 128 + G], A[0:G, hh, 0:G], xl, True, True, [xr, A], [px])
            ai += 1
            An = d["A"][gi % 2][ai % 2]
            self.cp("act", An[0:G, :, 0:G], pa[0:G, 0:256].rearrange("p (h c) -> p h c", c=128)[:, :, 0:G], [pa], [An])
            if not last:
                xi += 1
                Xn = d["X"][gi % 2][xi % 2]
                self.cp("dve", Xn[0:G, :, 0:G], px[0:G, 0:256].rearrange("p (h c) -> p h c", c=128)[:, :, 0:G], [px], [Xn])
                Xc = Xn
            A = An
            pq = self.ps(zp)
            for hh in range(2):
                self.mm(pq[0:G, hh * 128:hh * 128 + G], A[0:G, hh, 0:G], P[0:G, hh, 0:G], True, True, [A, P], [pq])
            pi += 1
            Pn = d["P"][gi % 2][pi % 2]
            self.tt("dve", Pn[0:G, :, 0:G], pq[0:G, 0:256].rearrange("p (h c) -> p h c", c=128)[:, :, 0:G], P[0:G, :, 0:G],
                    ALU.add, [pq, P], [Pn])
            P = Pn
        WT = d["WT"][gi % 2]
        pw = self.ps(zp)
        for hh in range(2):
            self.mm(pw[:, hh * 128:hh * 128 + G], TM[0:G, 0, :], P[0:G, hh, 0:G], True, True, [TM, P], [pw])
        self.cp("act", WT[0:64, 0:G], pw[0:64, 0:G], [pw], [WT])
        self.cp("act", WT[64:128, 0:G], pw[64:128, 128:128 + G], [pw], [WT])
        UT = d["UT"][gi % 2]
        pu = self.ps(zp)
        for hh in range(2):
            if rw:
                rhs = YB[0:G, hh * 64:hh * 64 + 64]
                rr = YB
            else:
                rhs = TM[0:G, 1, hh * 64:hh * 64 + 64]
                rr = TM
            self.mm(pu[0:G, hh * 64:hh * 64 + 64], P[0:G, hh, 0:G], rhs, True, True, [P, rr], [pu])
        self.cp("dve", UT[0:G, :], pu[0:G, 0:128], [pu], [UT])
        return dict(SC=SC, TM=TM, VZ=VZ, WT=WT, UT=UT)

    def delta_chunks(self, d, ctx, G, C, g0, rw, Rf, OA, dec_ap, dec_res):
        SC, TM, VZ, WT, UT = ctx["SC"], ctx["TM"], ctx["VZ"], ctx["WT"], ctx["UT"]
        H, UP, UZ = d["H"], d["UP"], d["UZ"]
        for ci in range(G // C):
            HB = d["HB"] if d["hbi"] % 2 == 0 else d["HB2"]
            HBn = d["HB2"] if d["hbi"] % 2 == 0 else d["HB"]
            d["hbi"] += 1
            cs = slice(ci * C, ci * C + C)
            tc0 = g0 + ci * C
            pU = self.ps()
            self.mm(pU[0:G, 0:128], WT[:, 0:G], HB[:, :], True, True, [WT, HB], [pU])
            if rw:
                self.tt("dve", UP[cs, :], pU[cs, 0:128], UT[cs, :], ALU.add, [pU, UT], [UP])
            else:
                self.tt("dve", UP[cs, :], UT[cs, :], pU[cs, 0:128], ALU.subtract, [pU, UT], [UP])
            pS = self.ps()
            self.mm(pS[:, 0:128], TM[cs, 2, :], UP[cs, :], True, not rw, [TM, UP], [pS])
            if rw:
                self.mm(pS[:, 0:128], TM[cs, 3, :], TM[cs, 1, :], False, True, [TM], [pS])
            dcol = dec_ap(tc0 + C - 1)
            for hh in range(2):
                rows = slice(hh * 64, hh * 64 + 64)
                cols = slice(hh * 64, hh * 64 + 64)
                self.stt(HBn[rows, cols], H[rows, cols], dcol[rows, :], pS[rows, cols], ALU.mult, ALU.add,
                         [H, pS] + dec_res, [HBn])
            for hh in range(2):
                self.cp("pool", UZ[cs, hh, hh * 64:hh * 64 + 64], UP[cs, hh * 64:hh * 64 + 64], [UP], [UZ])
            po = self.ps("a")
            self.mm(po[:, 0:C], HB[:, :], Rf[:, tc0:tc0 + C], True, False, [HB, Rf], [po])
            for hh in range(2):
                lastmm = (not rw) and hh == 1
                self.mm(po[:, 0:C], UZ[0:G, hh, :], SC[hh][0:G, 1, ci * C:ci * C + C], False, lastmm, [UZ, SC[hh]], [po])
            if rw:
                for hh in range(2):
                    self.mm(po[:, 0:C], VZ[0:G, hh, :], SC[hh][0:G, 3, ci * C:ci * C + C], False, hh == 1, [VZ, SC[hh]], [po])
            self.cp("act", OA[:, tc0:tc0 + C], po[:, 0:C], [po], [OA])
            for hh in range(2):
                rows = slice(hh * 64, hh * 64 + 64)
                cols = slice(hh * 64, hh * 64 + 64)
                self.stt(H[rows, cols], H[rows, cols], dcol[rows, :], pS[rows, cols], ALU.mult, ALU.add,
                         [H, pS] + dec_res, [H])

    def silu_mul(self, sc, n, zap, zres, Dn, out_ap, out_res, extra_scale=None, fpool="f"):
        E = sc.get(fpool)
        self.act(E[:, 0:n], zap, AF.Exp, zres, [E], scale=-1.0)
        self.act(E[:, 0:n], E[:, 0:n], AF.Ln, [E], [E], bias=1.0)
        R_ = sc.get(fpool)
        self.act(R_[:, 0:n], E[:, 0:n], AF.Exp, [E], [R_], scale=-1.0)
        self.tt("pool", R_[:, 0:n], R_[:, 0:n], zap, ALU.mult, [R_] + zres, [R_])
        if extra_scale is not None:
            self.stt(out_ap, Dn[:, 0:n], extra_scale, R_[:, 0:n], ALU.mult, ALU.mult, [Dn, R_, self.PV], out_res)
        else:
            self.tt("dve", out_ap, Dn[:, 0:n], R_[:, 0:n], ALU.mult, [Dn, R_], out_res)

    def zip_run(self, fns):
        if len(fns) == 1 or os.environ.get("NOZIP"):
            for f_ in fns:
                f_()
            return
        import threading
        n = len(fns)
        alive = [True] * n
        turn = [0]
        cv = threading.Condition()
        errs = []
        mk = self.mk

        def nxt(i):
            j = (i + 1) % n
            for _ in range(n):
                if alive[j]:
                    return j
                j = (j + 1) % n
            return -1

        def yp(i):
            with cv:
                turn[0] = nxt(i)
                cv.notify_all()
                while turn[0] != i:
                    cv.wait()

        def worker(i):
            with cv:
                while turn[0] != i:
                    cv.wait()
            try:
                mk.tl.hook = lambda: yp(i)
                fns[i]()
            except BaseException as e:
                errs.append(e)
            finally:
                mk.tl.hook = None
                with cv:
                    alive[i] = False
                    turn[0] = nxt(i)
                    cv.notify_all()
        ths = [threading.Thread(target=worker, args=(i,)) for i in range(n)]
        for t in ths:
            t.start()
        for t in ths:
            t.join()
        if errs:
            raise errs[0]

    def phaseA(self, l, pr):
        w3 = self.d["w_in"][l].rearrange("(k p) n -> p k n", p=128)
        a = OFF["a_sh"]
        self.load_w(w3, [(a + pr * 128, 128, 0), (a + 256 + pr * 128, 128, 128), (a + 512 + pr * 128, 128, 256),
                         (a + 768, 128, 384), (OFF["a_z"] + pr * 128, 128, 512)])
        n, G, C = 128, 128, 64
        NL = 2
        with Scope(self) as sc:
            W2B = sc.sb("W2B", [128, 128], BF)
            stg = self.get_ws()
            self.dma(stg[0:64, 0:2, :], self.d["w2"][l][:, pr * 128:(pr + 1) * 128].rearrange("p (a c) -> p a c", c=64), [], [stg])
            self.dma(stg[64:128, 0:2, :], self.d["a2"][l][:, pr * 128:(pr + 1) * 128].rearrange("p (a c) -> p a c", c=64), [], [stg])
            self.cp("pool", W2B[:, :].rearrange("p (a c) -> p a c", c=64), stg[:, 0:2, :], [stg], [W2B])
            LAST = sc.sb("LAST", [128, 4])
            M4 = sc.sb("M4", [128, 4, 128], BF)
            lanes = []
            for i in range(NL):
                ln = dict(i=i, PB=sc.sb("PB%d" % i, [128, 5, n + 1]), BR=sc.sb("BR%d" % i, [128, 2, n], BF),
                          OA=sc.sb("OA%d" % i, [128, n]), BON=sc.sb("BON%d" % i, [128, n]), PCt=sc.sb("PCt%d" % i, [128, n]),
                          AT=sc.sb("AT%d" % i, [128, n], BF), KTt=sc.sb("KTt%d" % i, [128, n], BF),
                          ADF=sc.sb("ADF%d" % i, [128, n], BF), KDF=sc.sb("KDF%d" % i, [128, n], BF),
                          VB=sc.sb("VB%d" % i, [128, n], BF), f="f%d" % i, b="b%d" % i)
                sc.pool(ln["f"], 12, [128, n])
                sc.pool(ln["b"], 5, [128, n], BF)
                lanes.append(ln)
            for m_, nm_ in enumerate(("msu", "miu", "msu", "miu")):
                self.cp("pool", M4[:, m_, :], self.cst(nm_), [self.CST], [M4])
            dd = self.delta_alloc(sc, n, True)
            self.rwkv_seq(l, pr, sc, dd, dict(n=n, G=G, C=C, nt=self.T // n, XT=self.XT, OT=self.OT, x0=0,
                                               prompt=True, o_sh=self.o["sh_p"][l], o_rw=self.o["rw_p"][l]),
                          lanes, LAST, M4, W2B)
            for s in range(NS if self.with_sample else 0):
                self.rwkv_seq(l, pr, sc, dd, dict(n=TS, G=TS, C=TS, nt=1, XT=self.XTs, OT=self.OTs, x0=s * TS,
                                                   prompt=False, s=s, o_sh=self.o["sh_s"][l, s], o_rw=self.o["rw_s"][l, s]),
                              lanes[s % NL:s % NL + 1], LAST, M4, W2B)

    def rwkv_seq(self, l, pr, sc, dd, sq, lanes, LAST, M4, W2B):
        n = sq["n"]
        H, HB = dd["H"], dd["HB"]
        dd["hbi"] = 0
        self.memset("pool", H[:, :], 0.0, [H])
        if sq["prompt"]:
            self.memset("pool", HB[:, :], 0.0, [HB])
            self.memset("pool", LAST[:, :], 0.0, [LAST])
        else:
            s = sq["s"]
            cols_ = (pr * 128, 256 + pr * 128, 512 + pr * 128, 768)
            for j in range(4):
                self.dma(LAST[:, j:j + 1], self.d["st_sh"][l, s, cols_[j]:cols_[j] + 128].rearrange("(p o) -> p o", o=1), [], [LAST])
            for hh in range(2):
                self.dma(H[hh * 64:hh * 64 + 64, hh * 64:hh * 64 + 64], self.d["st_rw"][l, s, 2 * pr + hh], [], [H])
            self.cp("pool", HB[:, :], H[:, :], [H], [HB])
        NL = len(lanes)
        for tt0 in range(0, sq["nt"], NL):
            tts = list(range(tt0, min(sq["nt"], tt0 + NL)))
            for i, tt in enumerate(tts):
                self.rwkv_proj(sq, lanes[i], tt, LAST)
            self.zip_run([(lambda i=i, tt=tt: self.rwkv_pre(l, pr, sc, sq, lanes[i], W2B)) for i, tt in enumerate(tts)])
            G_, C_ = sq["G"], sq["C"]
            ctxs = [None] * len(tts)
            gis = []
            for i in range(len(tts)):
                gis.append(dd["i"])
                dd["i"] += 1

            def mkpre(i):
                ln = lanes[i]

                def run():
                    ctxs[i] = self.delta_pre(dd, gis[i], G_, C_, 0,
                                             masks=lambda hh: (M4[0:G_, :, 0:G_], [M4]),
                                             score_mms=[(ln["AT"], ln["BR"], 2, 0), (ln["KTt"], ln["BR"], 2, 2)],
                                             tm_srcs=[_V(ln["BR"], 0), ln["VB"], ln["ADF"], ln["KDF"]], rw=True)
                return run
            self.zip_run([mkpre(i) for i in range(len(tts))])
            for i, tt in enumerate(tts):
                ln = lanes[i]
                self.delta_chunks(dd, ctxs[i], G_, C_, 0, True, _V(ln["BR"], 1), ln["OA"],
                                  lambda col, ln=ln: ln["PCt"][:, col:col + 1], [ln["PCt"]])
            self.zip_run([(lambda i=i, tt=tt: self.rwkv_post(l, pr, sc, sq, lanes[i], tt)) for i, tt in enumerate(tts)])
        o_sh, o_rw = sq["o_sh"], sq["o_rw"]
        cols = (pr * 128, 256 + pr * 128, 512 + pr * 128, 768)
        for j in range(4 if pr == 0 else 3):
            self.dma(o_sh[cols[j]:cols[j] + 128].rearrange("(p o) -> p o", o=1), LAST[:, j:j + 1], [LAST], [])
        HT = sc.get(lanes[0]["f"])
        for hh in range(2):
            pt = self.ps()
            rows = slice(hh * 64, hh * 64 + 64)
            self.tr(pt[0:64, 0:64], H[rows, hh * 64:hh * 64 + 64], self.cst("ident", 64, rows, hh * 64),
                    [H, self.CST], [pt])
            self.cp("dve", HT[0:64, hh * 64:hh * 64 + 64], pt[0:64, 0:64], [pt], [HT])
        for hh in range(2):
            self.dma(o_rw[2 * pr + hh], HT[0:64, hh * 64:hh * 64 + 64], [HT], [])

    def rwkv_proj(self, sq, ln, tt, LAST):
        n = sq["n"]
        t0 = sq["x0"] + tt * n
        PB = ln["PB"]
        self.cp("pool", PB[:, 0:4, 0], LAST[:, :], [LAST], [PB])
        for j in range(5):
            pp = self.ps()
            self.proj_fm(pp[:, 0:n], j * 128, 128, sq["XT"], t0, n, pp)
            self.cp("act" if j % 2 == 0 else "dve", PB[:, j, 1:n + 1], pp[:, 0:n], [pp], [PB])
        self.cp("pool", LAST[:, :], PB[:, 0:4, n], [PB], [LAST])

    def rwkv_pre(self, l, pr, sc, sq, ln, W2B):
        n, C = sq["n"], sq["C"]
        PB, BR, BON, PCt = ln["PB"], ln["BR"], ln["BON"], ln["PCt"]
        AT, KTt, ADF, KDF, VB = ln["AT"], ln["KTt"], ln["ADF"], ln["KDF"], ln["VB"]
        f = lambda: sc.get(ln["f"])
        b = lambda: sc.get(ln["b"])
        mu_idx = (pr, 2 + pr, 4 + pr, 6)
        for j in range(4):
            Dt = f()
            self.tt("pool", Dt[:, 0:n], PB[:, j, 0:n], PB[:, j, 1:n + 1], ALU.subtract, [PB], [Dt])
            self.stt(PB[:, j, 1:n + 1], Dt[:, 0:n], self.pvc("mu%d" % l, mu_idx[j]), PB[:, j, 1:n + 1], ALU.mult, ALU.add,
                     [Dt, PB, self.PV], [PB])
        r_, k_, v_ = PB[:, 0, 1:n + 1], PB[:, 1, 1:n + 1], PB[:, 2, 1:n + 1]
        E = f()
        self.act(E[0:64, 0:n], PB[0:64, 3, 1:n + 1], AF.Exp, [PB], [E], scale=-2.0)
        self.act(E[0:64, 0:n], E[0:64, 0:n], AF.Ln, [E], [E], bias=1.0)
        self.act(E[0:64, 0:n], E[0:64, 0:n], AF.Exp, [E], [E], scale=-1.0)
        TH = b()
        self.ts("dve", TH[0:64, 0:n], E[0:64, 0:n], 2.0, ALU.mult, [E], [TH], s2=-1.0, op1=ALU.add)
        self.cp("pool", TH[64:128, 0:n], PB[64:128, 3, 1:n + 1], [PB], [TH])
        pw = self.ps()
        self.mm(pw[:, 0:n], W2B[0:64, :], TH[0:64, 0:n], True, True, [W2B, TH], [pw])
        pa = self.ps()
        self.mm(pa[:, 0:n], W2B[64:128, :], TH[64:128, 0:n], True, True, [W2B, TH], [pa])
        SG = f()
        self.act(SG[:, 0:n], pw[:, 0:n], AF.Exp, [pw, self.PD], [SG], bias=self.PD[:, l, pr:pr + 1], scale=-1.0)
        self.act(SG[:, 0:n], SG[:, 0:n], AF.Ln, [SG], [SG], bias=1.0)
        self.act(SG[:, 0:n], SG[:, 0:n], AF.Exp, [SG], [SG], scale=-1.0)
        AA = f()
        self.act(AA[:, 0:n], pa[:, 0:n], AF.Exp, [pa, self.PD], [AA], bias=self.PD[:, l, 2 + pr:3 + pr], scale=-1.0)
        self.act(AA[:, 0:n], AA[:, 0:n], AF.Ln, [AA], [AA], bias=1.0)
        self.act(AA[:, 0:n], AA[:, 0:n], AF.Exp, [AA], [AA], scale=-1.0)
        KKu = f()
        self.ts("dve", KKu[:, 0:n], k_, self.pvc("kk%d" % l, pr), ALU.mult, [PB, self.PV], [KKu])
        KQ = b()
        self.act(KQ[:, 0:n], KKu[:, 0:n], AF.Square, [KKu], [KQ])
        pss = self.ps()
        self.mm(pss[:, 0:n], self.ONESBD, KQ[:, 0:n], True, True, [self.CB, KQ], [pss])
        RN = f()
        self.act(RN[:, 0:n], pss[:, 0:n], AF.Ln, [pss], [RN])
        self.act(RN[:, 0:n], RN[:, 0:n], AF.Exp, [RN], [RN], scale=-0.5)
        KKn = f()
        self.stt(KKn[:, 0:n], RN[:, 0:n], 1e12, KKu[:, 0:n], ALU.min, ALU.mult, [RN, KKu], [KKn])
        KM = f()
        self.ts("dve", KM[:, 0:n], AA[:, 0:n], self.pvc("ka%d" % l, pr), ALU.mult, [AA, self.PV, self.PD], [KM],
                s2=self.PD[:, l, 4 + pr:5 + pr], op1=ALU.add)
        self.tt("pool", KM[:, 0:n], KM[:, 0:n], k_, ALU.mult, [KM, PB], [KM])
        RK = f()
        self.tt("pool", RK[:, 0:n], r_, KM[:, 0:n], ALU.mult, [PB, KM], [RK])
        RKB = b()
        self.ts("dve", RKB[:, 0:n], RK[:, 0:n], self.pvc("rk%d" % l, pr), ALU.mult, [RK, self.PV], [RKB])
        pbs = self.ps()
        self.mm(pbs[:, 0:n], self.ONESBD, RKB[:, 0:n], True, True, [self.CB, RKB], [pbs])
        self.tt("dve", BON[:, 0:n], pbs[:, 0:n], v_, ALU.mult, [pbs, PB], [BON])
        LW = f()
        self.ts("dve", LW[:, 0:n], SG[:, 0:n], -math.exp(-0.5), ALU.mult, [SG], [LW])
        LC = f()
        self.scan(LC[:, 0:n], self.cst("reset", n), LW[:, 0:n], [self.CST, LW], [LC])
        LCX = f()
        self.tt("pool", LCX[:, 0:n], LC[:, 0:n], LW[:, 0:n], ALU.subtract, [LC, LW], [LCX])
        Pm = f()
        self.act(Pm[:, 0:n], LC[:, 0:n], AF.Exp, [LC], [Pm])
        self.act(LCX[:, 0:n], LCX[:, 0:n], AF.Exp, [LCX], [LCX])
        PINV = f()
        self.act(PINV[:, 0:n], LC[:, 0:n], AF.Exp, [LC], [PINV], scale=-1.0)
        KA_ = f()
        self.tt("pool", KA_[:, 0:n], KKn[:, 0:n], AA[:, 0:n], ALU.mult, [KKn, AA], [KA_])
        self.tt("dve", BR[:, 0, 0:n], KKn[:, 0:n], LCX[:, 0:n], ALU.mult, [KKn, LCX], [BR])
        self.tt("dve", BR[:, 1, 0:n], r_, Pm[:, 0:n], ALU.mult, [PB, Pm], [BR])
        self.stt(AT[:, 0:n], KA_[:, 0:n], -1.0, PINV[:, 0:n], ALU.mult, ALU.mult, [KA_, PINV], [AT])
        self.tt("pool", KTt[:, 0:n], KM[:, 0:n], PINV[:, 0:n], ALU.mult, [KM, PINV], [KTt])
        LCD = f()
        nch = n // C
        lc3 = LC[:, 0:n].rearrange("p (c t) -> p c t", t=C)
        self.tt("pool", LCD[:, 0:n].rearrange("p (c t) -> p c t", t=C), lc3[:, :, C - 1:C].to_broadcast([128, nch, C]), lc3,
                ALU.subtract, [LC], [LCD])
        self.act(LCD[:, 0:n], LCD[:, 0:n], AF.Exp, [LCD], [LCD])
        self.stt(ADF[:, 0:n], KA_[:, 0:n], -1.0, LCD[:, 0:n], ALU.mult, ALU.mult, [KA_, LCD], [ADF])
        self.tt("pool", KDF[:, 0:n], KM[:, 0:n], LCD[:, 0:n], ALU.mult, [KM, LCD], [KDF])
        self.cp("pool", VB[:, 0:n], v_, [PB], [VB])
        self.cp("pool", PCt[:, 0:n], Pm[:, 0:n], [Pm], [PCt])

    def rwkv_post(self, l, pr, sc, sq, ln, tt):
        n = sq["n"]
        t0 = sq["x0"] + tt * n
        PB, OA, BON = ln["PB"], ln["OA"], ln["BON"]
        f = lambda: sc.get(ln["f"])
        b = lambda: sc.get(ln["b"])
        z_ = PB[:, 4, 1:n + 1]
        OB = b()
        self.cp("act", OB[:, 0:n], OA[:, 0:n], [OA], [OB])
        OQ = b()
        self.act(OQ[:, 0:n], OA[:, 0:n], AF.Square, [OA], [OQ])
        p1 = self.ps()
        self.mm(p1[:, 0:n], self.ONESBD, OB[:, 0:n], True, True, [self.CB, OB], [p1])
        p2 = self.ps()
        self.mm(p2[:, 0:n], self.ONESBD, OQ[:, 0:n], True, True, [self.CB, OQ], [p2])
        ME = f()
        self.ts("dve", ME[:, 0:n], p1[:, 0:n], 1.0 / 64, ALU.mult, [p1], [ME])
        MS = f()
        self.tt("pool", MS[:, 0:n], ME[:, 0:n], ME[:, 0:n], ALU.mult, [ME], [MS])
        VA_ = f()
        self.stt(VA_[:, 0:n], p2[:, 0:n], 1.0 / 64, MS[:, 0:n], ALU.mult, ALU.subtract, [p2, MS], [VA_])
        self.act(VA_[:, 0:n], VA_[:, 0:n], AF.Ln, [VA_, self.EPS], [VA_], bias=self.EPS[:, 1:2])
        self.act(VA_[:, 0:n], VA_[:, 0:n], AF.Exp, [VA_], [VA_], scale=-0.5)
        Dn = f()
        self.tt("pool", Dn[:, 0:n], OA[:, 0:n], ME[:, 0:n], ALU.subtract, [OA, ME], [Dn])
        self.tt("dve", Dn[:, 0:n], Dn[:, 0:n], VA_[:, 0:n], ALU.mult, [Dn, VA_], [Dn])
        self.act(Dn[:, 0:n], Dn[:, 0:n], AF.Identity, [Dn, self.PV], [Dn], bias=self.pvc("gnb%d" % l, pr),
                 scale=self.pvc("gng%d" % l, pr))
        self.tt("pool", Dn[:, 0:n], Dn[:, 0:n], BON[:, 0:n], ALU.add, [Dn, BON], [Dn])
        self.silu_mul(sc, n, z_, [PB], Dn, sq["OT"][:, pr, t0:t0 + n], [sq["OT"]], fpool=ln["f"])

    def phaseC(self, l, pr):
        w3 = self.d["w_in"][l].rearrange("(k p) n -> p k n", p=128)
        c = OFF["c_qkv"]
        self.memset("pool", self.WB[:, :, 512:576], 0.0, [self.WB])
        self.load_w(w3, [(c + pr * 128, 128, 0), (c + 256 + pr * 128, 128, 128), (c + 512 + pr * 128, 128, 256),
                         (OFF["c_z"] + pr * 128, 128, 384), (OFF["c_b"], 4, 512), (OFF["c_a"], 4, 544)])
        n, G, C = 256, 128, 64
        with Scope(self) as sc:
            PC = sc.sb("PC", [128, 3, n + 3])
            KQ = sc.sb("KQ", [128, 2, n], BF)
            OA = sc.sb("OA", [128, n])
            EGCb = sc.sb("EGCb", [128, n])
            GC = sc.sb("GC", [128, n])
            GCT = [sc.sb("GCT%d" % i_, [128, 64]) for i_ in range(2)]
            DMM = [[sc.sb("DMM%d_%d" % (i, hh), [128, 2, 128]) for hh in range(2)] for i in range(2)]
            NMSU = sc.sb("NMSU", [128, 128])
            self.gd_long = [sc.sb("GL%d" % i, [128, n]) for i in range(4)]
            sc.pool("f", 9, [128, n])
            sc.pool("b", 8, [128, n], BF)
            sc.pool("df", 2, [128, 2, 128])
            self.ts("pool", NMSU[:, :], self.cst("msu"), -1.0, ALU.mult, [self.CST], [NMSU])
            dd = self.delta_alloc(sc, n, False)
            self.gdn_seq(l, pr, sc, dd, dict(n=n, G=G, C=C, nt=self.T // n, XT=self.XT, OT=self.OT, x0=0, prompt=True,
                                              o_cv=self.o["cv_p"][l], o_gd=self.o["gd_p"][l]),
                         PC, KQ, OA, EGCb, GC, GCT, DMM, NMSU)
            for s in range(NS if self.with_sample else 0):
                self.gdn_seq(l, pr, sc, dd, dict(n=TS, G=TS, C=TS, nt=1, XT=self.XTs, OT=self.OTs, x0=s * TS, prompt=False,
                                                  s=s, o_cv=self.o["cv_s"][l, s], o_gd=self.o["gd_s"][l, s]),
                             PC, KQ, OA, EGCb, GC, GCT, DMM, NMSU)

    def gdn_seq(self, l, pr, sc, dd, sq, PC, KQ, OA, EGCb, GC, GCT, DMM, NMSU):
        n, G, C = sq["n"], sq["G"], sq["C"]
        XT, OT = sq["XT"], sq["OT"]
        H, HB = dd["H"], dd["HB"]
        dd["hbi"] = 0
        self.memset("pool", H[:, :], 0.0, [H])
        self.memset("pool", PC[:, :, :], 0.0, [PC])
        if sq["prompt"]:
            self.memset("pool", HB[:, :], 0.0, [HB])
        else:
            s = sq["s"]
            for j in range(3):
                jj = 2 * j + pr
                self.dma(PC[:, j, 0:3], self.d["st_cv"][l, s][:, jj * 128:(jj + 1) * 128].rearrange("i p -> p i"), [], [PC],
                         allow_slow_non_contiguous=True)
            for hh in range(2):
                self.dma(H[hh * 64:hh * 64 + 64, hh * 64:hh * 64 + 64], self.d["st_gd"][l, s, 2 * pr + hh], [], [H])
            self.cp("pool", HB[:, :], H[:, :], [H], [HB])
        for tt in range(sq["nt"]):
            t0 = sq["x0"] + tt * n
            f = lambda: sc.get("f")
            b = lambda: sc.get("b")
            if tt > 0:
                CR = f()
                self.cp("pool", CR[:, 0:9].rearrange("p (j i) -> p j i", i=3), PC[:, :, n:n + 3], [PC], [CR])
                self.cp("pool", PC[:, :, 0:3], CR[:, 0:9].rearrange("p (j i) -> p j i", i=3), [CR], [PC])
            for j in range(3):
                pp = self.ps()
                self.proj_fm(pp[:, 0:n], j * 128, 128, XT, t0, n, pp)
                self.cp("act" if j % 2 == 0 else "dve", PC[:, j, 3:n + 3], pp[:, 0:n], [pp], [PC])
            PZ = self.gd_long[3]
            pp = self.ps()
            self.proj_fm(pp[:, 0:n], 384, 128, XT, t0, n, pp)
            self.cp("act", PZ[:, 0:n], pp[:, 0:n], [pp], [PZ])
            BA = f()
            pp = self.ps()
            self.proj_fm(pp[0:64, 0:n], 512, 64, XT, t0, n, pp)
            self.cp("dve", BA[0:64, 0:n], pp[0:64, 0:n], [pp], [BA])
            Y = []
            for j in range(3):
                jj = 2 * j + pr
                Yj = self.gd_long[j]
                self.ts("dve", Yj[:, 0:n], PC[:, j, 3:n + 3], self.pvc("cw%d" % l, 3 * 6 + jj), ALU.mult, [PC, self.PV], [Yj])
                for i in (2, 1, 0):
                    self.stt(Yj[:, 0:n], PC[:, j, i:i + n], self.pvc("cw%d" % l, i * 6 + jj), Yj[:, 0:n], ALU.mult, ALU.add,
                             [PC, self.PV, Yj], [Yj])
                E = f()
                self.act(E[:, 0:n], Yj[:, 0:n], AF.Exp, [Yj], [E], scale=-1.0)
                self.act(E[:, 0:n], E[:, 0:n], AF.Ln, [E], [E], bias=1.0)
                self.act(E[:, 0:n], E[:, 0:n], AF.Exp, [E], [E], scale=-1.0)
                self.tt("pool", Yj[:, 0:n], Yj[:, 0:n], E[:, 0:n], ALU.mult, [Yj, E], [Yj])
                Y.append(Yj)
            q_, k_, v_ = Y
            for (src_, scl) in ((q_, 0.125), (k_, 1.0)):
                Q2 = b()
                self.act(Q2[:, 0:n], src_[:, 0:n], AF.Square, [src_], [Q2])
                pss = self.ps()
                self.mm(pss[:, 0:n], self.ONESBD, Q2[:, 0:n], True, True, [self.CB, Q2], [pss])
                RN = f()
                self.act(RN[:, 0:n], pss[:, 0:n], AF.Ln, [pss, self.EPS], [RN], bias=self.EPS[:, 3:4])
                self.act(RN[:, 0:n], RN[:, 0:n], AF.Exp, [RN], [RN], scale=-0.5)
                self.stt(src_[:, 0:n], src_[:, 0:n], scl, RN[:, 0:n], ALU.mult, ALU.mult, [src_, RN], [src_])
            BG = f()
            self.act(BG[0:64, 0:n], BA[0:64, 0:n], AF.Exp, [BA], [BG], scale=-1.0)
            self.act(BG[0:64, 0:n], BG[0:64, 0:n], AF.Ln, [BG], [BG], bias=1.0)
            self.act(BG[0:64, 0:n], BG[0:64, 0:n], AF.Exp, [BG], [BG], scale=-1.0)
            SP = f()
            self.act(SP[0:64, 0:n], BA[0:64, 0:n], AF.Exp, [BA, self.PV], [SP], bias=self.pvc("dtb%d" % l, 0, slice(0, 64)))
            self.act(SP[0:64, 0:n], SP[0:64, 0:n], AF.Ln, [SP], [SP], bias=1.0)
            self.ts("dve", SP[0:64, 0:n], SP[0:64, 0:n], self.PD[0:64, l, 7:8], ALU.mult, [SP, self.PD], [SP])
            self.scan(GC[0:64, 0:n], self.cst("reset", n, slice(0, 64)), SP[0:64, 0:n], [self.CST, SP], [GC])
            EGC = f()
            self.act(EGC[0:64, 0:n], GC[0:64, 0:n], AF.Exp, [GC], [EGC])
            EGD = f()
            nch = n // C
            gc3 = GC[0:64, 0:n].rearrange("p (c t) -> p c t", t=C)
            self.tt("pool", EGD[0:64, 0:n].rearrange("p (c t) -> p c t", t=C), gc3[:, :, C - 1:C].to_broadcast([64, nch, C]), gc3,
                    ALU.subtract, [GC], [EGD])
            self.act(EGD[0:64, 0:n], EGD[0:64, 0:n], AF.Exp, [EGD], [EGD])
            pb_ = self.ps()
            self.mm(pb_[:, 0:n], self.cst("selb", 128, slice(0, 64), pr * 128), BG[0:64, 0:n], True, True, [self.CST, BG], [pb_])
            BETb = f()
            self.cp("act", BETb[:, 0:n], pb_[:, 0:n], [pb_], [BETb])
            pg_ = self.ps()
            self.mm(pg_[:, 0:n], self.cst("selg", 128, slice(0, 64), pr * 128), EGC[0:64, 0:n], True, True, [self.CST, EGC], [pg_])
            self.cp("act", EGCb[:, 0:n], pg_[:, 0:n], [pg_], [EGCb])
            pd_ = self.ps()
            self.mm(pd_[:, 0:n], self.cst("selg", 128, slice(0, 64), pr * 128), EGD[0:64, 0:n], True, True, [self.CST, EGD], [pd_])
            KD = b()
            self.tt("dve", KD[:, 0:n], k_[:, 0:n], pd_[:, 0:n], ALU.mult, [k_, pd_], [KD])
            KBf = f()
            self.tt("pool", KBf[:, 0:n], k_[:, 0:n], BETb[:, 0:n], ALU.mult, [k_, BETb], [KBf])
            self.cp("pool", KQ[:, 0, 0:n], KBf[:, 0:n], [KBf], [KQ])
            self.cp("pool", KQ[:, 1, 0:n], q_[:, 0:n], [q_], [KQ])
            QG = b()
            self.tt("dve", QG[:, 0:n], q_[:, 0:n], EGCb[:, 0:n], ALU.mult, [q_, EGCb], [QG])
            KBG = b()
            self.tt("dve", KBG[:, 0:n], KBf[:, 0:n], EGCb[:, 0:n], ALU.mult, [KBf, EGCb], [KBG])
            VBt = b()
            self.tt("pool", VBt[:, 0:n], v_[:, 0:n], BETb[:, 0:n], ALU.mult, [v_, BETb], [VBt])
            Kf = b()
            self.cp("pool", Kf[:, 0:n], k_[:, 0:n], [k_], [Kf])
            ng = n // G
            gis = []
            for g in range(ng):
                gis.append(dd["i"])
                dd["i"] += 1
            ctxs = [None] * ng

            def mkpre(g):
                g0 = g * G
                gi = gis[g]
                DMg = DMM[gi % 2]
                GCTg = GCT[gi % 2]

                def run():
                    pT = self.ps("z%d" % (gi % 2))
                    self.tr(pT[0:G, 0:64], GC[0:64, g0:g0 + G], self.cst("ident", 64, slice(0, 64)), [GC, self.CST], [pT])
                    self.cp("dve", GCTg[0:G, :], pT[0:G, 0:64], [pT], [GCTg])
                    pR = self.ps("z%d" % (gi % 2))
                    for hh in range(2):
                        h = 2 * pr + hh
                        self.mm(pR[0:G, hh * 128:hh * 128 + G], self.cst("selh", G, slice(0, 64), h * 128), GC[0:64, g0:g0 + G],
                                True, True, [self.CST, GC], [pR])
                    DF = sc.get("df")
                    for hh in range(2):
                        h = 2 * pr + hh
                        self.stt(DF[0:G, hh, 0:G], pR[0:G, hh * 128:hh * 128 + G], GCTg[0:G, 32 + h:33 + h],
                                 self.cst("miu", G, slice(0, G)), ALU.subtract, ALU.mult, [pR, GCTg, self.CST], [DF])
                    self.act(DF[0:G, :, 0:G], DF[0:G, :, 0:G], AF.Exp, [DF], [DF])
                    for hh in range(2):
                        self.tt("pool", DMg[hh][0:G, 0, 0:G], DF[0:G, hh, 0:G], NMSU[0:G, 0:G], ALU.mult, [DF, NMSU], [DMg[hh]])
                        self.tt("pool", DMg[hh][0:G, 1, 0:G], DF[0:G, hh, 0:G], self.cst("miu", G, slice(0, G)), ALU.mult,
                                [DF, self.CST], [DMg[hh]])
                    ctxs[g] = self.delta_pre(dd, gi, G, C, g0,
                                             masks=lambda hh: (DMg[hh][0:G, :, 0:G], [DMg[hh]]),
                                             score_mms=[(Kf, KQ, 2, 0)], tm_srcs=[KBG, VBt, KD], rw=False)
                return run
            self.zip_run([mkpre(g) for g in range(ng)])
            for g in range(ng):
                self.delta_chunks(dd, ctxs[g], G, C, g * G, False, QG, OA,
                                  lambda col: EGCb[:, col:col + 1], [EGCb])
            OQ = b()
            self.act(OQ[:, 0:n], OA[:, 0:n], AF.Square, [OA], [OQ])
            p2 = self.ps()
            self.mm(p2[:, 0:n], self.ONESBD, OQ[:, 0:n], True, True, [self.CB, OQ], [p2])
            RS = f()
            self.act(RS[:, 0:n], p2[:, 0:n], AF.Ln, [p2, self.EPS], [RS], bias=self.EPS[:, 2:3], scale=1.0 / 64)
            self.act(RS[:, 0:n], RS[:, 0:n], AF.Exp, [RS], [RS], scale=-0.5)
            Dn = f()
            self.tt("dve", Dn[:, 0:n], OA[:, 0:n], RS[:, 0:n], ALU.mult, [OA, RS], [Dn])
            self.silu_mul(sc, n, PZ[:, 0:n], [PZ], Dn, OT[:, 6 + pr, t0:t0 + n], [OT], extra_scale=self.pvc("gnorm%d" % l))
        o_cv, o_gd = sq["o_cv"], sq["o_gd"]
        for j in range(3):
            jj = 2 * j + pr
            self.dma(o_cv[:, jj * 128:(jj + 1) * 128].rearrange("i p -> p i"), PC[:, j, n:n + 3], [PC], [],
                     allow_slow_non_contiguous=True)
        for hh in range(2):
            self.dma(o_gd[2 * pr + hh], H[hh * 64:hh * 64 + 64, hh * 64:hh * 64 + 64], [H], [])

    def layer(self, l):
        if not getattr(self, "ot_zeroed", False):
            self.memset("pool", self.OT[:, :, :], 0.0, [self.OT])
            self.memset("pool", self.OTs[:, :, :], 0.0, [self.OTs])
            self.ot_zeroed = True
        if "A" in self.phases:
            for pr in range(2):
                self.phaseA(l, pr)
        if "C" in self.phases:
            for pr in range(2):
                self.phaseC(l, pr)
        if "B" in self.phases:
            self.phaseB(l)
        self.phaseD(l)


_CACHE = {}


def get_nc(T, PAST, **kw):
    key = (T, PAST, tuple(sorted(kw.items())))
    if key not in _CACHE:
        kb = KB(T, PAST, **kw)
        _CACHE[key] = (kb.build(), kb)
    return _CACHE[key]


def kernel(**inp):
    inp = {k: np.asarray(v) for k, v in inp.items()}
    B, T, _ = inp["x_prompt"].shape
    PAST = inp["cache_fox_k"].shape[2]
    nc, kb = get_nc(T, PAST, with_sample=not os.environ.get("NOSAMPLE"))
    pv = make_pv(inp)
    cst = make_cst()
    in_maps = []
    for c in range(NCORE):
        m = {"xp": np.ascontiguousarray(inp["x_prompt"][c]), "w_in": inp["w_in"], "w_out": inp["w_out"],
             "rwkv_w2": inp["rwkv_w2"], "rwkv_a2": inp["rwkv_a2"], "pv": pv, "cst": cst}
        if kb.with_sample:
            ss = slice(c * NS, (c + 1) * NS)
            m["xs"] = np.ascontiguousarray(inp["x_sample"][ss]).reshape(NS * TS, D_MODEL)
            m["ckT"] = np.ascontiguousarray(inp["cache_fox_k"][:, ss].transpose(0, 1, 3, 4, 2))
            m["cv"] = np.ascontiguousarray(inp["cache_fox_v"][:, ss]).reshape(L, NS, PAST, 512)
            m["clf"] = np.ascontiguousarray(inp["cache_fox_logf"][:, ss].transpose(0, 1, 3, 2))
            m["st_sh"] = np.ascontiguousarray(inp["state_rwkv_shift"][:, ss])
            m["st_rw"] = np.ascontiguousarray(inp["state_rwkv_wkv"][:, ss].transpose(0, 1, 2, 4, 3))
            m["st_cv"] = np.ascontiguousarray(inp["state_gdn_conv"][:, ss])
            m["st_gd"] = np.ascontiguousarray(inp["state_gdn_wkv"][:, ss])
        in_maps.append(m)
    if os.environ.get("KTRACE"):
        res = run_bass_kernel_spmd(nc, in_maps, core_ids=list(range(NCORE)), trace=True)
        print("EXEC_TIME_NS", res.exec_time_ns)
    else:
        res = run_bass_kernel_spmd(nc, in_maps, core_ids=list(range(NCORE)))
    R = res.results
    SB = NCORE * NS

    def gp(name, shape_tail, axis_layer=True):
        if name not in R[0]:
            return None
        return np.stack([np.asarray(R[c][name]) for c in range(NCORE)], axis=1)

    y_p = np.stack([R[c]["y_p"] for c in range(NCORE)])
    fk_p = gp("fk_p", None).reshape(L, NCORE, T, 8, 64)
    fv_p = gp("fv_p", None).reshape(L, NCORE, T, 8, 64)
    fl_p = gp("fl_p", None)
    sh_p = gp("sh_p", None)
    rw_p = gp("rw_p", None)
    cv_p = gp("cv_p", None)
    gd_p = gp("gd_p", None)

    def gs(name, tail):
        if name not in R[0]:
            return np.zeros((L, SB) + tail, np.float32)
        a = np.stack([np.asarray(R[c][name]) for c in range(NCORE)], axis=1)
        return a.reshape((L, SB) + tail)
    if "y_s" in R[0]:
        y_s = np.stack([R[c]["y_s"] for c in range(NCORE)]).reshape(SB, TS, D_MODEL)
    else:
        y_s = np.zeros((SB, TS, D_MODEL), np.float32)
    fk_s = gs("fk_s", (TS, 8, 64))
    fv_s = gs("fv_s", (TS, 8, 64))
    fl_s = gs("fl_s", (TS, 8))
    sh_s = gs("sh_s", (896,))
    rw_s = gs("rw_s", (4, 64, 64))
    cv_s = gs("cv_s", (3, 768))
    gd_s = gs("gd_s", (4, 64, 64))
    return (y_p, y_s, fk_p, fv_p, fl_p, sh_p, rw_p, cv_p, gd_p, fk_s, fv_s, fl_s, sh_s, rw_s, cv_s, gd_s)
```
